# Optimizing a Trainium2 kernel written in Bass

```python
import jax, jax.numpy as jnp
from jax import lax
import numpy as np

D_MODEL = 1024
BATCH = 8
SEQ = 2048
DEPTH = 2

MIX_WIDTH = D_MODEL // 2
N_BRANCH = 3
A_HEAD_DIM = 64
A_HEADS = MIX_WIDTH // A_HEAD_DIM
A_WIDTH = A_HEADS * A_HEAD_DIM
A_RANK_W = 64
A_RANK_A = 64
A_RANK_G = 128
A_GN_EPS = 64e-5
B_KEY_DIM = 128
B_VAL_DIM = 128
B_HEADS = MIX_WIDTH // B_VAL_DIM
B_WIDTH = B_HEADS * B_VAL_DIM
B_CHUNK = 64
C_HEAD_DIM = 64
C_HEADS = MIX_WIDTH // C_HEAD_DIM
C_KV_HEADS = 2
C_WIDTH = C_HEADS * C_HEAD_DIM
IDX_HEADS = 4
IDX_DIM = C_HEAD_DIM
TOPK_MAX = 256
Q_BLOCK = 128
ROPE_THETA = 10000.0
D_FF = 2816
CONV_WIDTH = 3
NORM_EPS = 1e-6

A_COLS = 3 * A_WIDTH + A_RANK_W + A_RANK_A + A_RANK_G
B_COLS = 4 * B_WIDTH
C_COLS = C_WIDTH + 2 * C_KV_HEADS * C_HEAD_DIM + IDX_HEADS * IDX_DIM + IDX_DIM + IDX_HEADS
GATE_COLS = N_BRANCH * D_MODEL
IN_COLS = A_COLS + B_COLS + C_COLS + GATE_COLS

kernel_name = 'hybrid_rwkv7_hgrn2_dsa_block'

F32 = jnp.float32


def _split(t, sizes):
    out, start = [], 0
    for s in sizes:
        out.append(t[..., start:start + s])
        start += s
    return out


def rms_norm(x, g):
    xf = x.astype(F32)
    y = xf * lax.rsqrt(jnp.mean(xf * xf, axis=-1, keepdims=True) + NORM_EPS)
    return (y * g.astype(F32)).astype(x.dtype)


def token_shift(p):
    return jnp.pad(p, ((0, 0), (1, 0), (0, 0)))[:, :-1]


def rope_tables(seq, dim):
    half = dim // 2
    inv_freq = ROPE_THETA ** (-jnp.arange(half, dtype=F32) * (2.0 / dim))
    ang = jnp.arange(seq, dtype=F32)[:, None] * inv_freq[None, :]
    return jnp.cos(ang), jnp.sin(ang)


def apply_rope(t, cos, sin):
    half = t.shape[-1] // 2
    tf = t.astype(F32)
    t1, t2 = tf[..., :half], tf[..., half:]
    c, s = cos[None, :, None, :], sin[None, :, None, :]
    return jnp.concatenate([t1 * c - t2 * s, t2 * c + t1 * s], axis=-1).astype(t.dtype)


def rwkv7_scan(r, w, k, v, a, b):
    bsz, _, heads, n = r.shape

    def step(state, inp):
        r_t, w_t, k_t, v_t, a_t, b_t = inp
        sa = jnp.einsum('bhij,bhj->bhi', state, a_t)
        state = (state * w_t[:, :, None, :] + sa[..., None] * b_t[:, :, None, :]
                 + v_t[..., None] * k_t[:, :, None, :])
        return state, jnp.einsum('bhij,bhj->bhi', state, r_t)

    xs = tuple(jnp.moveaxis(t, 1, 0) for t in (r, w, k, v, a, b))
    _, ys = lax.scan(step, jnp.zeros((bsz, heads, n, n), F32), xs)
    return jnp.moveaxis(ys, 0, 1)


def rwkv7_branch(p, mu, w0, w_up, a0, a_up, g_up, k_k, k_a, r_k, gn_g, gn_b):
    bsz, seq, _ = p.shape
    p = p + (token_shift(p) - p) * mu
    r, k, v, wd, ad, gd = _split(p, (A_WIDTH, A_WIDTH, A_WIDTH, A_RANK_W, A_RANK_A, A_RANK_G))
    w_log = -jax.nn.softplus(-(w0 + jnp.tanh(wd) @ w_up).astype(F32)) - 0.5
    decay = jnp.exp(-jnp.exp(w_log))
    a = jax.nn.sigmoid((a0 + ad @ a_up).astype(F32))
    g = jax.nn.sigmoid(gd) @ g_up

    def hd(t):
        return t.astype(F32).reshape(bsz, seq, A_HEADS, A_HEAD_DIM)

    kk = hd(k * k_k)
    kk = kk * lax.rsqrt(jnp.maximum(jnp.sum(kk * kk, axis=-1, keepdims=True), 1e-24))
    k_mod = hd(k.astype(F32) * (1.0 + (a - 1.0) * k_a.astype(F32)))
    r_h, v_h, a_h = hd(r), hd(v), hd(a)
    y = rwkv7_scan(r_h, hd(decay), k_mod, v_h, -kk, kk * a_h)
    mean = jnp.mean(y, axis=-1, keepdims=True)
    var = jnp.mean(jnp.square(y - mean), axis=-1, keepdims=True)
    y = ((y - mean) * lax.rsqrt(var + A_GN_EPS)).reshape(bsz, seq, A_WIDTH)
    y = y * gn_g.astype(F32) + gn_b.astype(F32)
    bonus = jnp.sum(r_h * k_mod * r_k.astype(F32), axis=-1, keepdims=True) * v_h
    y = y + bonus.reshape(bsz, seq, A_WIDTH)
    return (y * g.astype(F32)).astype(p.dtype)


def hgrn2_chunk_step(state, inp):
    q, k, g, v = inp
    b = jnp.cumsum(g, axis=2)
    causal = jnp.tril(jnp.ones((B_CHUNK, B_CHUNK), dtype=bool))
    diff = b[:, :, :, None, :] - b[:, :, None, :, :]
    decay = jnp.exp(jnp.where(causal[:, :, None], diff, -jnp.inf))
    scores = jnp.einsum('bhtd,bhsd,bhtsd->bhts', q, k, decay)
    o = (jnp.einsum('bhts,bhsv->bhtv', scores, v)
         + jnp.einsum('bhtd,bhdv->bhtv', q * jnp.exp(b), state))
    b_end = b[:, :, -1:, :]
    state = (state * jnp.exp(b_end)[:, :, 0, :, None]
             + jnp.einsum('bhsd,bhsv->bhdv', k * jnp.exp(b_end - b), v))
    return state, o


def hgrn2_branch(p, lb, gn_g):
    bsz, seq, _ = p.shape
    q, f, i, g = _split(p, (B_WIDTH, B_WIDTH, B_WIDTH, B_WIDTH))
    f = f.astype(F32)
    lb = lb.astype(F32)
    log_f = jnp.logaddexp(jnp.log(lb), jnp.log1p(-lb) + jax.nn.log_sigmoid(f))
    k_in = (1.0 - lb) * jax.nn.sigmoid(-f)
    n_chunk = seq // B_CHUNK

    def chunks(t, d):
        return t.astype(F32).reshape(bsz, n_chunk, B_CHUNK, B_HEADS, d).transpose(1, 0, 3, 2, 4)

    s0 = jnp.zeros((bsz, B_HEADS, B_KEY_DIM, B_VAL_DIM), F32)
    _, o = lax.scan(hgrn2_chunk_step, s0,
                    (chunks(q, B_KEY_DIM), chunks(k_in, B_KEY_DIM),
                     chunks(log_f, B_KEY_DIM), chunks(i, B_VAL_DIM)))
    o = o.transpose(1, 0, 3, 2, 4).reshape(bsz, seq, B_HEADS, B_VAL_DIM)
    o = o * lax.rsqrt(jnp.mean(o * o, axis=-1, keepdims=True) + NORM_EPS)
    o = o.reshape(bsz, seq, B_WIDTH) * gn_g.astype(F32) * jax.nn.silu(g.astype(F32))
    return o.astype(p.dtype)


def dsa_branch(p, cos, sin):
    bsz, seq, _ = p.shape
    kv_w = C_KV_HEADS * C_HEAD_DIM
    q, k, v, qi, ki, wi = _split(p, (C_WIDTH, kv_w, kv_w, IDX_HEADS * IDX_DIM, IDX_DIM, IDX_HEADS))
    q = apply_rope(q.reshape(bsz, seq, C_HEADS, C_HEAD_DIM), cos, sin)
    k = apply_rope(k.reshape(bsz, seq, C_KV_HEADS, C_HEAD_DIM), cos, sin)
    v = v.reshape(bsz, seq, C_KV_HEADS, C_HEAD_DIM)
    qi = apply_rope(qi.reshape(bsz, seq, IDX_HEADS, IDX_DIM), cos, sin)
    ki = apply_rope(ki[:, :, None, :], cos, sin)[:, :, 0, :].astype(F32)
    wi = wi.astype(F32) * (IDX_HEADS ** -0.5) * (IDX_DIM ** -0.5)
    n_blk = seq // Q_BLOCK
    k_sel = min(TOPK_MAX, seq // 4)
    group = C_HEADS // C_KV_HEADS
    s_pos = jnp.arange(seq)

    def blockify(t):
        return jnp.moveaxis(t.reshape(bsz, n_blk, Q_BLOCK, *t.shape[2:]), 1, 0)

    def attend(args):
        qb, qib, wib, t0 = args
        t_pos = t0 + jnp.arange(Q_BLOCK)
        causal = s_pos[None, :] <= t_pos[:, None]
        dots = jnp.einsum('bqhd,bsd->bqhs', qib.astype(F32), ki)
        score = jnp.einsum('bqhs,bqh->bqs', jax.nn.relu(dots), wib)
        score = jnp.where(causal[None], score, -jnp.inf)
        _, idx = lax.top_k(score, k_sel)
        kg = jax.vmap(lambda kb, ib: kb[ib])(k, idx)
        vg = jax.vmap(lambda vb, ib: vb[ib])(v, idx)
        valid = idx <= t_pos[None, :, None]
        qg = qb.reshape(bsz, Q_BLOCK, C_KV_HEADS, group, C_HEAD_DIM)
        logits = jnp.einsum('bqcgd,bqkcd->bqcgk', qg, kg).astype(F32) * (C_HEAD_DIM ** -0.5)
        logits = jnp.where(valid[:, :, None, None, :], logits, -jnp.inf)
        prob = jax.nn.softmax(logits, axis=-1).astype(vg.dtype)
        out = jnp.einsum('bqcgk,bqkcd->bqcgd', prob, vg)
        return out.reshape(bsz, Q_BLOCK, C_WIDTH)

    outs = lax.map(attend, (blockify(q), blockify(qi), blockify(wi),
                            jnp.arange(n_blk, dtype=jnp.int32) * Q_BLOCK))
    return jnp.moveaxis(outs, 0, 1).reshape(bsz, seq, C_WIDTH).astype(p.dtype)


def conv_glu(h, w_up, conv_w, conv_b, w_down):
    seq = h.shape[1]
    up = h @ w_up
    up_pad = jnp.pad(up, ((0, 0), (CONV_WIDTH - 1, 0), (0, 0)))
    c = conv_b
    for j in range(CONV_WIDTH):
        c = c + conv_w[j] * up_pad[:, j:j + seq]
    gate, val = _split(c, (D_FF, D_FF))
    return (jax.nn.silu(gate) * val) @ w_down


def setup_inputs(seed: int = 0) -> dict:
    key = jax.random.key(seed)
    ks = iter(jax.random.split(key, 32))

    def nrm(shape, scale):
        return jax.random.normal(next(ks), shape, F32) * scale

    def uni(shape, lo, hi):
        return jax.random.uniform(next(ks), shape, F32, lo, hi)

    f2 = 2 * D_FF
    return {
        'x': nrm((BATCH, SEQ, D_MODEL), 1.0),
        'norm_mix_g': 1.0 + nrm((DEPTH, D_MODEL), 0.02),
        'w_in': nrm((DEPTH, D_MODEL, IN_COLS), D_MODEL ** -0.5),
        'a_mu': uni((DEPTH, A_COLS), 0.0, 1.0),
        'a_w0': uni((DEPTH, A_WIDTH), -5.0, 0.0),
        'a_w_up': nrm((DEPTH, A_RANK_W, A_WIDTH), 0.5 * A_RANK_W ** -0.5),
        'a_a0': nrm((DEPTH, A_WIDTH), 0.5),
        'a_a_up': nrm((DEPTH, A_RANK_A, A_WIDTH), 0.5 * A_RANK_A ** -0.5),
        'a_g_up': nrm((DEPTH, A_RANK_G, A_WIDTH), A_RANK_G ** -0.5),
        'a_k_k': 0.85 + nrm((DEPTH, A_WIDTH), 0.05),
        'a_k_a': 1.0 + nrm((DEPTH, A_WIDTH), 0.05),
        'a_r_k': nrm((DEPTH, A_HEADS, A_HEAD_DIM), 0.1),
        'a_gn_g': 1.0 + nrm((DEPTH, A_WIDTH), 0.02),
        'a_gn_b': nrm((DEPTH, A_WIDTH), 0.02),
        'b_lb_logits': nrm((DEPTH, B_HEADS * B_KEY_DIM), 1.0),
        'b_gn_g': 1.0 + nrm((DEPTH, B_WIDTH), 0.02),
        'w_branch': nrm((DEPTH, N_BRANCH, MIX_WIDTH, D_MODEL), MIX_WIDTH ** -0.5),
        'w_o': nrm((DEPTH, D_MODEL, D_MODEL), D_MODEL ** -0.5),
        'norm_ffn_g': 1.0 + nrm((DEPTH, D_MODEL), 0.02),
        'w_up': nrm((DEPTH, D_MODEL, f2), D_MODEL ** -0.5),
        'conv_w': nrm((DEPTH, CONV_WIDTH, f2), CONV_WIDTH ** -0.5),
        'conv_b': nrm((DEPTH, f2), 0.02),
        'w_down': nrm((DEPTH, D_FF, D_MODEL), D_FF ** -0.5),
        'norm_final_g': 1.0 + nrm((D_MODEL,), 0.02),
    }


def reference(x, norm_mix_g, w_in, a_mu, a_w0, a_w_up, a_a0, a_a_up, a_g_up, a_k_k, a_k_a,
              a_r_k, a_gn_g, a_gn_b, b_lb_logits, b_gn_g, w_branch, w_o, norm_ffn_g,
              w_up, conv_w, conv_b, w_down, norm_final_g):
    bsz, seq, _ = x.shape
    cos, sin = rope_tables(seq, C_HEAD_DIM)
    lb_cum = jnp.cumsum(jax.nn.softmax(b_lb_logits.astype(F32), axis=0), axis=0)
    lower_bounds = lb_cum - lb_cum[0]
    h = x
    for layer in range(DEPTH):
        u = rms_norm(h, norm_mix_g[layer])
        p = u @ w_in[layer]
        p_a, p_b, p_c, p_gate = _split(p, (A_COLS, B_COLS, C_COLS, GATE_COLS))
        y_a = rwkv7_branch(p_a, a_mu[layer], a_w0[layer], a_w_up[layer], a_a0[layer],
                           a_a_up[layer], a_g_up[layer], a_k_k[layer], a_k_a[layer],
                           a_r_k[layer], a_gn_g[layer], a_gn_b[layer])
        y_b = hgrn2_branch(p_b, lower_bounds[layer], b_gn_g[layer])
        y_c = dsa_branch(p_c, cos, sin)
        ys = jnp.stack([y_a, y_b, y_c], axis=2)
        branch = jnp.einsum('bsnc,ncd->bsnd', ys, w_branch[layer])
        gates = jax.nn.sigmoid(p_gate.reshape(bsz, seq, N_BRANCH, D_MODEL))
        merged = jnp.sum(gates * branch, axis=2)
        h = h + merged @ w_o[layer]
        u = rms_norm(h, norm_ffn_g[layer])
        h = h + conv_glu(u, w_up[layer], conv_w[layer], conv_b[layer], w_down[layer])
    return rms_norm(h, norm_final_g)
```

```python
import numpy as np
import ml_dtypes
import concourse.bass as bass
import concourse.mybir as mybir
from concourse.bass_utils import run_bass_kernel_spmd

F32 = mybir.dt.float32
BF16 = mybir.dt.bfloat16
AF = mybir.ActivationFunctionType
ALU = mybir.AluOpType
AX = mybir.AxisListType

S = 2048; D = 1024; DEPTH = 2; NB = 8
TB = 512; NTB = S // TB; KC = D // 128
A_COLS = 1792; B_COLS = 2048; C_COLS = 1092; G_COLS = 3072; IN_COLS = 8004
A_OFF = 0; B_OFF = A_COLS; C_OFF = A_COLS + B_COLS; G_OFF = C_OFF + C_COLS
DFF = 2816; NFF = DFF // 128
NDMA = 24
NOSYNC_SAME = {"pe"}


class K:
    def __init__(self):
        self.nc = nc = bass.Bass("TRN2", target_bir_lowering=False)
        self.eng = {"pe": nc.tensor, "act": nc.scalar, "dve": nc.vector, "pool": nc.gpsimd, "sp": nc.sync}
        self.sem = {}
        self.cnt = {}
        for e in self.eng:
            self.sem[e] = nc.alloc_semaphore("s_" + e)
            self.cnt[e] = 0
        self.seen = {e: {} for e in self.eng}
        self.lastw = {}
        self.readers = {}
        self.dsem = [nc.alloc_semaphore("d%d" % i) for i in range(NDMA)]
        self.dval = [0] * NDMA
        self.di = 0
        self.nwait = 0
        self.ninst = 0

    def _wait(self, e, ev):
        name, sem, val, src = ev
        if src == e and e in NOSYNC_SAME:
            return
        if self.seen[e].get(name, 0) >= val:
            return
        self.eng[e].wait_ge(sem, val)
        self.nwait += 1
        self.seen[e][name] = val

    def _deps(self, e, reads, writes):
        for r in reads:
            ev = self.lastw.get(r)
            if ev is not None:
                self._wait(e, ev)
        for w in writes:
            ev = self.lastw.get(w)
            if ev is not None:
                self._wait(e, ev)
            for ev in self.readers.get(w, ()):
                self._wait(e, ev)

    def _record(self, ev, reads, writes):
        for w in writes:
            self.lastw[w] = ev
            self.readers[w] = []
        for r in reads:
            if r in writes:
                continue
            lst = self.readers.setdefault(r, [])
            lst[:] = [x for x in lst if x[0] != ev[0]]
            lst.append(ev)

    def op(self, e, fn, reads=(), writes=()):
        self._deps(e, reads, writes)
        ins = fn(self.eng[e])
        self.cnt[e] += 1
        ins.then_inc(self.sem[e], 1)
        ev = ("E" + e, self.sem[e], self.cnt[e], e)
        self.seen[e]["E" + e] = max(self.seen[e].get("E" + e, 0), 0)
        self._record(ev, reads, writes)
        self.ninst += 1
        return ins

    def dma(self, out, in_, reads=(), writes=(), q="sp", **kw):
        self._deps(q, reads, writes)
        j = self.di % NDMA
        self.di += 1
        if self.dval[j] > 0:
            self._wait(q, ("D%d" % j, self.dsem[j], self.dval[j], None))
        self.dval[j] += 16
        self.eng[q].dma_start(out=out, in_=in_, **kw).then_inc(self.dsem[j], 16)
        ev = ("D%d" % j, self.dsem[j], self.dval[j], None)
        self._record(ev, reads, writes)
        self.ninst += 1

    def drain(self, e="sp"):
        for e2 in self.eng:
            if e2 != e and self.cnt[e2] > 0:
                self._wait(e, ("E" + e2, self.sem[e2], self.cnt[e2], e2))
        for j in range(NDMA):
            if self.dval[j] > 0:
                self._wait(e, ("D%d" % j, self.dsem[j], self.dval[j], None))


    def phase_barrier(self):
        self.drain("sp")
        self.nc.all_engine_barrier()
        self.lastw.clear()
        self.readers.clear()


class Scope:
    def __init__(self, k):
        self.k = k
        self.stack = []

    def __enter__(self):
        return self

    _uid = [0]

    def sb(self, name, shape, dtype):
        Scope._uid[0] += 1
        g = self.k.nc.sbuf_tensor("%s_u%d" % (name, Scope._uid[0]), list(shape), dtype)
        t = g.__enter__()
        self.stack.append(g)
        return t

    def ring(self, name, shape, dtype, n):
        return Ring([self.sb("%s_%d" % (name, i), shape, dtype) for i in range(n)], name)

    def __exit__(self, *a):
        self.k.phase_barrier()
        for g in reversed(self.stack):
            g.__exit__(None, None, None)
        return False


class Ring:
    def __init__(self, tensors, name):
        self.t = tensors
        self.name = name
        self.i = 0

    def next(self):
        j = self.i % len(self.t)
        self.i += 1
        return self.t[j], (self.name, j)


def build(dbg=None):
    dbg = dbg or {}
    k = K()
    nc = k.nc
    op = k.op
    uid = [0]

    def din(name, shape, dt=F32):
        return nc.dram_tensor(name, list(shape), dt, kind="ExternalInput").ap()

    x = din("x", [S, D])
    P = {}
    for name, shape in [
        ("norm_mix_g", [DEPTH, D]), ("w_in", [DEPTH, D, IN_COLS]), ("a_mu", [DEPTH, A_COLS]), ("a_w0", [DEPTH, 512]),
        ("a_w_up", [DEPTH, 64, 512]), ("a_a0", [DEPTH, 512]), ("a_a_up", [DEPTH, 64, 512]), ("a_g_up", [DEPTH, 128, 512]),
        ("a_k_k", [DEPTH, 512]), ("a_k_a", [DEPTH, 512]), ("a_r_k", [DEPTH, 512]), ("a_gn_g", [DEPTH, 512]),
        ("a_gn_b", [DEPTH, 512]), ("b_lb_logits", [DEPTH, 512]), ("b_gn_g", [DEPTH, 512]),
        ("w_branch", [DEPTH, 3, 512, D]), ("w_o", [DEPTH, D, D]), ("norm_ffn_g", [DEPTH, D]),
        ("w_up", [DEPTH, D, 2 * DFF]), ("conv_w", [DEPTH, 3, 2 * DFF]), ("conv_b", [DEPTH, 2 * DFF]),
        ("w_down", [DEPTH, DFF, D]), ("norm_final_g", [D]),
    ]:
        P[name] = din(name, shape)
    c_ident = din("c_ident", [128, 128])
    out = nc.dram_tensor("out", [S, D], F32, kind="ExternalOutput").ap()
    h_dram = nc.dram_tensor("h_scr", [128, KC, S], F32, kind="Internal").ap()
    yT_dram = nc.dram_tensor("yT_scr", [3, 128, 4, S], BF16, kind="ExternalOutput" if dbg.get("dump_yT") else "Internal").ap()
    c_tri = din("c_tri", [7, 128, 128])
    c_rope = din("c_rope", [2, 128, S])
    yT_src = [[yT_dram[n] for n in range(3)] for l in range(DEPTH)]
    for n in range(3):
        if ("yT_in%d" % n) in dbg:
            t = din("yT_in%d" % n, [DEPTH, 128, 4, S], BF16)
            for l in range(DEPTH):
                yT_src[l][n] = t[l]

    ident = nc.alloc_sbuf_tensor("ident", [128, 128], F32)
    identb = nc.alloc_sbuf_tensor("identb", [128, 128], BF16)
    ones = nc.alloc_sbuf_tensor("ones", [128, 128], F32)
    epsc = nc.alloc_sbuf_tensor("epsc", [128, 1], F32)
    uT = nc.alloc_sbuf_tensor("uT", [128, KC, S + 2], BF16)
    gvec = nc.alloc_sbuf_tensor("gvec", [128, 2 * DEPTH + 1, KC], F32)
    psr = Ring([nc.alloc_psum_tensor("ps%d" % i, [128, 512], F32) for i in range(6)], "ps")
    psacc = Ring([nc.alloc_psum_tensor("psacc%d" % i, [128, 512], F32) for i in range(2)], "psacc")

    tri = nc.alloc_sbuf_tensor("tri", [128, 7, 128], F32)
    k.dma(tri.ap(), c_tri.rearrange("a p f -> p a f"), writes=["tri"])
    k.dma(ident.ap(), c_ident, writes=["ident"])
    op("dve", lambda e: e.tensor_copy(out=identb.ap(), in_=ident.ap()), reads=["ident"], writes=["identb"])
    op("pool", lambda e: e.memset(ones.ap(), 1.0), writes=["ones"])
    op("pool", lambda e: e.memset(epsc.ap(), 1e-6), writes=["epsc"])
    op("pool", lambda e: e.memset(uT.ap()[:, :, 0:2], 0.0), writes=["uTpad"])
    for l in range(DEPTH):
        k.dma(gvec.ap()[:, 2 * l, :], P["norm_mix_g"][l].rearrange("(c p) -> p c", p=128), writes=["gvec"],
              allow_slow_non_contiguous=True)
        k.dma(gvec.ap()[:, 2 * l + 1, :], P["norm_ffn_g"][l].rearrange("(c p) -> p c", p=128), writes=["gvec"],
              allow_slow_non_contiguous=True)
    k.dma(gvec.ap()[:, 2 * DEPTH, :], P["norm_final_g"].rearrange("(c p) -> p c", p=128), writes=["gvec"],
          allow_slow_non_contiguous=True)

    def hkeys(kc=None, tb=None):
        return [("hT", a, b) for a in (range(KC) if kc is None else [kc]) for b in (range(NTB) if tb is None else [tb])]

    def ukeys(kc=None, tb=None):
        return [("uT", a, b) for a in (range(KC) if kc is None else [kc]) for b in (range(NTB) if tb is None else [tb])]

    def tsl(tb):
        return slice(tb * TB, (tb + 1) * TB)

    def usl(tb, shift=0):
        return slice(2 + tb * TB - shift, 2 + (tb + 1) * TB - shift)

    def rmsnorm(hT, gi, dst_fn, sqr, tmpr):
        for tb in range(NTB):
            ps, pk = psr.next()
            for kc in range(KC):
                sq, sk = sqr.next()
                op("act", lambda e: e.activation(out=sq.ap(), in_=hT.ap()[:, kc, tsl(tb)], func=AF.Square),
                   reads=hkeys(kc, tb), writes=[sk])
                op("pe", lambda e: e.matmul(ps.ap(), lhsT=ones.ap(), rhs=sq.ap(), start=(kc == 0), stop=(kc == KC - 1)),
                   reads=[sk, "ones"], writes=[pk])
            sd, sdk = tmpr.next()
            op("act", lambda e: e.activation(out=sd.ap(), in_=ps.ap(), func=AF.Ln, bias=epsc.ap(), scale=1.0 / D),
               reads=[pk, "epsc"], writes=[sdk])
            rs, rsk = tmpr.next()
            op("act", lambda e: e.activation(out=rs.ap(), in_=sd.ap(), func=AF.Exp, scale=-0.5), reads=[sdk], writes=[rsk])
            for kc in range(KC):
                dst, dk = dst_fn(kc, tb)
                op("dve", lambda e: e.scalar_tensor_tensor(out=dst, in0=hT.ap()[:, kc, tsl(tb)],
                                                           scalar=gvec.ap()[:, gi, kc:kc + 1], in1=rs.ap(),
                                                           op0=ALU.mult, op1=ALU.mult),
                   reads=hkeys(kc, tb) + [rsk, "gvec"], writes=dk)

    def u_dst(kc, tb):
        return uT.ap()[:, kc, usl(tb)], ukeys(kc, tb)

    castsel = [0]

    def cast(dst, src, reads, writes, eng=None):
        ce = eng or "act"
        castsel[0] += 1
        if ce == "act":
            op("act", lambda e: e.copy(out=dst, in_=src), reads=reads, writes=writes)
        else:
            op(ce, lambda e: e.tensor_copy(out=dst, in_=src), reads=reads, writes=writes)

    def wload(src, nk, ncol, stg_ring, w_ring, q="sp"):
        stg, sk = stg_ring.next()
        wt, wk = w_ring.next()
        k.dma(stg.ap()[:, 0:nk, 0:ncol], src.rearrange("(c p) n -> p c n", p=128), writes=[sk], q=q)
        cast(wt.ap()[:, 0:nk, 0:ncol], stg.ap()[:, 0:nk, 0:ncol], [sk], [wk])
        return wt, wk

    def load_h(hT):
        for kc in range(KC):
            k.dma(hT.ap()[:, kc, :], h_dram[:, kc, :], reads=["h_dram"], writes=hkeys(kc))

    def store_h(hT):
        for kc in range(KC):
            k.dma(h_dram[:, kc, :], hT.ap()[:, kc, :], reads=hkeys(kc), writes=[("h_dram", kc)])

    with Scope(k) as sc:
        xs = sc.ring("xs", [128, D], F32, 2)
        xt = sc.ring("xt", [128, KC, 128], F32, 2)
        for tt in range(S // 128):
            a, ak = xs.next()
            b, bk = xt.next()
            k.dma(a.ap(), x[tt * 128:(tt + 1) * 128, :], writes=[ak])
            for half in range(2):
                ps, pk = psr.next()
                for j in range(4):
                    kc = half * 4 + j
                    op("pe", lambda e: e.transpose(out=ps.ap()[:, j * 128:(j + 1) * 128], in_=a.ap()[:, kc * 128:(kc + 1) * 128],
                                                   identity=ident.ap()), reads=[ak, "ident"], writes=[pk])
                cast(b.ap()[:, half * 4:half * 4 + 4, :], ps.ap().rearrange("p (a b) -> p a b", a=4), [pk], [(bk, half)],
                     eng=("dve", "act")[half])
            k.dma(h_dram[:, :, tt * 128:(tt + 1) * 128], b.ap(), reads=[(bk, 0), (bk, 1)], writes=[("h_dram", "x", tt)])


    def proj_fm(l, c0, ncol, sc_rings, consume, shift_w=None):
        stg_r, w_r = sc_rings
        wt, wk = wload(P["w_in"][l][:, c0:c0 + ncol], KC, ncol, stg_r, w_r)
        for tb in range(NTB):
            ps, pk = psr.next()
            for kc in range(KC):
                op("pe", lambda e: e.matmul(ps.ap()[0:ncol, :], lhsT=wt.ap()[:, kc, 0:ncol], rhs=uT.ap()[:, kc, usl(tb)],
                                            start=(kc == 0), stop=(kc == KC - 1)),
                   reads=[wk] + ukeys(kc, tb), writes=[pk])
            consume(tb, ps, pk)

    def mixer_hgrn(l):
        C = 64; NCH = S // C
        with Scope(k) as sc:
            stg_r = sc.ring("hstg", [128, KC, 128], F32, 2)
            w_r = sc.ring("hw", [128, KC, 128], BF16, 2)
            rings = (stg_r, w_r)
            lbt = sc.sb("lbt", [128, 2, 4], F32)
            lb = sc.sb("lb", [128, 4], F32)
            oml = sc.sb("oml", [128, 4], F32)
            noml = sc.sb("noml", [128, 4], F32)
            gng = sc.sb("gng", [128, 4], F32)
            onesS = sc.sb("onesS", [128, S], F32)
            Bc = sc.sb("Bc", [128, S], F32)
            sig = sc.sb("sig", [128, S], F32)
            bp = sc.sb("bp", [128, S], F32)
            Ex = sc.sb("Ex", [128, S], F32)
            ktl = sc.sb("ktl", [128, S], F32)
            qt = sc.sb("qt", [128, S], BF16)
            kt = sc.sb("kt", [128, S], BF16)
            qh = sc.sb("qh", [128, S], BF16)
            kcf = sc.sb("kcf", [128, S], F32)
            vtok = sc.sb("vtok", [64, NCH, 128], BF16)
            ysb = sc.sb("ysb", [128, S], F32)
            gs = sc.sb("gs", [128, S], F32)
            yo = sc.sb("yo", [128, S], BF16)
            dcol = sc.sb("dcol", [128, 4, NCH], F32)
            st = sc.sb("st", [128, 128], F32)
            stb_r = sc.ring("stb", [128, 128], BF16, 2)
            pt_r = sc.ring("pt", [64, 64], BF16, 3)
            kct_r = sc.ring("kct", [64, 128], BF16, 3)
            sq_r = sc.ring("hsq", [128, TB], F32, 2)
            nt_r = sc.ring("hnt", [128, TB], F32, 2)
            op("pool", lambda e: e.memset(onesS.ap(), 1.0), writes=["onesS"])
            k.dma(lbt.ap(), P["b_lb_logits"].rearrange("l (h p) -> p l h", p=128), writes=["lbt"], allow_slow_non_contiguous=True)
            k.dma(gng.ap(), P["b_gn_g"][l].rearrange("(h p) -> p h", p=128), writes=["gng"], allow_slow_non_contiguous=True)
            if l == 0:
                op("dve", lambda e: e.tensor_tensor(out=lb.ap(), in0=lbt.ap()[:, 0, :], in1=lbt.ap()[:, 0, :], op=ALU.subtract),
                   reads=["lbt"], writes=["lb"])
            else:
                op("dve", lambda e: e.tensor_tensor(out=lb.ap(), in0=lbt.ap()[:, 1, :], in1=lbt.ap()[:, 0, :], op=ALU.subtract),
                   reads=["lbt"], writes=["lb"])
                op("act", lambda e: e.activation(out=lb.ap(), in_=lb.ap(), func=AF.Sigmoid), reads=["lb"], writes=["lb"])
            op("dve", lambda e: e.tensor_scalar(out=oml.ap(), in0=lb.ap(), scalar1=-1.0, scalar2=1.0, op0=ALU.mult, op1=ALU.add),
               reads=["lb"], writes=["oml"])
            op("dve", lambda e: e.tensor_scalar(out=noml.ap(), in0=oml.ap(), scalar1=-1.0, scalar2=None, op0=ALU.mult),
               reads=["oml"], writes=["noml"])
            for h in range(4):
                cq = B_OFF + h * 128; cf = B_OFF + 512 + h * 128; ci = B_OFF + 1024 + h * 128; cg = B_OFF + 1536 + h * 128
                def f_consume(tb, ps, pk):
                    op("act", lambda e: e.activation(out=sig.ap()[:, tsl(tb)], in_=ps.ap(), func=AF.Sigmoid), reads=[pk], writes=[("sig", tb)])
                proj_fm(l, cf, 128, rings, f_consume)
                allsig = [("sig", tb) for tb in range(NTB)]
                op("dve", lambda e: e.tensor_scalar(out=ktl.ap(), in0=sig.ap(), scalar1=noml.ap()[:, h:h + 1], scalar2=oml.ap()[:, h:h + 1],
                                                    op0=ALU.mult, op1=ALU.add), reads=allsig + ["noml", "oml"], writes=["ktl"])
                op("dve", lambda e: e.tensor_scalar(out=sig.ap(), in0=sig.ap(), scalar1=oml.ap()[:, h:h + 1], scalar2=lb.ap()[:, h:h + 1],
                                                    op0=ALU.mult, op1=ALU.add), reads=allsig + ["oml", "lb"], writes=allsig)
                op("act", lambda e: e.activation(out=sig.ap(), in_=sig.ap(), func=AF.Ln), reads=allsig, writes=allsig)
                op("dve", lambda e: e.tensor_tensor_scan(out=Bc.ap(), data0=onesS.ap(), data1=sig.ap(), initial=0.0, op0=ALU.mult, op1=ALU.add),
                   reads=allsig + ["onesS"], writes=["Bc"])
                B3 = Bc.ap().rearrange("p (c t) -> p c t", t=C)
                op("dve", lambda e: e.tensor_tensor(out=bp.ap().rearrange("p (c t) -> p c t", t=C), in0=B3,
                                                    in1=B3[:, :, C // 2 - 1:C // 2].to_broadcast([128, NCH, C]), op=ALU.subtract),
                   reads=["Bc"], writes=["bp"] + [("bpq", tb) for tb in range(NTB)])
                op("dve", lambda e: e.tensor_tensor(out=dcol.ap()[:, 0, :], in0=B3[:, :, C - 1], in1=B3[:, :, C // 2 - 1], op=ALU.subtract),
                   reads=["Bc"], writes=[("dcol", 0)])
                op("dve", lambda e: e.tensor_copy(out=dcol.ap()[:, 1, 0:1], in_=B3[:, 0, C // 2 - 1:C // 2]), reads=["Bc"], writes=[("dcol", 1)])
                op("dve", lambda e: e.tensor_tensor(out=dcol.ap()[:, 1, 1:NCH], in0=B3[:, 1:NCH, C // 2 - 1], in1=B3[:, 0:NCH - 1, C - 1], op=ALU.subtract),
                   reads=["Bc", ("dcol", 1)], writes=[("dcol", 1)])
                op("dve", lambda e: e.tensor_tensor(out=dcol.ap()[:, 2, :], in0=dcol.ap()[:, 0, :], in1=dcol.ap()[:, 1, :], op=ALU.add),
                   reads=[("dcol", 0), ("dcol", 1)], writes=[("dcol", 2)])
                op("act", lambda e: e.activation(out=dcol.ap()[:, 0:3, :], in_=dcol.ap()[:, 0:3, :], func=AF.Exp),
                   reads=[("dcol", 0), ("dcol", 1), ("dcol", 2)], writes=[("dcol", 0), ("dcol", 1), ("dcol", 2)])
                op("act", lambda e: e.activation(out=Ex.ap(), in_=bp.ap(), func=AF.Exp, scale=-1.0), reads=["bp"], writes=["Ex"])
                op("dve", lambda e: e.tensor_tensor(out=ktl.ap(), in0=ktl.ap(), in1=Ex.ap(), op=ALU.mult), reads=["ktl", "Ex"], writes=["ktl"])
                op("act", lambda e: e.copy(out=kt.ap(), in_=ktl.ap()), reads=["ktl"], writes=["kt"])
                op("dve", lambda e: e.tensor_tensor(out=kcf.ap().rearrange("p (c t) -> p c t", t=C), in0=ktl.ap().rearrange("p (c t) -> p c t", t=C),
                                                    in1=dcol.ap()[:, 0, :].unsqueeze(2).to_broadcast([128, NCH, C]), op=ALU.mult),
                   reads=["ktl", ("dcol", 0)], writes=["kcf"])
                op("act", lambda e: e.activation(out=Ex.ap(), in_=bp.ap(), func=AF.Exp), reads=["bp"], writes=["Ex"])

                def q_consume(tb, ps, pk):
                    op("dve", lambda e: e.tensor_tensor(out=bp.ap()[:, tsl(tb)], in0=Ex.ap()[:, tsl(tb)], in1=ps.ap(), op=ALU.mult),
                       reads=["Ex", pk, "bp"], writes=[("bpq", tb)])
                proj_fm(l, cq, 128, rings, q_consume)
                allq = [("bpq", tb) for tb in range(NTB)]
                op("act", lambda e: e.copy(out=qt.ap(), in_=bp.ap()), reads=allq, writes=["qt"])
                op("dve", lambda e: e.tensor_tensor(out=qh.ap().rearrange("p (c t) -> p c t", t=C), in0=bp.ap().rearrange("p (c t) -> p c t", t=C),
                                                    in1=dcol.ap()[:, 1, :].unsqueeze(2).to_broadcast([128, NCH, C]), op=ALU.mult),
                   reads=allq + [("dcol", 1)], writes=["qh"])

                def g_consume(tb, ps, pk):
                    op("act", lambda e: e.activation(out=gs.ap()[:, tsl(tb)], in_=ps.ap(), func=AF.Silu), reads=[pk], writes=[("gs", tb)])
                proj_fm(l, cg, 128, rings, g_consume)
                wt, wk = wload(P["w_in"][l][:, ci:ci + 128], KC, 128, stg_r, w_r)
                for c4 in range(NCH // 4):
                    ps, pk = psr.next()
                    for j in range(4):
                        c = c4 * 4 + j
                        for kc in range(KC):
                            op("pe", lambda e: e.matmul(ps.ap()[0:C, j * 128:(j + 1) * 128], lhsT=uT.ap()[:, kc, 2 + c * C: 2 + (c + 1) * C],
                                                        rhs=wt.ap()[:, kc, :], start=(kc == 0), stop=(kc == KC - 1)),
                               reads=[wk] + ukeys(kc, (c * C) // TB), writes=[pk])
                    cast(vtok.ap()[:, c4 * 4:(c4 + 1) * 4, :], ps.ap()[0:C, :].rearrange("p (a b) -> p a b", a=4), [pk], [("vtok", c4)], eng="act")
                stb = None
                for c in range(NCH):
                    cs = slice(c * C, (c + 1) * C)
                    ps_s, pk_s = psr.next()
                    op("pe", lambda e: e.matmul(ps_s.ap()[0:C, 0:C], lhsT=kt.ap()[:, cs], rhs=qt.ap()[:, cs], start=True, stop=True),
                       reads=["kt", "qt"], writes=[pk_s])
                    pt, ptk = pt_r.next()
                    op("dve", lambda e: e.tensor_tensor(out=pt.ap(), in0=ps_s.ap()[0:C, 0:C], in1=tri.ap()[0:C, 0, 0:C], op=ALU.mult),
                       reads=[pk_s, "tri"], writes=[ptk])
                    ps_t, pk_t = psr.next()
                    op("pe", lambda e: e.transpose(out=ps_t.ap()[0:C, 0:128], in_=kcf.ap()[:, cs], identity=ident.ap()),
                       reads=["kcf", "ident"], writes=[pk_t])
                    kct, kctk = kct_r.next()
                    op("act", lambda e: e.copy(out=kct.ap(), in_=ps_t.ap()[0:C, 0:128]), reads=[pk_t], writes=[kctk])
                    ps_y, pk_y = psr.next()
                    if c > 0:
                        op("pe", lambda e: e.matmul(ps_y.ap()[:, 0:C], lhsT=stb[0].ap(), rhs=qh.ap()[:, cs], start=True, stop=False),
                           reads=[stb[1], "qh"], writes=[pk_y])
                    op("pe", lambda e: e.matmul(ps_y.ap()[:, 0:C], lhsT=vtok.ap()[:, c, :], rhs=pt.ap(), start=(c == 0), stop=True),
                       reads=[("vtok", c // 4), ptk], writes=[pk_y])
                    op("act", lambda e: e.copy(out=ysb.ap()[:, cs], in_=ps_y.ap()[:, 0:C]), reads=[pk_y], writes=[("ysb", c // 8)])
                    if c < NCH - 1:
                        ps_d, pk_d = psr.next()
                        op("pe", lambda e: e.matmul(ps_d.ap()[:, 0:128], lhsT=kct.ap(), rhs=vtok.ap()[:, c, :], start=True, stop=True),
                           reads=[kctk, ("vtok", c // 4)], writes=[pk_d])
                        if c == 0:
                            op("dve", lambda e: e.tensor_copy(out=st.ap(), in_=ps_d.ap()[:, 0:128]), reads=[pk_d], writes=["st"])
                        else:
                            op("dve", lambda e: e.scalar_tensor_tensor(out=st.ap(), in0=st.ap(), scalar=dcol.ap()[:, 2, c:c + 1], in1=ps_d.ap()[:, 0:128],
                                                                       op0=ALU.mult, op1=ALU.add), reads=["st", pk_d, ("dcol", 2)], writes=["st"])
                        stb = stb_r.next()
                        op("act", lambda e: e.copy(out=stb[0].ap(), in_=st.ap()), reads=["st"], writes=[stb[1]])
                for tb in range(NTB):
                    sq, sk = sq_r.next()
                    yk = [("ysb", tb * 2), ("ysb", tb * 2 + 1)]
                    op("act", lambda e: e.activation(out=sq.ap(), in_=ysb.ap()[:, tsl(tb)], func=AF.Square), reads=yk, writes=[sk])
                    ps, pk = psr.next()
                    op("pe", lambda e: e.matmul(ps.ap(), lhsT=ones.ap(), rhs=sq.ap(), start=True, stop=True), reads=[sk, "ones"], writes=[pk])
                    sd, sdk = nt_r.next()
                    op("act", lambda e: e.activation(out=sd.ap(), in_=ps.ap(), func=AF.Ln, bias=epsc.ap(), scale=1.0 / 128), reads=[pk, "epsc"], writes=[sdk])
                    op("act", lambda e: e.activation(out=sd.ap(), in_=sd.ap(), func=AF.Exp, scale=-0.5), reads=[sdk], writes=[sdk])
                    op("dve", lambda e: e.scalar_tensor_tensor(out=sd.ap(), in0=ysb.ap()[:, tsl(tb)], scalar=gng.ap()[:, h:h + 1], in1=sd.ap(),
                                                               op0=ALU.mult, op1=ALU.mult), reads=yk + [sdk, "gng"], writes=[sdk])
                    op("dve", lambda e: e.tensor_tensor(out=yo.ap()[:, tsl(tb)], in0=sd.ap(), in1=gs.ap()[:, tsl(tb)], op=ALU.mult),
                       reads=[sdk, ("gs", tb)], writes=[("yo", tb)])
                k.dma(yT_dram[1, :, h, :], yo.ap(), reads=[("yo", tb) for tb in range(NTB)], writes=[("yT_dram", 1, h)])


    def mixer_dsa(l):
        NQ = S // 128
        with Scope(k) as sc:
            qT = sc.sb("qT", [128, 4, S], BF16)
            kTz = sc.sb("kTz", [128, 4, S], BF16)
            qiT = sc.sb("qiT", [128, 2, S], BF16)
            kiTz = sc.sb("kiTz", [128, 2, S], BF16)
            vtok = sc.sb("cvtok", [128, 2, NQ, 128], BF16)
            wi = sc.sb("wi", [128, NQ, 4], F32)
            wabs = sc.sb("wabs", [128, NQ, 4], F32)
            wsg = sc.sb("wsg", [128, NQ, 4], F32)
            yo = sc.sb("cyo", [128, 4, S], BF16)
            onesb = sc.sb("onesb", [128, 128], BF16)
            rt_r = sc.ring("rt", [128, TB], F32, 3)
            op("pool", lambda e: e.memset(onesb.ap(), 1.0), writes=["onesb"])
            with Scope(k) as s2:
                stg_r = s2.ring("cstg", [128, KC, 128], F32, 2)
                w_r = s2.ring("cw", [128, KC, 128], BF16, 2)
                wsw_r = s2.ring("cwsw", [128, KC, 128], BF16, 2)
                rope = s2.sb("rope", [128, 2, S], F32)
                k.dma(rope.ap(), c_rope.rearrange("a p s -> p a s"), writes=["rope"])

                def wload_cols(col_list, swap):
                    stg, sk = stg_r.next()
                    wt, wk = w_r.next()
                    o = 0
                    for (c0, n) in col_list:
                        if c0 is None:
                            op("pool", lambda e: e.memset(stg.ap()[:, :, o:o + n], 0.0), writes=[(sk, o), sk])
                        else:
                            k.dma(stg.ap()[:, :, o:o + n], P["w_in"][l][:, c0:c0 + n].rearrange("(c p) n -> p c n", p=128), writes=[(sk, o), sk])
                        o += n
                    rk = [(sk, oo) for oo in np.cumsum([0] + [n for _, n in col_list[:-1]]).tolist()] + [sk]
                    cast(wt.ap()[:, :, 0:o], stg.ap()[:, :, 0:o], rk, [wk])
                    if not swap:
                        return wt, wk, None, None
                    ws, wsk = wsw_r.next()
                    for b0 in range(0, o, 64):
                        cast(ws.ap()[:, :, b0:b0 + 32], stg.ap()[:, :, b0 + 32:b0 + 64], rk, [(wsk, b0), wsk])
                        cast(ws.ap()[:, :, b0 + 32:b0 + 64], stg.ap()[:, :, b0:b0 + 32], rk, [(wsk, b0 + 32), wsk])
                    return wt, wk, ws, [(wsk, b0) for b0 in range(0, o, 32)] + [wsk]

                def rope_proj(col_list, dst_fn, dkey):
                    wt, wk, ws, wsk = wload_cols(col_list, True)
                    for tb in range(NTB):
                        p1, pk1 = psr.next()
                        p2, pk2 = psr.next()
                        for kc in range(KC):
                            op("pe", lambda e: e.matmul(p1.ap(), lhsT=wt.ap()[:, kc, :], rhs=uT.ap()[:, kc, usl(tb)], start=(kc == 0), stop=(kc == KC - 1)),
                               reads=[wk] + ukeys(kc, tb), writes=[pk1])
                        for kc in range(KC):
                            op("pe", lambda e: e.matmul(p2.ap(), lhsT=ws.ap()[:, kc, :], rhs=uT.ap()[:, kc, usl(tb)], start=(kc == 0), stop=(kc == KC - 1)),
                               reads=wsk + ukeys(kc, tb), writes=[pk2])
                        t1, tk1 = rt_r.next()
                        t2, tk2 = rt_r.next()
                        op("dve", lambda e: e.tensor_tensor(out=t1.ap(), in0=rope.ap()[:, 0, tsl(tb)], in1=p1.ap(), op=ALU.mult), reads=["rope", pk1], writes=[tk1])
                        op("dve", lambda e: e.tensor_tensor(out=t2.ap(), in0=rope.ap()[:, 1, tsl(tb)], in1=p2.ap(), op=ALU.mult), reads=["rope", pk2], writes=[tk2])
                        op("pool", lambda e: e.tensor_tensor(out=dst_fn(tb), in0=t1.ap(), in1=t2.ap(), op=ALU.add), reads=[tk1, tk2], writes=[(dkey, tb)])

                co = C_OFF
                for ch in range(4):
                    rope_proj([(co + ch * 128, 128)], lambda tb: qT.ap()[:, ch, tsl(tb)], ("qT", ch))
                for c in range(2):
                    rope_proj([(co + 512 + c * 64, 64), (None, 64)], lambda tb: kTz.ap()[:, c * 2, tsl(tb)], ("kTz", c * 2))
                    rope_proj([(None, 64), (co + 512 + c * 64, 64)], lambda tb: kTz.ap()[:, c * 2 + 1, tsl(tb)], ("kTz", c * 2 + 1))
                for ch in range(2):
                    rope_proj([(co + 768 + ch * 128, 128)], lambda tb: qiT.ap()[:, ch, tsl(tb)], ("qiT", ch))
                rope_proj([(co + 1024, 64), (None, 64)], lambda tb: kiTz.ap()[:, 0, tsl(tb)], ("kiTz", 0))
                rope_proj([(None, 64), (co + 1024, 64)], lambda tb: kiTz.ap()[:, 1, tsl(tb)], ("kiTz", 1))
                for c in range(2):
                    wt, wk, _, _ = wload_cols([(co + 640 + c * 64, 64), (co + 640 + c * 64, 64)], False)
                    for s4 in range(NQ // 4):
                        ps, pk = psr.next()
                        for j in range(4):
                            sb = s4 * 4 + j
                            for kc in range(KC):
                                op("pe", lambda e: e.matmul(ps.ap()[:, j * 128:(j + 1) * 128], lhsT=uT.ap()[:, kc, 2 + sb * 128: 2 + (sb + 1) * 128],
                                                            rhs=wt.ap()[:, kc, :], start=(kc == 0), stop=(kc == KC - 1)),
                                   reads=[wk] + ukeys(kc, sb // 4), writes=[pk])
                        cast(vtok.ap()[:, c, s4 * 4:(s4 + 1) * 4, :], ps.ap().rearrange("p (a b) -> p a b", a=4), [pk], [("cvtok", c, s4)], eng="act")
                wt, wk, _, _ = wload_cols([(co + 1088, 4)], False)
                ps, pk = psr.next()
                for qb in range(NQ):
                    for kc in range(KC):
                        op("pe", lambda e: e.matmul(ps.ap()[:, qb * 4:(qb + 1) * 4], lhsT=uT.ap()[:, kc, 2 + qb * 128: 2 + (qb + 1) * 128],
                                                    rhs=wt.ap()[:, kc, 0:4], start=(kc == 0), stop=(kc == KC - 1)),
                           reads=[wk] + ukeys(kc, qb // 4), writes=[pk])
                op("dve", lambda e: e.tensor_copy(out=wi.ap(), in_=ps.ap()[:, 0:NQ * 4].rearrange("p (a b) -> p a b", b=4)), reads=[pk], writes=["wi"])
                op("act", lambda e: e.activation(out=wabs.ap(), in_=wi.ap(), func=AF.Abs), reads=["wi"], writes=["wabs"])
                op("act", lambda e: e.activation(out=wsg.ap(), in_=wi.ap(), func=AF.Sign), reads=["wi"], writes=["wsg"])
            NCHN = 4
            score2 = [sc.sb("score%d" % i, [128, S], F32) for i in range(NCHN)]
            Mf2 = [sc.sb("Mf%d" % i, [128, S], BF16) for i in range(NCHN)]
            MT2 = [sc.sb("MT%d" % i, [128, NQ, 128], BF16) for i in range(NCHN)]
            m82 = [sc.sb("m8_%d" % i, [128, 8], F32) for i in range(NCHN)]
            e_r = sc.ring("cE", [128, 4, 128], BF16, 3)
            p_r = sc.ring("cP", [128, 4, 128], BF16, 3)
            rec_r = sc.ring("crec", [128, TB], F32, 2)
            SENT = -3e30

            def scores(qb, i):
                score = score2[i]
                ncol = (qb + 1) * 128
                qs = slice(qb * 128, (qb + 1) * 128)
                for g0 in range(0, ncol, TB):
                    gn = min(TB, ncol - g0)
                    for hi in range(4):
                        ps, pk = psr.next()
                        op("pe", lambda e: e.matmul(ps.ap()[:, 0:gn], lhsT=qiT.ap()[:, hi // 2, qs], rhs=kiTz.ap()[:, hi % 2, g0:g0 + gn], start=True, stop=True),
                           reads=[(("qiT", hi // 2), qb // 4)] + [(("kiTz", hi % 2), tb) for tb in range(g0 // TB, (g0 + gn - 1) // TB + 1)], writes=[pk])
                        r, rk = rt_r.next()
                        op("act", lambda e: e.activation(out=r.ap()[:, 0:gn], in_=ps.ap()[:, 0:gn], func=AF.Relu, scale=wabs.ap()[:, qb, hi:hi + 1]),
                           reads=[pk, "wabs"], writes=[rk])
                        if hi == 0:
                            op("dve", lambda e: e.tensor_scalar(out=score.ap()[:, g0:g0 + gn], in0=r.ap()[:, 0:gn], scalar1=wsg.ap()[:, qb, hi:hi + 1],
                                                                scalar2=None, op0=ALU.mult), reads=[rk, "wsg"], writes=[("score", i, g0)])
                        else:
                            op("dve", lambda e: e.scalar_tensor_tensor(out=score.ap()[:, g0:g0 + gn], in0=r.ap()[:, 0:gn], scalar=wsg.ap()[:, qb, hi:hi + 1],
                                                                       in1=score.ap()[:, g0:g0 + gn], op0=ALU.mult, op1=ALU.add),
                               reads=[rk, "wsg", ("score", i, g0)], writes=[("score", i, g0)])
                sk_all = [("score", i, g0) for g0 in range(0, S, TB)]
                op("dve", lambda e: e.tensor_tensor(out=score.ap()[:, qs], in0=score.ap()[:, qs], in1=tri.ap()[:, 2, :], op=ALU.mult), reads=sk_all + ["tri"], writes=sk_all)
                op("dve", lambda e: e.tensor_tensor(out=score.ap()[:, qs], in0=score.ap()[:, qs], in1=tri.ap()[:, 3, :], op=ALU.add), reads=sk_all + ["tri"], writes=sk_all)

            def topk_gen(blocks):
                sk_all = lambda i: [("score", i, g0) for g0 in range(0, S, TB)]
                act = [(i, qb, (qb + 1) * 128) for i, qb in enumerate(blocks) if (qb + 1) * 128 > 256]
                for r_ in range(32 if act else 0):
                    for i, qb, ncol in act:
                        op("dve", lambda e: e.max(out=m82[i].ap(), in_=score2[i].ap()[:, 0:ncol]), reads=sk_all(i), writes=[("m8", i)])
                    for i, qb, ncol in act:
                        op("dve", lambda e: e.match_replace(out=score2[i].ap()[:, 0:ncol], in_to_replace=m82[i].ap(), in_values=score2[i].ap()[:, 0:ncol], imm_value=SENT),
                           reads=sk_all(i) + [("m8", i)], writes=sk_all(i))
                    yield
                for i, qb in enumerate(blocks):
                    ncol = (qb + 1) * 128
                    if ncol > 256:
                        op("dve", lambda e: e.tensor_scalar(out=Mf2[i].ap()[:, 0:ncol], in0=score2[i].ap()[:, 0:ncol], scalar1=SENT, scalar2=None, op0=ALU.is_equal),
                           reads=sk_all(i), writes=[("Mf", i)])
                    else:
                        op("dve", lambda e: e.tensor_scalar(out=Mf2[i].ap()[:, 0:ncol], in0=score2[i].ap()[:, 0:ncol], scalar1=-1e29, scalar2=None, op0=ALU.is_ge),
                           reads=sk_all(i), writes=[("Mf", i)])
                yield

            def maskT(qb, i):
                for s4 in range(0, qb + 1, 4):
                    nb = min(4, qb + 1 - s4)
                    ps, pk = psr.next()
                    for j in range(nb):
                        sb = s4 + j
                        op("pe", lambda e: e.transpose(out=ps.ap().bitcast(BF16)[:, j * 128:(j + 1) * 128], in_=Mf2[i].ap()[:, sb * 128:(sb + 1) * 128], identity=identb.ap()),
                           reads=[("Mf", i), "identb"], writes=[pk])
                    op("act", lambda e: e.copy(out=MT2[i].ap()[:, s4:s4 + nb, :], in_=ps.ap().bitcast(BF16)[:, 0:nb * 128].rearrange("p (a b) -> p a b", b=128)),
                       reads=[pk], writes=[("MT", i, s4)])

            def attn_gen(qb, i):
                qs = slice(qb * 128, (qb + 1) * 128)
                MT = MT2[i]
                for c in range(2):
                    ps_o, pk_o = psacc.next()
                    ps_d, pk_d = psacc.next()
                    for sb in range(qb + 1):
                        ps, pk = psr.next()
                        for j in range(4):
                            hq = c * 4 + j
                            op("pe", lambda e: e.matmul(ps.ap()[:, j * 128:(j + 1) * 128], lhsT=kTz.ap()[:, c * 2 + hq % 2, sb * 128:(sb + 1) * 128],
                                                        rhs=qT.ap()[:, hq // 2, qs], start=True, stop=True),
                               reads=[(("kTz", c * 2 + hq % 2), sb // 4), (("qT", hq // 2), qb // 4)], writes=[pk])
                        E, Ek = e_r.next()
                        op("act", lambda e: e.activation(out=E.ap(), in_=ps.ap().rearrange("p (a b) -> p a b", a=4), func=AF.Exp, scale=0.125), reads=[pk], writes=[Ek])
                        Pm, Pk = p_r.next()
                        op("pool", lambda e: e.tensor_tensor(out=Pm.ap(), in0=E.ap(), in1=MT.ap()[:, sb:sb + 1, :].to_broadcast([128, 4, 128]), op=ALU.mult),
                           reads=[Ek, ("MT", i, (sb // 4) * 4)], writes=[Pk])
                        op("pe", lambda e: e.matmul(ps_o.ap(), lhsT=vtok.ap()[:, c, sb, :], rhs=Pm.ap().rearrange("p a b -> p (a b)"), start=(sb == 0), stop=(sb == qb)),
                           reads=[("cvtok", c, sb // 4), Pk], writes=[pk_o])
                        op("pe", lambda e: e.matmul(ps_d.ap(), lhsT=onesb.ap(), rhs=Pm.ap().rearrange("p a b -> p (a b)"), start=(sb == 0), stop=(sb == qb)),
                           reads=["onesb", Pk], writes=[pk_d])
                        yield
                    rec, reck = rec_r.next()
                    op("dve", lambda e: e.reciprocal(out=rec.ap(), in_=ps_d.ap()), reads=[pk_d], writes=[reck])
                    for j in range(4):
                        hq = c * 4 + j
                        hb = (hq % 2) * 64
                        op("dve", lambda e: e.tensor_tensor(out=yo.ap()[hb:hb + 64, hq // 2, qs], in0=rec.ap()[hb:hb + 64, j * 128:(j + 1) * 128],
                                                            in1=ps_o.ap()[hb:hb + 64, j * 128:(j + 1) * 128], op=ALU.mult),
                           reads=[reck, pk_o], writes=[("cyo", hq // 2, hq % 2)])

            def run(g):
                for _ in g:
                    pass

            def zipgens(gt, ga_list, n_t, n_a):
                ga = (x for g in ga_list for x in g)
                per = max(1, -(-n_a // max(1, n_t)))
                done_a = False
                for _ in gt:
                    for _j in range(per):
                        if next(ga, "END") == "END":
                            done_a = True
                            break
                if not done_a:
                    for _ in ga:
                        pass

            grps = [tuple(range(g * NCHN, (g + 1) * NCHN)) for g in range(NQ // NCHN)]
            for i_, qb_ in enumerate(grps[0]):
                scores(qb_, i_)
            run(topk_gen(grps[0]))
            for i_, qb_ in enumerate(grps[0]):
                maskT(qb_, i_)
            for p in range(len(grps)):
                cur = grps[p]
                if p + 1 < len(grps):
                    nxt = grps[p + 1]
                    for i_, qb_ in enumerate(nxt):
                        scores(qb_, i_)
                    n_a = sum(2 * (q_ + 1) for q_ in cur)
                    zipgens(topk_gen(nxt), [attn_gen(q_, i_) for i_, q_ in enumerate(cur)], 33, n_a)
                    for i_, qb_ in enumerate(nxt):
                        maskT(qb_, i_)
                else:
                    for i_, q_ in enumerate(cur):
                        run(attn_gen(q_, i_))
            for ch in range(4):
                k.dma(yT_dram[2, :, ch, :], yo.ap()[:, ch, :], reads=[("cyo", ch, 0), ("cyo", ch, 1)], writes=[("yT_dram", 2, ch)])

    def mixer_rwkv(l):
        C = 128; NCH = S // C
        with Scope(k) as sc:
            stg_r = sc.ring("astg", [128, KC, 128], F32, 1)
            stm_r = sc.ring("astm", [128, KC, 128], F32, 1)
            w1_r = sc.ring("aw1", [128, KC, 128], BF16, 1)
            wm_r = sc.ring("awm", [128, KC, 128], BF16, 1)
            mub_r = sc.ring("amub", [128, 128], F32, 2)
            lrb = sc.sb("lrb", [128, 3, 512], BF16)
            with Scope(k) as s2:
                lrs = s2.sb("lrs", [128, 3, 512], F32)
                k.dma(lrs.ap()[0:64, 0, :], P["a_w_up"][l], writes=["lrs0"]); k.dma(lrs.ap()[0:64, 1, :], P["a_a_up"][l], writes=["lrs1"])
                k.dma(lrs.ap()[:, 2, :], P["a_g_up"][l], writes=["lrs2"])
                op("dve", lambda e: e.tensor_copy(out=lrb.ap()[0:64, 0:2, :], in_=lrs.ap()[0:64, 0:2, :]), reads=["lrs0", "lrs1"], writes=["lrb01"])
                op("dve", lambda e: e.tensor_copy(out=lrb.ap()[:, 2, :], in_=lrs.ap()[:, 2, :]), reads=["lrs2"], writes=["lrb2"])

            def proj_shift(c0, ncol, consume, token_major=None):
                stg, sk = stg_r.next(); stm, smk = stm_r.next(); w1, w1k = w1_r.next(); wm, wmk = wm_r.next(); mub, mk = mub_r.next()
                k.dma(stg.ap()[:, :, 0:ncol], P["w_in"][l][:, c0:c0 + ncol].rearrange("(c p) n -> p c n", p=128), writes=[sk])
                k.dma(mub.ap()[:, 0:ncol], P["a_mu"][l][c0 - A_OFF:c0 - A_OFF + ncol].partition_broadcast(128), writes=[mk])
                op("dve", lambda e: e.tensor_tensor(out=stm.ap()[:, :, 0:ncol], in0=stg.ap()[:, :, 0:ncol],
                                                    in1=mub.ap()[:, 0:ncol].unsqueeze(1).to_broadcast([128, KC, ncol]), op=ALU.mult),
                   reads=[sk, mk], writes=[smk])
                op("act", lambda e: e.copy(out=wm.ap()[:, :, 0:ncol], in_=stm.ap()[:, :, 0:ncol]), reads=[smk], writes=[wmk])
                op("dve", lambda e: e.tensor_tensor(out=w1.ap()[:, :, 0:ncol], in0=stg.ap()[:, :, 0:ncol], in1=stm.ap()[:, :, 0:ncol], op=ALU.subtract),
                   reads=[sk, smk], writes=[w1k])
                if token_major is not None:
                    token_major(w1, w1k, wm, wmk)
                    return
                for tb in range(NTB):
                    ps, pk = psr.next()
                    for kc in range(KC):
                        op("pe", lambda e: e.matmul(ps.ap()[0:ncol, :], lhsT=w1.ap()[:, kc, 0:ncol], rhs=uT.ap()[:, kc, usl(tb)], start=(kc == 0), stop=False),
                           reads=[w1k] + ukeys(kc, tb), writes=[pk])
                    for kc in range(KC):
                        op("pe", lambda e: e.matmul(ps.ap()[0:ncol, :], lhsT=wm.ap()[:, kc, 0:ncol], rhs=uT.ap()[:, kc, usl(tb, 1)], start=False, stop=(kc == KC - 1)),
                           reads=[wmk] + ukeys(kc, tb) + (ukeys(kc, tb - 1) if tb else []) + ["uTpad"], writes=[pk])
                    consume(tb, ps, pk)

            tw = sc.sb("tw", [64, S], BF16); adT = sc.sb("adT", [64, S], BF16); sgT = sc.sb("sgT", [128, S], BF16)
            proj_shift(A_OFF + 1536, 64, lambda tb, ps, pk: op("act", lambda e: e.activation(out=tw.ap()[:, tsl(tb)], in_=ps.ap()[0:64, :], func=AF.Tanh), reads=[pk], writes=[("tw", tb)]))
            proj_shift(A_OFF + 1600, 64, lambda tb, ps, pk: op("act", lambda e: e.copy(out=adT.ap()[:, tsl(tb)], in_=ps.ap()[0:64, :]), reads=[pk], writes=[("adT", tb)]))
            proj_shift(A_OFF + 1664, 128, lambda tb, ps, pk: op("act", lambda e: e.activation(out=sgT.ap()[:, tsl(tb)], in_=ps.ap(), func=AF.Sigmoid), reads=[pk], writes=[("sgT", tb)]))
            pc = sc.sb("pc", [128, 8, 4], F32)
            for i_, nm in enumerate(["a_w0", "a_a0", "a_k_k", "a_k_a", "a_r_k", "a_gn_g", "a_gn_b"]):
                k.dma(pc.ap()[:, i_, :], P[nm][l].rearrange("(h p) -> p h", p=128), writes=[("pc", i_)], allow_slow_non_contiguous=True)
            op("dve", lambda e: e.tensor_scalar(out=pc.ap()[:, 7, :], in0=pc.ap()[:, 3, :], scalar1=-1.0, scalar2=1.0, op0=ALU.mult, op1=ALU.add), reads=[("pc", 3)], writes=[("pc", 7)])
            pcall = [("pc", i_) for i_ in range(8)]
            gnc = sc.sb("gnc", [128, 1], F32)
            op("pool", lambda e: e.memset(gnc.ap(), 64e-5), writes=["gnc"])
            F = {n_: sc.sb("af_" + n_, [128, S], F32) for n_ in ["r", "k", "v", "a", "L", "lw", "t1", "t2"]}
            F["g"] = sc.sb("af_g", [128, S], BF16)
            Bh = {n_: [sc.sb("ab_%s%d" % (n_, h), [128, S], BF16) for h in range(2)] for n_ in ["bt", "kt", "bc", "kc"]}
            Bh["rh"] = [sc.sb("ab_rh", [128, S], BF16)] * 2
            Bh["ah"] = [sc.sb("ab_ah", [128, S], BF16)] * 2
            AR = sc.sb("AR", [128, NCH, 2, C], BF16)
            dcol = sc.sb("adcol", [128, 3, NCH], F32)
            vtok = sc.sb("avtok", [128, NCH, 128], BF16)
            vpad = [sc.sb("avpad%d" % h, [128, NCH, 128], BF16) for h in range(2)]
            H = [sc.sb("aH%d" % h, [128, 128], F32) for h in range(2)]
            Hb_r = [sc.ring("aHb%d" % h, [128, 128], BF16, 2) for h in range(2)]
            upad = [sc.ring("aupad%d" % h, [128, 128], BF16, 2) for h in range(2)]
            mt_r = sc.ring("amt", [128, 128], BF16, 8)
            kp_r = sc.ring("akp", [128, 128], BF16, 12)
            wu_r = sc.ring("awu", [128, 128], BF16, 2)
            tk_r = sc.ring("atk", [128, 128], BF16, 6)
            ya = F["t1"]
            yo = Bh["rh"][0]
            sq_r = sc.ring("asq", [128, TB], F32, 2)
            hm = lambda h: tri.ap()[:, 6, h:h + 1]
            for h in range(2):
                for r_ in upad[h].t:
                    op("pool", lambda e: e.memset(r_.ap(), 0.0), writes=[("upad0", h, r_.name)])
            for hp in range(4):
                co = A_OFF + hp * 128
                prm = lambda i_: pc.ap()[:, i_, hp:hp + 1]
                for nm, cc in (("r", co), ("k", co + 512), ("v", co + 1024)):
                    proj_shift(cc, 128, lambda tb, ps, pk: op("act", lambda e: e.copy(out=F[nm].ap()[:, tsl(tb)], in_=ps.ap()), reads=[pk], writes=[(nm, tb)]))
                allk = lambda nm: [(nm, tb) for tb in range(NTB)]

                def vtm(w1, w1k, wm, wmk):
                    for c4 in range(NCH // 4):
                        ps, pk = psr.next()
                        for j in range(4):
                            c = c4 * 4 + j
                            for kc in range(KC):
                                op("pe", lambda e: e.matmul(ps.ap()[:, j * 128:(j + 1) * 128], lhsT=uT.ap()[:, kc, 2 + c * C:2 + (c + 1) * C], rhs=w1.ap()[:, kc, :], start=(kc == 0), stop=False),
                                   reads=[w1k] + ukeys(kc, c // 4), writes=[pk])
                            for kc in range(KC):
                                op("pe", lambda e: e.matmul(ps.ap()[:, j * 128:(j + 1) * 128], lhsT=uT.ap()[:, kc, 1 + c * C:1 + (c + 1) * C], rhs=wm.ap()[:, kc, :], start=False, stop=(kc == KC - 1)),
                                   reads=[wmk] + ukeys(kc, c // 4) + (ukeys(kc, c // 4 - 1) if c >= 4 else []) + ["uTpad"], writes=[pk])
                        cast(vtok.ap()[:, c4 * 4:(c4 + 1) * 4, :], ps.ap().rearrange("p (a b) -> p a b", a=4), [pk], [("avtok", c4)], eng="act")
                proj_shift(co + 1024, 128, None, token_major=vtm)
                vk = [("avtok", c4) for c4 in range(NCH // 4)]
                for h in range(2):
                    op("pool", lambda e: e.tensor_tensor(out=vpad[h].ap(), in0=vtok.ap(), in1=tri.ap()[:, 5, h * 64:h * 64 + 1].to_broadcast([128, NCH, 128]) if False else
                                                         tri.ap()[0:1, 5, :].partition_broadcast(128).unsqueeze(1).to_broadcast([128, NCH, 128]) if False else vtok.ap(), op=ALU.mult) if False else
                       e.memset(vpad[h].ap(), 0.0), reads=vk, writes=[("vpad", h)])
                    op("act", lambda e: e.copy(out=vpad[h].ap()[:, :, h * 64:(h + 1) * 64], in_=vtok.ap()[:, :, h * 64:(h + 1) * 64]), reads=vk + [("vpad", h)], writes=[("vpad", h)])
                for tb in range(NTB):
                    ps, pk = psr.next()
                    op("pe", lambda e: e.matmul(ps.ap(), lhsT=lrb.ap()[0:64, 0, hp * 128:(hp + 1) * 128], rhs=tw.ap()[:, tsl(tb)], start=True, stop=True), reads=["lrb01", ("tw", tb)], writes=[pk])
                    op("act", lambda e: e.activation(out=F["lw"].ap()[:, tsl(tb)], in_=ps.ap(), func=AF.Sigmoid, bias=prm(0)), reads=[pk] + pcall, writes=[("lw", tb)])
                    ps, pk = psr.next()
                    op("pe", lambda e: e.matmul(ps.ap(), lhsT=lrb.ap()[0:64, 1, hp * 128:(hp + 1) * 128], rhs=adT.ap()[:, tsl(tb)], start=True, stop=True), reads=["lrb01", ("adT", tb)], writes=[pk])
                    op("act", lambda e: e.activation(out=F["a"].ap()[:, tsl(tb)], in_=ps.ap(), func=AF.Sigmoid, bias=prm(1)), reads=[pk] + pcall, writes=[("a", tb)])
                    ps, pk = psr.next()
                    op("pe", lambda e: e.matmul(ps.ap(), lhsT=lrb.ap()[:, 2, hp * 128:(hp + 1) * 128], rhs=sgT.ap()[:, tsl(tb)], start=True, stop=True), reads=["lrb2", ("sgT", tb)], writes=[pk])
                    op("act", lambda e: e.copy(out=F["g"].ap()[:, tsl(tb)], in_=ps.ap()), reads=[pk], writes=[("g", tb)])
                op("dve", lambda e: e.tensor_scalar(out=F["lw"].ap(), in0=F["lw"].ap(), scalar1=-0.6065306597126334, scalar2=None, op0=ALU.mult), reads=allk("lw"), writes=allk("lw"))
                op("dve", lambda e: e.tensor_tensor_scan(out=F["L"].ap(), data0=ones.ap()[:, 0:1].to_broadcast([128, S]), data1=F["lw"].ap(), initial=0.0, op0=ALU.mult, op1=ALU.add), reads=allk("lw") + ["ones"], writes=["L"])
                L3 = F["L"].ap().rearrange("p (c t) -> p c t", t=C)
                op("dve", lambda e: e.tensor_tensor(out=dcol.ap()[:, 0, :], in0=L3[:, :, C - 1], in1=L3[:, :, C // 2 - 1], op=ALU.subtract), reads=["L"], writes=[("dc", 0)])
                op("dve", lambda e: e.tensor_copy(out=dcol.ap()[:, 1, 0:1], in_=L3[:, 0, C // 2 - 1:C // 2]), reads=["L"], writes=[("dc", 1)])
                op("dve", lambda e: e.tensor_tensor(out=dcol.ap()[:, 1, 1:NCH], in0=L3[:, 1:NCH, C // 2 - 1], in1=L3[:, 0:NCH - 1, C - 1], op=ALU.subtract), reads=["L", ("dc", 1)], writes=[("dc", 1)])
                op("dve", lambda e: e.tensor_tensor(out=dcol.ap()[:, 2, :], in0=dcol.ap()[:, 0, :], in1=dcol.ap()[:, 1, :], op=ALU.add), reads=[("dc", 0), ("dc", 1)], writes=[("dc", 2)])
                dck = [("dc", 0), ("dc", 1), ("dc", 2)]
                op("act", lambda e: e.activation(out=dcol.ap(), in_=dcol.ap(), func=AF.Exp), reads=dck, writes=dck)
                t1, t2 = F["t1"], F["t2"]
                op("dve", lambda e: e.tensor_scalar(out=t1.ap(), in0=F["k"].ap(), scalar1=prm(2), scalar2=None, op0=ALU.mult), reads=allk("k") + pcall, writes=["t1"])
                for tb in range(NTB):
                    sq, sqk = sq_r.next()
                    op("act", lambda e: e.activation(out=sq.ap(), in_=t1.ap()[:, tsl(tb)], func=AF.Square), reads=["t1"], writes=[sqk])
                    ps, pk = psr.next()
                    op("pe", lambda e: e.matmul(ps.ap(), lhsT=tri.ap()[:, 5, :], rhs=sq.ap(), start=True, stop=True), reads=[sqk, "tri"], writes=[pk])
                    op("dve", lambda e: e.tensor_scalar(out=sq.ap(), in0=ps.ap(), scalar1=1e-24, scalar2=None, op0=ALU.max), reads=[pk], writes=[sqk])
                    op("act", lambda e: e.activation(out=sq.ap(), in_=sq.ap(), func=AF.Sqrt), reads=[sqk], writes=[sqk])
                    op("dve", lambda e: e.reciprocal(out=sq.ap(), in_=sq.ap()), reads=[sqk], writes=[sqk])
                    op("dve", lambda e: e.tensor_tensor(out=t1.ap()[:, tsl(tb)], in0=t1.ap()[:, tsl(tb)], in1=sq.ap(), op=ALU.mult), reads=["t1", sqk], writes=["t1"])
                op("dve", lambda e: e.tensor_scalar(out=t2.ap(), in0=F["a"].ap(), scalar1=prm(3), scalar2=prm(7), op0=ALU.mult, op1=ALU.add), reads=allk("a") + pcall, writes=["t2"])
                op("dve", lambda e: e.tensor_tensor(out=F["k"].ap(), in0=F["k"].ap(), in1=t2.ap(), op=ALU.mult), reads=allk("k") + ["t2"], writes=allk("k"))
                op("dve", lambda e: e.tensor_tensor(out=F["a"].ap(), in0=F["a"].ap(), in1=t1.ap(), op=ALU.mult), reads=allk("a") + ["t1"], writes=allk("a"))
                op("dve", lambda e: e.scalar_tensor_tensor(out=t2.ap(), in0=F["r"].ap(), scalar=prm(4), in1=F["k"].ap(), op0=ALU.mult, op1=ALU.mult), reads=allk("r") + allk("k") + pcall, writes=["t2"])
                for tb in range(NTB):
                    sq, sqk = sq_r.next()
                    op("act", lambda e: e.copy(out=sq.ap(), in_=t2.ap()[:, tsl(tb)]), reads=["t2"], writes=[sqk])
                    ps, pk = psr.next()
                    op("pe", lambda e: e.matmul(ps.ap(), lhsT=tri.ap()[:, 5, :], rhs=sq.ap(), start=True, stop=True), reads=[sqk, "tri"], writes=[pk])
                    op("dve", lambda e: e.tensor_tensor(out=F["v"].ap()[:, tsl(tb)], in0=F["v"].ap()[:, tsl(tb)], in1=ps.ap(), op=ALU.mult), reads=[("v", tb), pk], writes=[("v", tb)])
                m_b = L3[:, :, C // 2 - 1:C // 2].to_broadcast([128, NCH, C])
                v3 = lambda t_: t_.ap().rearrange("p (c t) -> p c t", t=C)
                op("dve", lambda e: e.tensor_tensor(out=v3(t2), in0=L3, in1=m_b, op=ALU.subtract), reads=["L"], writes=["t2"])
                op("dve", lambda e: e.tensor_tensor(out=F["lw"].ap(), in0=t2.ap(), in1=F["lw"].ap(), op=ALU.subtract), reads=["t2"] + allk("lw"), writes=allk("lw"))
                op("act", lambda e: e.activation(out=F["lw"].ap(), in_=F["lw"].ap(), func=AF.Exp), reads=allk("lw"), writes=allk("lw"))
                op("act", lambda e: e.activation(out=F["L"].ap(), in_=t2.ap(), func=AF.Exp, scale=-1.0), reads=["t2"], writes=["L"])
                op("act", lambda e: e.activation(out=t2.ap(), in_=t2.ap(), func=AF.Exp), reads=["t2"], writes=["t2"])
                op("dve", lambda e: e.tensor_tensor(out=F["r"].ap(), in0=F["r"].ap(), in1=t2.ap(), op=ALU.mult), reads=allk("r") + ["t2"], writes=allk("r"))
                op("act", lambda e: e.copy(out=AR.ap()[:, :, 1, :], in_=v3(F["r"])), reads=allk("r"), writes=["AR1"])
                dB = dcol.ap()[:, 1, :].unsqueeze(2).to_broadcast([128, NCH, C]); dA = dcol.ap()[:, 0, :].unsqueeze(2).to_broadcast([128, NCH, C])
                op("dve", lambda e: e.tensor_tensor(out=v3(t2), in0=v3(F["r"]), in1=dB, op=ALU.mult), reads=allk("r") + dck, writes=["t2"])
                op("act", lambda e: e.copy(out=Bh["rh"][0].ap(), in_=t2.ap()), reads=["t2"], writes=[("rh", 0), ("rh", 1)] + [("yo", tb) for tb in range(NTB)])
                op("dve", lambda e: e.scalar_tensor_tensor(out=F["lw"].ap(), in0=t1.ap(), scalar=-1.0, in1=F["lw"].ap(), op0=ALU.mult, op1=ALU.mult), reads=["t1"] + allk("lw"), writes=allk("lw"))
                op("act", lambda e: e.copy(out=AR.ap()[:, :, 0, :], in_=v3(F["lw"])), reads=allk("lw"), writes=["AR0"])
                op("dve", lambda e: e.tensor_tensor(out=v3(t2), in0=v3(F["lw"]), in1=dB, op=ALU.mult), reads=allk("lw") + dck + [("rh", 0), ("rh", 1)], writes=["t2"])
                op("act", lambda e: e.copy(out=Bh["ah"][0].ap(), in_=t2.ap()), reads=["t2"], writes=[("ah", 0), ("ah", 1)])
                for src, nt, ncn in ((F["a"], "bt", "bc"), (F["k"], "kt", "kc")):
                    sk_ = allk("a") if nt == "bt" else allk("k")
                    op("dve", lambda e: e.tensor_tensor(out=src.ap(), in0=src.ap(), in1=F["L"].ap(), op=ALU.mult), reads=sk_ + ["L"], writes=sk_)
                    op("dve", lambda e: e.tensor_tensor(out=v3(t1) if nt == "kt" else v3(t2), in0=v3(src), in1=dA, op=ALU.mult), reads=sk_ + dck + [("ah", 0), ("ah", 1)], writes=["t1" if nt == "kt" else "t2"])
                    tt_ = t1 if nt == "kt" else t2
                    for h in range(2):
                        op("act", lambda e: e.activation(out=Bh[nt][h].ap(), in_=src.ap(), func=AF.Copy, scale=hm(h)), reads=sk_ + ["tri"], writes=[(nt, h)])
                        op("act", lambda e: e.activation(out=Bh[ncn][h].ap(), in_=tt_.ap(), func=AF.Copy, scale=hm(h)), reads=["t1" if nt == "kt" else "t2", "tri"], writes=[(ncn, h)])
                k.phase_barrier()
                pool_t = []
                for nm_ in ("r", "k", "a", "L", "lw", "t2"):
                    vb = F[nm_].ap().bitcast(BF16)
                    pool_t += [vb[:, j * 128:(j + 1) * 128] for j in range(32)]
                RES = {}
                it_ = iter(pool_t)
                for c_ in range(NCH):
                    for h_ in range(2):
                        for nm_ in ("TT", "rb", "ak", "rk", "bcT", "kcT"):
                            RES[(nm_, h_, c_)] = next(it_)
                vt_b = vtok.ap().rearrange("p a b -> p (a b)")
                tmp_t = [t_.ap() for t_ in mt_r.t] + [t_.ap() for t_ in tk_r.t] + [t_.ap() for t_ in kp_r.t] + [vt_b[:, j * 128:(j + 1) * 128] for j in range(16)]
                GCH = 8
                assert len(tmp_t) >= 5 * GCH, len(tmp_t)

                def A_gen(group):
                    st = {}
                    for gi_, (h, c) in enumerate(group):
                        cs = slice(c * C, (c + 1) * C)
                        T5 = tmp_t[gi_ * 5:(gi_ + 1) * 5]
                        tk_ = lambda j: ("tmp", gi_, j)
                        ps, pk = psr.next()
                        op("pe", lambda e: e.matmul(ps.ap()[:, 0:256], lhsT=Bh["bt"][h].ap()[:, cs], rhs=AR.ap()[:, c].rearrange("p a b -> p (a b)"), start=True, stop=True), reads=[("bt", h), "AR0", "AR1"], writes=[pk])
                        op("dve", lambda e: e.tensor_tensor(out=T5[1], in0=ps.ap()[:, 0:128], in1=tri.ap()[:, 1, :], op=ALU.mult), reads=[pk, "tri"], writes=[tk_(1)])
                        op("dve", lambda e: e.tensor_tensor(out=RES[("rb", h, c)], in0=ps.ap()[:, 128:256], in1=tri.ap()[:, 0, :], op=ALU.mult), reads=[pk, "tri"], writes=[("res", "rb", h, c)])
                        ps, pk = psr.next()
                        op("pe", lambda e: e.matmul(ps.ap()[:, 0:256], lhsT=Bh["kt"][h].ap()[:, cs], rhs=AR.ap()[:, c].rearrange("p a b -> p (a b)"), start=True, stop=True), reads=[("kt", h), "AR0", "AR1"], writes=[pk])
                        op("dve", lambda e: e.tensor_tensor(out=RES[("ak", h, c)], in0=ps.ap()[:, 0:128], in1=tri.ap()[:, 1, :], op=ALU.mult), reads=[pk, "tri"], writes=[("res", "ak", h, c)])
                        op("dve", lambda e: e.tensor_tensor(out=RES[("rk", h, c)], in0=ps.ap()[:, 128:256], in1=tri.ap()[:, 0, :], op=ALU.mult), reads=[pk, "tri"], writes=[("res", "rk", h, c)])
                        ps, pk = psr.next()
                        op("pe", lambda e: e.matmul(ps.ap()[:, 0:128], lhsT=AR.ap()[:, c, 0, :], rhs=Bh["bt"][h].ap()[:, cs], start=True, stop=True), reads=["AR0", ("bt", h)], writes=[pk])
                        op("dve", lambda e: e.tensor_tensor(out=T5[0], in0=ps.ap()[:, 0:128], in1=tri.ap()[:, 4, :], op=ALU.mult), reads=[pk, "tri"], writes=[tk_(0)])
                        op("dve", lambda e: e.tensor_tensor(out=T5[4], in0=T5[1], in1=identb.ap(), op=ALU.add), reads=[tk_(1), "identb"], writes=[tk_(4)])
                        pst, pkt = psr.next()
                        op("pe", lambda e: e.transpose(out=pst.ap().bitcast(BF16)[:, 0:128], in_=Bh["bc"][h].ap()[:, cs], identity=identb.ap()), reads=[("bc", h), "identb"], writes=[pkt])
                        op("pe", lambda e: e.transpose(out=pst.ap().bitcast(BF16)[:, 128:256], in_=Bh["kc"][h].ap()[:, cs], identity=identb.ap()), reads=[("kc", h), "identb"], writes=[pkt])
                        op("act", lambda e: e.copy(out=RES[("bcT", h, c)], in_=pst.ap().bitcast(BF16)[:, 0:128]), reads=[pkt], writes=[("res", "bcT", h, c)])
                        op("act", lambda e: e.copy(out=RES[("kcT", h, c)], in_=pst.ap().bitcast(BF16)[:, 128:256]), reads=[pkt], writes=[("res", "kcT", h, c)])
                        st[gi_] = [0, 1, 2, 3]
                    yield
                    for lev in range(6):
                        for gi_, (h, c) in enumerate(group):
                            T5 = tmp_t[gi_ * 5:(gi_ + 1) * 5]
                            tk_ = lambda j: ("tmp", gi_, j)
                            iX, iXT, iXn, iXTn = st[gi_]
                            psx, pkx = psr.next()
                            op("pe", lambda e: e.matmul(psx.ap()[:, 0:128], lhsT=T5[iXT], rhs=T5[iX], start=True, stop=True), reads=[tk_(iXT), tk_(iX)], writes=[pkx])
                            if lev < 5:
                                op("pe", lambda e: e.matmul(psx.ap()[:, 128:256], lhsT=T5[iX], rhs=T5[iXT], start=True, stop=True), reads=[tk_(iXT), tk_(iX)], writes=[pkx])
                            op("act", lambda e: e.copy(out=T5[iXn], in_=psx.ap()[:, 0:128]), reads=[pkx], writes=[tk_(iXn)])
                            if lev < 5:
                                op("act", lambda e: e.copy(out=T5[iXTn], in_=psx.ap()[:, 128:256]), reads=[pkx], writes=[tk_(iXTn)])
                            st[gi_] = [iXn, iXTn, iX, iXT]
                        yield
                        for gi_, (h, c) in enumerate(group):
                            T5 = tmp_t[gi_ * 5:(gi_ + 1) * 5]
                            tk_ = lambda j: ("tmp", gi_, j)
                            iX = st[gi_][0]
                            psq, pkq = psr.next()
                            op("pe", lambda e: e.matmul(psq.ap()[:, 0:128], lhsT=T5[iX], rhs=T5[4], start=True, stop=True), reads=[tk_(iX), tk_(4)], writes=[pkq])
                            if lev < 5:
                                op("dve", lambda e: e.tensor_tensor(out=T5[4], in0=T5[4], in1=psq.ap()[:, 0:128], op=ALU.add), reads=[tk_(4), pkq], writes=[tk_(4)])
                            else:
                                op("dve", lambda e: e.tensor_tensor(out=RES[("TT", h, c)], in0=T5[4], in1=psq.ap()[:, 0:128], op=ALU.add), reads=[tk_(4), pkq], writes=[("res", "TT", h, c)])
                        yield

                Hb = [None, None]

                def B_gen(chunks):
                    for c in chunks:
                        cs = slice(c * C, (c + 1) * C)
                        rk_ = lambda nm_, h: ("res", nm_, h, c)
                        wus = []
                        for h in range(2):
                            psw, pkw = psr.next()
                            if c > 0:
                                op("pe", lambda e: e.matmul(psw.ap()[:, 0:128], lhsT=Bh["ah"][h].ap()[:, cs], rhs=Hb[h][0].ap(), start=True, stop=False), reads=[("ah", h), Hb[h][1]], writes=[pkw])
                            op("pe", lambda e: e.matmul(psw.ap()[:, 0:128], lhsT=RES[("ak", h, c)], rhs=vpad[h].ap()[:, c, :], start=(c == 0), stop=True), reads=[rk_("ak", h), ("vpad", h)], writes=[pkw])
                            wu, wuk = wu_r.next()
                            op("act", lambda e: e.copy(out=wu.ap(), in_=psw.ap()[:, 0:128]), reads=[pkw], writes=[wuk])
                            wus.append((wu, wuk))
                        yield
                        Ups = []
                        for h in range(2):
                            psu, pku = psr.next()
                            op("pe", lambda e: e.matmul(psu.ap()[:, 0:128], lhsT=RES[("TT", h, c)], rhs=wus[h][0].ap(), start=True, stop=True), reads=[rk_("TT", h), wus[h][1]], writes=[pku])
                            up_, upk = upad[h].next()
                            op("dve", lambda e: e.tensor_copy(out=up_.ap(), in_=psu.ap()[:, 0:128]), reads=[pku, ("upad0", h, up_.name)], writes=[upk])
                            Ups.append((up_, upk))
                        yield
                        if c < NCH - 1:
                            for h in range(2):
                                psh, pkh = psr.next()
                                op("pe", lambda e: e.matmul(psh.ap()[:, 0:128], lhsT=RES[("bcT", h, c)], rhs=Ups[h][0].ap(), start=True, stop=False), reads=[rk_("bcT", h), Ups[h][1]], writes=[pkh])
                                op("pe", lambda e: e.matmul(psh.ap()[:, 0:128], lhsT=RES[("kcT", h, c)], rhs=vpad[h].ap()[:, c, :], start=False, stop=True), reads=[rk_("kcT", h), ("vpad", h)], writes=[pkh])
                                if c == 0:
                                    op("dve", lambda e: e.tensor_copy(out=H[h].ap(), in_=psh.ap()[:, 0:128]), reads=[pkh], writes=[("H", h)])
                                else:
                                    op("dve", lambda e: e.scalar_tensor_tensor(out=H[h].ap(), in0=H[h].ap(), scalar=dcol.ap()[:, 2, c:c + 1], in1=psh.ap()[:, 0:128], op0=ALU.mult, op1=ALU.add),
                                       reads=[("H", h), pkh] + dck, writes=[("H", h)])
                        psy, pky = psr.next()
                        first = True
                        for h in range(2):
                            if c > 0:
                                op("pe", lambda e: e.matmul(psy.ap()[:, 0:128], lhsT=Hb[h][0].ap(), rhs=Bh["rh"][h].ap()[:, cs], start=first, stop=False), reads=[Hb[h][1], ("rh", h)], writes=[pky]); first = False
                            op("pe", lambda e: e.matmul(psy.ap()[:, 0:128], lhsT=Ups[h][0].ap(), rhs=RES[("rb", h, c)], start=first, stop=False), reads=[Ups[h][1], rk_("rb", h)], writes=[pky]); first = False
                            op("pe", lambda e: e.matmul(psy.ap()[:, 0:128], lhsT=vpad[h].ap()[:, c, :], rhs=RES[("rk", h, c)], start=False, stop=(h == 1)), reads=[("vpad", h), rk_("rk", h)], writes=[pky])
                        op("act", lambda e: e.copy(out=ya.ap()[:, cs], in_=psy.ap()[:, 0:128]), reads=[pky], writes=[("ya", c // 4)])
                        if c < NCH - 1:
                            for h in range(2):
                                Hb[h] = Hb_r[h].next()
                                op("act", lambda e: e.copy(out=Hb[h][0].ap(), in_=H[h].ap()), reads=[("H", h)], writes=[Hb[h][1]])
                        yield

                def zip2(ga, gb):
                    da = db = False
                    while not (da and db):
                        if not da and next(ga, "END") == "END":
                            da = True
                        if not db and next(gb, "END") == "END":
                            db = True

                groups = [[(h, c) for c in range(g * 4, g * 4 + 4) for h in range(2)] for g in range(NCH // 4)]
                for _ in A_gen(groups[0]):
                    pass
                for g in range(len(groups)):
                    bg = B_gen(range(g * 4, g * 4 + 4))
                    if g + 1 < len(groups):
                        zip2(A_gen(groups[g + 1]), bg)
                    else:
                        for _ in bg:
                            pass
                k.phase_barrier()
                for tb in range(NTB):
                    yk = [("ya", tb)]
                    ps, pk = psr.next()
                    op("pe", lambda e: e.matmul(ps.ap(), lhsT=tri.ap()[:, 5, :], rhs=ya.ap()[:, tsl(tb)], start=True, stop=True), reads=yk + ["tri"], writes=[pk])
                    sq, sqk = sq_r.next()
                    op("dve", lambda e: e.scalar_tensor_tensor(out=sq.ap(), in0=ps.ap(), scalar=-1.0 / 64, in1=ya.ap()[:, tsl(tb)], op0=ALU.mult, op1=ALU.add), reads=[pk] + yk, writes=[sqk])
                    sq2, sq2k = sq_r.next()
                    op("act", lambda e: e.activation(out=sq2.ap(), in_=sq.ap(), func=AF.Square), reads=[sqk], writes=[sq2k])
                    ps2, pk2 = psr.next()
                    op("pe", lambda e: e.matmul(ps2.ap(), lhsT=tri.ap()[:, 5, :], rhs=sq2.ap(), start=True, stop=True), reads=[sq2k, "tri"], writes=[pk2])
                    op("act", lambda e: e.activation(out=sq2.ap(), in_=ps2.ap(), func=AF.Ln, bias=gnc.ap(), scale=1.0 / 64), reads=[pk2, "gnc"], writes=[sq2k])
                    op("act", lambda e: e.activation(out=sq2.ap(), in_=sq2.ap(), func=AF.Exp, scale=-0.5), reads=[sq2k], writes=[sq2k])
                    op("dve", lambda e: e.tensor_tensor(out=sq.ap(), in0=sq.ap(), in1=sq2.ap(), op=ALU.mult), reads=[sqk, sq2k], writes=[sqk])
                    op("dve", lambda e: e.tensor_scalar(out=sq.ap(), in0=sq.ap(), scalar1=prm(5), scalar2=prm(6), op0=ALU.mult, op1=ALU.add), reads=[sqk] + pcall, writes=[sqk])
                    op("dve", lambda e: e.tensor_tensor(out=sq.ap(), in0=sq.ap(), in1=F["v"].ap()[:, tsl(tb)], op=ALU.add), reads=[sqk, ("v", tb)], writes=[sqk])
                    op("dve", lambda e: e.tensor_tensor(out=yo.ap()[:, tsl(tb)], in0=sq.ap(), in1=F["g"].ap()[:, tsl(tb)], op=ALU.mult), reads=[sqk, ("g", tb)], writes=[("yo", tb)])
                k.dma(yT_dram[0, :, hp, :], yo.ap(), reads=[("yo", tb) for tb in range(NTB)], writes=[("yT_dram", 0, hp)])

    for l in range(DEPTH):
        with Scope(k) as sc:
            hT = sc.sb("hT", [128, KC, S], F32)
            sqr = sc.ring("sq", [128, TB], F32, 3)
            tmpr = sc.ring("nt", [128, TB], F32, 4)
            load_h(hT)
            rmsnorm(hT, 2 * l, u_dst, sqr, tmpr)
        if "yT_in0" not in dbg:
            mixer_rwkv(l)
        if "yT_in1" not in dbg:
            mixer_hgrn(l)
        if "yT_in2" not in dbg:
            mixer_dsa(l)
        with Scope(k) as sc:
            yT = sc.sb("yT", [128, 3, 4, S], BF16)
            mT = sc.sb("mT", [128, KC, S], BF16)
            stg_g = sc.ring("stg_g", [128, 3, KC, 128], F32, 2)
            wg_r = sc.ring("wg", [128, 3, KC, 128], BF16, 2)
            stg_b = sc.ring("stg_b", [128, 3, 4, 128], F32, 2)
            wb_r = sc.ring("wb", [128, 3, 4, 128], BF16, 2)
            sgr = sc.ring("sg", [128, TB], F32, 3)
            accr = sc.ring("acc", [128, TB], F32, 2)
            for n in range(3):
                k.dma(yT.ap()[:, n], yT_src[l][n], writes=[("yT", n)])
            for dc in range(KC):
                sg_t, sgk = stg_g.next(); wg, wgk = wg_r.next()
                sb_t, sbk = stg_b.next(); wb, wbk = wb_r.next()
                for n in range(3):
                    c0 = G_OFF + n * D + dc * 128
                    k.dma(sg_t.ap()[:, n], P["w_in"][l][:, c0:c0 + 128].rearrange("(c p) n -> p c n", p=128), writes=[(sgk, n)])
                    k.dma(sb_t.ap()[:, n], P["w_branch"][l, n][:, dc * 128:(dc + 1) * 128].rearrange("(c p) n -> p c n", p=128),
                          writes=[(sbk, n)])
                cast(wg.ap(), sg_t.ap(), [(sgk, n) for n in range(3)], [wgk])
                cast(wb.ap(), sb_t.ap(), [(sbk, n) for n in range(3)], [wbk])
                for tb in range(NTB):
                    acc, acck = accr.next()
                    for n in range(3):
                        pg, pgk = psr.next()
                        for kc in range(KC):
                            op("pe", lambda e: e.matmul(pg.ap(), lhsT=wg.ap()[:, n, kc, :], rhs=uT.ap()[:, kc, usl(tb)],
                                                        start=(kc == 0), stop=(kc == KC - 1)),
                               reads=[wgk] + ukeys(kc, tb), writes=[pgk])
                        pb, pbk = psr.next()
                        for c in range(4):
                            op("pe", lambda e: e.matmul(pb.ap(), lhsT=wb.ap()[:, n, c, :], rhs=yT.ap()[:, n, c, tsl(tb)],
                                                        start=(c == 0), stop=(c == 3)),
                               reads=[wbk, ("yT", n)], writes=[pbk])
                        sg, sgk2 = sgr.next()
                        op("act", lambda e: e.activation(out=sg.ap(), in_=pg.ap(), func=AF.Sigmoid), reads=[pgk], writes=[sgk2])
                        if n == 0:
                            op("dve", lambda e: e.tensor_tensor(out=acc.ap(), in0=sg.ap(), in1=pb.ap(), op=ALU.mult),
                               reads=[sgk2, pbk], writes=[acck])
                        else:
                            op("dve", lambda e: e.tensor_tensor(out=sg.ap(), in0=sg.ap(), in1=pb.ap(), op=ALU.mult),
                               reads=[sgk2, pbk], writes=[sgk2])
                            dstap = mT.ap()[:, dc, tsl(tb)] if n == 2 else acc.ap()
                            op("pool" if n == 1 else "dve", lambda e: e.tensor_tensor(out=dstap, in0=sg.ap(), in1=acc.ap(), op=ALU.add),
                               reads=[sgk2, acck], writes=[("mT", dc, tb)] if n == 2 else [acck])
            stg_o = sc.ring("stg_o", [128, KC, 128], F32, 2)
            wo_r = sc.ring("wo", [128, KC, 128], BF16, 2)
            hr = sc.ring("hr", [128, TB], F32, 3)
            for dc in range(KC):
                wo, wok = wload(P["w_o"][l][:, dc * 128:(dc + 1) * 128], KC, 128, stg_o, wo_r)
                for tb in range(NTB):
                    ht, htk = hr.next()
                    k.dma(ht.ap(), h_dram[:, dc, tsl(tb)], writes=[htk])
                    ps, pk = psr.next()
                    for c in range(KC):
                        op("pe", lambda e: e.matmul(ps.ap(), lhsT=wo.ap()[:, c, :], rhs=mT.ap()[:, c, tsl(tb)],
                                                    start=(c == 0), stop=(c == KC - 1)),
                           reads=[wok, ("mT", c, tb)], writes=[pk])
                    op("dve", lambda e: e.tensor_tensor(out=ht.ap(), in0=ht.ap(), in1=ps.ap(), op=ALU.add),
                       reads=[htk, pk], writes=[htk])
                    k.dma(h_dram[:, dc, tsl(tb)], ht.ap(), reads=[htk], writes=[("h_dram", dc, tb)])
        with Scope(k) as sc:
            hT = sc.sb("hT", [128, KC, S], F32)
            sc2 = Scope(k)
            sc_outer = sc; sc = sc2
            sqr = sc.ring("sq", [128, TB], F32, 2)
            tmpr = sc.ring("nt", [128, TB], F32, 3)
            load_h(hT)
            rmsnorm(hT, 2 * l + 1, u_dst, sqr, tmpr)
            cw = sc.sb("cw", [128, 3, 2 * NFF], F32)
            cb = sc.sb("cb", [128, 2 * NFF], F32)
            for j in range(3):
                k.dma(cw.ap()[:, j, :], P["conv_w"][l, j].rearrange("(c p) -> p c", p=128), writes=["cw"], allow_slow_non_contiguous=True)
            k.dma(cb.ap(), P["conv_b"][l].rearrange("(c p) -> p c", p=128), writes=["cb"], allow_slow_non_contiguous=True)
            G = 2
            stg_u = sc.ring("stg_u", [128, 2, KC, 128], F32, 2)
            wu_r = sc.ring("wu", [128, 2, KC, 128], BF16, 2)
            stg_d = sc.ring("stg_d", [128, G, D], F32, 1)
            wd_r = sc.ring("wd", [128, G, D], BF16, 2)
            upr = sc.ring("up", [128, 2, S + 2], F32, 1)
            cr = sc.ring("cc", [128, 2, TB], F32, 2)
            actr = sc.ring("actT", [128, G, S], BF16, 2)
            up, upk = upr.next()
            op("pool", lambda e: e.memset(up.ap()[:, :, 0:2], 0.0), writes=["uppad"])
            for g0 in range(0, NFF, G):
                actT, actk = actr.next()
                wd, wdk = wload(P["w_down"][l][g0 * 128:(g0 + G) * 128, :], G, D, stg_d, wd_r)
                for gi in range(G):
                    j = g0 + gi
                    su, suk = stg_u.next(); wu, wuk = wu_r.next()
                    for hv in range(2):
                        c0 = hv * DFF + j * 128
                        k.dma(su.ap()[:, hv], P["w_up"][l][:, c0:c0 + 128].rearrange("(c p) n -> p c n", p=128), writes=[(suk, hv)])
                    cast(wu.ap(), su.ap(), [(suk, 0), (suk, 1)], [wuk])
                    for tb in range(NTB):
                        cc, cck = cr.next()
                        for hv in range(2):
                            ps, pk = psr.next()
                            for kc in range(KC):
                                op("pe", lambda e: e.matmul(ps.ap(), lhsT=wu.ap()[:, hv, kc, :], rhs=uT.ap()[:, kc, usl(tb)],
                                                            start=(kc == 0), stop=(kc == KC - 1)),
                                   reads=[wuk] + ukeys(kc, tb), writes=[pk])
                            ch = hv * NFF + j
                            op("act", lambda e: e.copy(out=up.ap()[:, hv, usl(tb)], in_=ps.ap()), reads=[pk, "uppad"], writes=[("up", hv, tb)])
                            op("dve", lambda e: e.tensor_scalar(out=cc.ap()[:, hv, :], in0=up.ap()[:, hv, usl(tb)],
                                                                scalar1=cw.ap()[:, 2, ch:ch + 1], scalar2=cb.ap()[:, ch:ch + 1],
                                                                op0=ALU.mult, op1=ALU.add),
                               reads=[("up", hv, tb), "cw", "cb"], writes=[(cck, hv)])
                            for sh in (1, 2):
                                rk = [("up", hv, tb)] + ([("up", hv, tb - 1)] if tb > 0 else [])
                                op("dve", lambda e: e.scalar_tensor_tensor(out=cc.ap()[:, hv, :], in0=up.ap()[:, hv, usl(tb, sh)],
                                                                           scalar=cw.ap()[:, 2 - sh, ch:ch + 1], in1=cc.ap()[:, hv, :],
                                                                           op0=ALU.mult, op1=ALU.add),
                                   reads=rk + ["cw", (cck, hv)], writes=[(cck, hv)])
                        op("act", lambda e: e.activation(out=cc.ap()[:, 0, :], in_=cc.ap()[:, 0, :], func=AF.Silu),
                           reads=[(cck, 0)], writes=[(cck, 0)])
                        op("pool", lambda e: e.tensor_tensor(out=actT.ap()[:, gi, tsl(tb)], in0=cc.ap()[:, 0, :], in1=cc.ap()[:, 1, :], op=ALU.mult),
                           reads=[(cck, 0), (cck, 1)], writes=[(actk, gi, tb)])
                for dc in range(KC):
                    for tb in range(NTB):
                        ps, pk = psr.next()
                        for gi in range(G):
                            op("pe", lambda e: e.matmul(ps.ap(), lhsT=wd.ap()[:, gi, dc * 128:(dc + 1) * 128], rhs=actT.ap()[:, gi, tsl(tb)],
                                                        start=(gi == 0), stop=(gi == G - 1)),
                               reads=[wdk, (actk, gi, tb)], writes=[pk])
                        op("dve", lambda e: e.tensor_tensor(out=hT.ap()[:, dc, tsl(tb)], in0=hT.ap()[:, dc, tsl(tb)], in1=ps.ap(), op=ALU.add),
                           reads=hkeys(dc, tb) + [pk], writes=hkeys(dc, tb))
            sc2.__exit__(None, None, None)
            sc = Scope(k)
            sqr = sc.ring("sq", [128, TB], F32, 2)
            tmpr = sc.ring("nt", [128, TB], F32, 3)
            if l < DEPTH - 1:
                store_h(hT)
            else:
                fo = sc.ring("fo", [128, KC, TB], F32, 1)
                ot = sc.ring("ot", [128, D], F32, 2)
                fT, fk = fo.next()
                cur = [0]

                def f_dst(kc, tb):
                    return fT.ap()[:, kc, :], [("fT", kc)]
                for tb in range(NTB):
                    ps, pk = psr.next()
                    for kc in range(KC):
                        sq, sk = sqr.next()
                        op("act", lambda e: e.activation(out=sq.ap(), in_=hT.ap()[:, kc, tsl(tb)], func=AF.Square),
                           reads=hkeys(kc, tb), writes=[sk])
                        op("pe", lambda e: e.matmul(ps.ap(), lhsT=ones.ap(), rhs=sq.ap(), start=(kc == 0), stop=(kc == KC - 1)),
                           reads=[sk, "ones"], writes=[pk])
                    sd, sdk = tmpr.next()
                    op("act", lambda e: e.activation(out=sd.ap(), in_=ps.ap(), func=AF.Ln, bias=epsc.ap(), scale=1.0 / D),
                       reads=[pk, "epsc"], writes=[sdk])
                    rs, rsk = tmpr.next()
                    op("act", lambda e: e.activation(out=rs.ap(), in_=sd.ap(), func=AF.Exp, scale=-0.5), reads=[sdk], writes=[rsk])
                    for kc in range(KC):
                        op("dve", lambda e: e.scalar_tensor_tensor(out=fT.ap()[:, kc, :], in0=hT.ap()[:, kc, tsl(tb)],
                                                                   scalar=gvec.ap()[:, 2 * DEPTH, kc:kc + 1], in1=rs.ap(),
                                                                   op0=ALU.mult, op1=ALU.mult),
                           reads=hkeys(kc, tb) + [rsk, "gvec"], writes=[("fT", kc)])
                    for t4 in range(4):
                        o, okk = ot.next()
                        for half in range(2):
                            ps2, pk2 = psr.next()
                            for j in range(4):
                                kc = half * 4 + j
                                op("pe", lambda e: e.transpose(out=ps2.ap()[:, j * 128:(j + 1) * 128],
                                                               in_=fT.ap()[:, kc, t4 * 128:(t4 + 1) * 128], identity=ident.ap()),
                                   reads=[("fT", kc), "ident"], writes=[pk2])
                            cast(o.ap()[:, half * 512:(half + 1) * 512], ps2.ap(), [pk2], [(okk, half)], eng=("dve", "act")[half])
                        r0 = tb * TB + t4 * 128
                        k.dma(out[r0:r0 + 128, :], o.ap(), reads=[(okk, 0), (okk, 1)], writes=[("out", r0)])
            sc.__exit__(None, None, None)
    k.drain("sp")
    return k


PARAM_NAMES = ["norm_mix_g", "w_in", "a_mu", "a_w0", "a_w_up", "a_a0", "a_a_up", "a_g_up", "a_k_k", "a_k_a", "a_r_k",
               "a_gn_g", "a_gn_b", "b_lb_logits", "b_gn_g", "w_branch", "w_o", "norm_ffn_g", "w_up", "conv_w", "conv_b",
               "w_down", "norm_final_g"]


def make_consts():
    i = np.arange(128)
    low = (i[:, None] >= i[None, :])
    bd = ((i[:, None] // 64) == (i[None, :] // 64))
    hm = np.zeros((128, 128)); hm[:64, 0] = 1; hm[64:, 1] = 1
    tri = np.stack([(i[:, None] <= i[None, :]), (i[:, None] < i[None, :]), low, np.where(low, 0.0, -1e30),
                    (i[:, None] > i[None, :]), bd, hm]).astype(np.float32)
    inv_freq = (np.float32(10000.0) ** (-np.arange(32, dtype=np.float32) * np.float32(2.0 / 64))).astype(np.float32)
    ang = (np.arange(S, dtype=np.float32)[:, None] * inv_freq[None, :]).astype(np.float32)
    cos = np.cos(ang).astype(np.float32).T; sin = np.sin(ang).astype(np.float32).T
    cosF = np.concatenate([cos, cos, cos, cos], axis=0)
    sinF = np.concatenate([-sin, sin, -sin, sin], axis=0)
    return {"c_ident": np.eye(128, dtype=np.float32), "c_tri": tri, "c_rope": np.stack([cosF, sinF]).astype(np.float32)}


def make_in_map(inputs, b, consts):
    m = {"x": np.ascontiguousarray(inputs["x"][b])}
    for n in PARAM_NAMES:
        a = np.ascontiguousarray(np.asarray(inputs[n], dtype=np.float32))
        if n == "a_r_k":
            a = a.reshape(DEPTH, 512)
        m[n] = a
    m.update(consts)
    return m


def kernel(**inputs):
    k = build()
    consts = make_consts()
    in_maps = [make_in_map(inputs, b, consts) for b in range(NB)]
    res = run_bass_kernel_spmd(k.nc, in_maps, core_ids=list(range(NB)))
    return np.stack([np.asarray(r["out"], dtype=np.float32) for r in res.results], axis=0)
```

```python
import numpy as np
import ml_dtypes
import concourse.bass as bass
import concourse.mybir as mybir
from concourse.bass_utils import run_bass_kernel_spmd

F32 = mybir.dt.float32
BF16 = mybir.dt.bfloat16
AF = mybir.ActivationFunctionType
ALU = mybir.AluOpType
AX = mybir.AxisListType

S = 2048; D = 1024; DEPTH = 2; NB = 8
TB = 512; NTB = S // TB; KC = D // 128
A_COLS = 1792; B_COLS = 2048; C_COLS = 1092; G_COLS = 3072; IN_COLS = 8004
A_OFF = 0; B_OFF = A_COLS; C_OFF = A_COLS + B_COLS; G_OFF = C_OFF + C_COLS
DFF = 2816; NFF = DFF // 128
NDMA = 24
NOSYNC_SAME = {"pe"}


class K:
    def __init__(self):
        self.nc = nc = bass.Bass("TRN2", target_bir_lowering=False)
        self.eng = {"pe": nc.tensor, "act": nc.scalar, "dve": nc.vector, "pool": nc.gpsimd, "sp": nc.sync}
        self.sem = {}
        self.cnt = {}
        for e in self.eng:
            self.sem[e] = nc.alloc_semaphore("s_" + e)
            self.cnt[e] = 0
        self.seen = {e: {} for e in self.eng}
        self.lastw = {}
        self.readers = {}
        self.dsem = [nc.alloc_semaphore("d%d" % i) for i in range(NDMA)]
        self.dval = [0] * NDMA
        self.di = 0
        self.nwait = 0
        self.ninst = 0

    def _wait(self, e, ev):
        name, sem, val, src = ev
        if src == e and e in NOSYNC_SAME:
            return
        if self.seen[e].get(name, 0) >= val:
            return
        self.eng[e].wait_ge(sem, val)
        self.nwait += 1
        self.seen[e][name] = val

    def _deps(self, e, reads, writes):
        for r in reads:
            ev = self.lastw.get(r)
            if ev is not None:
                self._wait(e, ev)
        for w in writes:
            ev = self.lastw.get(w)
            if ev is not None:
                self._wait(e, ev)
            for ev in self.readers.get(w, ()):
                self._wait(e, ev)

    def _record(self, ev, reads, writes):
        for w in writes:
            self.lastw[w] = ev
            self.readers[w] = []
        for r in reads:
            if r in writes:
                continue
            lst = self.readers.setdefault(r, [])
            lst[:] = [x for x in lst if x[0] != ev[0]]
            lst.append(ev)

    def op(self, e, fn, reads=(), writes=()):
        self._deps(e, reads, writes)
        ins = fn(self.eng[e])
        self.cnt[e] += 1
        ins.then_inc(self.sem[e], 1)
        ev = ("E" + e, self.sem[e], self.cnt[e], e)
        self.seen[e]["E" + e] = max(self.seen[e].get("E" + e, 0), 0)
        self._record(ev, reads, writes)
        self.ninst += 1
        return ins

    def dma(self, out, in_, reads=(), writes=(), q="sp", **kw):
        self._deps(q, reads, writes)
        j = self.di % NDMA
        self.di += 1
        if self.dval[j] > 0:
            self._wait(q, ("D%d" % j, self.dsem[j], self.dval[j], None))
        self.dval[j] += 16
        self.eng[q].dma_start(out=out, in_=in_, **kw).then_inc(self.dsem[j], 16)
        ev = ("D%d" % j, self.dsem[j], self.dval[j], None)
        self._record(ev, reads, writes)
        self.ninst += 1

    def drain(self, e="sp"):
        for e2 in self.eng:
            if e2 != e and self.cnt[e2] > 0:
                self._wait(e, ("E" + e2, self.sem[e2], self.cnt[e2], e2))
        for j in range(NDMA):
            if self.dval[j] > 0:
                self._wait(e, ("D%d" % j, self.dsem[j], self.dval[j], None))


    def phase_barrier(self):
        self.drain("sp")
        self.nc.all_engine_barrier()
        self.lastw.clear()
        self.readers.clear()


class Scope:
    def __init__(self, k):
        self.k = k
        self.stack = []

    def __enter__(self):
        return self

    _uid = [0]

    def sb(self, name, shape, dtype):
        Scope._uid[0] += 1
        g = self.k.nc.sbuf_tensor("%s_u%d" % (name, Scope._uid[0]), list(shape), dtype)
        t = g.__enter__()
        self.stack.append(g)
        return t

    def ring(self, name, shape, dtype, n):
        return Ring([self.sb("%s_%d" % (name, i), shape, dtype) for i in range(n)], name)

    def __exit__(self, *a):
        self.k.phase_barrier()
        for g in reversed(self.stack):
            g.__exit__(None, None, None)
        return False


class Ring:
    def __init__(self, tensors, name):
        self.t = tensors
        self.name = name
        self.i = 0

    def next(self):
        j = self.i % len(self.t)
        self.i += 1
        return self.t[j], (self.name, j)


def build(dbg=None):
    dbg = dbg or {}
    k = K()
    nc = k.nc
    op = k.op
    uid = [0]

    def din(name, shape, dt=F32):
        return nc.dram_tensor(name, list(shape), dt, kind="ExternalInput").ap()

    x = din("x", [S, D])
    P = {}
    for name, shape in [
        ("norm_mix_g", [DEPTH, D]), ("w_in", [DEPTH, D, IN_COLS]), ("a_mu", [DEPTH, A_COLS]), ("a_w0", [DEPTH, 512]),
        ("a_w_up", [DEPTH, 64, 512]), ("a_a0", [DEPTH, 512]), ("a_a_up", [DEPTH, 64, 512]), ("a_g_up", [DEPTH, 128, 512]),
        ("a_k_k", [DEPTH, 512]), ("a_k_a", [DEPTH, 512]), ("a_r_k", [DEPTH, 512]), ("a_gn_g", [DEPTH, 512]),
        ("a_gn_b", [DEPTH, 512]), ("b_lb_logits", [DEPTH, 512]), ("b_gn_g", [DEPTH, 512]),
        ("w_branch", [DEPTH, 3, 512, D]), ("w_o", [DEPTH, D, D]), ("norm_ffn_g", [DEPTH, D]),
        ("w_up", [DEPTH, D, 2 * DFF]), ("conv_w", [DEPTH, 3, 2 * DFF]), ("conv_b", [DEPTH, 2 * DFF]),
        ("w_down", [DEPTH, DFF, D]), ("norm_final_g", [D]),
    ]:
        P[name] = din(name, shape)
    c_ident = din("c_ident", [128, 128])
    out = nc.dram_tensor("out", [S, D], F32, kind="ExternalOutput").ap()
    h_dram = nc.dram_tensor("h_scr", [128, KC, S], F32, kind="Internal").ap()
    yT_dram = nc.dram_tensor("yT_scr", [3, 128, 4, S], BF16, kind="ExternalOutput" if dbg.get("dump_yT") else "Internal").ap()
    c_tri = din("c_tri", [7, 128, 128])
    c_rope = din("c_rope", [2, 128, S])
    yT_src = [[yT_dram[n] for n in range(3)] for l in range(DEPTH)]
    for n in range(3):
        if ("yT_in%d" % n) in dbg:
            t = din("yT_in%d" % n, [DEPTH, 128, 4, S], BF16)
            for l in range(DEPTH):
                yT_src[l][n] = t[l]

    ident = nc.alloc_sbuf_tensor("ident", [128, 128], F32)
    identb = nc.alloc_sbuf_tensor("identb", [128, 128], BF16)
    ones = nc.alloc_sbuf_tensor("ones", [128, 128], F32)
    epsc = nc.alloc_sbuf_tensor("epsc", [128, 1], F32)
    uT = nc.alloc_sbuf_tensor("uT", [128, KC, S + 2], BF16)
    gvec = nc.alloc_sbuf_tensor("gvec", [128, 2 * DEPTH + 1, KC], F32)
    psr = Ring([nc.alloc_psum_tensor("ps%d" % i, [128, 512], F32) for i in range(6)], "ps")
    psacc = Ring([nc.alloc_psum_tensor("psacc%d" % i, [128, 512], F32) for i in range(2)], "psacc")

    tri = nc.alloc_sbuf_tensor("tri", [128, 7, 128], F32)
    k.dma(tri.ap(), c_tri.rearrange("a p f -> p a f"), writes=["tri"])
    k.dma(ident.ap(), c_ident, writes=["ident"])
    op("dve", lambda e: e.tensor_copy(out=identb.ap(), in_=ident.ap()), reads=["ident"], writes=["identb"])
    op("pool", lambda e: e.memset(ones.ap(), 1.0), writes=["ones"])
    op("pool", lambda e: e.memset(epsc.ap(), 1e-6), writes=["epsc"])
    op("pool", lambda e: e.memset(uT.ap()[:, :, 0:2], 0.0), writes=["uTpad"])
    for l in range(DEPTH):
        k.dma(gvec.ap()[:, 2 * l, :], P["norm_mix_g"][l].rearrange("(c p) -> p c", p=128), writes=["gvec"],
              allow_slow_non_contiguous=True)
        k.dma(gvec.ap()[:, 2 * l + 1, :], P["norm_ffn_g"][l].rearrange("(c p) -> p c", p=128), writes=["gvec"],
              allow_slow_non_contiguous=True)
    k.dma(gvec.ap()[:, 2 * DEPTH, :], P["norm_final_g"].rearrange("(c p) -> p c", p=128), writes=["gvec"],
          allow_slow_non_contiguous=True)

    def hkeys(kc=None, tb=None):
        return [("hT", a, b) for a in (range(KC) if kc is None else [kc]) for b in (range(NTB) if tb is None else [tb])]

    def ukeys(kc=None, tb=None):
        return [("uT", a, b) for a in (range(KC) if kc is None else [kc]) for b in (range(NTB) if tb is None else [tb])]

    def tsl(tb):
        return slice(tb * TB, (tb + 1) * TB)

    def usl(tb, shift=0):
        return slice(2 + tb * TB - shift, 2 + (tb + 1) * TB - shift)

    def rmsnorm(hT, gi, dst_fn, sqr, tmpr):
        for tb in range(NTB):
            ps, pk = psr.next()
            for kc in range(KC):
                sq, sk = sqr.next()
                op("act", lambda e: e.activation(out=sq.ap(), in_=hT.ap()[:, kc, tsl(tb)], func=AF.Square),
                   reads=hkeys(kc, tb), writes=[sk])
                op("pe", lambda e: e.matmul(ps.ap(), lhsT=ones.ap(), rhs=sq.ap(), start=(kc == 0), stop=(kc == KC - 1)),
                   reads=[sk, "ones"], writes=[pk])
            sd, sdk = tmpr.next()
            op("act", lambda e: e.activation(out=sd.ap(), in_=ps.ap(), func=AF.Ln, bias=epsc.ap(), scale=1.0 / D),
               reads=[pk, "epsc"], writes=[sdk])
            rs, rsk = tmpr.next()
            op("act", lambda e: e.activation(out=rs.ap(), in_=sd.ap(), func=AF.Exp, scale=-0.5), reads=[sdk], writes=[rsk])
            for kc in range(KC):
                dst, dk = dst_fn(kc, tb)
                op("dve", lambda e: e.scalar_tensor_tensor(out=dst, in0=hT.ap()[:, kc, tsl(tb)],
                                                           scalar=gvec.ap()[:, gi, kc:kc + 1], in1=rs.ap(),
                                                           op0=ALU.mult, op1=ALU.mult),
                   reads=hkeys(kc, tb) + [rsk, "gvec"], writes=dk)

    def u_dst(kc, tb):
        return uT.ap()[:, kc, usl(tb)], ukeys(kc, tb)

    castsel = [0]

    def cast(dst, src, reads, writes, eng=None):
        ce = eng or "act"
        castsel[0] += 1
        if ce == "act":
            op("act", lambda e: e.copy(out=dst, in_=src), reads=reads, writes=writes)
        else:
            op(ce, lambda e: e.tensor_copy(out=dst, in_=src), reads=reads, writes=writes)

    def wload(src, nk, ncol, stg_ring, w_ring, q="sp"):
        stg, sk = stg_ring.next()
        wt, wk = w_ring.next()
        k.dma(stg.ap()[:, 0:nk, 0:ncol], src.rearrange("(c p) n -> p c n", p=128), writes=[sk], q=q)
        cast(wt.ap()[:, 0:nk, 0:ncol], stg.ap()[:, 0:nk, 0:ncol], [sk], [wk])
        return wt, wk

    def load_h(hT):
        for kc in range(KC):
            k.dma(hT.ap()[:, kc, :], h_dram[:, kc, :], reads=["h_dram"], writes=hkeys(kc))

    def store_h(hT):
        for kc in range(KC):
            k.dma(h_dram[:, kc, :], hT.ap()[:, kc, :], reads=hkeys(kc), writes=[("h_dram", kc)])

    with Scope(k) as sc:
        xs = sc.ring("xs", [128, D], F32, 2)
        xt = sc.ring("xt", [128, KC, 128], F32, 2)
        for tt in range(S // 128):
            a, ak = xs.next()
            b, bk = xt.next()
            k.dma(a.ap(), x[tt * 128:(tt + 1) * 128, :], writes=[ak])
            for half in range(2):
                ps, pk = psr.next()
                for j in range(4):
                    kc = half * 4 + j
                    op("pe", lambda e: e.transpose(out=ps.ap()[:, j * 128:(j + 1) * 128], in_=a.ap()[:, kc * 128:(kc + 1) * 128],
                                                   identity=ident.ap()), reads=[ak, "ident"], writes=[pk])
                cast(b.ap()[:, half * 4:half * 4 + 4, :], ps.ap().rearrange("p (a b) -> p a b", a=4), [pk], [(bk, half)],
                     eng=("dve", "act")[half])
            k.dma(h_dram[:, :, tt * 128:(tt + 1) * 128], b.ap(), reads=[(bk, 0), (bk, 1)], writes=[("h_dram", "x", tt)])


    def proj_fm(l, c0, ncol, sc_rings, consume, shift_w=None):
        stg_r, w_r = sc_rings
        wt, wk = wload(P["w_in"][l][:, c0:c0 + ncol], KC, ncol, stg_r, w_r)
        for tb in range(NTB):
            ps, pk = psr.next()
            for kc in range(KC):
                op("pe", lambda e: e.matmul(ps.ap()[0:ncol, :], lhsT=wt.ap()[:, kc, 0:ncol], rhs=uT.ap()[:, kc, usl(tb)],
                                            start=(kc == 0), stop=(kc == KC - 1)),
                   reads=[wk] + ukeys(kc, tb), writes=[pk])
            consume(tb, ps, pk)

    def mixer_hgrn(l):
        C = 64; NCH = S // C
        with Scope(k) as sc:
            stg_r = sc.ring("hstg", [128, KC, 128], F32, 2)
            w_r = sc.ring("hw", [128, KC, 128], BF16, 2)
            rings = (stg_r, w_r)
            lbt = sc.sb("lbt", [128, 2, 4], F32)
            lb = sc.sb("lb", [128, 4], F32)
            oml = sc.sb("oml", [128, 4], F32)
            noml = sc.sb("noml", [128, 4], F32)
            gng = sc.sb("gng", [128, 4], F32)
            onesS = sc.sb("onesS", [128, S], F32)
            Bc = sc.sb("Bc", [128, S], F32)
            sig = sc.sb("sig", [128, S], F32)
            bp = sc.sb("bp", [128, S], F32)
            Ex = sc.sb("Ex", [128, S], F32)
            ktl = sc.sb("ktl", [128, S], F32)
            qt = sc.sb("qt", [128, S], BF16)
            kt = sc.sb("kt", [128, S], BF16)
            qh = sc.sb("qh", [128, S], BF16)
            kcf = sc.sb("kcf", [128, S], F32)
            vtok = sc.sb("vtok", [64, NCH, 128], BF16)
            ysb = sc.sb("ysb", [128, S], F32)
            gs = sc.sb("gs", [128, S], F32)
            yo = sc.sb("yo", [128, S], BF16)
            dcol = sc.sb("dcol", [128, 4, NCH], F32)
            st = sc.sb("st", [128, 128], F32)
            stb_r = sc.ring("stb", [128, 128], BF16, 2)
            pt_r = sc.ring("pt", [64, 64], BF16, 3)
            kct_r = sc.ring("kct", [64, 128], BF16, 3)
            sq_r = sc.ring("hsq", [128, TB], F32, 2)
            nt_r = sc.ring("hnt", [128, TB], F32, 2)
            op("pool", lambda e: e.memset(onesS.ap(), 1.0), writes=["onesS"])
            k.dma(lbt.ap(), P["b_lb_logits"].rearrange("l (h p) -> p l h", p=128), writes=["lbt"], allow_slow_non_contiguous=True)
            k.dma(gng.ap(), P["b_gn_g"][l].rearrange("(h p) -> p h", p=128), writes=["gng"], allow_slow_non_contiguous=True)
            if l == 0:
                op("dve", lambda e: e.tensor_tensor(out=lb.ap(), in0=lbt.ap()[:, 0, :], in1=lbt.ap()[:, 0, :], op=ALU.subtract),
                   reads=["lbt"], writes=["lb"])
            else:
                op("dve", lambda e: e.tensor_tensor(out=lb.ap(), in0=lbt.ap()[:, 1, :], in1=lbt.ap()[:, 0, :], op=ALU.subtract),
                   reads=["lbt"], writes=["lb"])
                op("act", lambda e: e.activation(out=lb.ap(), in_=lb.ap(), func=AF.Sigmoid), reads=["lb"], writes=["lb"])
            op("dve", lambda e: e.tensor_scalar(out=oml.ap(), in0=lb.ap(), scalar1=-1.0, scalar2=1.0, op0=ALU.mult, op1=ALU.add),
               reads=["lb"], writes=["oml"])
            op("dve", lambda e: e.tensor_scalar(out=noml.ap(), in0=oml.ap(), scalar1=-1.0, scalar2=None, op0=ALU.mult),
               reads=["oml"], writes=["noml"])
            for h in range(4):
                cq = B_OFF + h * 128; cf = B_OFF + 512 + h * 128; ci = B_OFF + 1024 + h * 128; cg = B_OFF + 1536 + h * 128
                def f_consume(tb, ps, pk):
                    op("act", lambda e: e.activation(out=sig.ap()[:, tsl(tb)], in_=ps.ap(), func=AF.Sigmoid), reads=[pk], writes=[("sig", tb)])
                proj_fm(l, cf, 128, rings, f_consume)
                allsig = [("sig", tb) for tb in range(NTB)]
                op("dve", lambda e: e.tensor_scalar(out=ktl.ap(), in0=sig.ap(), scalar1=noml.ap()[:, h:h + 1], scalar2=oml.ap()[:, h:h + 1],
                                                    op0=ALU.mult, op1=ALU.add), reads=allsig + ["noml", "oml"], writes=["ktl"])
                op("dve", lambda e: e.tensor_scalar(out=sig.ap(), in0=sig.ap(), scalar1=oml.ap()[:, h:h + 1], scalar2=lb.ap()[:, h:h + 1],
                                                    op0=ALU.mult, op1=ALU.add), reads=allsig + ["oml", "lb"], writes=allsig)
                op("act", lambda e: e.activation(out=sig.ap(), in_=sig.ap(), func=AF.Ln), reads=allsig, writes=allsig)
                op("dve", lambda e: e.tensor_tensor_scan(out=Bc.ap(), data0=onesS.ap(), data1=sig.ap(), initial=0.0, op0=ALU.mult, op1=ALU.add),
                   reads=allsig + ["onesS"], writes=["Bc"])
                B3 = Bc.ap().rearrange("p (c t) -> p c t", t=C)
                op("dve", lambda e: e.tensor_tensor(out=bp.ap().rearrange("p (c t) -> p c t", t=C), in0=B3,
                                                    in1=B3[:, :, C // 2 - 1:C // 2].to_broadcast([128, NCH, C]), op=ALU.subtract),
                   reads=["Bc"], writes=["bp"] + [("bpq", tb) for tb in range(NTB)])
                op("dve", lambda e: e.tensor_tensor(out=dcol.ap()[:, 0, :], in0=B3[:, :, C - 1], in1=B3[:, :, C // 2 - 1], op=ALU.subtract),
                   reads=["Bc"], writes=[("dcol", 0)])
                op("dve", lambda e: e.tensor_copy(out=dcol.ap()[:, 1, 0:1], in_=B3[:, 0, C // 2 - 1:C // 2]), reads=["Bc"], writes=[("dcol", 1)])
                op("dve", lambda e: e.tensor_tensor(out=dcol.ap()[:, 1, 1:NCH], in0=B3[:, 1:NCH, C // 2 - 1], in1=B3[:, 0:NCH - 1, C - 1], op=ALU.subtract),
                   reads=["Bc", ("dcol", 1)], writes=[("dcol", 1)])
                op("dve", lambda e: e.tensor_tensor(out=dcol.ap()[:, 2, :], in0=dcol.ap()[:, 0, :], in1=dcol.ap()[:, 1, :], op=ALU.add),
                   reads=[("dcol", 0), ("dcol", 1)], writes=[("dcol", 2)])
                op("act", lambda e: e.activation(out=dcol.ap()[:, 0:3, :], in_=dcol.ap()[:, 0:3, :], func=AF.Exp),
                   reads=[("dcol", 0), ("dcol", 1), ("dcol", 2)], writes=[("dcol", 0), ("dcol", 1), ("dcol", 2)])
                op("act", lambda e: e.activation(out=Ex.ap(), in_=bp.ap(), func=AF.Exp, scale=-1.0), reads=["bp"], writes=["Ex"])
                op("dve", lambda e: e.tensor_tensor(out=ktl.ap(), in0=ktl.ap(), in1=Ex.ap(), op=ALU.mult), reads=["ktl", "Ex"], writes=["ktl"])
                op("act", lambda e: e.copy(out=kt.ap(), in_=ktl.ap()), reads=["ktl"], writes=["kt"])
                op("dve", lambda e: e.tensor_tensor(out=kcf.ap().rearrange("p (c t) -> p c t", t=C), in0=ktl.ap().rearrange("p (c t) -> p c t", t=C),
                                                    in1=dcol.ap()[:, 0, :].unsqueeze(2).to_broadcast([128, NCH, C]), op=ALU.mult),
                   reads=["ktl", ("dcol", 0)], writes=["kcf"])
                op("act", lambda e: e.activation(out=Ex.ap(), in_=bp.ap(), func=AF.Exp), reads=["bp"], writes=["Ex"])

                def q_consume(tb, ps, pk):
                    op("dve", lambda e: e.tensor_tensor(out=bp.ap()[:, tsl(tb)], in0=Ex.ap()[:, tsl(tb)], in1=ps.ap(), op=ALU.mult),
                       reads=["Ex", pk, "bp"], writes=[("bpq", tb)])
                proj_fm(l, cq, 128, rings, q_consume)
                allq = [("bpq", tb) for tb in range(NTB)]
                op("act", lambda e: e.copy(out=qt.ap(), in_=bp.ap()), reads=allq, writes=["qt"])
                op("dve", lambda e: e.tensor_tensor(out=qh.ap().rearrange("p (c t) -> p c t", t=C), in0=bp.ap().rearrange("p (c t) -> p c t", t=C),
                                                    in1=dcol.ap()[:, 1, :].unsqueeze(2).to_broadcast([128, NCH, C]), op=ALU.mult),
                   reads=allq + [("dcol", 1)], writes=["qh"])

                def g_consume(tb, ps, pk):
                    op("act", lambda e: e.activation(out=gs.ap()[:, tsl(tb)], in_=ps.ap(), func=AF.Silu), reads=[pk], writes=[("gs", tb)])
                proj_fm(l, cg, 128, rings, g_consume)
                wt, wk = wload(P["w_in"][l][:, ci:ci + 128], KC, 128, stg_r, w_r)
                for c4 in range(NCH // 4):
                    ps, pk = psr.next()
                    for j in range(4):
                        c = c4 * 4 + j
                        for kc in range(KC):
                            op("pe", lambda e: e.matmul(ps.ap()[0:C, j * 128:(j + 1) * 128], lhsT=uT.ap()[:, kc, 2 + c * C: 2 + (c + 1) * C],
                                                        rhs=wt.ap()[:, kc, :], start=(kc == 0), stop=(kc == KC - 1)),
                               reads=[wk] + ukeys(kc, (c * C) // TB), writes=[pk])
                    cast(vtok.ap()[:, c4 * 4:(c4 + 1) * 4, :], ps.ap()[0:C, :].rearrange("p (a b) -> p a b", a=4), [pk], [("vtok", c4)], eng="act")
                stb = None
                for c in range(NCH):
                    cs = slice(c * C, (c + 1) * C)
                    ps_s, pk_s = psr.next()
                    op("pe", lambda e: e.matmul(ps_s.ap()[0:C, 0:C], lhsT=kt.ap()[:, cs], rhs=qt.ap()[:, cs], start=True, stop=True),
                       reads=["kt", "qt"], writes=[pk_s])
                    pt, ptk = pt_r.next()
                    op("dve", lambda e: e.tensor_tensor(out=pt.ap(), in0=ps_s.ap()[0:C, 0:C], in1=tri.ap()[0:C, 0, 0:C], op=ALU.mult),
                       reads=[pk_s, "tri"], writes=[ptk])
                    ps_t, pk_t = psr.next()
                    op("pe", lambda e: e.transpose(out=ps_t.ap()[0:C, 0:128], in_=kcf.ap()[:, cs], identity=ident.ap()),
                       reads=["kcf", "ident"], writes=[pk_t])
                    kct, kctk = kct_r.next()
                    op("act", lambda e: e.copy(out=kct.ap(), in_=ps_t.ap()[0:C, 0:128]), reads=[pk_t], writes=[kctk])
                    ps_y, pk_y = psr.next()
                    if c > 0:
                        op("pe", lambda e: e.matmul(ps_y.ap()[:, 0:C], lhsT=stb[0].ap(), rhs=qh.ap()[:, cs], start=True, stop=False),
                           reads=[stb[1], "qh"], writes=[pk_y])
                    op("pe", lambda e: e.matmul(ps_y.ap()[:, 0:C], lhsT=vtok.ap()[:, c, :], rhs=pt.ap(), start=(c == 0), stop=True),
                       reads=[("vtok", c // 4), ptk], writes=[pk_y])
                    op("act", lambda e: e.copy(out=ysb.ap()[:, cs], in_=ps_y.ap()[:, 0:C]), reads=[pk_y], writes=[("ysb", c // 8)])
                    if c < NCH - 1:
                        ps_d, pk_d = psr.next()
                        op("pe", lambda e: e.matmul(ps_d.ap()[:, 0:128], lhsT=kct.ap(), rhs=vtok.ap()[:, c, :], start=True, stop=True),
                           reads=[kctk, ("vtok", c // 4)], writes=[pk_d])
                        if c == 0:
                            op("dve", lambda e: e.tensor_copy(out=st.ap(), in_=ps_d.ap()[:, 0:128]), reads=[pk_d], writes=["st"])
                        else:
                            op("dve", lambda e: e.scalar_tensor_tensor(out=st.ap(), in0=st.ap(), scalar=dcol.ap()[:, 2, c:c + 1], in1=ps_d.ap()[:, 0:128],
                                                                       op0=ALU.mult, op1=ALU.add), reads=["st", pk_d, ("dcol", 2)], writes=["st"])
                        stb = stb_r.next()
                        op("act", lambda e: e.copy(out=stb[0].ap(), in_=st.ap()), reads=["st"], writes=[stb[1]])
                for tb in range(NTB):
                    sq, sk = sq_r.next()
                    yk = [("ysb", tb * 2), ("ysb", tb * 2 + 1)]
                    op("act", lambda e: e.activation(out=sq.ap(), in_=ysb.ap()[:, tsl(tb)], func=AF.Square), reads=yk, writes=[sk])
                    ps, pk = psr.next()
                    op("pe", lambda e: e.matmul(ps.ap(), lhsT=ones.ap(), rhs=sq.ap(), start=True, stop=True), reads=[sk, "ones"], writes=[pk])
                    sd, sdk = nt_r.next()
                    op("act", lambda e: e.activation(out=sd.ap(), in_=ps.ap(), func=AF.Ln, bias=epsc.ap(), scale=1.0 / 128), reads=[pk, "epsc"], writes=[sdk])
                    op("act", lambda e: e.activation(out=sd.ap(), in_=sd.ap(), func=AF.Exp, scale=-0.5), reads=[sdk], writes=[sdk])
                    op("dve", lambda e: e.scalar_tensor_tensor(out=sd.ap(), in0=ysb.ap()[:, tsl(tb)], scalar=gng.ap()[:, h:h + 1], in1=sd.ap(),
                                                               op0=ALU.mult, op1=ALU.mult), reads=yk + [sdk, "gng"], writes=[sdk])
                    op("dve", lambda e: e.tensor_tensor(out=yo.ap()[:, tsl(tb)], in0=sd.ap(), in1=gs.ap()[:, tsl(tb)], op=ALU.mult),
                       reads=[sdk, ("gs", tb)], writes=[("yo", tb)])
                k.dma(yT_dram[1, :, h, :], yo.ap(), reads=[("yo", tb) for tb in range(NTB)], writes=[("yT_dram", 1, h)])


    def mixer_dsa(l):
        NQ = S // 128
        with Scope(k) as sc:
            qT = sc.sb("qT", [128, 4, S], BF16)
            kTz = sc.sb("kTz", [128, 4, S], BF16)
            qiT = sc.sb("qiT", [128, 2, S], BF16)
            kiTz = sc.sb("kiTz", [128, 2, S], BF16)
            vtok = sc.sb("cvtok", [128, 2, NQ, 128], BF16)
            wi = sc.sb("wi", [128, NQ, 4], F32)
            wabs = sc.sb("wabs", [128, NQ, 4], F32)
            wsg = sc.sb("wsg", [128, NQ, 4], F32)
            yo = sc.sb("cyo", [128, 4, S], BF16)
            onesb = sc.sb("onesb", [128, 128], BF16)
            rt_r = sc.ring("rt", [128, TB], F32, 3)
            op("pool", lambda e: e.memset(onesb.ap(), 1.0), writes=["onesb"])
            with Scope(k) as s2:
                stg_r = s2.ring("cstg", [128, KC, 128], F32, 2)
                w_r = s2.ring("cw", [128, KC, 128], BF16, 2)
                wsw_r = s2.ring("cwsw", [128, KC, 128], BF16, 2)
                rope = s2.sb("rope", [128, 2, S], F32)
                k.dma(rope.ap(), c_rope.rearrange("a p s -> p a s"), writes=["rope"])

                def wload_cols(col_list, swap):
                    stg, sk = stg_r.next()
                    wt, wk = w_r.next()
                    o = 0
                    for (c0, n) in col_list:
                        if c0 is None:
                            op("pool", lambda e: e.memset(stg.ap()[:, :, o:o + n], 0.0), writes=[(sk, o), sk])
                        else:
                            k.dma(stg.ap()[:, :, o:o + n], P["w_in"][l][:, c0:c0 + n].rearrange("(c p) n -> p c n", p=128), writes=[(sk, o), sk])
                        o += n
                    rk = [(sk, oo) for oo in np.cumsum([0] + [n for _, n in col_list[:-1]]).tolist()] + [sk]
                    cast(wt.ap()[:, :, 0:o], stg.ap()[:, :, 0:o], rk, [wk])
                    if not swap:
                        return wt, wk, None, None
                    ws, wsk = wsw_r.next()
                    for b0 in range(0, o, 64):
                        cast(ws.ap()[:, :, b0:b0 + 32], stg.ap()[:, :, b0 + 32:b0 + 64], rk, [(wsk, b0), wsk])
                        cast(ws.ap()[:, :, b0 + 32:b0 + 64], stg.ap()[:, :, b0:b0 + 32], rk, [(wsk, b0 + 32), wsk])
                    return wt, wk, ws, [(wsk, b0) for b0 in range(0, o, 32)] + [wsk]

                def rope_proj(col_list, dst_fn, dkey):
                    wt, wk, ws, wsk = wload_cols(col_list, True)
                    for tb in range(NTB):
                        p1, pk1 = psr.next()
                        p2, pk2 = psr.next()
                        for kc in range(KC):
                            op("pe", lambda e: e.matmul(p1.ap(), lhsT=wt.ap()[:, kc, :], rhs=uT.ap()[:, kc, usl(tb)], start=(kc == 0), stop=(kc == KC - 1)),
                               reads=[wk] + ukeys(kc, tb), writes=[pk1])
                        for kc in range(KC):
                            op("pe", lambda e: e.matmul(p2.ap(), lhsT=ws.ap()[:, kc, :], rhs=uT.ap()[:, kc, usl(tb)], start=(kc == 0), stop=(kc == KC - 1)),
                               reads=wsk + ukeys(kc, tb), writes=[pk2])
                        t1, tk1 = rt_r.next()
                        t2, tk2 = rt_r.next()
                        op("dve", lambda e: e.tensor_tensor(out=t1.ap(), in0=rope.ap()[:, 0, tsl(tb)], in1=p1.ap(), op=ALU.mult), reads=["rope", pk1], writes=[tk1])
                        op("dve", lambda e: e.tensor_tensor(out=t2.ap(), in0=rope.ap()[:, 1, tsl(tb)], in1=p2.ap(), op=ALU.mult), reads=["rope", pk2], writes=[tk2])
                        op("pool", lambda e: e.tensor_tensor(out=dst_fn(tb), in0=t1.ap(), in1=t2.ap(), op=ALU.add), reads=[tk1, tk2], writes=[(dkey, tb)])

                co = C_OFF
                for ch in range(4):
                    rope_proj([(co + ch * 128, 128)], lambda tb: qT.ap()[:, ch, tsl(tb)], ("qT", ch))
                for c in range(2):
                    rope_proj([(co + 512 + c * 64, 64), (None, 64)], lambda tb: kTz.ap()[:, c * 2, tsl(tb)], ("kTz", c * 2))
                    rope_proj([(None, 64), (co + 512 + c * 64, 64)], lambda tb: kTz.ap()[:, c * 2 + 1, tsl(tb)], ("kTz", c * 2 + 1))
                for ch in range(2):
                    rope_proj([(co + 768 + ch * 128, 128)], lambda tb: qiT.ap()[:, ch, tsl(tb)], ("qiT", ch))
                rope_proj([(co + 1024, 64), (None, 64)], lambda tb: kiTz.ap()[:, 0, tsl(tb)], ("kiTz", 0))
                rope_proj([(None, 64), (co + 1024, 64)], lambda tb: kiTz.ap()[:, 1, tsl(tb)], ("kiTz", 1))
                for c in range(2):
                    wt, wk, _, _ = wload_cols([(co + 640 + c * 64, 64), (co + 640 + c * 64, 64)], False)
                    for s4 in range(NQ // 4):
                        ps, pk = psr.next()
                        for j in range(4):
                            sb = s4 * 4 + j
                            for kc in range(KC):
                                op("pe", lambda e: e.matmul(ps.ap()[:, j * 128:(j + 1) * 128], lhsT=uT.ap()[:, kc, 2 + sb * 128: 2 + (sb + 1) * 128],
                                                            rhs=wt.ap()[:, kc, :], start=(kc == 0), stop=(kc == KC - 1)),
                                   reads=[wk] + ukeys(kc, sb // 4), writes=[pk])
                        cast(vtok.ap()[:, c, s4 * 4:(s4 + 1) * 4, :], ps.ap().rearrange("p (a b) -> p a b", a=4), [pk], [("cvtok", c, s4)], eng="act")
                wt, wk, _, _ = wload_cols([(co + 1088, 4)], False)
                ps, pk = psr.next()
                for qb in range(NQ):
                    for kc in range(KC):
                        op("pe", lambda e: e.matmul(ps.ap()[:, qb * 4:(qb + 1) * 4], lhsT=uT.ap()[:, kc, 2 + qb * 128: 2 + (qb + 1) * 128],
                                                    rhs=wt.ap()[:, kc, 0:4], start=(kc == 0), stop=(kc == KC - 1)),
                           reads=[wk] + ukeys(kc, qb // 4), writes=[pk])
                op("dve", lambda e: e.tensor_copy(out=wi.ap(), in_=ps.ap()[:, 0:NQ * 4].rearrange("p (a b) -> p a b", b=4)), reads=[pk], writes=["wi"])
                op("act", lambda e: e.activation(out=wabs.ap(), in_=wi.ap(), func=AF.Abs), reads=["wi"], writes=["wabs"])
                op("act", lambda e: e.activation(out=wsg.ap(), in_=wi.ap(), func=AF.Sign), reads=["wi"], writes=["wsg"])
            NCHN = 4
            score2 = [sc.sb("score%d" % i, [128, S], F32) for i in range(NCHN)]
            Mf2 = [sc.sb("Mf%d" % i, [128, S], BF16) for i in range(NCHN)]
            MT2 = [sc.sb("MT%d" % i, [128, NQ, 128], BF16) for i in range(NCHN)]
            m82 = [sc.sb("m8_%d" % i, [128, 8], F32) for i in range(NCHN)]
            e_r = sc.ring("cE", [128, 4, 128], BF16, 3)
            p_r = sc.ring("cP", [128, 4, 128], BF16, 3)
            rec_r = sc.ring("crec", [128, TB], F32, 2)
            SENT = -3e30

            def scores(qb, i):
                score = score2[i]
                ncol = (qb + 1) * 128
                qs = slice(qb * 128, (qb + 1) * 128)
                for g0 in range(0, ncol, TB):
                    gn = min(TB, ncol - g0)
                    for hi in range(4):
                        ps, pk = psr.next()
                        op("pe", lambda e: e.matmul(ps.ap()[:, 0:gn], lhsT=qiT.ap()[:, hi // 2, qs], rhs=kiTz.ap()[:, hi % 2, g0:g0 + gn], start=True, stop=True),
                           reads=[(("qiT", hi // 2), qb // 4)] + [(("kiTz", hi % 2), tb) for tb in range(g0 // TB, (g0 + gn - 1) // TB + 1)], writes=[pk])
                        r, rk = rt_r.next()
                        op("act", lambda e: e.activation(out=r.ap()[:, 0:gn], in_=ps.ap()[:, 0:gn], func=AF.Relu, scale=wabs.ap()[:, qb, hi:hi + 1]),
                           reads=[pk, "wabs"], writes=[rk])
                        if hi == 0:
                            op("dve", lambda e: e.tensor_scalar(out=score.ap()[:, g0:g0 + gn], in0=r.ap()[:, 0:gn], scalar1=wsg.ap()[:, qb, hi:hi + 1],
                                                                scalar2=None, op0=ALU.mult), reads=[rk, "wsg"], writes=[("score", i, g0)])
                        else:
                            op("dve", lambda e: e.scalar_tensor_tensor(out=score.ap()[:, g0:g0 + gn], in0=r.ap()[:, 0:gn], scalar=wsg.ap()[:, qb, hi:hi + 1],
                                                                       in1=score.ap()[:, g0:g0 + gn], op0=ALU.mult, op1=ALU.add),
                               reads=[rk, "wsg", ("score", i, g0)], writes=[("score", i, g0)])
                sk_all = [("score", i, g0) for g0 in range(0, S, TB)]
                op("dve", lambda e: e.tensor_tensor(out=score.ap()[:, qs], in0=score.ap()[:, qs], in1=tri.ap()[:, 2, :], op=ALU.mult), reads=sk_all + ["tri"], writes=sk_all)
                op("dve", lambda e: e.tensor_tensor(out=score.ap()[:, qs], in0=score.ap()[:, qs], in1=tri.ap()[:, 3, :], op=ALU.add), reads=sk_all + ["tri"], writes=sk_all)

            NIT = 28
            W0 = 8192.0
            bs_S = [sc.sb("bsS%d" % i, [128, 1], F32) for i in range(NCHN)]
            bs_inc = [sc.sb("bsI%d" % i, [128, 1], F32) for i in range(NCHN)]
            bs_mid = [sc.sb("bsM%d" % i, [128, 1], F32) for i in range(NCHN)]
            junkA = sc.sb("junkA", [128, S], BF16)
            junkD = sc.sb("junkD", [128, S], BF16)

            def topk_gen(blocks):
                sk_all = lambda i: [("score", i, g0) for g0 in range(0, S, TB)]
                act = [(i, qb, (qb + 1) * 128) for i, qb in enumerate(blocks) if (qb + 1) * 128 > 256]
                on_dve = lambda i: (i == NCHN - 1)
                for i, qb, ncol in act:
                    op("dve", lambda e: e.memset(bs_mid[i].ap(), 0.0), writes=[("bsM", i)])
                w = W0
                for it in range(NIT if act else 0):
                    for i, qb, ncol in act:
                        if on_dve(i):
                            op("dve", lambda e: e.tensor_scalar(out=junkD.ap()[:, 0:ncol], in0=score2[i].ap()[:, 0:ncol], scalar1=bs_mid[i].ap(), scalar2=None,
                                                                op0=ALU.is_ge, op1=ALU.add, accum_out=bs_S[i].ap()),
                               reads=sk_all(i) + [("bsM", i)], writes=[("bsS", i)])
                        else:
                            op("act", lambda e: e.activation(out=junkA.ap()[:, 0:ncol], in_=score2[i].ap()[:, 0:ncol], func=AF.Sign, bias=bs_mid[i].ap(), scale=1.0,
                                                             accum_out=bs_S[i].ap()),
                               reads=sk_all(i) + [("bsM", i)], writes=[("bsS", i)])
                    for i, qb, ncol in act:
                        kthr = 256.0 if on_dve(i) else float(512 - ncol)
                        op("dve", lambda e: e.tensor_scalar(out=bs_inc[i].ap(), in0=bs_S[i].ap(), scalar1=kthr, scalar2=w / 2, op0=ALU.is_ge, op1=ALU.mult),
                           reads=[("bsS", i)], writes=[("bsI", i)])
                        if on_dve(i):
                            op("dve", lambda e: e.scalar_tensor_tensor(out=bs_mid[i].ap(), in0=bs_mid[i].ap(), scalar=-w / 4, in1=bs_inc[i].ap(), op0=ALU.add, op1=ALU.add),
                               reads=[("bsM", i), ("bsI", i)], writes=[("bsM", i)])
                        else:
                            op("dve", lambda e: e.scalar_tensor_tensor(out=bs_mid[i].ap(), in0=bs_mid[i].ap(), scalar=w / 4, in1=bs_inc[i].ap(), op0=ALU.add, op1=ALU.subtract),
                               reads=[("bsM", i), ("bsI", i)], writes=[("bsM", i)])
                    w = w / 2
                    yield
                for i, qb in enumerate(blocks):
                    ncol = (qb + 1) * 128
                    if ncol > 256:
                        if on_dve(i):
                            op("dve", lambda e: e.tensor_scalar(out=bs_inc[i].ap(), in0=bs_mid[i].ap(), scalar1=-w / 2, scalar2=None, op0=ALU.add), reads=[("bsM", i)], writes=[("bsI", i)])
                        else:
                            op("dve", lambda e: e.tensor_scalar(out=bs_inc[i].ap(), in0=bs_mid[i].ap(), scalar1=-1.0, scalar2=-w / 2, op0=ALU.mult, op1=ALU.add), reads=[("bsM", i)], writes=[("bsI", i)])
                        op("dve", lambda e: e.tensor_scalar(out=Mf2[i].ap()[:, 0:ncol], in0=score2[i].ap()[:, 0:ncol], scalar1=bs_inc[i].ap(), scalar2=None, op0=ALU.is_ge),
                           reads=sk_all(i) + [("bsI", i)], writes=[("Mf", i)])
                    else:
                        op("dve", lambda e: e.tensor_scalar(out=Mf2[i].ap()[:, 0:ncol], in0=score2[i].ap()[:, 0:ncol], scalar1=-1e29, scalar2=None, op0=ALU.is_ge),
                           reads=sk_all(i), writes=[("Mf", i)])
                yield

            def maskT(qb, i):
                for s4 in range(0, qb + 1, 4):
                    nb = min(4, qb + 1 - s4)
                    ps, pk = psr.next()
                    for j in range(nb):
                        sb = s4 + j
                        op("pe", lambda e: e.transpose(out=ps.ap().bitcast(BF16)[:, j * 128:(j + 1) * 128], in_=Mf2[i].ap()[:, sb * 128:(sb + 1) * 128], identity=identb.ap()),
                           reads=[("Mf", i), "identb"], writes=[pk])
                    op("act", lambda e: e.copy(out=MT2[i].ap()[:, s4:s4 + nb, :], in_=ps.ap().bitcast(BF16)[:, 0:nb * 128].rearrange("p (a b) -> p a b", b=128)),
                       reads=[pk], writes=[("MT", i, s4)])

            def attn_gen(qb, i):
                qs = slice(qb * 128, (qb + 1) * 128)
                MT = MT2[i]
                for c in range(2):
                    ps_o, pk_o = psacc.next()
                    ps_d, pk_d = psacc.next()
                    for sb in range(qb + 1):
                        ps, pk = psr.next()
                        for j in range(4):
                            hq = c * 4 + j
                            op("pe", lambda e: e.matmul(ps.ap()[:, j * 128:(j + 1) * 128], lhsT=kTz.ap()[:, c * 2 + hq % 2, sb * 128:(sb + 1) * 128],
                                                        rhs=qT.ap()[:, hq // 2, qs], start=True, stop=True),
                               reads=[(("kTz", c * 2 + hq % 2), sb // 4), (("qT", hq // 2), qb // 4)], writes=[pk])
                        E, Ek = e_r.next()
                        op("act", lambda e: e.activation(out=E.ap(), in_=ps.ap().rearrange("p (a b) -> p a b", a=4), func=AF.Exp, scale=0.125), reads=[pk], writes=[Ek])
                        Pm, Pk = p_r.next()
                        op("pool", lambda e: e.tensor_tensor(out=Pm.ap(), in0=E.ap(), in1=MT.ap()[:, sb:sb + 1, :].to_broadcast([128, 4, 128]), op=ALU.mult),
                           reads=[Ek, ("MT", i, (sb // 4) * 4)], writes=[Pk])
                        op("pe", lambda e: e.matmul(ps_o.ap(), lhsT=vtok.ap()[:, c, sb, :], rhs=Pm.ap().rearrange("p a b -> p (a b)"), start=(sb == 0), stop=(sb == qb)),
                           reads=[("cvtok", c, sb // 4), Pk], writes=[pk_o])
                        op("pe", lambda e: e.matmul(ps_d.ap(), lhsT=onesb.ap(), rhs=Pm.ap().rearrange("p a b -> p (a b)"), start=(sb == 0), stop=(sb == qb)),
                           reads=["onesb", Pk], writes=[pk_d])
                        yield
                    rec, reck = rec_r.next()
                    op("dve", lambda e: e.reciprocal(out=rec.ap(), in_=ps_d.ap()), reads=[pk_d], writes=[reck])
                    for j in range(4):
                        hq = c * 4 + j
                        hb = (hq % 2) * 64
                        op("dve", lambda e: e.tensor_tensor(out=yo.ap()[hb:hb + 64, hq // 2, qs], in0=rec.ap()[hb:hb + 64, j * 128:(j + 1) * 128],
                                                            in1=ps_o.ap()[hb:hb + 64, j * 128:(j + 1) * 128], op=ALU.mult),
                           reads=[reck, pk_o], writes=[("cyo", hq // 2, hq % 2)])

            def run(g):
                for _ in g:
                    pass

            def zipgens(gt, ga_list, n_t, n_a):
                ga = (x for g in ga_list for x in g)
                per = max(1, -(-n_a // max(1, n_t)))
                done_a = False
                for _ in gt:
                    for _j in range(per):
                        if next(ga, "END") == "END":
                            done_a = True
                            break
                if not done_a:
                    for _ in ga:
                        pass

            grps = [tuple(range(g * NCHN, (g + 1) * NCHN)) for g in range(NQ // NCHN)]
            for i_, qb_ in enumerate(grps[0]):
                scores(qb_, i_)
            run(topk_gen(grps[0]))
            for i_, qb_ in enumerate(grps[0]):
                maskT(qb_, i_)
            for p in range(len(grps)):
                cur = grps[p]
                if p + 1 < len(grps):
                    nxt = grps[p + 1]
                    for i_, qb_ in enumerate(nxt):
                        scores(qb_, i_)
                    n_a = sum(2 * (q_ + 1) for q_ in cur)
                    zipgens(topk_gen(nxt), [attn_gen(q_, i_) for i_, q_ in enumerate(cur)], NIT + 1, n_a)
                    for i_, qb_ in enumerate(nxt):
                        maskT(qb_, i_)
                else:
                    for i_, q_ in enumerate(cur):
                        run(attn_gen(q_, i_))
            for ch in range(4):
                k.dma(yT_dram[2, :, ch, :], yo.ap()[:, ch, :], reads=[("cyo", ch, 0), ("cyo", ch, 1)], writes=[("yT_dram", 2, ch)])

    def mixer_rwkv(l):
        C = 128; NCH = S // C
        with Scope(k) as sc:
            stg_r = sc.ring("astg", [128, KC, 128], F32, 1)
            stm_r = sc.ring("astm", [128, KC, 128], F32, 1)
            w1_r = sc.ring("aw1", [128, KC, 128], BF16, 1)
            wm_r = sc.ring("awm", [128, KC, 128], BF16, 1)
            mub_r = sc.ring("amub", [128, 128], F32, 2)
            lrb = sc.sb("lrb", [128, 3, 512], BF16)
            with Scope(k) as s2:
                lrs = s2.sb("lrs", [128, 3, 512], F32)
                k.dma(lrs.ap()[0:64, 0, :], P["a_w_up"][l], writes=["lrs0"]); k.dma(lrs.ap()[0:64, 1, :], P["a_a_up"][l], writes=["lrs1"])
                k.dma(lrs.ap()[:, 2, :], P["a_g_up"][l], writes=["lrs2"])
                op("dve", lambda e: e.tensor_copy(out=lrb.ap()[0:64, 0:2, :], in_=lrs.ap()[0:64, 0:2, :]), reads=["lrs0", "lrs1"], writes=["lrb01"])
                op("dve", lambda e: e.tensor_copy(out=lrb.ap()[:, 2, :], in_=lrs.ap()[:, 2, :]), reads=["lrs2"], writes=["lrb2"])

            def proj_shift(c0, ncol, consume, token_major=None):
                stg, sk = stg_r.next(); stm, smk = stm_r.next(); w1, w1k = w1_r.next(); wm, wmk = wm_r.next(); mub, mk = mub_r.next()
                k.dma(stg.ap()[:, :, 0:ncol], P["w_in"][l][:, c0:c0 + ncol].rearrange("(c p) n -> p c n", p=128), writes=[sk])
                k.dma(mub.ap()[:, 0:ncol], P["a_mu"][l][c0 - A_OFF:c0 - A_OFF + ncol].partition_broadcast(128), writes=[mk])
                op("dve", lambda e: e.tensor_tensor(out=stm.ap()[:, :, 0:ncol], in0=stg.ap()[:, :, 0:ncol],
                                                    in1=mub.ap()[:, 0:ncol].unsqueeze(1).to_broadcast([128, KC, ncol]), op=ALU.mult),
                   reads=[sk, mk], writes=[smk])
                op("act", lambda e: e.copy(out=wm.ap()[:, :, 0:ncol], in_=stm.ap()[:, :, 0:ncol]), reads=[smk], writes=[wmk])
                op("dve", lambda e: e.tensor_tensor(out=w1.ap()[:, :, 0:ncol], in0=stg.ap()[:, :, 0:ncol], in1=stm.ap()[:, :, 0:ncol], op=ALU.subtract),
                   reads=[sk, smk], writes=[w1k])
                if token_major is not None:
                    token_major(w1, w1k, wm, wmk)
                    return
                for tb in range(NTB):
                    ps, pk = psr.next()
                    for kc in range(KC):
                        op("pe", lambda e: e.matmul(ps.ap()[0:ncol, :], lhsT=w1.ap()[:, kc, 0:ncol], rhs=uT.ap()[:, kc, usl(tb)], start=(kc == 0), stop=False),
                           reads=[w1k] + ukeys(kc, tb), writes=[pk])
                    for kc in range(KC):
                        op("pe", lambda e: e.matmul(ps.ap()[0:ncol, :], lhsT=wm.ap()[:, kc, 0:ncol], rhs=uT.ap()[:, kc, usl(tb, 1)], start=False, stop=(kc == KC - 1)),
                           reads=[wmk] + ukeys(kc, tb) + (ukeys(kc, tb - 1) if tb else []) + ["uTpad"], writes=[pk])
                    consume(tb, ps, pk)

            tw = sc.sb("tw", [64, S], BF16); adT = sc.sb("adT", [64, S], BF16); sgT = sc.sb("sgT", [128, S], BF16)
            proj_shift(A_OFF + 1536, 64, lambda tb, ps, pk: op("act", lambda e: e.activation(out=tw.ap()[:, tsl(tb)], in_=ps.ap()[0:64, :], func=AF.Tanh), reads=[pk], writes=[("tw", tb)]))
            proj_shift(A_OFF + 1600, 64, lambda tb, ps, pk: op("act", lambda e: e.copy(out=adT.ap()[:, tsl(tb)], in_=ps.ap()[0:64, :]), reads=[pk], writes=[("adT", tb)]))
            proj_shift(A_OFF + 1664, 128, lambda tb, ps, pk: op("act", lambda e: e.activation(out=sgT.ap()[:, tsl(tb)], in_=ps.ap(), func=AF.Sigmoid), reads=[pk], writes=[("sgT", tb)]))
            pc = sc.sb("pc", [128, 8, 4], F32)
            for i_, nm in enumerate(["a_w0", "a_a0", "a_k_k", "a_k_a", "a_r_k", "a_gn_g", "a_gn_b"]):
                k.dma(pc.ap()[:, i_, :], P[nm][l].rearrange("(h p) -> p h", p=128), writes=[("pc", i_)], allow_slow_non_contiguous=True)
            op("dve", lambda e: e.tensor_scalar(out=pc.ap()[:, 7, :], in0=pc.ap()[:, 3, :], scalar1=-1.0, scalar2=1.0, op0=ALU.mult, op1=ALU.add), reads=[("pc", 3)], writes=[("pc", 7)])
            pcall = [("pc", i_) for i_ in range(8)]
            gnc = sc.sb("gnc", [128, 1], F32)
            op("pool", lambda e: e.memset(gnc.ap(), 64e-5), writes=["gnc"])
            F = {n_: sc.sb("af_" + n_, [128, S], F32) for n_ in ["r", "k", "v", "a", "L", "lw", "t1", "t2"]}
            F["g"] = sc.sb("af_g", [128, S], BF16)
            Bh = {n_: [sc.sb("ab_%s%d" % (n_, h), [128, S], BF16) for h in range(2)] for n_ in ["bt", "kt", "bc", "kc"]}
            Bh["rh"] = [sc.sb("ab_rh", [128, S], BF16)] * 2
            Bh["ah"] = [sc.sb("ab_ah", [128, S], BF16)] * 2
            AR = sc.sb("AR", [128, NCH, 2, C], BF16)
            dcol = sc.sb("adcol", [128, 3, NCH], F32)
            vtok = sc.sb("avtok", [128, NCH, 128], BF16)
            vpad = [sc.sb("avpad%d" % h, [128, NCH, 128], BF16) for h in range(2)]
            H = [sc.sb("aH%d" % h, [128, 128], F32) for h in range(2)]
            Hb_r = [sc.ring("aHb%d" % h, [128, 128], BF16, 2) for h in range(2)]
            upad = [sc.ring("aupad%d" % h, [128, 128], BF16, 2) for h in range(2)]
            mt_r = sc.ring("amt", [128, 128], BF16, 8)
            kp_r = sc.ring("akp", [128, 128], BF16, 12)
            wu_r = sc.ring("awu", [128, 128], BF16, 2)
            tk_r = sc.ring("atk", [128, 128], BF16, 6)
            ya = F["t1"]
            yo = Bh["rh"][0]
            sq_r = sc.ring("asq", [128, TB], F32, 2)
            hm = lambda h: tri.ap()[:, 6, h:h + 1]
            for h in range(2):
                for r_ in upad[h].t:
                    op("pool", lambda e: e.memset(r_.ap(), 0.0), writes=[("upad0", h, r_.name)])
            for hp in range(4):
                co = A_OFF + hp * 128
                prm = lambda i_: pc.ap()[:, i_, hp:hp + 1]
                for nm, cc in (("r", co), ("k", co + 512), ("v", co + 1024)):
                    proj_shift(cc, 128, lambda tb, ps, pk: op("act", lambda e: e.copy(out=F[nm].ap()[:, tsl(tb)], in_=ps.ap()), reads=[pk], writes=[(nm, tb)]))
                allk = lambda nm: [(nm, tb) for tb in range(NTB)]

                def vtm(w1, w1k, wm, wmk):
                    for c4 in range(NCH // 4):
                        ps, pk = psr.next()
                        for j in range(4):
                            c = c4 * 4 + j
                            for kc in range(KC):
                                op("pe", lambda e: e.matmul(ps.ap()[:, j * 128:(j + 1) * 128], lhsT=uT.ap()[:, kc, 2 + c * C:2 + (c + 1) * C], rhs=w1.ap()[:, kc, :], start=(kc == 0), stop=False),
                                   reads=[w1k] + ukeys(kc, c // 4), writes=[pk])
                            for kc in range(KC):
                                op("pe", lambda e: e.matmul(ps.ap()[:, j * 128:(j + 1) * 128], lhsT=uT.ap()[:, kc, 1 + c * C:1 + (c + 1) * C], rhs=wm.ap()[:, kc, :], start=False, stop=(kc == KC - 1)),
                                   reads=[wmk] + ukeys(kc, c // 4) + (ukeys(kc, c // 4 - 1) if c >= 4 else []) + ["uTpad"], writes=[pk])
                        cast(vtok.ap()[:, c4 * 4:(c4 + 1) * 4, :], ps.ap().rearrange("p (a b) -> p a b", a=4), [pk], [("avtok", c4)], eng="act")
                proj_shift(co + 1024, 128, None, token_major=vtm)
                vk = [("avtok", c4) for c4 in range(NCH // 4)]
                for h in range(2):
                    op("pool", lambda e: e.tensor_tensor(out=vpad[h].ap(), in0=vtok.ap(), in1=tri.ap()[:, 5, h * 64:h * 64 + 1].to_broadcast([128, NCH, 128]) if False else
                                                         tri.ap()[0:1, 5, :].partition_broadcast(128).unsqueeze(1).to_broadcast([128, NCH, 128]) if False else vtok.ap(), op=ALU.mult) if False else
                       e.memset(vpad[h].ap(), 0.0), reads=vk, writes=[("vpad", h)])
                    op("act", lambda e: e.copy(out=vpad[h].ap()[:, :, h * 64:(h + 1) * 64], in_=vtok.ap()[:, :, h * 64:(h + 1) * 64]), reads=vk + [("vpad", h)], writes=[("vpad", h)])
                for tb in range(NTB):
                    ps, pk = psr.next()
                    op("pe", lambda e: e.matmul(ps.ap(), lhsT=lrb.ap()[0:64, 0, hp * 128:(hp + 1) * 128], rhs=tw.ap()[:, tsl(tb)], start=True, stop=True), reads=["lrb01", ("tw", tb)], writes=[pk])
                    op("act", lambda e: e.activation(out=F["lw"].ap()[:, tsl(tb)], in_=ps.ap(), func=AF.Sigmoid, bias=prm(0)), reads=[pk] + pcall, writes=[("lw", tb)])
                    ps, pk = psr.next()
                    op("pe", lambda e: e.matmul(ps.ap(), lhsT=lrb.ap()[0:64, 1, hp * 128:(hp + 1) * 128], rhs=adT.ap()[:, tsl(tb)], start=True, stop=True), reads=["lrb01", ("adT", tb)], writes=[pk])
                    op("act", lambda e: e.activation(out=F["a"].ap()[:, tsl(tb)], in_=ps.ap(), func=AF.Sigmoid, bias=prm(1)), reads=[pk] + pcall, writes=[("a", tb)])
                    ps, pk = psr.next()
                    op("pe", lambda e: e.matmul(ps.ap(), lhsT=lrb.ap()[:, 2, hp * 128:(hp + 1) * 128], rhs=sgT.ap()[:, tsl(tb)], start=True, stop=True), reads=["lrb2", ("sgT", tb)], writes=[pk])
                    op("act", lambda e: e.copy(out=F["g"].ap()[:, tsl(tb)], in_=ps.ap()), reads=[pk], writes=[("g", tb)])
                op("dve", lambda e: e.tensor_scalar(out=F["lw"].ap(), in0=F["lw"].ap(), scalar1=-0.6065306597126334, scalar2=None, op0=ALU.mult), reads=allk("lw"), writes=allk("lw"))
                op("dve", lambda e: e.tensor_tensor_scan(out=F["L"].ap(), data0=ones.ap()[:, 0:1].to_broadcast([128, S]), data1=F["lw"].ap(), initial=0.0, op0=ALU.mult, op1=ALU.add), reads=allk("lw") + ["ones"], writes=["L"])
                L3 = F["L"].ap().rearrange("p (c t) -> p c t", t=C)
                op("dve", lambda e: e.tensor_tensor(out=dcol.ap()[:, 0, :], in0=L3[:, :, C - 1], in1=L3[:, :, C // 2 - 1], op=ALU.subtract), reads=["L"], writes=[("dc", 0)])
                op("dve", lambda e: e.tensor_copy(out=dcol.ap()[:, 1, 0:1], in_=L3[:, 0, C // 2 - 1:C // 2]), reads=["L"], writes=[("dc", 1)])
                op("dve", lambda e: e.tensor_tensor(out=dcol.ap()[:, 1, 1:NCH], in0=L3[:, 1:NCH, C // 2 - 1], in1=L3[:, 0:NCH - 1, C - 1], op=ALU.subtract), reads=["L", ("dc", 1)], writes=[("dc", 1)])
                op("dve", lambda e: e.tensor_tensor(out=dcol.ap()[:, 2, :], in0=dcol.ap()[:, 0, :], in1=dcol.ap()[:, 1, :], op=ALU.add), reads=[("dc", 0), ("dc", 1)], writes=[("dc", 2)])
                dck = [("dc", 0), ("dc", 1), ("dc", 2)]
                op("act", lambda e: e.activation(out=dcol.ap(), in_=dcol.ap(), func=AF.Exp), reads=dck, writes=dck)
                t1, t2 = F["t1"], F["t2"]
                op("dve", lambda e: e.tensor_scalar(out=t1.ap(), in0=F["k"].ap(), scalar1=prm(2), scalar2=None, op0=ALU.mult), reads=allk("k") + pcall, writes=["t1"])
                for tb in range(NTB):
                    sq, sqk = sq_r.next()
                    op("act", lambda e: e.activation(out=sq.ap(), in_=t1.ap()[:, tsl(tb)], func=AF.Square), reads=["t1"], writes=[sqk])
                    ps, pk = psr.next()
                    op("pe", lambda e: e.matmul(ps.ap(), lhsT=tri.ap()[:, 5, :], rhs=sq.ap(), start=True, stop=True), reads=[sqk, "tri"], writes=[pk])
                    op("dve", lambda e: e.tensor_scalar(out=sq.ap(), in0=ps.ap(), scalar1=1e-24, scalar2=None, op0=ALU.max), reads=[pk], writes=[sqk])
                    op("act", lambda e: e.activation(out=sq.ap(), in_=sq.ap(), func=AF.Sqrt), reads=[sqk], writes=[sqk])
                    op("dve", lambda e: e.reciprocal(out=sq.ap(), in_=sq.ap()), reads=[sqk], writes=[sqk])
                    op("dve", lambda e: e.tensor_tensor(out=t1.ap()[:, tsl(tb)], in0=t1.ap()[:, tsl(tb)], in1=sq.ap(), op=ALU.mult), reads=["t1", sqk], writes=["t1"])
                op("dve", lambda e: e.tensor_scalar(out=t2.ap(), in0=F["a"].ap(), scalar1=prm(3), scalar2=prm(7), op0=ALU.mult, op1=ALU.add), reads=allk("a") + pcall, writes=["t2"])
                op("dve", lambda e: e.tensor_tensor(out=F["k"].ap(), in0=F["k"].ap(), in1=t2.ap(), op=ALU.mult), reads=allk("k") + ["t2"], writes=allk("k"))
                op("dve", lambda e: e.tensor_tensor(out=F["a"].ap(), in0=F["a"].ap(), in1=t1.ap(), op=ALU.mult), reads=allk("a") + ["t1"], writes=allk("a"))
                op("dve", lambda e: e.scalar_tensor_tensor(out=t2.ap(), in0=F["r"].ap(), scalar=prm(4), in1=F["k"].ap(), op0=ALU.mult, op1=ALU.mult), reads=allk("r") + allk("k") + pcall, writes=["t2"])
                for tb in range(NTB):
                    sq, sqk = sq_r.next()
                    op("act", lambda e: e.copy(out=sq.ap(), in_=t2.ap()[:, tsl(tb)]), reads=["t2"], writes=[sqk])
                    ps, pk = psr.next()
                    op("pe", lambda e: e.matmul(ps.ap(), lhsT=tri.ap()[:, 5, :], rhs=sq.ap(), start=True, stop=True), reads=[sqk, "tri"], writes=[pk])
                    op("dve", lambda e: e.tensor_tensor(out=F["v"].ap()[:, tsl(tb)], in0=F["v"].ap()[:, tsl(tb)], in1=ps.ap(), op=ALU.mult), reads=[("v", tb), pk], writes=[("v", tb)])
                m_b = L3[:, :, C // 2 - 1:C // 2].to_broadcast([128, NCH, C])
                v3 = lambda t_: t_.ap().rearrange("p (c t) -> p c t", t=C)
                op("dve", lambda e: e.tensor_tensor(out=v3(t2), in0=L3, in1=m_b, op=ALU.subtract), reads=["L"], writes=["t2"])
                op("dve", lambda e: e.tensor_tensor(out=F["lw"].ap(), in0=t2.ap(), in1=F["lw"].ap(), op=ALU.subtract), reads=["t2"] + allk("lw"), writes=allk("lw"))
                op("act", lambda e: e.activation(out=F["lw"].ap(), in_=F["lw"].ap(), func=AF.Exp), reads=allk("lw"), writes=allk("lw"))
                op("act", lambda e: e.activation(out=F["L"].ap(), in_=t2.ap(), func=AF.Exp, scale=-1.0), reads=["t2"], writes=["L"])
                op("act", lambda e: e.activation(out=t2.ap(), in_=t2.ap(), func=AF.Exp), reads=["t2"], writes=["t2"])
                op("dve", lambda e: e.tensor_tensor(out=F["r"].ap(), in0=F["r"].ap(), in1=t2.ap(), op=ALU.mult), reads=allk("r") + ["t2"], writes=allk("r"))
                op("act", lambda e: e.copy(out=AR.ap()[:, :, 1, :], in_=v3(F["r"])), reads=allk("r"), writes=["AR1"])
                dB = dcol.ap()[:, 1, :].unsqueeze(2).to_broadcast([128, NCH, C]); dA = dcol.ap()[:, 0, :].unsqueeze(2).to_broadcast([128, NCH, C])
                op("dve", lambda e: e.tensor_tensor(out=v3(t2), in0=v3(F["r"]), in1=dB, op=ALU.mult), reads=allk("r") + dck, writes=["t2"])
                op("act", lambda e: e.copy(out=Bh["rh"][0].ap(), in_=t2.ap()), reads=["t2"], writes=[("rh", 0), ("rh", 1)] + [("yo", tb) for tb in range(NTB)])
                op("dve", lambda e: e.scalar_tensor_tensor(out=F["lw"].ap(), in0=t1.ap(), scalar=-1.0, in1=F["lw"].ap(), op0=ALU.mult, op1=ALU.mult), reads=["t1"] + allk("lw"), writes=allk("lw"))
                op("act", lambda e: e.copy(out=AR.ap()[:, :, 0, :], in_=v3(F["lw"])), reads=allk("lw"), writes=["AR0"])
                op("dve", lambda e: e.tensor_tensor(out=v3(t2), in0=v3(F["lw"]), in1=dB, op=ALU.mult), reads=allk("lw") + dck + [("rh", 0), ("rh", 1)], writes=["t2"])
                op("act", lambda e: e.copy(out=Bh["ah"][0].ap(), in_=t2.ap()), reads=["t2"], writes=[("ah", 0), ("ah", 1)])
                for src, nt, ncn in ((F["a"], "bt", "bc"), (F["k"], "kt", "kc")):
                    sk_ = allk("a") if nt == "bt" else allk("k")
                    op("dve", lambda e: e.tensor_tensor(out=src.ap(), in0=src.ap(), in1=F["L"].ap(), op=ALU.mult), reads=sk_ + ["L"], writes=sk_)
                    op("dve", lambda e: e.tensor_tensor(out=v3(t1) if nt == "kt" else v3(t2), in0=v3(src), in1=dA, op=ALU.mult), reads=sk_ + dck + [("ah", 0), ("ah", 1)], writes=["t1" if nt == "kt" else "t2"])
                    tt_ = t1 if nt == "kt" else t2
                    for h in range(2):
                        op("act", lambda e: e.activation(out=Bh[nt][h].ap(), in_=src.ap(), func=AF.Copy, scale=hm(h)), reads=sk_ + ["tri"], writes=[(nt, h)])
                        op("act", lambda e: e.activation(out=Bh[ncn][h].ap(), in_=tt_.ap(), func=AF.Copy, scale=hm(h)), reads=["t1" if nt == "kt" else "t2", "tri"], writes=[(ncn, h)])
                k.phase_barrier()
                pool_t = []
                for nm_ in ("r", "k", "a", "L", "lw", "t2"):
                    vb = F[nm_].ap().bitcast(BF16)
                    pool_t += [vb[:, j * 128:(j + 1) * 128] for j in range(32)]
                RES = {}
                it_ = iter(pool_t)
                for c_ in range(NCH):
                    for h_ in range(2):
                        for nm_ in ("TT", "rb", "ak", "rk", "bcT", "kcT"):
                            RES[(nm_, h_, c_)] = next(it_)
                vt_b = vtok.ap().rearrange("p a b -> p (a b)")
                tmp_t = [t_.ap() for t_ in mt_r.t] + [t_.ap() for t_ in tk_r.t] + [t_.ap() for t_ in kp_r.t] + [vt_b[:, j * 128:(j + 1) * 128] for j in range(16)]
                GCH = 8
                assert len(tmp_t) >= 5 * GCH, len(tmp_t)

                def A_gen(group):
                    st = {}
                    for gi_, (h, c) in enumerate(group):
                        cs = slice(c * C, (c + 1) * C)
                        T5 = tmp_t[gi_ * 5:(gi_ + 1) * 5]
                        tk_ = lambda j: ("tmp", gi_, j)
                        ps, pk = psr.next()
                        op("pe", lambda e: e.matmul(ps.ap()[:, 0:256], lhsT=Bh["bt"][h].ap()[:, cs], rhs=AR.ap()[:, c].rearrange("p a b -> p (a b)"), start=True, stop=True), reads=[("bt", h), "AR0", "AR1"], writes=[pk])
                        op("dve", lambda e: e.tensor_tensor(out=T5[1], in0=ps.ap()[:, 0:128], in1=tri.ap()[:, 1, :], op=ALU.mult), reads=[pk, "tri"], writes=[tk_(1)])
                        op("dve", lambda e: e.tensor_tensor(out=RES[("rb", h, c)], in0=ps.ap()[:, 128:256], in1=tri.ap()[:, 0, :], op=ALU.mult), reads=[pk, "tri"], writes=[("res", "rb", h, c)])
                        ps, pk = psr.next()
                        op("pe", lambda e: e.matmul(ps.ap()[:, 0:256], lhsT=Bh["kt"][h].ap()[:, cs], rhs=AR.ap()[:, c].rearrange("p a b -> p (a b)"), start=True, stop=True), reads=[("kt", h), "AR0", "AR1"], writes=[pk])
                        op("dve", lambda e: e.tensor_tensor(out=RES[("ak", h, c)], in0=ps.ap()[:, 0:128], in1=tri.ap()[:, 1, :], op=ALU.mult), reads=[pk, "tri"], writes=[("res", "ak", h, c)])
                        op("dve", lambda e: e.tensor_tensor(out=RES[("rk", h, c)], in0=ps.ap()[:, 128:256], in1=tri.ap()[:, 0, :], op=ALU.mult), reads=[pk, "tri"], writes=[("res", "rk", h, c)])
                        ps, pk = psr.next()
                        op("pe", lambda e: e.matmul(ps.ap()[:, 0:128], lhsT=AR.ap()[:, c, 0, :], rhs=Bh["bt"][h].ap()[:, cs], start=True, stop=True), reads=["AR0", ("bt", h)], writes=[pk])
                        op("dve", lambda e: e.tensor_tensor(out=T5[0], in0=ps.ap()[:, 0:128], in1=tri.ap()[:, 4, :], op=ALU.mult), reads=[pk, "tri"], writes=[tk_(0)])
                        op("dve", lambda e: e.tensor_tensor(out=T5[4], in0=T5[1], in1=identb.ap(), op=ALU.add), reads=[tk_(1), "identb"], writes=[tk_(4)])
                        pst, pkt = psr.next()
                        op("pe", lambda e: e.transpose(out=pst.ap().bitcast(BF16)[:, 0:128], in_=Bh["bc"][h].ap()[:, cs], identity=identb.ap()), reads=[("bc", h), "identb"], writes=[pkt])
                        op("pe", lambda e: e.transpose(out=pst.ap().bitcast(BF16)[:, 128:256], in_=Bh["kc"][h].ap()[:, cs], identity=identb.ap()), reads=[("kc", h), "identb"], writes=[pkt])
                        op("act", lambda e: e.copy(out=RES[("bcT", h, c)], in_=pst.ap().bitcast(BF16)[:, 0:128]), reads=[pkt], writes=[("res", "bcT", h, c)])
                        op("act", lambda e: e.copy(out=RES[("kcT", h, c)], in_=pst.ap().bitcast(BF16)[:, 128:256]), reads=[pkt], writes=[("res", "kcT", h, c)])
                        st[gi_] = [0, 1, 2, 3]
                    yield
                    for lev in range(6):
                        for gi_, (h, c) in enumerate(group):
                            T5 = tmp_t[gi_ * 5:(gi_ + 1) * 5]
                            tk_ = lambda j: ("tmp", gi_, j)
                            iX, iXT, iXn, iXTn = st[gi_]
                            psx, pkx = psr.next()
                            op("pe", lambda e: e.matmul(psx.ap()[:, 0:128], lhsT=T5[iXT], rhs=T5[iX], start=True, stop=True), reads=[tk_(iXT), tk_(iX)], writes=[pkx])
                            if lev < 5:
                                op("pe", lambda e: e.matmul(psx.ap()[:, 128:256], lhsT=T5[iX], rhs=T5[iXT], start=True, stop=True), reads=[tk_(iXT), tk_(iX)], writes=[pkx])
                            op("act", lambda e: e.copy(out=T5[iXn], in_=psx.ap()[:, 0:128]), reads=[pkx], writes=[tk_(iXn)])
                            if lev < 5:
                                op("act", lambda e: e.copy(out=T5[iXTn], in_=psx.ap()[:, 128:256]), reads=[pkx], writes=[tk_(iXTn)])
                            st[gi_] = [iXn, iXTn, iX, iXT]
                        yield
                        for gi_, (h, c) in enumerate(group):
                            T5 = tmp_t[gi_ * 5:(gi_ + 1) * 5]
                            tk_ = lambda j: ("tmp", gi_, j)
                            iX = st[gi_][0]
                            psq, pkq = psr.next()
                            op("pe", lambda e: e.matmul(psq.ap()[:, 0:128], lhsT=T5[iX], rhs=T5[4], start=True, stop=True), reads=[tk_(iX), tk_(4)], writes=[pkq])
                            if lev < 5:
                                op("dve", lambda e: e.tensor_tensor(out=T5[4], in0=T5[4], in1=psq.ap()[:, 0:128], op=ALU.add), reads=[tk_(4), pkq], writes=[tk_(4)])
                            else:
                                op("dve", lambda e: e.tensor_tensor(out=RES[("TT", h, c)], in0=T5[4], in1=psq.ap()[:, 0:128], op=ALU.add), reads=[tk_(4), pkq], writes=[("res", "TT", h, c)])
                        yield

                Hb = [None, None]

                def B_gen(chunks):
                    for c in chunks:
                        cs = slice(c * C, (c + 1) * C)
                        rk_ = lambda nm_, h: ("res", nm_, h, c)
                        wus = []
                        for h in range(2):
                            psw, pkw = psr.next()
                            if c > 0:
                                op("pe", lambda e: e.matmul(psw.ap()[:, 0:128], lhsT=Bh["ah"][h].ap()[:, cs], rhs=Hb[h][0].ap(), start=True, stop=False), reads=[("ah", h), Hb[h][1]], writes=[pkw])
                            op("pe", lambda e: e.matmul(psw.ap()[:, 0:128], lhsT=RES[("ak", h, c)], rhs=vpad[h].ap()[:, c, :], start=(c == 0), stop=True), reads=[rk_("ak", h), ("vpad", h)], writes=[pkw])
                            wu, wuk = wu_r.next()
                            op("act", lambda e: e.copy(out=wu.ap(), in_=psw.ap()[:, 0:128]), reads=[pkw], writes=[wuk])
                            wus.append((wu, wuk))
                        yield
                        Ups = []
                        for h in range(2):
                            psu, pku = psr.next()
                            op("pe", lambda e: e.matmul(psu.ap()[:, 0:128], lhsT=RES[("TT", h, c)], rhs=wus[h][0].ap(), start=True, stop=True), reads=[rk_("TT", h), wus[h][1]], writes=[pku])
                            up_, upk = upad[h].next()
                            op("dve", lambda e: e.tensor_copy(out=up_.ap(), in_=psu.ap()[:, 0:128]), reads=[pku, ("upad0", h, up_.name)], writes=[upk])
                            Ups.append((up_, upk))
                        yield
                        if c < NCH - 1:
                            for h in range(2):
                                psh, pkh = psr.next()
                                op("pe", lambda e: e.matmul(psh.ap()[:, 0:128], lhsT=RES[("bcT", h, c)], rhs=Ups[h][0].ap(), start=True, stop=False), reads=[rk_("bcT", h), Ups[h][1]], writes=[pkh])
                                op("pe", lambda e: e.matmul(psh.ap()[:, 0:128], lhsT=RES[("kcT", h, c)], rhs=vpad[h].ap()[:, c, :], start=False, stop=True), reads=[rk_("kcT", h), ("vpad", h)], writes=[pkh])
                                if c == 0:
                                    op("dve", lambda e: e.tensor_copy(out=H[h].ap(), in_=psh.ap()[:, 0:128]), reads=[pkh], writes=[("H", h)])
                                else:
                                    op("dve", lambda e: e.scalar_tensor_tensor(out=H[h].ap(), in0=H[h].ap(), scalar=dcol.ap()[:, 2, c:c + 1], in1=psh.ap()[:, 0:128], op0=ALU.mult, op1=ALU.add),
                                       reads=[("H", h), pkh] + dck, writes=[("H", h)])
                        psy, pky = psr.next()
                        first = True
                        for h in range(2):
                            if c > 0:
                                op("pe", lambda e: e.matmul(psy.ap()[:, 0:128], lhsT=Hb[h][0].ap(), rhs=Bh["rh"][h].ap()[:, cs], start=first, stop=False), reads=[Hb[h][1], ("rh", h)], writes=[pky]); first = False
                            op("pe", lambda e: e.matmul(psy.ap()[:, 0:128], lhsT=Ups[h][0].ap(), rhs=RES[("rb", h, c)], start=first, stop=False), reads=[Ups[h][1], rk_("rb", h)], writes=[pky]); first = False
                            op("pe", lambda e: e.matmul(psy.ap()[:, 0:128], lhsT=vpad[h].ap()[:, c, :], rhs=RES[("rk", h, c)], start=False, stop=(h == 1)), reads=[("vpad", h), rk_("rk", h)], writes=[pky])
                        op("act", lambda e: e.copy(out=ya.ap()[:, cs], in_=psy.ap()[:, 0:128]), reads=[pky], writes=[("ya", c // 4)])
                        if c < NCH - 1:
                            for h in range(2):
                                Hb[h] = Hb_r[h].next()
                                op("act", lambda e: e.copy(out=Hb[h][0].ap(), in_=H[h].ap()), reads=[("H", h)], writes=[Hb[h][1]])
                        yield

                def zip2(ga, gb):
                    da = db = False
                    while not (da and db):
                        if not da and next(ga, "END") == "END":
                            da = True
                        if not db and next(gb, "END") == "END":
                            db = True

                groups = [[(h, c) for c in range(g * 4, g * 4 + 4) for h in range(2)] for g in range(NCH // 4)]
                for _ in A_gen(groups[0]):
                    pass
                for g in range(len(groups)):
                    bg = B_gen(range(g * 4, g * 4 + 4))
                    if g + 1 < len(groups):
                        zip2(A_gen(groups[g + 1]), bg)
                    else:
                        for _ in bg:
                            pass
                k.phase_barrier()
                for tb in range(NTB):
                    yk = [("ya", tb)]
                    ps, pk = psr.next()
                    op("pe", lambda e: e.matmul(ps.ap(), lhsT=tri.ap()[:, 5, :], rhs=ya.ap()[:, tsl(tb)], start=True, stop=True), reads=yk + ["tri"], writes=[pk])
                    sq, sqk = sq_r.next()
                    op("dve", lambda e: e.scalar_tensor_tensor(out=sq.ap(), in0=ps.ap(), scalar=-1.0 / 64, in1=ya.ap()[:, tsl(tb)], op0=ALU.mult, op1=ALU.add), reads=[pk] + yk, writes=[sqk])
                    sq2, sq2k = sq_r.next()
                    op("act", lambda e: e.activation(out=sq2.ap(), in_=sq.ap(), func=AF.Square), reads=[sqk], writes=[sq2k])
                    ps2, pk2 = psr.next()
                    op("pe", lambda e: e.matmul(ps2.ap(), lhsT=tri.ap()[:, 5, :], rhs=sq2.ap(), start=True, stop=True), reads=[sq2k, "tri"], writes=[pk2])
                    op("act", lambda e: e.activation(out=sq2.ap(), in_=ps2.ap(), func=AF.Ln, bias=gnc.ap(), scale=1.0 / 64), reads=[pk2, "gnc"], writes=[sq2k])
                    op("act", lambda e: e.activation(out=sq2.ap(), in_=sq2.ap(), func=AF.Exp, scale=-0.5), reads=[sq2k], writes=[sq2k])
                    op("dve", lambda e: e.tensor_tensor(out=sq.ap(), in0=sq.ap(), in1=sq2.ap(), op=ALU.mult), reads=[sqk, sq2k], writes=[sqk])
                    op("dve", lambda e: e.tensor_scalar(out=sq.ap(), in0=sq.ap(), scalar1=prm(5), scalar2=prm(6), op0=ALU.mult, op1=ALU.add), reads=[sqk] + pcall, writes=[sqk])
                    op("dve", lambda e: e.tensor_tensor(out=sq.ap(), in0=sq.ap(), in1=F["v"].ap()[:, tsl(tb)], op=ALU.add), reads=[sqk, ("v", tb)], writes=[sqk])
                    op("dve", lambda e: e.tensor_tensor(out=yo.ap()[:, tsl(tb)], in0=sq.ap(), in1=F["g"].ap()[:, tsl(tb)], op=ALU.mult), reads=[sqk, ("g", tb)], writes=[("yo", tb)])
                k.dma(yT_dram[0, :, hp, :], yo.ap(), reads=[("yo", tb) for tb in range(NTB)], writes=[("yT_dram", 0, hp)])

    for l in range(DEPTH):
        with Scope(k) as sc:
            hT = sc.sb("hT", [128, KC, S], F32)
            sqr = sc.ring("sq", [128, TB], F32, 3)
            tmpr = sc.ring("nt", [128, TB], F32, 4)
            load_h(hT)
            rmsnorm(hT, 2 * l, u_dst, sqr, tmpr)
        if "yT_in0" not in dbg:
            mixer_rwkv(l)
        if "yT_in1" not in dbg:
            mixer_hgrn(l)
        if "yT_in2" not in dbg:
            mixer_dsa(l)
        with Scope(k) as sc:
            yT = sc.sb("yT", [128, 3, 4, S], BF16)
            mT = sc.sb("mT", [128, KC, S], BF16)
            stg_g = sc.ring("stg_g", [128, 3, KC, 128], F32, 2)
            wg_r = sc.ring("wg", [128, 3, KC, 128], BF16, 2)
            stg_b = sc.ring("stg_b", [128, 3, 4, 128], F32, 2)
            wb_r = sc.ring("wb", [128, 3, 4, 128], BF16, 2)
            sgr = sc.ring("sg", [128, TB], F32, 3)
            accr = sc.ring("acc", [128, TB], F32, 2)
            for n in range(3):
                k.dma(yT.ap()[:, n], yT_src[l][n], writes=[("yT", n)])
            for dc in range(KC):
                sg_t, sgk = stg_g.next(); wg, wgk = wg_r.next()
                sb_t, sbk = stg_b.next(); wb, wbk = wb_r.next()
                for n in range(3):
                    c0 = G_OFF + n * D + dc * 128
                    k.dma(sg_t.ap()[:, n], P["w_in"][l][:, c0:c0 + 128].rearrange("(c p) n -> p c n", p=128), writes=[(sgk, n)])
                    k.dma(sb_t.ap()[:, n], P["w_branch"][l, n][:, dc * 128:(dc + 1) * 128].rearrange("(c p) n -> p c n", p=128),
                          writes=[(sbk, n)])
                cast(wg.ap(), sg_t.ap(), [(sgk, n) for n in range(3)], [wgk])
                cast(wb.ap(), sb_t.ap(), [(sbk, n) for n in range(3)], [wbk])
                for tb in range(NTB):
                    acc, acck = accr.next()
                    for n in range(3):
                        pg, pgk = psr.next()
                        for kc in range(KC):
                            op("pe", lambda e: e.matmul(pg.ap(), lhsT=wg.ap()[:, n, kc, :], rhs=uT.ap()[:, kc, usl(tb)],
                                                        start=(kc == 0), stop=(kc == KC - 1)),
                               reads=[wgk] + ukeys(kc, tb), writes=[pgk])
                        pb, pbk = psr.next()
                        for c in range(4):
                            op("pe", lambda e: e.matmul(pb.ap(), lhsT=wb.ap()[:, n, c, :], rhs=yT.ap()[:, n, c, tsl(tb)],
                                                        start=(c == 0), stop=(c == 3)),
                               reads=[wbk, ("yT", n)], writes=[pbk])
                        sg, sgk2 = sgr.next()
                        op("act", lambda e: e.activation(out=sg.ap(), in_=pg.ap(), func=AF.Sigmoid), reads=[pgk], writes=[sgk2])
                        if n == 0:
                            op("dve", lambda e: e.tensor_tensor(out=acc.ap(), in0=sg.ap(), in1=pb.ap(), op=ALU.mult),
                               reads=[sgk2, pbk], writes=[acck])
                        else:
                            op("dve", lambda e: e.tensor_tensor(out=sg.ap(), in0=sg.ap(), in1=pb.ap(), op=ALU.mult),
                               reads=[sgk2, pbk], writes=[sgk2])
                            dstap = mT.ap()[:, dc, tsl(tb)] if n == 2 else acc.ap()
                            op("pool" if n == 1 else "dve", lambda e: e.tensor_tensor(out=dstap, in0=sg.ap(), in1=acc.ap(), op=ALU.add),
                               reads=[sgk2, acck], writes=[("mT", dc, tb)] if n == 2 else [acck])
            stg_o = sc.ring("stg_o", [128, KC, 128], F32, 2)
            wo_r = sc.ring("wo", [128, KC, 128], BF16, 2)
            hr = sc.ring("hr", [128, TB], F32, 3)
            for dc in range(KC):
                wo, wok = wload(P["w_o"][l][:, dc * 128:(dc + 1) * 128], KC, 128, stg_o, wo_r)
                for tb in range(NTB):
                    ht, htk = hr.next()
                    k.dma(ht.ap(), h_dram[:, dc, tsl(tb)], writes=[htk])
                    ps, pk = psr.next()
                    for c in range(KC):
                        op("pe", lambda e: e.matmul(ps.ap(), lhsT=wo.ap()[:, c, :], rhs=mT.ap()[:, c, tsl(tb)],
                                                    start=(c == 0), stop=(c == KC - 1)),
                           reads=[wok, ("mT", c, tb)], writes=[pk])
                    op("dve", lambda e: e.tensor_tensor(out=ht.ap(), in0=ht.ap(), in1=ps.ap(), op=ALU.add),
                       reads=[htk, pk], writes=[htk])
                    k.dma(h_dram[:, dc, tsl(tb)], ht.ap(), reads=[htk], writes=[("h_dram", dc, tb)])
        with Scope(k) as sc:
            hT = sc.sb("hT", [128, KC, S], F32)
            sc2 = Scope(k)
            sc_outer = sc; sc = sc2
            sqr = sc.ring("sq", [128, TB], F32, 2)
            tmpr = sc.ring("nt", [128, TB], F32, 3)
            load_h(hT)
            rmsnorm(hT, 2 * l + 1, u_dst, sqr, tmpr)
            cw = sc.sb("cw", [128, 3, 2 * NFF], F32)
            cb = sc.sb("cb", [128, 2 * NFF], F32)
            for j in range(3):
                k.dma(cw.ap()[:, j, :], P["conv_w"][l, j].rearrange("(c p) -> p c", p=128), writes=["cw"], allow_slow_non_contiguous=True)
            k.dma(cb.ap(), P["conv_b"][l].rearrange("(c p) -> p c", p=128), writes=["cb"], allow_slow_non_contiguous=True)
            G = 2
            stg_u = sc.ring("stg_u", [128, 2, KC, 128], F32, 2)
            wu_r = sc.ring("wu", [128, 2, KC, 128], BF16, 2)
            stg_d = sc.ring("stg_d", [128, G, D], F32, 1)
            wd_r = sc.ring("wd", [128, G, D], BF16, 2)
            upr = sc.ring("up", [128, 2, S + 2], F32, 1)
            cr = sc.ring("cc", [128, 2, TB], F32, 2)
            actr = sc.ring("actT", [128, G, S], BF16, 2)
            up, upk = upr.next()
            op("pool", lambda e: e.memset(up.ap()[:, :, 0:2], 0.0), writes=["uppad"])
            for g0 in range(0, NFF, G):
                actT, actk = actr.next()
                wd, wdk = wload(P["w_down"][l][g0 * 128:(g0 + G) * 128, :], G, D, stg_d, wd_r)
                for gi in range(G):
                    j = g0 + gi
                    su, suk = stg_u.next(); wu, wuk = wu_r.next()
                    for hv in range(2):
                        c0 = hv * DFF + j * 128
                        k.dma(su.ap()[:, hv], P["w_up"][l][:, c0:c0 + 128].rearrange("(c p) n -> p c n", p=128), writes=[(suk, hv)])
                    cast(wu.ap(), su.ap(), [(suk, 0), (suk, 1)], [wuk])
                    for tb in range(NTB):
                        cc, cck = cr.next()
                        for hv in range(2):
                            ps, pk = psr.next()
                            for kc in range(KC):
                                op("pe", lambda e: e.matmul(ps.ap(), lhsT=wu.ap()[:, hv, kc, :], rhs=uT.ap()[:, kc, usl(tb)],
                                                            start=(kc == 0), stop=(kc == KC - 1)),
                                   reads=[wuk] + ukeys(kc, tb), writes=[pk])
                            ch = hv * NFF + j
                            op("act", lambda e: e.copy(out=up.ap()[:, hv, usl(tb)], in_=ps.ap()), reads=[pk, "uppad"], writes=[("up", hv, tb)])
                            op("dve", lambda e: e.tensor_scalar(out=cc.ap()[:, hv, :], in0=up.ap()[:, hv, usl(tb)],
                                                                scalar1=cw.ap()[:, 2, ch:ch + 1], scalar2=cb.ap()[:, ch:ch + 1],
                                                                op0=ALU.mult, op1=ALU.add),
                               reads=[("up", hv, tb), "cw", "cb"], writes=[(cck, hv)])
                            for sh in (1, 2):
                                rk = [("up", hv, tb)] + ([("up", hv, tb - 1)] if tb > 0 else [])
                                op("dve", lambda e: e.scalar_tensor_tensor(out=cc.ap()[:, hv, :], in0=up.ap()[:, hv, usl(tb, sh)],
                                                                           scalar=cw.ap()[:, 2 - sh, ch:ch + 1], in1=cc.ap()[:, hv, :],
                                                                           op0=ALU.mult, op1=ALU.add),
                                   reads=rk + ["cw", (cck, hv)], writes=[(cck, hv)])
                        op("act", lambda e: e.activation(out=cc.ap()[:, 0, :], in_=cc.ap()[:, 0, :], func=AF.Silu),
                           reads=[(cck, 0)], writes=[(cck, 0)])
                        op("pool", lambda e: e.tensor_tensor(out=actT.ap()[:, gi, tsl(tb)], in0=cc.ap()[:, 0, :], in1=cc.ap()[:, 1, :], op=ALU.mult),
                           reads=[(cck, 0), (cck, 1)], writes=[(actk, gi, tb)])
                for dc in range(KC):
                    for tb in range(NTB):
                        ps, pk = psr.next()
                        for gi in range(G):
                            op("pe", lambda e: e.matmul(ps.ap(), lhsT=wd.ap()[:, gi, dc * 128:(dc + 1) * 128], rhs=actT.ap()[:, gi, tsl(tb)],
                                                        start=(gi == 0), stop=(gi == G - 1)),
                               reads=[wdk, (actk, gi, tb)], writes=[pk])
                        op("dve", lambda e: e.tensor_tensor(out=hT.ap()[:, dc, tsl(tb)], in0=hT.ap()[:, dc, tsl(tb)], in1=ps.ap(), op=ALU.add),
                           reads=hkeys(dc, tb) + [pk], writes=hkeys(dc, tb))
            sc2.__exit__(None, None, None)
            sc = Scope(k)
            sqr = sc.ring("sq", [128, TB], F32, 2)
            tmpr = sc.ring("nt", [128, TB], F32, 3)
            if l < DEPTH - 1:
                store_h(hT)
            else:
                fo = sc.ring("fo", [128, KC, TB], F32, 1)
                ot = sc.ring("ot", [128, D], F32, 2)
                fT, fk = fo.next()
                cur = [0]

                def f_dst(kc, tb):
                    return fT.ap()[:, kc, :], [("fT", kc)]
                for tb in range(NTB):
                    ps, pk = psr.next()
                    for kc in range(KC):
                        sq, sk = sqr.next()
                        op("act", lambda e: e.activation(out=sq.ap(), in_=hT.ap()[:, kc, tsl(tb)], func=AF.Square),
                           reads=hkeys(kc, tb), writes=[sk])
                        op("pe", lambda e: e.matmul(ps.ap(), lhsT=ones.ap(), rhs=sq.ap(), start=(kc == 0), stop=(kc == KC - 1)),
                           reads=[sk, "ones"], writes=[pk])
                    sd, sdk = tmpr.next()
                    op("act", lambda e: e.activation(out=sd.ap(), in_=ps.ap(), func=AF.Ln, bias=epsc.ap(), scale=1.0 / D),
                       reads=[pk, "epsc"], writes=[sdk])
                    rs, rsk = tmpr.next()
                    op("act", lambda e: e.activation(out=rs.ap(), in_=sd.ap(), func=AF.Exp, scale=-0.5), reads=[sdk], writes=[rsk])
                    for kc in range(KC):
                        op("dve", lambda e: e.scalar_tensor_tensor(out=fT.ap()[:, kc, :], in0=hT.ap()[:, kc, tsl(tb)],
                                                                   scalar=gvec.ap()[:, 2 * DEPTH, kc:kc + 1], in1=rs.ap(),
                                                                   op0=ALU.mult, op1=ALU.mult),
                           reads=hkeys(kc, tb) + [rsk, "gvec"], writes=[("fT", kc)])
                    for t4 in range(4):
                        o, okk = ot.next()
                        for half in range(2):
                            ps2, pk2 = psr.next()
                            for j in range(4):
                                kc = half * 4 + j
                                op("pe", lambda e: e.transpose(out=ps2.ap()[:, j * 128:(j + 1) * 128],
                                                               in_=fT.ap()[:, kc, t4 * 128:(t4 + 1) * 128], identity=ident.ap()),
                                   reads=[("fT", kc), "ident"], writes=[pk2])
                            cast(o.ap()[:, half * 512:(half + 1) * 512], ps2.ap(), [pk2], [(okk, half)], eng=("dve", "act")[half])
                        r0 = tb * TB + t4 * 128
                        k.dma(out[r0:r0 + 128, :], o.ap(), reads=[(okk, 0), (okk, 1)], writes=[("out", r0)])
            sc.__exit__(None, None, None)
    k.drain("sp")
    return k


PARAM_NAMES = ["norm_mix_g", "w_in", "a_mu", "a_w0", "a_w_up", "a_a0", "a_a_up", "a_g_up", "a_k_k", "a_k_a", "a_r_k",
               "a_gn_g", "a_gn_b", "b_lb_logits", "b_gn_g", "w_branch", "w_o", "norm_ffn_g", "w_up", "conv_w", "conv_b",
               "w_down", "norm_final_g"]


def make_consts():
    i = np.arange(128)
    low = (i[:, None] >= i[None, :])
    bd = ((i[:, None] // 64) == (i[None, :] // 64))
    hm = np.zeros((128, 128)); hm[:64, 0] = 1; hm[64:, 1] = 1
    tri = np.stack([(i[:, None] <= i[None, :]), (i[:, None] < i[None, :]), low, np.where(low, 0.0, -1e30),
                    (i[:, None] > i[None, :]), bd, hm]).astype(np.float32)
    inv_freq = (np.float32(10000.0) ** (-np.arange(32, dtype=np.float32) * np.float32(2.0 / 64))).astype(np.float32)
    ang = (np.arange(S, dtype=np.float32)[:, None] * inv_freq[None, :]).astype(np.float32)
    cos = np.cos(ang).astype(np.float32).T; sin = np.sin(ang).astype(np.float32).T
    cosF = np.concatenate([cos, cos, cos, cos], axis=0)
    sinF = np.concatenate([-sin, sin, -sin, sin], axis=0)
    return {"c_ident": np.eye(128, dtype=np.float32), "c_tri": tri, "c_rope": np.stack([cosF, sinF]).astype(np.float32)}


def make_in_map(inputs, b, consts):
    m = {"x": np.ascontiguousarray(inputs["x"][b])}
    for n in PARAM_NAMES:
        a = np.ascontiguousarray(np.asarray(inputs[n], dtype=np.float32))
        if n == "a_r_k":
            a = a.reshape(DEPTH, 512)
        m[n] = a
    m.update(consts)
    return m


def kernel(**inputs):
    k = build()
    consts = make_consts()
    in_maps = [make_in_map(inputs, b, consts) for b in range(NB)]
    res = run_bass_kernel_spmd(k.nc, in_maps, core_ids=list(range(NB)))
    return np.stack([np.asarray(r["out"], dtype=np.float32) for r in res.results], axis=0)
```

```python
import numpy as np
import ml_dtypes
import concourse.bass as bass
import concourse.mybir as mybir
from concourse.bass_utils import run_bass_kernel_spmd

F32 = mybir.dt.float32
BF16 = mybir.dt.bfloat16
AF = mybir.ActivationFunctionType
ALU = mybir.AluOpType
AX = mybir.AxisListType

S = 2048; D = 1024; DEPTH = 2; NB = 8
TB = 512; NTB = S // TB; KC = D // 128
A_COLS = 1792; B_COLS = 2048; C_COLS = 1092; G_COLS = 3072; IN_COLS = 8004
A_OFF = 0; B_OFF = A_COLS; C_OFF = A_COLS + B_COLS; G_OFF = C_OFF + C_COLS
DFF = 2816; NFF = DFF // 128
NDMA = 24
NOSYNC_SAME = {"pe"}


class K:
    def __init__(self):
        self.nc = nc = bass.Bass("TRN2", target_bir_lowering=False)
        self.eng = {"pe": nc.tensor, "act": nc.scalar, "dve": nc.vector, "pool": nc.gpsimd, "sp": nc.sync}
        self.sem = {}
        self.cnt = {}
        for e in self.eng:
            self.sem[e] = nc.alloc_semaphore("s_" + e)
            self.cnt[e] = 0
        self.seen = {e: {} for e in self.eng}
        self.lastw = {}
        self.readers = {}
        self.dsem = [nc.alloc_semaphore("d%d" % i) for i in range(NDMA)]
        self.dval = [0] * NDMA
        self.di = 0
        self.nwait = 0
        self.ninst = 0

    def _wait(self, e, ev):
        name, sem, val, src = ev
        if src == e and e in NOSYNC_SAME:
            return
        if self.seen[e].get(name, 0) >= val:
            return
        self.eng[e].wait_ge(sem, val)
        self.nwait += 1
        self.seen[e][name] = val

    def _deps(self, e, reads, writes):
        for r in reads:
            ev = self.lastw.get(r)
            if ev is not None:
                self._wait(e, ev)
        for w in writes:
            ev = self.lastw.get(w)
            if ev is not None:
                self._wait(e, ev)
            for ev in self.readers.get(w, ()):
                self._wait(e, ev)

    def _record(self, ev, reads, writes):
        for w in writes:
            self.lastw[w] = ev
            self.readers[w] = []
        for r in reads:
            if r in writes:
                continue
            lst = self.readers.setdefault(r, [])
            lst[:] = [x for x in lst if x[0] != ev[0]]
            lst.append(ev)

    def op(self, e, fn, reads=(), writes=()):
        self._deps(e, reads, writes)
        ins = fn(self.eng[e])
        self.cnt[e] += 1
        ins.then_inc(self.sem[e], 1)
        ev = ("E" + e, self.sem[e], self.cnt[e], e)
        self.seen[e]["E" + e] = max(self.seen[e].get("E" + e, 0), 0)
        self._record(ev, reads, writes)
        self.ninst += 1
        return ins

    def dma(self, out, in_, reads=(), writes=(), q="sp", **kw):
        self._deps(q, reads, writes)
        j = self.di % NDMA
        self.di += 1
        if self.dval[j] > 0:
            self._wait(q, ("D%d" % j, self.dsem[j], self.dval[j], None))
        self.dval[j] += 16
        self.eng[q].dma_start(out=out, in_=in_, **kw).then_inc(self.dsem[j], 16)
        ev = ("D%d" % j, self.dsem[j], self.dval[j], None)
        self._record(ev, reads, writes)
        self.ninst += 1

    def drain(self, e="sp"):
        for e2 in self.eng:
            if e2 != e and self.cnt[e2] > 0:
                self._wait(e, ("E" + e2, self.sem[e2], self.cnt[e2], e2))
        for j in range(NDMA):
            if self.dval[j] > 0:
                self._wait(e, ("D%d" % j, self.dsem[j], self.dval[j], None))


    def phase_barrier(self):
        self.drain("sp")
        self.nc.all_engine_barrier()
        self.lastw.clear()
        self.readers.clear()


class Scope:
    def __init__(self, k):
        self.k = k
        self.stack = []

    def __enter__(self):
        return self

    _uid = [0]

    def sb(self, name, shape, dtype):
        Scope._uid[0] += 1
        g = self.k.nc.sbuf_tensor("%s_u%d" % (name, Scope._uid[0]), list(shape), dtype)
        t = g.__enter__()
        self.stack.append(g)
        return t

    def ring(self, name, shape, dtype, n):
        return Ring([self.sb("%s_%d" % (name, i), shape, dtype) for i in range(n)], name)

    def __exit__(self, *a):
        self.k.phase_barrier()
        for g in reversed(self.stack):
            g.__exit__(None, None, None)
        return False


class Ring:
    def __init__(self, tensors, name):
        self.t = tensors
        self.name = name
        self.i = 0

    def next(self):
        j = self.i % len(self.t)
        self.i += 1
        return self.t[j], (self.name, j)


def build(dbg=None):
    dbg = dbg or {}
    k = K()
    nc = k.nc
    op = k.op
    uid = [0]

    def din(name, shape, dt=F32):
        return nc.dram_tensor(name, list(shape), dt, kind="ExternalInput").ap()

    x = din("x", [S, D])
    P = {}
    for name, shape in [
        ("norm_mix_g", [DEPTH, D]), ("w_in", [DEPTH, D, IN_COLS]), ("a_mu", [DEPTH, A_COLS]), ("a_w0", [DEPTH, 512]),
        ("a_w_up", [DEPTH, 64, 512]), ("a_a0", [DEPTH, 512]), ("a_a_up", [DEPTH, 64, 512]), ("a_g_up", [DEPTH, 128, 512]),
        ("a_k_k", [DEPTH, 512]), ("a_k_a", [DEPTH, 512]), ("a_r_k", [DEPTH, 512]), ("a_gn_g", [DEPTH, 512]),
        ("a_gn_b", [DEPTH, 512]), ("b_lb_logits", [DEPTH, 512]), ("b_gn_g", [DEPTH, 512]),
        ("w_branch", [DEPTH, 3, 512, D]), ("w_o", [DEPTH, D, D]), ("norm_ffn_g", [DEPTH, D]),
        ("w_up", [DEPTH, D, 2 * DFF]), ("conv_w", [DEPTH, 3, 2 * DFF]), ("conv_b", [DEPTH, 2 * DFF]),
        ("w_down", [DEPTH, DFF, D]), ("norm_final_g", [D]),
    ]:
        P[name] = din(name, shape)
    c_ident = din("c_ident", [128, 128])
    out = nc.dram_tensor("out", [S, D], F32, kind="ExternalOutput").ap()
    h_dram = nc.dram_tensor("h_scr", [128, KC, S], F32, kind="Internal").ap()
    yT_dram = nc.dram_tensor("yT_scr", [3, 128, 4, S], BF16, kind="ExternalOutput" if dbg.get("dump_yT") else "Internal").ap()
    c_tri = din("c_tri", [7, 128, 128])
    c_rope = din("c_rope", [2, 128, S])
    yT_src = [[yT_dram[n] for n in range(3)] for l in range(DEPTH)]
    for n in range(3):
        if ("yT_in%d" % n) in dbg:
            t = din("yT_in%d" % n, [DEPTH, 128, 4, S], BF16)
            for l in range(DEPTH):
                yT_src[l][n] = t[l]

    ident = nc.alloc_sbuf_tensor("ident", [128, 128], F32)
    identb = nc.alloc_sbuf_tensor("identb", [128, 128], BF16)
    ones = nc.alloc_sbuf_tensor("ones", [128, 128], F32)
    epsc = nc.alloc_sbuf_tensor("epsc", [128, 1], F32)
    uT = nc.alloc_sbuf_tensor("uT", [128, KC, S + 2], BF16)
    gvec = nc.alloc_sbuf_tensor("gvec", [128, 2 * DEPTH + 1, KC], F32)
    psr = Ring([nc.alloc_psum_tensor("ps%d" % i, [128, 512], F32) for i in range(6)], "ps")
    psacc = Ring([nc.alloc_psum_tensor("psacc%d" % i, [128, 512], F32) for i in range(2)], "psacc")

    tri = nc.alloc_sbuf_tensor("tri", [128, 7, 128], F32)
    k.dma(tri.ap(), c_tri.rearrange("a p f -> p a f"), writes=["tri"])
    k.dma(ident.ap(), c_ident, writes=["ident"])
    op("dve", lambda e: e.tensor_copy(out=identb.ap(), in_=ident.ap()), reads=["ident"], writes=["identb"])
    op("pool", lambda e: e.memset(ones.ap(), 1.0), writes=["ones"])
    op("pool", lambda e: e.memset(epsc.ap(), 1e-6), writes=["epsc"])
    op("pool", lambda e: e.memset(uT.ap()[:, :, 0:2], 0.0), writes=["uTpad"])
    for l in range(DEPTH):
        k.dma(gvec.ap()[:, 2 * l, :], P["norm_mix_g"][l].rearrange("(c p) -> p c", p=128), writes=["gvec"],
              allow_slow_non_contiguous=True)
        k.dma(gvec.ap()[:, 2 * l + 1, :], P["norm_ffn_g"][l].rearrange("(c p) -> p c", p=128), writes=["gvec"],
              allow_slow_non_contiguous=True)
    k.dma(gvec.ap()[:, 2 * DEPTH, :], P["norm_final_g"].rearrange("(c p) -> p c", p=128), writes=["gvec"],
          allow_slow_non_contiguous=True)

    def hkeys(kc=None, tb=None):
        return [("hT", a, b) for a in (range(KC) if kc is None else [kc]) for b in (range(NTB) if tb is None else [tb])]

    def ukeys(kc=None, tb=None):
        return [("uT", a, b) for a in (range(KC) if kc is None else [kc]) for b in (range(NTB) if tb is None else [tb])]

    def tsl(tb):
        return slice(tb * TB, (tb + 1) * TB)

    def usl(tb, shift=0):
        return slice(2 + tb * TB - shift, 2 + (tb + 1) * TB - shift)

    def rmsnorm(hT, gi, dst_fn, sqr, tmpr):
        for tb in range(NTB):
            ps, pk = psr.next()
            for kc in range(KC):
                sq, sk = sqr.next()
                op("act", lambda e: e.activation(out=sq.ap(), in_=hT.ap()[:, kc, tsl(tb)], func=AF.Square),
                   reads=hkeys(kc, tb), writes=[sk])
                op("pe", lambda e: e.matmul(ps.ap(), lhsT=ones.ap(), rhs=sq.ap(), start=(kc == 0), stop=(kc == KC - 1)),
                   reads=[sk, "ones"], writes=[pk])
            sd, sdk = tmpr.next()
            op("act", lambda e: e.activation(out=sd.ap(), in_=ps.ap(), func=AF.Ln, bias=epsc.ap(), scale=1.0 / D),
               reads=[pk, "epsc"], writes=[sdk])
            rs, rsk = tmpr.next()
            op("act", lambda e: e.activation(out=rs.ap(), in_=sd.ap(), func=AF.Exp, scale=-0.5), reads=[sdk], writes=[rsk])
            for kc in range(KC):
                dst, dk = dst_fn(kc, tb)
                op("dve", lambda e: e.scalar_tensor_tensor(out=dst, in0=hT.ap()[:, kc, tsl(tb)],
                                                           scalar=gvec.ap()[:, gi, kc:kc + 1], in1=rs.ap(),
                                                           op0=ALU.mult, op1=ALU.mult),
                   reads=hkeys(kc, tb) + [rsk, "gvec"], writes=dk)

    def u_dst(kc, tb):
        return uT.ap()[:, kc, usl(tb)], ukeys(kc, tb)

    castsel = [0]

    def cast(dst, src, reads, writes, eng=None):
        ce = eng or "act"
        castsel[0] += 1
        if ce == "act":
            op("act", lambda e: e.copy(out=dst, in_=src), reads=reads, writes=writes)
        else:
            op(ce, lambda e: e.tensor_copy(out=dst, in_=src), reads=reads, writes=writes)

    def wload(src, nk, ncol, stg_ring, w_ring, q="sp"):
        stg, sk = stg_ring.next()
        wt, wk = w_ring.next()
        k.dma(stg.ap()[:, 0:nk, 0:ncol], src.rearrange("(c p) n -> p c n", p=128), writes=[sk], q=q)
        cast(wt.ap()[:, 0:nk, 0:ncol], stg.ap()[:, 0:nk, 0:ncol], [sk], [wk])
        return wt, wk

    def load_h(hT):
        for kc in range(KC):
            k.dma(hT.ap()[:, kc, :], h_dram[:, kc, :], reads=["h_dram"], writes=hkeys(kc))

    def store_h(hT):
        for kc in range(KC):
            k.dma(h_dram[:, kc, :], hT.ap()[:, kc, :], reads=hkeys(kc), writes=[("h_dram", kc)])

    with Scope(k) as sc:
        xs = sc.ring("xs", [128, D], F32, 2)
        xt = sc.ring("xt", [128, KC, 128], F32, 2)
        for tt in range(S // 128):
            a, ak = xs.next()
            b, bk = xt.next()
            k.dma(a.ap(), x[tt * 128:(tt + 1) * 128, :], writes=[ak])
            for half in range(2):
                ps, pk = psr.next()
                for j in range(4):
                    kc = half * 4 + j
                    op("pe", lambda e: e.transpose(out=ps.ap()[:, j * 128:(j + 1) * 128], in_=a.ap()[:, kc * 128:(kc + 1) * 128],
                                                   identity=ident.ap()), reads=[ak, "ident"], writes=[pk])
                cast(b.ap()[:, half * 4:half * 4 + 4, :], ps.ap().rearrange("p (a b) -> p a b", a=4), [pk], [(bk, half)],
                     eng=("dve", "act")[half])
            k.dma(h_dram[:, :, tt * 128:(tt + 1) * 128], b.ap(), reads=[(bk, 0), (bk, 1)], writes=[("h_dram", "x", tt)])


    def proj_fm(l, c0, ncol, sc_rings, consume, shift_w=None):
        stg_r, w_r = sc_rings
        wt, wk = wload(P["w_in"][l][:, c0:c0 + ncol], KC, ncol, stg_r, w_r)
        for tb in range(NTB):
            ps, pk = psr.next()
            for kc in range(KC):
                op("pe", lambda e: e.matmul(ps.ap()[0:ncol, :], lhsT=wt.ap()[:, kc, 0:ncol], rhs=uT.ap()[:, kc, usl(tb)],
                                            start=(kc == 0), stop=(kc == KC - 1)),
                   reads=[wk] + ukeys(kc, tb), writes=[pk])
            consume(tb, ps, pk)

    def mixer_hgrn(l):
        C = 64; NCH = S // C
        with Scope(k) as sc:
            stg_r = sc.ring("hstg", [128, KC, 128], F32, 2)
            w_r = sc.ring("hw", [128, KC, 128], BF16, 2)
            rings = (stg_r, w_r)
            lbt = sc.sb("lbt", [128, 2, 4], F32)
            lb = sc.sb("lb", [128, 4], F32)
            oml = sc.sb("oml", [128, 4], F32)
            noml = sc.sb("noml", [128, 4], F32)
            gng = sc.sb("gng", [128, 4], F32)
            onesS = sc.sb("onesS", [128, S], F32)
            Bc = sc.sb("Bc", [128, S], F32)
            sig = sc.sb("sig", [128, S], F32)
            bp = sc.sb("bp", [128, S], F32)
            Ex = sc.sb("Ex", [128, S], F32)
            ktl = sc.sb("ktl", [128, S], F32)
            qt = sc.sb("qt", [128, S], BF16)
            kt = sc.sb("kt", [128, S], BF16)
            qh = sc.sb("qh", [128, S], BF16)
            kcf = sc.sb("kcf", [128, S], F32)
            vtok = sc.sb("vtok", [64, NCH, 128], BF16)
            ysb = sc.sb("ysb", [128, S], F32)
            gs = sc.sb("gs", [128, S], F32)
            yo = sc.sb("yo", [128, S], BF16)
            dcol = sc.sb("dcol", [128, 4, NCH], F32)
            st = sc.sb("st", [128, 128], F32)
            stb_r = sc.ring("stb", [128, 128], BF16, 2)
            pt_r = sc.ring("pt", [64, 64], BF16, 4)
            kct_r = sc.ring("kct", [64, 128], BF16, 4)
            sq_r = sc.ring("hsq", [128, TB], F32, 2)
            nt_r = sc.ring("hnt", [128, TB], F32, 2)
            op("pool", lambda e: e.memset(onesS.ap(), 1.0), writes=["onesS"])
            k.dma(lbt.ap(), P["b_lb_logits"].rearrange("l (h p) -> p l h", p=128), writes=["lbt"], allow_slow_non_contiguous=True)
            k.dma(gng.ap(), P["b_gn_g"][l].rearrange("(h p) -> p h", p=128), writes=["gng"], allow_slow_non_contiguous=True)
            if l == 0:
                op("dve", lambda e: e.tensor_tensor(out=lb.ap(), in0=lbt.ap()[:, 0, :], in1=lbt.ap()[:, 0, :], op=ALU.subtract),
                   reads=["lbt"], writes=["lb"])
            else:
                op("dve", lambda e: e.tensor_tensor(out=lb.ap(), in0=lbt.ap()[:, 1, :], in1=lbt.ap()[:, 0, :], op=ALU.subtract),
                   reads=["lbt"], writes=["lb"])
                op("act", lambda e: e.activation(out=lb.ap(), in_=lb.ap(), func=AF.Sigmoid), reads=["lb"], writes=["lb"])
            op("dve", lambda e: e.tensor_scalar(out=oml.ap(), in0=lb.ap(), scalar1=-1.0, scalar2=1.0, op0=ALU.mult, op1=ALU.add),
               reads=["lb"], writes=["oml"])
            op("dve", lambda e: e.tensor_scalar(out=noml.ap(), in0=oml.ap(), scalar1=-1.0, scalar2=None, op0=ALU.mult),
               reads=["oml"], writes=["noml"])
            for h in range(4):
                cq = B_OFF + h * 128; cf = B_OFF + 512 + h * 128; ci = B_OFF + 1024 + h * 128; cg = B_OFF + 1536 + h * 128
                def f_consume(tb, ps, pk):
                    op("act", lambda e: e.activation(out=sig.ap()[:, tsl(tb)], in_=ps.ap(), func=AF.Sigmoid), reads=[pk], writes=[("sig", tb)])
                proj_fm(l, cf, 128, rings, f_consume)
                allsig = [("sig", tb) for tb in range(NTB)]
                op("dve", lambda e: e.tensor_scalar(out=ktl.ap(), in0=sig.ap(), scalar1=noml.ap()[:, h:h + 1], scalar2=oml.ap()[:, h:h + 1],
                                                    op0=ALU.mult, op1=ALU.add), reads=allsig + ["noml", "oml"], writes=["ktl"])
                op("dve", lambda e: e.tensor_scalar(out=sig.ap(), in0=sig.ap(), scalar1=oml.ap()[:, h:h + 1], scalar2=lb.ap()[:, h:h + 1],
                                                    op0=ALU.mult, op1=ALU.add), reads=allsig + ["oml", "lb"], writes=allsig)
                op("act", lambda e: e.activation(out=sig.ap(), in_=sig.ap(), func=AF.Ln), reads=allsig, writes=allsig)
                op("dve", lambda e: e.tensor_tensor_scan(out=Bc.ap(), data0=onesS.ap(), data1=sig.ap(), initial=0.0, op0=ALU.mult, op1=ALU.add),
                   reads=allsig + ["onesS"], writes=["Bc"])
                B3 = Bc.ap().rearrange("p (c t) -> p c t", t=C)
                op("dve", lambda e: e.tensor_tensor(out=bp.ap().rearrange("p (c t) -> p c t", t=C), in0=B3,
                                                    in1=B3[:, :, C // 2 - 1:C // 2].to_broadcast([128, NCH, C]), op=ALU.subtract),
                   reads=["Bc"], writes=["bp"] + [("bpq", tb) for tb in range(NTB)])
                op("dve", lambda e: e.tensor_tensor(out=dcol.ap()[:, 0, :], in0=B3[:, :, C - 1], in1=B3[:, :, C // 2 - 1], op=ALU.subtract),
                   reads=["Bc"], writes=[("dcol", 0)])
                op("dve", lambda e: e.tensor_copy(out=dcol.ap()[:, 1, 0:1], in_=B3[:, 0, C // 2 - 1:C // 2]), reads=["Bc"], writes=[("dcol", 1)])
                op("dve", lambda e: e.tensor_tensor(out=dcol.ap()[:, 1, 1:NCH], in0=B3[:, 1:NCH, C // 2 - 1], in1=B3[:, 0:NCH - 1, C - 1], op=ALU.subtract),
                   reads=["Bc", ("dcol", 1)], writes=[("dcol", 1)])
                op("dve", lambda e: e.tensor_tensor(out=dcol.ap()[:, 2, :], in0=dcol.ap()[:, 0, :], in1=dcol.ap()[:, 1, :], op=ALU.add),
                   reads=[("dcol", 0), ("dcol", 1)], writes=[("dcol", 2)])
                op("act", lambda e: e.activation(out=dcol.ap()[:, 0:3, :], in_=dcol.ap()[:, 0:3, :], func=AF.Exp),
                   reads=[("dcol", 0), ("dcol", 1), ("dcol", 2)], writes=[("dcol", 0), ("dcol", 1), ("dcol", 2)])
                op("act", lambda e: e.activation(out=Ex.ap(), in_=bp.ap(), func=AF.Exp, scale=-1.0), reads=["bp"], writes=["Ex"])
                op("dve", lambda e: e.tensor_tensor(out=ktl.ap(), in0=ktl.ap(), in1=Ex.ap(), op=ALU.mult), reads=["ktl", "Ex"], writes=["ktl"])
                op("act", lambda e: e.copy(out=kt.ap(), in_=ktl.ap()), reads=["ktl"], writes=["kt"])
                op("dve", lambda e: e.tensor_tensor(out=kcf.ap().rearrange("p (c t) -> p c t", t=C), in0=ktl.ap().rearrange("p (c t) -> p c t", t=C),
                                                    in1=dcol.ap()[:, 0, :].unsqueeze(2).to_broadcast([128, NCH, C]), op=ALU.mult),
                   reads=["ktl", ("dcol", 0)], writes=["kcf"])
                op("act", lambda e: e.activation(out=Ex.ap(), in_=bp.ap(), func=AF.Exp), reads=["bp"], writes=["Ex"])

                def q_consume(tb, ps, pk):
                    op("dve", lambda e: e.tensor_tensor(out=bp.ap()[:, tsl(tb)], in0=Ex.ap()[:, tsl(tb)], in1=ps.ap(), op=ALU.mult),
                       reads=["Ex", pk, "bp"], writes=[("bpq", tb)])
                proj_fm(l, cq, 128, rings, q_consume)
                allq = [("bpq", tb) for tb in range(NTB)]
                op("act", lambda e: e.copy(out=qt.ap(), in_=bp.ap()), reads=allq, writes=["qt"])
                op("dve", lambda e: e.tensor_tensor(out=qh.ap().rearrange("p (c t) -> p c t", t=C), in0=bp.ap().rearrange("p (c t) -> p c t", t=C),
                                                    in1=dcol.ap()[:, 1, :].unsqueeze(2).to_broadcast([128, NCH, C]), op=ALU.mult),
                   reads=allq + [("dcol", 1)], writes=["qh"])

                def g_consume(tb, ps, pk):
                    op("act", lambda e: e.activation(out=gs.ap()[:, tsl(tb)], in_=ps.ap(), func=AF.Silu), reads=[pk], writes=[("gs", tb)])
                proj_fm(l, cg, 128, rings, g_consume)
                wt, wk = wload(P["w_in"][l][:, ci:ci + 128], KC, 128, stg_r, w_r)
                for c4 in range(NCH // 4):
                    ps, pk = psr.next()
                    for j in range(4):
                        c = c4 * 4 + j
                        for kc in range(KC):
                            op("pe", lambda e: e.matmul(ps.ap()[0:C, j * 128:(j + 1) * 128], lhsT=uT.ap()[:, kc, 2 + c * C: 2 + (c + 1) * C],
                                                        rhs=wt.ap()[:, kc, :], start=(kc == 0), stop=(kc == KC - 1)),
                               reads=[wk] + ukeys(kc, (c * C) // TB), writes=[pk])
                    cast(vtok.ap()[:, c4 * 4:(c4 + 1) * 4, :], ps.ap()[0:C, :].rearrange("p (a b) -> p a b", a=4), [pk], [("vtok", c4)], eng="act")
                def pre(c):
                    cs = slice(c * C, (c + 1) * C)
                    ps_s, pk_s = psr.next()
                    op("pe", lambda e: e.matmul(ps_s.ap()[0:C, 0:C], lhsT=kt.ap()[:, cs], rhs=qt.ap()[:, cs], start=True, stop=True),
                       reads=["kt", "qt"], writes=[pk_s])
                    pt, ptk = pt_r.next()
                    op("dve", lambda e: e.tensor_tensor(out=pt.ap(), in0=ps_s.ap()[0:C, 0:C], in1=tri.ap()[0:C, 0, 0:C], op=ALU.mult),
                       reads=[pk_s, "tri"], writes=[ptk])
                    ps_t, pk_t = psr.next()
                    op("pe", lambda e: e.transpose(out=ps_t.ap()[0:C, 0:128], in_=kcf.ap()[:, cs], identity=ident.ap()),
                       reads=["kcf", "ident"], writes=[pk_t])
                    kct, kctk = kct_r.next()
                    op("act", lambda e: e.copy(out=kct.ap(), in_=ps_t.ap()[0:C, 0:128]), reads=[pk_t], writes=[kctk])
                    return pt, ptk, kct, kctk
                stb = None
                nxt_pre = pre(0)
                for c in range(NCH):
                    cs = slice(c * C, (c + 1) * C)
                    pt, ptk, kct, kctk = nxt_pre
                    if c + 1 < NCH:
                        nxt_pre = pre(c + 1)
                    if c < NCH - 1:
                        ps_d, pk_d = psacc.next()
                        op("pe", lambda e: e.matmul(ps_d.ap()[:, 0:128], lhsT=kct.ap(), rhs=vtok.ap()[:, c, :], start=True, stop=True),
                           reads=[kctk, ("vtok", c // 4)], writes=[pk_d])
                    ps_y, pk_y = psr.next()
                    if c > 0:
                        op("pe", lambda e: e.matmul(ps_y.ap()[:, 0:C], lhsT=stb[0].ap(), rhs=qh.ap()[:, cs], start=True, stop=False),
                           reads=[stb[1], "qh"], writes=[pk_y])
                    op("pe", lambda e: e.matmul(ps_y.ap()[:, 0:C], lhsT=vtok.ap()[:, c, :], rhs=pt.ap(), start=(c == 0), stop=True),
                       reads=[("vtok", c // 4), ptk], writes=[pk_y])
                    op("act", lambda e: e.copy(out=ysb.ap()[:, cs], in_=ps_y.ap()[:, 0:C]), reads=[pk_y], writes=[("ysb", c // 8)])
                    if c < NCH - 1:
                        if c == 0:
                            op("dve", lambda e: e.tensor_copy(out=st.ap(), in_=ps_d.ap()[:, 0:128]), reads=[pk_d], writes=["st"])
                        else:
                            op("dve", lambda e: e.scalar_tensor_tensor(out=st.ap(), in0=st.ap(), scalar=dcol.ap()[:, 2, c:c + 1], in1=ps_d.ap()[:, 0:128],
                                                                       op0=ALU.mult, op1=ALU.add), reads=["st", pk_d, ("dcol", 2)], writes=["st"])
                        stb = stb_r.next()
                        op("act", lambda e: e.copy(out=stb[0].ap(), in_=st.ap()), reads=["st"], writes=[stb[1]])
                for tb in range(NTB):
                    sq, sk = sq_r.next()
                    yk = [("ysb", tb * 2), ("ysb", tb * 2 + 1)]
                    op("act", lambda e: e.activation(out=sq.ap(), in_=ysb.ap()[:, tsl(tb)], func=AF.Square), reads=yk, writes=[sk])
                    ps, pk = psr.next()
                    op("pe", lambda e: e.matmul(ps.ap(), lhsT=ones.ap(), rhs=sq.ap(), start=True, stop=True), reads=[sk, "ones"], writes=[pk])
                    sd, sdk = nt_r.next()
                    op("act", lambda e: e.activation(out=sd.ap(), in_=ps.ap(), func=AF.Ln, bias=epsc.ap(), scale=1.0 / 128), reads=[pk, "epsc"], writes=[sdk])
                    op("act", lambda e: e.activation(out=sd.ap(), in_=sd.ap(), func=AF.Exp, scale=-0.5), reads=[sdk], writes=[sdk])
                    op("dve", lambda e: e.scalar_tensor_tensor(out=sd.ap(), in0=ysb.ap()[:, tsl(tb)], scalar=gng.ap()[:, h:h + 1], in1=sd.ap(),
                                                               op0=ALU.mult, op1=ALU.mult), reads=yk + [sdk, "gng"], writes=[sdk])
                    op("dve", lambda e: e.tensor_tensor(out=yo.ap()[:, tsl(tb)], in0=sd.ap(), in1=gs.ap()[:, tsl(tb)], op=ALU.mult),
                       reads=[sdk, ("gs", tb)], writes=[("yo", tb)])
                k.dma(yT_dram[1, :, h, :], yo.ap(), reads=[("yo", tb) for tb in range(NTB)], writes=[("yT_dram", 1, h)])


    def mixer_dsa(l):
        NQ = S // 128
        with Scope(k) as sc:
            qT = sc.sb("qT", [128, 4, S], BF16)
            kTz = sc.sb("kTz", [128, 4, S], BF16)
            qiT = sc.sb("qiT", [128, 2, S], BF16)
            kiTz = sc.sb("kiTz", [128, 2, S], BF16)
            vtok = sc.sb("cvtok", [128, 2, NQ, 128], BF16)
            wi = sc.sb("wi", [128, NQ, 4], F32)
            wabs = sc.sb("wabs", [128, NQ, 4], F32)
            wsg = sc.sb("wsg", [128, NQ, 4], F32)
            yo = sc.sb("cyo", [128, 4, S], BF16)
            onesb = sc.sb("onesb", [128, 128], BF16)
            rt_r = sc.ring("rt", [128, TB], F32, 3)
            op("pool", lambda e: e.memset(onesb.ap(), 1.0), writes=["onesb"])
            with Scope(k) as s2:
                stg_r = s2.ring("cstg", [128, KC, 128], F32, 2)
                w_r = s2.ring("cw", [128, KC, 128], BF16, 2)
                wsw_r = s2.ring("cwsw", [128, KC, 128], BF16, 2)
                rope = s2.sb("rope", [128, 2, S], F32)
                k.dma(rope.ap(), c_rope.rearrange("a p s -> p a s"), writes=["rope"])

                def wload_cols(col_list, swap):
                    stg, sk = stg_r.next()
                    wt, wk = w_r.next()
                    o = 0
                    for (c0, n) in col_list:
                        if c0 is None:
                            op("pool", lambda e: e.memset(stg.ap()[:, :, o:o + n], 0.0), writes=[(sk, o), sk])
                        else:
                            k.dma(stg.ap()[:, :, o:o + n], P["w_in"][l][:, c0:c0 + n].rearrange("(c p) n -> p c n", p=128), writes=[(sk, o), sk])
                        o += n
                    rk = [(sk, oo) for oo in np.cumsum([0] + [n for _, n in col_list[:-1]]).tolist()] + [sk]
                    cast(wt.ap()[:, :, 0:o], stg.ap()[:, :, 0:o], rk, [wk])
                    if not swap:
                        return wt, wk, None, None
                    ws, wsk = wsw_r.next()
                    for b0 in range(0, o, 64):
                        cast(ws.ap()[:, :, b0:b0 + 32], stg.ap()[:, :, b0 + 32:b0 + 64], rk, [(wsk, b0), wsk])
                        cast(ws.ap()[:, :, b0 + 32:b0 + 64], stg.ap()[:, :, b0:b0 + 32], rk, [(wsk, b0 + 32), wsk])
                    return wt, wk, ws, [(wsk, b0) for b0 in range(0, o, 32)] + [wsk]

                def rope_proj(col_list, dst_fn, dkey):
                    wt, wk, ws, wsk = wload_cols(col_list, True)
                    for tb in range(NTB):
                        p1, pk1 = psr.next()
                        p2, pk2 = psr.next()
                        for kc in range(KC):
                            op("pe", lambda e: e.matmul(p1.ap(), lhsT=wt.ap()[:, kc, :], rhs=uT.ap()[:, kc, usl(tb)], start=(kc == 0), stop=(kc == KC - 1)),
                               reads=[wk] + ukeys(kc, tb), writes=[pk1])
                        for kc in range(KC):
                            op("pe", lambda e: e.matmul(p2.ap(), lhsT=ws.ap()[:, kc, :], rhs=uT.ap()[:, kc, usl(tb)], start=(kc == 0), stop=(kc == KC - 1)),
                               reads=wsk + ukeys(kc, tb), writes=[pk2])
                        t1, tk1 = rt_r.next()
                        t2, tk2 = rt_r.next()
                        op("dve", lambda e: e.tensor_tensor(out=t1.ap(), in0=rope.ap()[:, 0, tsl(tb)], in1=p1.ap(), op=ALU.mult), reads=["rope", pk1], writes=[tk1])
                        op("dve", lambda e: e.tensor_tensor(out=t2.ap(), in0=rope.ap()[:, 1, tsl(tb)], in1=p2.ap(), op=ALU.mult), reads=["rope", pk2], writes=[tk2])
                        op("pool", lambda e: e.tensor_tensor(out=dst_fn(tb), in0=t1.ap(), in1=t2.ap(), op=ALU.add), reads=[tk1, tk2], writes=[(dkey, tb)])

                co = C_OFF
                for ch in range(4):
                    rope_proj([(co + ch * 128, 128)], lambda tb: qT.ap()[:, ch, tsl(tb)], ("qT", ch))
                for c in range(2):
                    rope_proj([(co + 512 + c * 64, 64), (None, 64)], lambda tb: kTz.ap()[:, c * 2, tsl(tb)], ("kTz", c * 2))
                    rope_proj([(None, 64), (co + 512 + c * 64, 64)], lambda tb: kTz.ap()[:, c * 2 + 1, tsl(tb)], ("kTz", c * 2 + 1))
                for ch in range(2):
                    rope_proj([(co + 768 + ch * 128, 128)], lambda tb: qiT.ap()[:, ch, tsl(tb)], ("qiT", ch))
                rope_proj([(co + 1024, 64), (None, 64)], lambda tb: kiTz.ap()[:, 0, tsl(tb)], ("kiTz", 0))
                rope_proj([(None, 64), (co + 1024, 64)], lambda tb: kiTz.ap()[:, 1, tsl(tb)], ("kiTz", 1))
                for c in range(2):
                    wt, wk, _, _ = wload_cols([(co + 640 + c * 64, 64), (co + 640 + c * 64, 64)], False)
                    for s4 in range(NQ // 4):
                        ps, pk = psr.next()
                        for j in range(4):
                            sb = s4 * 4 + j
                            for kc in range(KC):
                                op("pe", lambda e: e.matmul(ps.ap()[:, j * 128:(j + 1) * 128], lhsT=uT.ap()[:, kc, 2 + sb * 128: 2 + (sb + 1) * 128],
                                                            rhs=wt.ap()[:, kc, :], start=(kc == 0), stop=(kc == KC - 1)),
                                   reads=[wk] + ukeys(kc, sb // 4), writes=[pk])
                        cast(vtok.ap()[:, c, s4 * 4:(s4 + 1) * 4, :], ps.ap().rearrange("p (a b) -> p a b", a=4), [pk], [("cvtok", c, s4)], eng="act")
                wt, wk, _, _ = wload_cols([(co + 1088, 4)], False)
                ps, pk = psr.next()
                for qb in range(NQ):
                    for kc in range(KC):
                        op("pe", lambda e: e.matmul(ps.ap()[:, qb * 4:(qb + 1) * 4], lhsT=uT.ap()[:, kc, 2 + qb * 128: 2 + (qb + 1) * 128],
                                                    rhs=wt.ap()[:, kc, 0:4], start=(kc == 0), stop=(kc == KC - 1)),
                           reads=[wk] + ukeys(kc, qb // 4), writes=[pk])
                op("dve", lambda e: e.tensor_copy(out=wi.ap(), in_=ps.ap()[:, 0:NQ * 4].rearrange("p (a b) -> p a b", b=4)), reads=[pk], writes=["wi"])
                op("act", lambda e: e.activation(out=wabs.ap(), in_=wi.ap(), func=AF.Abs), reads=["wi"], writes=["wabs"])
                op("act", lambda e: e.activation(out=wsg.ap(), in_=wi.ap(), func=AF.Sign), reads=["wi"], writes=["wsg"])
            NCHN = 4
            score2 = [sc.sb("score%d" % i, [128, S], F32) for i in range(NCHN)]
            Mf2 = [sc.sb("Mf%d" % i, [128, S], BF16) for i in range(NCHN)]
            MT2 = [sc.sb("MT%d" % i, [128, NQ, 128], BF16) for i in range(NCHN)]
            m82 = [sc.sb("m8_%d" % i, [128, 8], F32) for i in range(NCHN)]
            e_r = sc.ring("cE", [128, 4, 128], BF16, 3)
            p_r = sc.ring("cP", [128, 4, 128], BF16, 3)
            rec_r = sc.ring("crec", [128, TB], F32, 2)
            SENT = -3e30

            def scores(qb, i):
                score = score2[i]
                ncol = (qb + 1) * 128
                qs = slice(qb * 128, (qb + 1) * 128)
                for g0 in range(0, ncol, TB):
                    gn = min(TB, ncol - g0)
                    for hi in range(4):
                        ps, pk = psr.next()
                        op("pe", lambda e: e.matmul(ps.ap()[:, 0:gn], lhsT=qiT.ap()[:, hi // 2, qs], rhs=kiTz.ap()[:, hi % 2, g0:g0 + gn], start=True, stop=True),
                           reads=[(("qiT", hi // 2), qb // 4)] + [(("kiTz", hi % 2), tb) for tb in range(g0 // TB, (g0 + gn - 1) // TB + 1)], writes=[pk])
                        r, rk = rt_r.next()
                        op("act", lambda e: e.activation(out=r.ap()[:, 0:gn], in_=ps.ap()[:, 0:gn], func=AF.Relu, scale=wabs.ap()[:, qb, hi:hi + 1]),
                           reads=[pk, "wabs"], writes=[rk])
                        if hi == 0:
                            op("dve", lambda e: e.tensor_scalar(out=score.ap()[:, g0:g0 + gn], in0=r.ap()[:, 0:gn], scalar1=wsg.ap()[:, qb, hi:hi + 1],
                                                                scalar2=None, op0=ALU.mult), reads=[rk, "wsg"], writes=[("score", i, g0)])
                        else:
                            op("dve", lambda e: e.scalar_tensor_tensor(out=score.ap()[:, g0:g0 + gn], in0=r.ap()[:, 0:gn], scalar=wsg.ap()[:, qb, hi:hi + 1],
                                                                       in1=score.ap()[:, g0:g0 + gn], op0=ALU.mult, op1=ALU.add),
                               reads=[rk, "wsg", ("score", i, g0)], writes=[("score", i, g0)])
                sk_all = [("score", i, g0) for g0 in range(0, S, TB)]
                op("dve", lambda e: e.tensor_tensor(out=score.ap()[:, qs], in0=score.ap()[:, qs], in1=tri.ap()[:, 2, :], op=ALU.mult), reads=sk_all + ["tri"], writes=sk_all)
                op("dve", lambda e: e.tensor_tensor(out=score.ap()[:, qs], in0=score.ap()[:, qs], in1=tri.ap()[:, 3, :], op=ALU.add), reads=sk_all + ["tri"], writes=sk_all)

            NIT = 28
            W0 = 8192.0
            bs_S = [sc.sb("bsS%d" % i, [128, 1], F32) for i in range(NCHN)]
            bs_inc = [sc.sb("bsI%d" % i, [128, 1], F32) for i in range(NCHN)]
            bs_mid = [sc.sb("bsM%d" % i, [128, 1], F32) for i in range(NCHN)]
            junkA = sc.sb("junkA", [128, S], BF16)
            junkD = sc.sb("junkD", [128, S], BF16)

            def topk_gen(blocks):
                sk_all = lambda i: [("score", i, g0) for g0 in range(0, S, TB)]
                act = [(i, qb, (qb + 1) * 128) for i, qb in enumerate(blocks) if (qb + 1) * 128 > 256]
                on_dve = lambda i: (i == NCHN - 1)
                for i, qb, ncol in act:
                    op("dve", lambda e: e.memset(bs_mid[i].ap(), 0.0), writes=[("bsM", i)])
                w = W0
                for it in range(NIT if act else 0):
                    for i, qb, ncol in act:
                        if on_dve(i):
                            op("dve", lambda e: e.tensor_scalar(out=junkD.ap()[:, 0:ncol], in0=score2[i].ap()[:, 0:ncol], scalar1=bs_mid[i].ap(), scalar2=None,
                                                                op0=ALU.is_ge, op1=ALU.add, accum_out=bs_S[i].ap()),
                               reads=sk_all(i) + [("bsM", i)], writes=[("bsS", i)])
                        else:
                            op("act", lambda e: e.activation(out=junkA.ap()[:, 0:ncol], in_=score2[i].ap()[:, 0:ncol], func=AF.Sign, bias=bs_mid[i].ap(), scale=1.0,
                                                             accum_out=bs_S[i].ap()),
                               reads=sk_all(i) + [("bsM", i)], writes=[("bsS", i)])
                    for i, qb, ncol in act:
                        kthr = 256.0 if on_dve(i) else float(512 - ncol)
                        op("dve", lambda e: e.tensor_scalar(out=bs_inc[i].ap(), in0=bs_S[i].ap(), scalar1=kthr, scalar2=w / 2, op0=ALU.is_ge, op1=ALU.mult),
                           reads=[("bsS", i)], writes=[("bsI", i)])
                        if on_dve(i):
                            op("dve", lambda e: e.scalar_tensor_tensor(out=bs_mid[i].ap(), in0=bs_mid[i].ap(), scalar=-w / 4, in1=bs_inc[i].ap(), op0=ALU.add, op1=ALU.add),
                               reads=[("bsM", i), ("bsI", i)], writes=[("bsM", i)])
                        else:
                            op("dve", lambda e: e.scalar_tensor_tensor(out=bs_mid[i].ap(), in0=bs_mid[i].ap(), scalar=w / 4, in1=bs_inc[i].ap(), op0=ALU.add, op1=ALU.subtract),
                               reads=[("bsM", i), ("bsI", i)], writes=[("bsM", i)])
                    w = w / 2
                    yield
                for i, qb in enumerate(blocks):
                    ncol = (qb + 1) * 128
                    if ncol > 256:
                        if on_dve(i):
                            op("dve", lambda e: e.tensor_scalar(out=bs_inc[i].ap(), in0=bs_mid[i].ap(), scalar1=-w / 2, scalar2=None, op0=ALU.add), reads=[("bsM", i)], writes=[("bsI", i)])
                        else:
                            op("dve", lambda e: e.tensor_scalar(out=bs_inc[i].ap(), in0=bs_mid[i].ap(), scalar1=-1.0, scalar2=-w / 2, op0=ALU.mult, op1=ALU.add), reads=[("bsM", i)], writes=[("bsI", i)])
                        op("dve", lambda e: e.tensor_scalar(out=Mf2[i].ap()[:, 0:ncol], in0=score2[i].ap()[:, 0:ncol], scalar1=bs_inc[i].ap(), scalar2=None, op0=ALU.is_ge),
                           reads=sk_all(i) + [("bsI", i)], writes=[("Mf", i)])
                    else:
                        op("dve", lambda e: e.tensor_scalar(out=Mf2[i].ap()[:, 0:ncol], in0=score2[i].ap()[:, 0:ncol], scalar1=-1e29, scalar2=None, op0=ALU.is_ge),
                           reads=sk_all(i), writes=[("Mf", i)])
                yield

            def maskT(qb, i):
                for s4 in range(0, qb + 1, 4):
                    nb = min(4, qb + 1 - s4)
                    ps, pk = psr.next()
                    for j in range(nb):
                        sb = s4 + j
                        op("pe", lambda e: e.transpose(out=ps.ap().bitcast(BF16)[:, j * 128:(j + 1) * 128], in_=Mf2[i].ap()[:, sb * 128:(sb + 1) * 128], identity=identb.ap()),
                           reads=[("Mf", i), "identb"], writes=[pk])
                    op("act", lambda e: e.copy(out=MT2[i].ap()[:, s4:s4 + nb, :], in_=ps.ap().bitcast(BF16)[:, 0:nb * 128].rearrange("p (a b) -> p a b", b=128)),
                       reads=[pk], writes=[("MT", i, s4)])

            def attn_gen(qb, i):
                qs = slice(qb * 128, (qb + 1) * 128)
                MT = MT2[i]
                for c in range(2):
                    ps_o, pk_o = psacc.next()
                    ps_d, pk_d = psacc.next()

                    def front(sb):
                        ps, pk = psr.next()
                        for j in range(4):
                            hq = c * 4 + j
                            op("pe", lambda e: e.matmul(ps.ap()[:, j * 128:(j + 1) * 128], lhsT=kTz.ap()[:, c * 2 + hq % 2, sb * 128:(sb + 1) * 128],
                                                        rhs=qT.ap()[:, hq // 2, qs], start=True, stop=True),
                               reads=[(("kTz", c * 2 + hq % 2), sb // 4), (("qT", hq // 2), qb // 4)], writes=[pk])
                        E, Ek = e_r.next()
                        op("act", lambda e: e.activation(out=E.ap(), in_=ps.ap().rearrange("p (a b) -> p a b", a=4), func=AF.Exp, scale=0.125), reads=[pk], writes=[Ek])
                        Pm, Pk = p_r.next()
                        me = "pool" if (sb % 3) else "dve"
                        op(me, lambda e: e.tensor_tensor(out=Pm.ap(), in0=E.ap(), in1=MT.ap()[:, sb:sb + 1, :].to_broadcast([128, 4, 128]), op=ALU.mult),
                           reads=[Ek, ("MT", i, (sb // 4) * 4)], writes=[Pk])
                        return Pm, Pk
                    nxt_p = front(0)
                    for sb in range(qb + 1):
                        Pm, Pk = nxt_p
                        if sb + 1 <= qb:
                            nxt_p = front(sb + 1)
                        op("pe", lambda e: e.matmul(ps_o.ap(), lhsT=vtok.ap()[:, c, sb, :], rhs=Pm.ap().rearrange("p a b -> p (a b)"), start=(sb == 0), stop=(sb == qb)),
                           reads=[("cvtok", c, sb // 4), Pk], writes=[pk_o])
                        op("pe", lambda e: e.matmul(ps_d.ap(), lhsT=onesb.ap(), rhs=Pm.ap().rearrange("p a b -> p (a b)"), start=(sb == 0), stop=(sb == qb)),
                           reads=["onesb", Pk], writes=[pk_d])
                        yield
                    rec, reck = rec_r.next()
                    op("dve", lambda e: e.reciprocal(out=rec.ap(), in_=ps_d.ap()), reads=[pk_d], writes=[reck])
                    for j in range(4):
                        hq = c * 4 + j
                        hb = (hq % 2) * 64
                        op("dve", lambda e: e.tensor_tensor(out=yo.ap()[hb:hb + 64, hq // 2, qs], in0=rec.ap()[hb:hb + 64, j * 128:(j + 1) * 128],
                                                            in1=ps_o.ap()[hb:hb + 64, j * 128:(j + 1) * 128], op=ALU.mult),
                           reads=[reck, pk_o], writes=[("cyo", hq // 2, hq % 2)])

            def run(g):
                for _ in g:
                    pass

            def zipgens(gt, ga_list, n_t, n_a):
                ga = (x for g in ga_list for x in g)
                per = max(1, -(-n_a // max(1, n_t)))
                done_a = False
                for _ in gt:
                    for _j in range(per):
                        if next(ga, "END") == "END":
                            done_a = True
                            break
                if not done_a:
                    for _ in ga:
                        pass

            grps = [tuple(range(g * NCHN, (g + 1) * NCHN)) for g in range(NQ // NCHN)]
            for i_, qb_ in enumerate(grps[0]):
                scores(qb_, i_)
            run(topk_gen(grps[0]))
            for i_, qb_ in enumerate(grps[0]):
                maskT(qb_, i_)
            for p in range(len(grps)):
                cur = grps[p]
                if p + 1 < len(grps):
                    nxt = grps[p + 1]
                    for i_, qb_ in enumerate(nxt):
                        scores(qb_, i_)
                    n_a = sum(2 * (q_ + 1) for q_ in cur)
                    zipgens(topk_gen(nxt), [attn_gen(q_, i_) for i_, q_ in enumerate(cur)], NIT + 1, n_a)
                    for i_, qb_ in enumerate(nxt):
                        maskT(qb_, i_)
                else:
                    for i_, q_ in enumerate(cur):
                        run(attn_gen(q_, i_))
            for ch in range(4):
                k.dma(yT_dram[2, :, ch, :], yo.ap()[:, ch, :], reads=[("cyo", ch, 0), ("cyo", ch, 1)], writes=[("yT_dram", 2, ch)])

    def mixer_rwkv(l):
        C = 128; NCH = S // C
        with Scope(k) as sc:
            stg_r = sc.ring("astg", [128, KC, 128], F32, 1)
            stm_r = sc.ring("astm", [128, KC, 128], F32, 1)
            w1_r = sc.ring("aw1", [128, KC, 128], BF16, 1)
            wm_r = sc.ring("awm", [128, KC, 128], BF16, 1)
            mub_r = sc.ring("amub", [128, 128], F32, 2)
            lrb = sc.sb("lrb", [128, 3, 512], BF16)
            with Scope(k) as s2:
                lrs = s2.sb("lrs", [128, 3, 512], F32)
                k.dma(lrs.ap()[0:64, 0, :], P["a_w_up"][l], writes=["lrs0"]); k.dma(lrs.ap()[0:64, 1, :], P["a_a_up"][l], writes=["lrs1"])
                k.dma(lrs.ap()[:, 2, :], P["a_g_up"][l], writes=["lrs2"])
                op("dve", lambda e: e.tensor_copy(out=lrb.ap()[0:64, 0:2, :], in_=lrs.ap()[0:64, 0:2, :]), reads=["lrs0", "lrs1"], writes=["lrb01"])
                op("dve", lambda e: e.tensor_copy(out=lrb.ap()[:, 2, :], in_=lrs.ap()[:, 2, :]), reads=["lrs2"], writes=["lrb2"])

            def proj_shift(c0, ncol, consume, token_major=None):
                stg, sk = stg_r.next(); stm, smk = stm_r.next(); w1, w1k = w1_r.next(); wm, wmk = wm_r.next(); mub, mk = mub_r.next()
                k.dma(stg.ap()[:, :, 0:ncol], P["w_in"][l][:, c0:c0 + ncol].rearrange("(c p) n -> p c n", p=128), writes=[sk])
                k.dma(mub.ap()[:, 0:ncol], P["a_mu"][l][c0 - A_OFF:c0 - A_OFF + ncol].partition_broadcast(128), writes=[mk])
                op("dve", lambda e: e.tensor_tensor(out=stm.ap()[:, :, 0:ncol], in0=stg.ap()[:, :, 0:ncol],
                                                    in1=mub.ap()[:, 0:ncol].unsqueeze(1).to_broadcast([128, KC, ncol]), op=ALU.mult),
                   reads=[sk, mk], writes=[smk])
                op("act", lambda e: e.copy(out=wm.ap()[:, :, 0:ncol], in_=stm.ap()[:, :, 0:ncol]), reads=[smk], writes=[wmk])
                op("dve", lambda e: e.tensor_tensor(out=w1.ap()[:, :, 0:ncol], in0=stg.ap()[:, :, 0:ncol], in1=stm.ap()[:, :, 0:ncol], op=ALU.subtract),
                   reads=[sk, smk], writes=[w1k])
                if token_major is not None:
                    token_major(w1, w1k, wm, wmk)
                    return
                for tb in range(NTB):
                    ps, pk = psr.next()
                    for kc in range(KC):
                        op("pe", lambda e: e.matmul(ps.ap()[0:ncol, :], lhsT=w1.ap()[:, kc, 0:ncol], rhs=uT.ap()[:, kc, usl(tb)], start=(kc == 0), stop=False),
                           reads=[w1k] + ukeys(kc, tb), writes=[pk])
                    for kc in range(KC):
                        op("pe", lambda e: e.matmul(ps.ap()[0:ncol, :], lhsT=wm.ap()[:, kc, 0:ncol], rhs=uT.ap()[:, kc, usl(tb, 1)], start=False, stop=(kc == KC - 1)),
                           reads=[wmk] + ukeys(kc, tb) + (ukeys(kc, tb - 1) if tb else []) + ["uTpad"], writes=[pk])
                    consume(tb, ps, pk)

            tw = sc.sb("tw", [64, S], BF16); adT = sc.sb("adT", [64, S], BF16); sgT = sc.sb("sgT", [128, S], BF16)
            proj_shift(A_OFF + 1536, 64, lambda tb, ps, pk: op("act", lambda e: e.activation(out=tw.ap()[:, tsl(tb)], in_=ps.ap()[0:64, :], func=AF.Tanh), reads=[pk], writes=[("tw", tb)]))
            proj_shift(A_OFF + 1600, 64, lambda tb, ps, pk: op("act", lambda e: e.copy(out=adT.ap()[:, tsl(tb)], in_=ps.ap()[0:64, :]), reads=[pk], writes=[("adT", tb)]))
            proj_shift(A_OFF + 1664, 128, lambda tb, ps, pk: op("act", lambda e: e.activation(out=sgT.ap()[:, tsl(tb)], in_=ps.ap(), func=AF.Sigmoid), reads=[pk], writes=[("sgT", tb)]))
            pc = sc.sb("pc", [128, 8, 4], F32)
            for i_, nm in enumerate(["a_w0", "a_a0", "a_k_k", "a_k_a", "a_r_k", "a_gn_g", "a_gn_b"]):
                k.dma(pc.ap()[:, i_, :], P[nm][l].rearrange("(h p) -> p h", p=128), writes=[("pc", i_)], allow_slow_non_contiguous=True)
            op("dve", lambda e: e.tensor_scalar(out=pc.ap()[:, 7, :], in0=pc.ap()[:, 3, :], scalar1=-1.0, scalar2=1.0, op0=ALU.mult, op1=ALU.add), reads=[("pc", 3)], writes=[("pc", 7)])
            pcall = [("pc", i_) for i_ in range(8)]
            gnc = sc.sb("gnc", [128, 1], F32)
            op("pool", lambda e: e.memset(gnc.ap(), 64e-5), writes=["gnc"])
            F = {n_: sc.sb("af_" + n_, [128, S], F32) for n_ in ["r", "k", "v", "a", "L", "lw", "t1", "t2"]}
            F["g"] = sc.sb("af_g", [128, S], BF16)
            Bh = {n_: [sc.sb("ab_%s%d" % (n_, h), [128, S], BF16) for h in range(2)] for n_ in ["bt", "kt", "bc", "kc"]}
            Bh["rh"] = [sc.sb("ab_rh", [128, S], BF16)] * 2
            Bh["ah"] = [sc.sb("ab_ah", [128, S], BF16)] * 2
            AR = sc.sb("AR", [128, NCH, 2, C], BF16)
            dcol = sc.sb("adcol", [128, 3, NCH], F32)
            vtok = sc.sb("avtok", [128, NCH, 128], BF16)
            vpad = [sc.sb("avpad%d" % h, [128, NCH, 128], BF16) for h in range(2)]
            H = [sc.sb("aH%d" % h, [128, 128], F32) for h in range(2)]
            Hb_r = [sc.ring("aHb%d" % h, [128, 128], BF16, 2) for h in range(2)]
            upad = [sc.ring("aupad%d" % h, [128, 128], BF16, 2) for h in range(2)]
            mt_r = sc.ring("amt", [128, 128], BF16, 8)
            kp_r = sc.ring("akp", [128, 128], BF16, 12)
            wu_r = sc.ring("awu", [128, 128], BF16, 2)
            tk_r = sc.ring("atk", [128, 128], BF16, 6)
            ya = F["t1"]
            yo = Bh["rh"][0]
            sq_r = sc.ring("asq", [128, TB], F32, 2)
            hm = lambda h: tri.ap()[:, 6, h:h + 1]
            for h in range(2):
                for r_ in upad[h].t:
                    op("pool", lambda e: e.memset(r_.ap(), 0.0), writes=[("upad0", h, r_.name)])
            for hp in range(4):
                co = A_OFF + hp * 128
                prm = lambda i_: pc.ap()[:, i_, hp:hp + 1]
                for nm, cc in (("r", co), ("k", co + 512), ("v", co + 1024)):
                    proj_shift(cc, 128, lambda tb, ps, pk: op("act", lambda e: e.copy(out=F[nm].ap()[:, tsl(tb)], in_=ps.ap()), reads=[pk], writes=[(nm, tb)]))
                allk = lambda nm: [(nm, tb) for tb in range(NTB)]

                def vtm(w1, w1k, wm, wmk):
                    for c4 in range(NCH // 4):
                        ps, pk = psr.next()
                        for j in range(4):
                            c = c4 * 4 + j
                            for kc in range(KC):
                                op("pe", lambda e: e.matmul(ps.ap()[:, j * 128:(j + 1) * 128], lhsT=uT.ap()[:, kc, 2 + c * C:2 + (c + 1) * C], rhs=w1.ap()[:, kc, :], start=(kc == 0), stop=False),
                                   reads=[w1k] + ukeys(kc, c // 4), writes=[pk])
                            for kc in range(KC):
                                op("pe", lambda e: e.matmul(ps.ap()[:, j * 128:(j + 1) * 128], lhsT=uT.ap()[:, kc, 1 + c * C:1 + (c + 1) * C], rhs=wm.ap()[:, kc, :], start=False, stop=(kc == KC - 1)),
                                   reads=[wmk] + ukeys(kc, c // 4) + (ukeys(kc, c // 4 - 1) if c >= 4 else []) + ["uTpad"], writes=[pk])
                        cast(vtok.ap()[:, c4 * 4:(c4 + 1) * 4, :], ps.ap().rearrange("p (a b) -> p a b", a=4), [pk], [("avtok", c4)], eng="act")
                proj_shift(co + 1024, 128, None, token_major=vtm)
                vk = [("avtok", c4) for c4 in range(NCH // 4)]
                for h in range(2):
                    op("pool", lambda e: e.tensor_tensor(out=vpad[h].ap(), in0=vtok.ap(), in1=tri.ap()[:, 5, h * 64:h * 64 + 1].to_broadcast([128, NCH, 128]) if False else
                                                         tri.ap()[0:1, 5, :].partition_broadcast(128).unsqueeze(1).to_broadcast([128, NCH, 128]) if False else vtok.ap(), op=ALU.mult) if False else
                       e.memset(vpad[h].ap(), 0.0), reads=vk, writes=[("vpad", h)])
                    op("act", lambda e: e.copy(out=vpad[h].ap()[:, :, h * 64:(h + 1) * 64], in_=vtok.ap()[:, :, h * 64:(h + 1) * 64]), reads=vk + [("vpad", h)], writes=[("vpad", h)])
                for tb in range(NTB):
                    ps, pk = psr.next()
                    op("pe", lambda e: e.matmul(ps.ap(), lhsT=lrb.ap()[0:64, 0, hp * 128:(hp + 1) * 128], rhs=tw.ap()[:, tsl(tb)], start=True, stop=True), reads=["lrb01", ("tw", tb)], writes=[pk])
                    op("act", lambda e: e.activation(out=F["lw"].ap()[:, tsl(tb)], in_=ps.ap(), func=AF.Sigmoid, bias=prm(0)), reads=[pk] + pcall, writes=[("lw", tb)])
                    ps, pk = psr.next()
                    op("pe", lambda e: e.matmul(ps.ap(), lhsT=lrb.ap()[0:64, 1, hp * 128:(hp + 1) * 128], rhs=adT.ap()[:, tsl(tb)], start=True, stop=True), reads=["lrb01", ("adT", tb)], writes=[pk])
                    op("act", lambda e: e.activation(out=F["a"].ap()[:, tsl(tb)], in_=ps.ap(), func=AF.Sigmoid, bias=prm(1)), reads=[pk] + pcall, writes=[("a", tb)])
                    ps, pk = psr.next()
                    op("pe", lambda e: e.matmul(ps.ap(), lhsT=lrb.ap()[:, 2, hp * 128:(hp + 1) * 128], rhs=sgT.ap()[:, tsl(tb)], start=True, stop=True), reads=["lrb2", ("sgT", tb)], writes=[pk])
                    op("act", lambda e: e.copy(out=F["g"].ap()[:, tsl(tb)], in_=ps.ap()), reads=[pk], writes=[("g", tb)])
                op("dve", lambda e: e.tensor_scalar(out=F["lw"].ap(), in0=F["lw"].ap(), scalar1=-0.6065306597126334, scalar2=None, op0=ALU.mult), reads=allk("lw"), writes=allk("lw"))
                op("dve", lambda e: e.tensor_tensor_scan(out=F["L"].ap(), data0=ones.ap()[:, 0:1].to_broadcast([128, S]), data1=F["lw"].ap(), initial=0.0, op0=ALU.mult, op1=ALU.add), reads=allk("lw") + ["ones"], writes=["L"])
                L3 = F["L"].ap().rearrange("p (c t) -> p c t", t=C)
                op("dve", lambda e: e.tensor_tensor(out=dcol.ap()[:, 0, :], in0=L3[:, :, C - 1], in1=L3[:, :, C // 2 - 1], op=ALU.subtract), reads=["L"], writes=[("dc", 0)])
                op("dve", lambda e: e.tensor_copy(out=dcol.ap()[:, 1, 0:1], in_=L3[:, 0, C // 2 - 1:C // 2]), reads=["L"], writes=[("dc", 1)])
                op("dve", lambda e: e.tensor_tensor(out=dcol.ap()[:, 1, 1:NCH], in0=L3[:, 1:NCH, C // 2 - 1], in1=L3[:, 0:NCH - 1, C - 1], op=ALU.subtract), reads=["L", ("dc", 1)], writes=[("dc", 1)])
                op("dve", lambda e: e.tensor_tensor(out=dcol.ap()[:, 2, :], in0=dcol.ap()[:, 0, :], in1=dcol.ap()[:, 1, :], op=ALU.add), reads=[("dc", 0), ("dc", 1)], writes=[("dc", 2)])
                dck = [("dc", 0), ("dc", 1), ("dc", 2)]
                op("act", lambda e: e.activation(out=dcol.ap(), in_=dcol.ap(), func=AF.Exp), reads=dck, writes=dck)
                t1, t2 = F["t1"], F["t2"]
                op("dve", lambda e: e.tensor_scalar(out=t1.ap(), in0=F["k"].ap(), scalar1=prm(2), scalar2=None, op0=ALU.mult), reads=allk("k") + pcall, writes=["t1"])
                for tb in range(NTB):
                    sq, sqk = sq_r.next()
                    op("act", lambda e: e.activation(out=sq.ap(), in_=t1.ap()[:, tsl(tb)], func=AF.Square), reads=["t1"], writes=[sqk])
                    ps, pk = psr.next()
                    op("pe", lambda e: e.matmul(ps.ap(), lhsT=tri.ap()[:, 5, :], rhs=sq.ap(), start=True, stop=True), reads=[sqk, "tri"], writes=[pk])
                    op("dve", lambda e: e.tensor_scalar(out=sq.ap(), in0=ps.ap(), scalar1=1e-24, scalar2=None, op0=ALU.max), reads=[pk], writes=[sqk])
                    op("act", lambda e: e.activation(out=sq.ap(), in_=sq.ap(), func=AF.Sqrt), reads=[sqk], writes=[sqk])
                    op("dve", lambda e: e.reciprocal(out=sq.ap(), in_=sq.ap()), reads=[sqk], writes=[sqk])
                    op("dve", lambda e: e.tensor_tensor(out=t1.ap()[:, tsl(tb)], in0=t1.ap()[:, tsl(tb)], in1=sq.ap(), op=ALU.mult), reads=["t1", sqk], writes=["t1"])
                op("dve", lambda e: e.tensor_scalar(out=t2.ap(), in0=F["a"].ap(), scalar1=prm(3), scalar2=prm(7), op0=ALU.mult, op1=ALU.add), reads=allk("a") + pcall, writes=["t2"])
                op("dve", lambda e: e.tensor_tensor(out=F["k"].ap(), in0=F["k"].ap(), in1=t2.ap(), op=ALU.mult), reads=allk("k") + ["t2"], writes=allk("k"))
                op("dve", lambda e: e.tensor_tensor(out=F["a"].ap(), in0=F["a"].ap(), in1=t1.ap(), op=ALU.mult), reads=allk("a") + ["t1"], writes=allk("a"))
                op("dve", lambda e: e.scalar_tensor_tensor(out=t2.ap(), in0=F["r"].ap(), scalar=prm(4), in1=F["k"].ap(), op0=ALU.mult, op1=ALU.mult), reads=allk("r") + allk("k") + pcall, writes=["t2"])
                for tb in range(NTB):
                    sq, sqk = sq_r.next()
                    op("act", lambda e: e.copy(out=sq.ap(), in_=t2.ap()[:, tsl(tb)]), reads=["t2"], writes=[sqk])
                    ps, pk = psr.next()
                    op("pe", lambda e: e.matmul(ps.ap(), lhsT=tri.ap()[:, 5, :], rhs=sq.ap(), start=True, stop=True), reads=[sqk, "tri"], writes=[pk])
                    op("dve", lambda e: e.tensor_tensor(out=F["v"].ap()[:, tsl(tb)], in0=F["v"].ap()[:, tsl(tb)], in1=ps.ap(), op=ALU.mult), reads=[("v", tb), pk], writes=[("v", tb)])
                m_b = L3[:, :, C // 2 - 1:C // 2].to_broadcast([128, NCH, C])
                v3 = lambda t_: t_.ap().rearrange("p (c t) -> p c t", t=C)
                op("dve", lambda e: e.tensor_tensor(out=v3(t2), in0=L3, in1=m_b, op=ALU.subtract), reads=["L"], writes=["t2"])
                op("dve", lambda e: e.tensor_tensor(out=F["lw"].ap(), in0=t2.ap(), in1=F["lw"].ap(), op=ALU.subtract), reads=["t2"] + allk("lw"), writes=allk("lw"))
                op("act", lambda e: e.activation(out=F["lw"].ap(), in_=F["lw"].ap(), func=AF.Exp), reads=allk("lw"), writes=allk("lw"))
                op("act", lambda e: e.activation(out=F["L"].ap(), in_=t2.ap(), func=AF.Exp, scale=-1.0), reads=["t2"], writes=["L"])
                op("act", lambda e: e.activation(out=t2.ap(), in_=t2.ap(), func=AF.Exp), reads=["t2"], writes=["t2"])
                op("dve", lambda e: e.tensor_tensor(out=F["r"].ap(), in0=F["r"].ap(), in1=t2.ap(), op=ALU.mult), reads=allk("r") + ["t2"], writes=allk("r"))
                op("act", lambda e: e.copy(out=AR.ap()[:, :, 1, :], in_=v3(F["r"])), reads=allk("r"), writes=["AR1"])
                dB = dcol.ap()[:, 1, :].unsqueeze(2).to_broadcast([128, NCH, C]); dA = dcol.ap()[:, 0, :].unsqueeze(2).to_broadcast([128, NCH, C])
                op("dve", lambda e: e.tensor_tensor(out=v3(t2), in0=v3(F["r"]), in1=dB, op=ALU.mult), reads=allk("r") + dck, writes=["t2"])
                op("act", lambda e: e.copy(out=Bh["rh"][0].ap(), in_=t2.ap()), reads=["t2"], writes=[("rh", 0), ("rh", 1)] + [("yo", tb) for tb in range(NTB)])
                op("dve", lambda e: e.scalar_tensor_tensor(out=F["lw"].ap(), in0=t1.ap(), scalar=-1.0, in1=F["lw"].ap(), op0=ALU.mult, op1=ALU.mult), reads=["t1"] + allk("lw"), writes=allk("lw"))
                op("act", lambda e: e.copy(out=AR.ap()[:, :, 0, :], in_=v3(F["lw"])), reads=allk("lw"), writes=["AR0"])
                op("dve", lambda e: e.tensor_tensor(out=v3(t2), in0=v3(F["lw"]), in1=dB, op=ALU.mult), reads=allk("lw") + dck + [("rh", 0), ("rh", 1)], writes=["t2"])
                op("act", lambda e: e.copy(out=Bh["ah"][0].ap(), in_=t2.ap()), reads=["t2"], writes=[("ah", 0), ("ah", 1)])
                for src, nt, ncn in ((F["a"], "bt", "bc"), (F["k"], "kt", "kc")):
                    sk_ = allk("a") if nt == "bt" else allk("k")
                    op("dve", lambda e: e.tensor_tensor(out=src.ap(), in0=src.ap(), in1=F["L"].ap(), op=ALU.mult), reads=sk_ + ["L"], writes=sk_)
                    op("dve", lambda e: e.tensor_tensor(out=v3(t1) if nt == "kt" else v3(t2), in0=v3(src), in1=dA, op=ALU.mult), reads=sk_ + dck + [("ah", 0), ("ah", 1)], writes=["t1" if nt == "kt" else "t2"])
                    tt_ = t1 if nt == "kt" else t2
                    for h in range(2):
                        op("act", lambda e: e.activation(out=Bh[nt][h].ap(), in_=src.ap(), func=AF.Copy, scale=hm(h)), reads=sk_ + ["tri"], writes=[(nt, h)])
                        op("act", lambda e: e.activation(out=Bh[ncn][h].ap(), in_=tt_.ap(), func=AF.Copy, scale=hm(h)), reads=["t1" if nt == "kt" else "t2", "tri"], writes=[(ncn, h)])
                k.phase_barrier()
                pool_t = []
                for nm_ in ("r", "k", "a", "L", "lw", "t2"):
                    vb = F[nm_].ap().bitcast(BF16)
                    pool_t += [vb[:, j * 128:(j + 1) * 128] for j in range(32)]
                RES = {}
                it_ = iter(pool_t)
                for c_ in range(NCH):
                    for h_ in range(2):
                        for nm_ in ("TT", "rb", "ak", "rk", "bcT", "kcT"):
                            RES[(nm_, h_, c_)] = next(it_)
                vt_b = vtok.ap().rearrange("p a b -> p (a b)")
                tmp_t = [t_.ap() for t_ in mt_r.t] + [t_.ap() for t_ in tk_r.t] + [t_.ap() for t_ in kp_r.t] + [vt_b[:, j * 128:(j + 1) * 128] for j in range(16)]
                GCH = 8
                assert len(tmp_t) >= 5 * GCH, len(tmp_t)

                def A_gen(group):
                    st = {}
                    for gi_, (h, c) in enumerate(group):
                        cs = slice(c * C, (c + 1) * C)
                        T5 = tmp_t[gi_ * 5:(gi_ + 1) * 5]
                        tk_ = lambda j: ("tmp", gi_, j)
                        ps, pk = psr.next()
                        op("pe", lambda e: e.matmul(ps.ap()[:, 0:256], lhsT=Bh["bt"][h].ap()[:, cs], rhs=AR.ap()[:, c].rearrange("p a b -> p (a b)"), start=True, stop=True), reads=[("bt", h), "AR0", "AR1"], writes=[pk])
                        op("dve", lambda e: e.tensor_tensor(out=T5[1], in0=ps.ap()[:, 0:128], in1=tri.ap()[:, 1, :], op=ALU.mult), reads=[pk, "tri"], writes=[tk_(1)])
                        op("dve", lambda e: e.tensor_tensor(out=RES[("rb", h, c)], in0=ps.ap()[:, 128:256], in1=tri.ap()[:, 0, :], op=ALU.mult), reads=[pk, "tri"], writes=[("res", "rb", h, c)])
                        ps, pk = psr.next()
                        op("pe", lambda e: e.matmul(ps.ap()[:, 0:256], lhsT=Bh["kt"][h].ap()[:, cs], rhs=AR.ap()[:, c].rearrange("p a b -> p (a b)"), start=True, stop=True), reads=[("kt", h), "AR0", "AR1"], writes=[pk])
                        op("dve", lambda e: e.tensor_tensor(out=RES[("ak", h, c)], in0=ps.ap()[:, 0:128], in1=tri.ap()[:, 1, :], op=ALU.mult), reads=[pk, "tri"], writes=[("res", "ak", h, c)])
                        op("dve", lambda e: e.tensor_tensor(out=RES[("rk", h, c)], in0=ps.ap()[:, 128:256], in1=tri.ap()[:, 0, :], op=ALU.mult), reads=[pk, "tri"], writes=[("res", "rk", h, c)])
                        ps, pk = psr.next()
                        op("pe", lambda e: e.matmul(ps.ap()[:, 0:128], lhsT=AR.ap()[:, c, 0, :], rhs=Bh["bt"][h].ap()[:, cs], start=True, stop=True), reads=["AR0", ("bt", h)], writes=[pk])
                        op("dve", lambda e: e.tensor_tensor(out=T5[0], in0=ps.ap()[:, 0:128], in1=tri.ap()[:, 4, :], op=ALU.mult), reads=[pk, "tri"], writes=[tk_(0)])
                        op("dve", lambda e: e.tensor_tensor(out=T5[4], in0=T5[1], in1=identb.ap(), op=ALU.add), reads=[tk_(1), "identb"], writes=[tk_(4)])
                        pst, pkt = psr.next()
                        op("pe", lambda e: e.transpose(out=pst.ap().bitcast(BF16)[:, 0:128], in_=Bh["bc"][h].ap()[:, cs], identity=identb.ap()), reads=[("bc", h), "identb"], writes=[pkt])
                        op("pe", lambda e: e.transpose(out=pst.ap().bitcast(BF16)[:, 128:256], in_=Bh["kc"][h].ap()[:, cs], identity=identb.ap()), reads=[("kc", h), "identb"], writes=[pkt])
                        op("act", lambda e: e.copy(out=RES[("bcT", h, c)], in_=pst.ap().bitcast(BF16)[:, 0:128]), reads=[pkt], writes=[("res", "bcT", h, c)])
                        op("act", lambda e: e.copy(out=RES[("kcT", h, c)], in_=pst.ap().bitcast(BF16)[:, 128:256]), reads=[pkt], writes=[("res", "kcT", h, c)])
                        st[gi_] = [0, 1, 2, 3]
                    yield
                    for lev in range(6):
                        for gi_, (h, c) in enumerate(group):
                            T5 = tmp_t[gi_ * 5:(gi_ + 1) * 5]
                            tk_ = lambda j: ("tmp", gi_, j)
                            iX, iXT, iXn, iXTn = st[gi_]
                            psx, pkx = psr.next()
                            op("pe", lambda e: e.matmul(psx.ap()[:, 0:128], lhsT=T5[iXT], rhs=T5[iX], start=True, stop=True), reads=[tk_(iXT), tk_(iX)], writes=[pkx])
                            if lev < 5:
                                op("pe", lambda e: e.matmul(psx.ap()[:, 128:256], lhsT=T5[iX], rhs=T5[iXT], start=True, stop=True), reads=[tk_(iXT), tk_(iX)], writes=[pkx])
                            op("act", lambda e: e.copy(out=T5[iXn], in_=psx.ap()[:, 0:128]), reads=[pkx], writes=[tk_(iXn)])
                            if lev < 5:
                                op("act", lambda e: e.copy(out=T5[iXTn], in_=psx.ap()[:, 128:256]), reads=[pkx], writes=[tk_(iXTn)])
                            st[gi_] = [iXn, iXTn, iX, iXT]
                        yield
                        for gi_, (h, c) in enumerate(group):
                            T5 = tmp_t[gi_ * 5:(gi_ + 1) * 5]
                            tk_ = lambda j: ("tmp", gi_, j)
                            iX = st[gi_][0]
                            psq, pkq = psr.next()
                            op("pe", lambda e: e.matmul(psq.ap()[:, 0:128], lhsT=T5[iX], rhs=T5[4], start=True, stop=True), reads=[tk_(iX), tk_(4)], writes=[pkq])
                            if lev < 5:
                                op("dve", lambda e: e.tensor_tensor(out=T5[4], in0=T5[4], in1=psq.ap()[:, 0:128], op=ALU.add), reads=[tk_(4), pkq], writes=[tk_(4)])
                            else:
                                op("dve", lambda e: e.tensor_tensor(out=RES[("TT", h, c)], in0=T5[4], in1=psq.ap()[:, 0:128], op=ALU.add), reads=[tk_(4), pkq], writes=[("res", "TT", h, c)])
                        yield

                Hb = [None, None]

                def B_gen(chunks):
                    for c in chunks:
                        cs = slice(c * C, (c + 1) * C)
                        rk_ = lambda nm_, h: ("res", nm_, h, c)
                        wus = []
                        for h in range(2):
                            psw, pkw = psr.next()
                            if c > 0:
                                op("pe", lambda e: e.matmul(psw.ap()[:, 0:128], lhsT=Bh["ah"][h].ap()[:, cs], rhs=Hb[h][0].ap(), start=True, stop=False), reads=[("ah", h), Hb[h][1]], writes=[pkw])
                            op("pe", lambda e: e.matmul(psw.ap()[:, 0:128], lhsT=RES[("ak", h, c)], rhs=vpad[h].ap()[:, c, :], start=(c == 0), stop=True), reads=[rk_("ak", h), ("vpad", h)], writes=[pkw])
                            wu, wuk = wu_r.next()
                            op("act", lambda e: e.copy(out=wu.ap(), in_=psw.ap()[:, 0:128]), reads=[pkw], writes=[wuk])
                            wus.append((wu, wuk))
                        yield
                        Ups = []
                        for h in range(2):
                            psu, pku = psr.next()
                            op("pe", lambda e: e.matmul(psu.ap()[:, 0:128], lhsT=RES[("TT", h, c)], rhs=wus[h][0].ap(), start=True, stop=True), reads=[rk_("TT", h), wus[h][1]], writes=[pku])
                            up_, upk = upad[h].next()
                            op("dve", lambda e: e.tensor_copy(out=up_.ap(), in_=psu.ap()[:, 0:128]), reads=[pku, ("upad0", h, up_.name)], writes=[upk])
                            Ups.append((up_, upk))
                        yield
                        if c < NCH - 1:
                            for h in range(2):
                                psh, pkh = psr.next()
                                op("pe", lambda e: e.matmul(psh.ap()[:, 0:128], lhsT=RES[("bcT", h, c)], rhs=Ups[h][0].ap(), start=True, stop=False), reads=[rk_("bcT", h), Ups[h][1]], writes=[pkh])
                                op("pe", lambda e: e.matmul(psh.ap()[:, 0:128], lhsT=RES[("kcT", h, c)], rhs=vpad[h].ap()[:, c, :], start=False, stop=True), reads=[rk_("kcT", h), ("vpad", h)], writes=[pkh])
                                if c == 0:
                                    op("dve", lambda e: e.tensor_copy(out=H[h].ap(), in_=psh.ap()[:, 0:128]), reads=[pkh], writes=[("H", h)])
                                else:
                                    op("dve", lambda e: e.scalar_tensor_tensor(out=H[h].ap(), in0=H[h].ap(), scalar=dcol.ap()[:, 2, c:c + 1], in1=psh.ap()[:, 0:128], op0=ALU.mult, op1=ALU.add),
                                       reads=[("H", h), pkh] + dck, writes=[("H", h)])
                        psy, pky = psr.next()
                        first = True
                        for h in range(2):
                            if c > 0:
                                op("pe", lambda e: e.matmul(psy.ap()[:, 0:128], lhsT=Hb[h][0].ap(), rhs=Bh["rh"][h].ap()[:, cs], start=first, stop=False), reads=[Hb[h][1], ("rh", h)], writes=[pky]); first = False
                            op("pe", lambda e: e.matmul(psy.ap()[:, 0:128], lhsT=Ups[h][0].ap(), rhs=RES[("rb", h, c)], start=first, stop=False), reads=[Ups[h][1], rk_("rb", h)], writes=[pky]); first = False
                            op("pe", lambda e: e.matmul(psy.ap()[:, 0:128], lhsT=vpad[h].ap()[:, c, :], rhs=RES[("rk", h, c)], start=False, stop=(h == 1)), reads=[("vpad", h), rk_("rk", h)], writes=[pky])
                        op("act", lambda e: e.copy(out=ya.ap()[:, cs], in_=psy.ap()[:, 0:128]), reads=[pky], writes=[("ya", c // 4)])
                        if c < NCH - 1:
                            for h in range(2):
                                Hb[h] = Hb_r[h].next()
                                op("act", lambda e: e.copy(out=Hb[h][0].ap(), in_=H[h].ap()), reads=[("H", h)], writes=[Hb[h][1]])
                        yield

                def zip2(ga, gb):
                    da = db = False
                    while not (da and db):
                        if not da and next(ga, "END") == "END":
                            da = True
                        if not db and next(gb, "END") == "END":
                            db = True

                groups = [[(h, c) for c in range(g * 4, g * 4 + 4) for h in range(2)] for g in range(NCH // 4)]
                for _ in A_gen(groups[0]):
                    pass
                for g in range(len(groups)):
                    bg = B_gen(range(g * 4, g * 4 + 4))
                    if g + 1 < len(groups):
                        zip2(A_gen(groups[g + 1]), bg)
                    else:
                        for _ in bg:
                            pass
                k.phase_barrier()
                for tb in range(NTB):
                    yk = [("ya", tb)]
                    ps, pk = psr.next()
                    op("pe", lambda e: e.matmul(ps.ap(), lhsT=tri.ap()[:, 5, :], rhs=ya.ap()[:, tsl(tb)], start=True, stop=True), reads=yk + ["tri"], writes=[pk])
                    sq, sqk = sq_r.next()
                    op("dve", lambda e: e.scalar_tensor_tensor(out=sq.ap(), in0=ps.ap(), scalar=-1.0 / 64, in1=ya.ap()[:, tsl(tb)], op0=ALU.mult, op1=ALU.add), reads=[pk] + yk, writes=[sqk])
                    sq2, sq2k = sq_r.next()
                    op("act", lambda e: e.activation(out=sq2.ap(), in_=sq.ap(), func=AF.Square), reads=[sqk], writes=[sq2k])
                    ps2, pk2 = psr.next()
                    op("pe", lambda e: e.matmul(ps2.ap(), lhsT=tri.ap()[:, 5, :], rhs=sq2.ap(), start=True, stop=True), reads=[sq2k, "tri"], writes=[pk2])
                    op("act", lambda e: e.activation(out=sq2.ap(), in_=ps2.ap(), func=AF.Ln, bias=gnc.ap(), scale=1.0 / 64), reads=[pk2, "gnc"], writes=[sq2k])
                    op("act", lambda e: e.activation(out=sq2.ap(), in_=sq2.ap(), func=AF.Exp, scale=-0.5), reads=[sq2k], writes=[sq2k])
                    op("dve", lambda e: e.tensor_tensor(out=sq.ap(), in0=sq.ap(), in1=sq2.ap(), op=ALU.mult), reads=[sqk, sq2k], writes=[sqk])
                    op("dve", lambda e: e.tensor_scalar(out=sq.ap(), in0=sq.ap(), scalar1=prm(5), scalar2=prm(6), op0=ALU.mult, op1=ALU.add), reads=[sqk] + pcall, writes=[sqk])
                    op("dve", lambda e: e.tensor_tensor(out=sq.ap(), in0=sq.ap(), in1=F["v"].ap()[:, tsl(tb)], op=ALU.add), reads=[sqk, ("v", tb)], writes=[sqk])
                    op("dve", lambda e: e.tensor_tensor(out=yo.ap()[:, tsl(tb)], in0=sq.ap(), in1=F["g"].ap()[:, tsl(tb)], op=ALU.mult), reads=[sqk, ("g", tb)], writes=[("yo", tb)])
                k.dma(yT_dram[0, :, hp, :], yo.ap(), reads=[("yo", tb) for tb in range(NTB)], writes=[("yT_dram", 0, hp)])

    for l in range(DEPTH):
        with Scope(k) as sc:
            hT = sc.sb("hT", [128, KC, S], F32)
            sqr = sc.ring("sq", [128, TB], F32, 3)
            tmpr = sc.ring("nt", [128, TB], F32, 4)
            load_h(hT)
            rmsnorm(hT, 2 * l, u_dst, sqr, tmpr)
        if "yT_in0" not in dbg:
            mixer_rwkv(l)
        if "yT_in1" not in dbg:
            mixer_hgrn(l)
        if "yT_in2" not in dbg:
            mixer_dsa(l)
        with Scope(k) as sc:
            yT = sc.sb("yT", [128, 3, 4, S], BF16)
            mT = sc.sb("mT", [128, KC, S], BF16)
            stg_g = sc.ring("stg_g", [128, 3, KC, 128], F32, 2)
            wg_r = sc.ring("wg", [128, 3, KC, 128], BF16, 2)
            stg_b = sc.ring("stg_b", [128, 3, 4, 128], F32, 2)
            wb_r = sc.ring("wb", [128, 3, 4, 128], BF16, 2)
            sgr = sc.ring("sg", [128, TB], F32, 3)
            accr = sc.ring("acc", [128, TB], F32, 2)
            for n in range(3):
                k.dma(yT.ap()[:, n], yT_src[l][n], writes=[("yT", n)])
            for dc in range(KC):
                sg_t, sgk = stg_g.next(); wg, wgk = wg_r.next()
                sb_t, sbk = stg_b.next(); wb, wbk = wb_r.next()
                for n in range(3):
                    c0 = G_OFF + n * D + dc * 128
                    k.dma(sg_t.ap()[:, n], P["w_in"][l][:, c0:c0 + 128].rearrange("(c p) n -> p c n", p=128), writes=[(sgk, n)])
                    k.dma(sb_t.ap()[:, n], P["w_branch"][l, n][:, dc * 128:(dc + 1) * 128].rearrange("(c p) n -> p c n", p=128),
                          writes=[(sbk, n)])
                cast(wg.ap(), sg_t.ap(), [(sgk, n) for n in range(3)], [wgk])
                cast(wb.ap(), sb_t.ap(), [(sbk, n) for n in range(3)], [wbk])
                for tb in range(NTB):
                    acc, acck = accr.next()
                    for n in range(3):
                        pg, pgk = psr.next()
                        for kc in range(KC):
                            op("pe", lambda e: e.matmul(pg.ap(), lhsT=wg.ap()[:, n, kc, :], rhs=uT.ap()[:, kc, usl(tb)],
                                                        start=(kc == 0), stop=(kc == KC - 1)),
                               reads=[wgk] + ukeys(kc, tb), writes=[pgk])
                        pb, pbk = psr.next()
                        for c in range(4):
                            op("pe", lambda e: e.matmul(pb.ap(), lhsT=wb.ap()[:, n, c, :], rhs=yT.ap()[:, n, c, tsl(tb)],
                                                        start=(c == 0), stop=(c == 3)),
                               reads=[wbk, ("yT", n)], writes=[pbk])
                        sg, sgk2 = sgr.next()
                        op("act", lambda e: e.activation(out=sg.ap(), in_=pg.ap(), func=AF.Sigmoid), reads=[pgk], writes=[sgk2])
                        if n == 0:
                            op("dve", lambda e: e.tensor_tensor(out=acc.ap(), in0=sg.ap(), in1=pb.ap(), op=ALU.mult),
                               reads=[sgk2, pbk], writes=[acck])
                        else:
                            op("dve", lambda e: e.tensor_tensor(out=sg.ap(), in0=sg.ap(), in1=pb.ap(), op=ALU.mult),
                               reads=[sgk2, pbk], writes=[sgk2])
                            dstap = mT.ap()[:, dc, tsl(tb)] if n == 2 else acc.ap()
                            op("pool" if n == 1 else "dve", lambda e: e.tensor_tensor(out=dstap, in0=sg.ap(), in1=acc.ap(), op=ALU.add),
                               reads=[sgk2, acck], writes=[("mT", dc, tb)] if n == 2 else [acck])
            stg_o = sc.ring("stg_o", [128, KC, 128], F32, 2)
            wo_r = sc.ring("wo", [128, KC, 128], BF16, 2)
            hr = sc.ring("hr", [128, TB], F32, 3)
            for dc in range(KC):
                wo, wok = wload(P["w_o"][l][:, dc * 128:(dc + 1) * 128], KC, 128, stg_o, wo_r)
                for tb in range(NTB):
                    ht, htk = hr.next()
                    k.dma(ht.ap(), h_dram[:, dc, tsl(tb)], writes=[htk])
                    ps, pk = psr.next()
                    for c in range(KC):
                        op("pe", lambda e: e.matmul(ps.ap(), lhsT=wo.ap()[:, c, :], rhs=mT.ap()[:, c, tsl(tb)],
                                                    start=(c == 0), stop=(c == KC - 1)),
                           reads=[wok, ("mT", c, tb)], writes=[pk])
                    op("dve", lambda e: e.tensor_tensor(out=ht.ap(), in0=ht.ap(), in1=ps.ap(), op=ALU.add),
                       reads=[htk, pk], writes=[htk])
                    k.dma(h_dram[:, dc, tsl(tb)], ht.ap(), reads=[htk], writes=[("h_dram", dc, tb)])
        with Scope(k) as sc:
            hT = sc.sb("hT", [128, KC, S], F32)
            sc2 = Scope(k)
            sc_outer = sc; sc = sc2
            sqr = sc.ring("sq", [128, TB], F32, 2)
            tmpr = sc.ring("nt", [128, TB], F32, 3)
            load_h(hT)
            rmsnorm(hT, 2 * l + 1, u_dst, sqr, tmpr)
            cw = sc.sb("cw", [128, 3, 2 * NFF], F32)
            cb = sc.sb("cb", [128, 2 * NFF], F32)
            for j in range(3):
                k.dma(cw.ap()[:, j, :], P["conv_w"][l, j].rearrange("(c p) -> p c", p=128), writes=["cw"], allow_slow_non_contiguous=True)
            k.dma(cb.ap(), P["conv_b"][l].rearrange("(c p) -> p c", p=128), writes=["cb"], allow_slow_non_contiguous=True)
            G = 2
            stg_u = sc.ring("stg_u", [128, 2, KC, 128], F32, 2)
            wu_r = sc.ring("wu", [128, 2, KC, 128], BF16, 2)
            stg_d = sc.ring("stg_d", [128, G, D], F32, 1)
            wd_r = sc.ring("wd", [128, G, D], BF16, 2)
            upr = sc.ring("up", [128, 2, S + 2], F32, 1)
            cr = sc.ring("cc", [128, 2, TB], F32, 2)
            actr = sc.ring("actT", [128, G, S], BF16, 2)
            up, upk = upr.next()
            op("pool", lambda e: e.memset(up.ap()[:, :, 0:2], 0.0), writes=["uppad"])
            for g0 in range(0, NFF, G):
                actT, actk = actr.next()
                wd, wdk = wload(P["w_down"][l][g0 * 128:(g0 + G) * 128, :], G, D, stg_d, wd_r)
                for gi in range(G):
                    j = g0 + gi
                    su, suk = stg_u.next(); wu, wuk = wu_r.next()
                    for hv in range(2):
                        c0 = hv * DFF + j * 128
                        k.dma(su.ap()[:, hv], P["w_up"][l][:, c0:c0 + 128].rearrange("(c p) n -> p c n", p=128), writes=[(suk, hv)])
                    cast(wu.ap(), su.ap(), [(suk, 0), (suk, 1)], [wuk])
                    for tb in range(NTB):
                        cc, cck = cr.next()
                        for hv in range(2):
                            ps, pk = psr.next()
                            for kc in range(KC):
                                op("pe", lambda e: e.matmul(ps.ap(), lhsT=wu.ap()[:, hv, kc, :], rhs=uT.ap()[:, kc, usl(tb)],
                                                            start=(kc == 0), stop=(kc == KC - 1)),
                                   reads=[wuk] + ukeys(kc, tb), writes=[pk])
                            ch = hv * NFF + j
                            op("act", lambda e: e.copy(out=up.ap()[:, hv, usl(tb)], in_=ps.ap()), reads=[pk, "uppad"], writes=[("up", hv, tb)])
                            op("dve", lambda e: e.tensor_scalar(out=cc.ap()[:, hv, :], in0=up.ap()[:, hv, usl(tb)],
                                                                scalar1=cw.ap()[:, 2, ch:ch + 1], scalar2=cb.ap()[:, ch:ch + 1],
                                                                op0=ALU.mult, op1=ALU.add),
                               reads=[("up", hv, tb), "cw", "cb"], writes=[(cck, hv)])
                            for sh in (1, 2):
                                rk = [("up", hv, tb)] + ([("up", hv, tb - 1)] if tb > 0 else [])
                                op("dve", lambda e: e.scalar_tensor_tensor(out=cc.ap()[:, hv, :], in0=up.ap()[:, hv, usl(tb, sh)],
                                                                           scalar=cw.ap()[:, 2 - sh, ch:ch + 1], in1=cc.ap()[:, hv, :],
                                                                           op0=ALU.mult, op1=ALU.add),
                                   reads=rk + ["cw", (cck, hv)], writes=[(cck, hv)])
                        op("act", lambda e: e.activation(out=cc.ap()[:, 0, :], in_=cc.ap()[:, 0, :], func=AF.Silu),
                           reads=[(cck, 0)], writes=[(cck, 0)])
                        op("pool", lambda e: e.tensor_tensor(out=actT.ap()[:, gi, tsl(tb)], in0=cc.ap()[:, 0, :], in1=cc.ap()[:, 1, :], op=ALU.mult),
                           reads=[(cck, 0), (cck, 1)], writes=[(actk, gi, tb)])
                for dc in range(KC):
                    for tb in range(NTB):
                        ps, pk = psr.next()
                        for gi in range(G):
                            op("pe", lambda e: e.matmul(ps.ap(), lhsT=wd.ap()[:, gi, dc * 128:(dc + 1) * 128], rhs=actT.ap()[:, gi, tsl(tb)],
                                                        start=(gi == 0), stop=(gi == G - 1)),
                               reads=[wdk, (actk, gi, tb)], writes=[pk])
                        op("dve", lambda e: e.tensor_tensor(out=hT.ap()[:, dc, tsl(tb)], in0=hT.ap()[:, dc, tsl(tb)], in1=ps.ap(), op=ALU.add),
                           reads=hkeys(dc, tb) + [pk], writes=hkeys(dc, tb))
            sc2.__exit__(None, None, None)
            sc = Scope(k)
            sqr = sc.ring("sq", [128, TB], F32, 2)
            tmpr = sc.ring("nt", [128, TB], F32, 3)
            if l < DEPTH - 1:
                store_h(hT)
            else:
                fo = sc.ring("fo", [128, KC, TB], F32, 1)
                ot = sc.ring("ot", [128, D], F32, 2)
                fT, fk = fo.next()
                cur = [0]

                def f_dst(kc, tb):
                    return fT.ap()[:, kc, :], [("fT", kc)]
                for tb in range(NTB):
                    ps, pk = psr.next()
                    for kc in range(KC):
                        sq, sk = sqr.next()
                        op("act", lambda e: e.activation(out=sq.ap(), in_=hT.ap()[:, kc, tsl(tb)], func=AF.Square),
                           reads=hkeys(kc, tb), writes=[sk])
                        op("pe", lambda e: e.matmul(ps.ap(), lhsT=ones.ap(), rhs=sq.ap(), start=(kc == 0), stop=(kc == KC - 1)),
                           reads=[sk, "ones"], writes=[pk])
                    sd, sdk = tmpr.next()
                    op("act", lambda e: e.activation(out=sd.ap(), in_=ps.ap(), func=AF.Ln, bias=epsc.ap(), scale=1.0 / D),
                       reads=[pk, "epsc"], writes=[sdk])
                    rs, rsk = tmpr.next()
                    op("act", lambda e: e.activation(out=rs.ap(), in_=sd.ap(), func=AF.Exp, scale=-0.5), reads=[sdk], writes=[rsk])
                    for kc in range(KC):
                        op("dve", lambda e: e.scalar_tensor_tensor(out=fT.ap()[:, kc, :], in0=hT.ap()[:, kc, tsl(tb)],
                                                                   scalar=gvec.ap()[:, 2 * DEPTH, kc:kc + 1], in1=rs.ap(),
                                                                   op0=ALU.mult, op1=ALU.mult),
                           reads=hkeys(kc, tb) + [rsk, "gvec"], writes=[("fT", kc)])
                    for t4 in range(4):
                        o, okk = ot.next()
                        for half in range(2):
                            ps2, pk2 = psr.next()
                            for j in range(4):
                                kc = half * 4 + j
                                op("pe", lambda e: e.transpose(out=ps2.ap()[:, j * 128:(j + 1) * 128],
                                                               in_=fT.ap()[:, kc, t4 * 128:(t4 + 1) * 128], identity=ident.ap()),
                                   reads=[("fT", kc), "ident"], writes=[pk2])
                            cast(o.ap()[:, half * 512:(half + 1) * 512], ps2.ap(), [pk2], [(okk, half)], eng=("dve", "act")[half])
                        r0 = tb * TB + t4 * 128
                        k.dma(out[r0:r0 + 128, :], o.ap(), reads=[(okk, 0), (okk, 1)], writes=[("out", r0)])
            sc.__exit__(None, None, None)
    k.drain("sp")
    return k


PARAM_NAMES = ["norm_mix_g", "w_in", "a_mu", "a_w0", "a_w_up", "a_a0", "a_a_up", "a_g_up", "a_k_k", "a_k_a", "a_r_k",
               "a_gn_g", "a_gn_b", "b_lb_logits", "b_gn_g", "w_branch", "w_o", "norm_ffn_g", "w_up", "conv_w", "conv_b",
               "w_down", "norm_final_g"]


def make_consts():
    i = np.arange(128)
    low = (i[:, None] >= i[None, :])
    bd = ((i[:, None] // 64) == (i[None, :] // 64))
    hm = np.zeros((128, 128)); hm[:64, 0] = 1; hm[64:, 1] = 1
    tri = np.stack([(i[:, None] <= i[None, :]), (i[:, None] < i[None, :]), low, np.where(low, 0.0, -1e30),
                    (i[:, None] > i[None, :]), bd, hm]).astype(np.float32)
    inv_freq = (np.float32(10000.0) ** (-np.arange(32, dtype=np.float32) * np.float32(2.0 / 64))).astype(np.float32)
    ang = (np.arange(S, dtype=np.float32)[:, None] * inv_freq[None, :]).astype(np.float32)
    cos = np.cos(ang).astype(np.float32).T; sin = np.sin(ang).astype(np.float32).T
    cosF = np.concatenate([cos, cos, cos, cos], axis=0)
    sinF = np.concatenate([-sin, sin, -sin, sin], axis=0)
    return {"c_ident": np.eye(128, dtype=np.float32), "c_tri": tri, "c_rope": np.stack([cosF, sinF]).astype(np.float32)}


def make_in_map(inputs, b, consts):
    m = {"x": np.ascontiguousarray(inputs["x"][b])}
    for n in PARAM_NAMES:
        a = np.ascontiguousarray(np.asarray(inputs[n], dtype=np.float32))
        if n == "a_r_k":
            a = a.reshape(DEPTH, 512)
        m[n] = a
    m.update(consts)
    return m


def kernel(**inputs):
    k = build()
    consts = make_consts()
    in_maps = [make_in_map(inputs, b, consts) for b in range(NB)]
    res = run_bass_kernel_spmd(k.nc, in_maps, core_ids=list(range(NB)))
    return np.stack([np.asarray(r["out"], dtype=np.float32) for r in res.results], axis=0)
```

```python
import numpy as np
import ml_dtypes
import concourse.bass as bass
import concourse.mybir as mybir
from concourse.bass_utils import run_bass_kernel_spmd

F32 = mybir.dt.float32
BF16 = mybir.dt.bfloat16
AF = mybir.ActivationFunctionType
ALU = mybir.AluOpType
AX = mybir.AxisListType

S = 2048; D = 1024; DEPTH = 2; NB = 8
TB = 512; NTB = S // TB; KC = D // 128
A_COLS = 1792; B_COLS = 2048; C_COLS = 1092; G_COLS = 3072; IN_COLS = 8004
A_OFF = 0; B_OFF = A_COLS; C_OFF = A_COLS + B_COLS; G_OFF = C_OFF + C_COLS
DFF = 2816; NFF = DFF // 128
NDMA = 24
NOSYNC_SAME = {"pe"}


class K:
    def __init__(self):
        self.nc = nc = bass.Bass("TRN2", target_bir_lowering=False)
        self.eng = {"pe": nc.tensor, "act": nc.scalar, "dve": nc.vector, "pool": nc.gpsimd, "sp": nc.sync}
        self.sem = {}
        self.cnt = {}
        for e in self.eng:
            self.sem[e] = nc.alloc_semaphore("s_" + e)
            self.cnt[e] = 0
        self.seen = {e: {} for e in self.eng}
        self.lastw = {}
        self.readers = {}
        self.dsem = [nc.alloc_semaphore("d%d" % i) for i in range(NDMA)]
        self.dval = [0] * NDMA
        self.di = 0
        self.nwait = 0
        self.ninst = 0

    def _wait(self, e, ev):
        name, sem, val, src = ev
        if src == e and e in NOSYNC_SAME:
            return
        if self.seen[e].get(name, 0) >= val:
            return
        self.eng[e].wait_ge(sem, val)
        self.nwait += 1
        self.seen[e][name] = val

    def _deps(self, e, reads, writes):
        for r in reads:
            ev = self.lastw.get(r)
            if ev is not None:
                self._wait(e, ev)
        for w in writes:
            ev = self.lastw.get(w)
            if ev is not None:
                self._wait(e, ev)
            for ev in self.readers.get(w, ()):
                self._wait(e, ev)

    def _record(self, ev, reads, writes):
        for w in writes:
            self.lastw[w] = ev
            self.readers[w] = []
        for r in reads:
            if r in writes:
                continue
            lst = self.readers.setdefault(r, [])
            lst[:] = [x for x in lst if x[0] != ev[0]]
            lst.append(ev)

    def op(self, e, fn, reads=(), writes=()):
        self._deps(e, reads, writes)
        ins = fn(self.eng[e])
        self.cnt[e] += 1
        ins.then_inc(self.sem[e], 1)
        ev = ("E" + e, self.sem[e], self.cnt[e], e)
        self.seen[e]["E" + e] = max(self.seen[e].get("E" + e, 0), 0)
        self._record(ev, reads, writes)
        self.ninst += 1
        return ins

    def dma(self, out, in_, reads=(), writes=(), q="sp", **kw):
        self._deps(q, reads, writes)
        j = self.di % NDMA
        self.di += 1
        if self.dval[j] > 0:
            self._wait(q, ("D%d" % j, self.dsem[j], self.dval[j], None))
        self.dval[j] += 16
        self.eng[q].dma_start(out=out, in_=in_, **kw).then_inc(self.dsem[j], 16)
        ev = ("D%d" % j, self.dsem[j], self.dval[j], None)
        self._record(ev, reads, writes)
        self.ninst += 1

    def drain(self, e="sp"):
        for e2 in self.eng:
            if e2 != e and self.cnt[e2] > 0:
                self._wait(e, ("E" + e2, self.sem[e2], self.cnt[e2], e2))
        for j in range(NDMA):
            if self.dval[j] > 0:
                self._wait(e, ("D%d" % j, self.dsem[j], self.dval[j], None))


    def phase_barrier(self):
        self.drain("sp")
        self.nc.all_engine_barrier()
        self.lastw.clear()
        self.readers.clear()


class Scope:
    def __init__(self, k):
        self.k = k
        self.stack = []

    def __enter__(self):
        return self

    _uid = [0]

    def sb(self, name, shape, dtype):
        Scope._uid[0] += 1
        g = self.k.nc.sbuf_tensor("%s_u%d" % (name, Scope._uid[0]), list(shape), dtype)
        t = g.__enter__()
        self.stack.append(g)
        return t

    def ring(self, name, shape, dtype, n):
        return Ring([self.sb("%s_%d" % (name, i), shape, dtype) for i in range(n)], name)

    def __exit__(self, *a):
        self.k.phase_barrier()
        for g in reversed(self.stack):
            g.__exit__(None, None, None)
        return False


class Ring:
    def __init__(self, tensors, name):
        self.t = tensors
        self.name = name
        self.i = 0

    def next(self):
        j = self.i % len(self.t)
        self.i += 1
        return self.t[j], (self.name, j)


def build(dbg=None):
    dbg = dbg or {}
    k = K()
    nc = k.nc
    op = k.op
    uid = [0]

    def din(name, shape, dt=F32):
        return nc.dram_tensor(name, list(shape), dt, kind="ExternalInput").ap()

    x = din("x", [S, D])
    P = {}
    for name, shape in [
        ("norm_mix_g", [DEPTH, D]), ("w_in", [DEPTH, D, IN_COLS]), ("a_mu", [DEPTH, A_COLS]), ("a_w0", [DEPTH, 512]),
        ("a_w_up", [DEPTH, 64, 512]), ("a_a0", [DEPTH, 512]), ("a_a_up", [DEPTH, 64, 512]), ("a_g_up", [DEPTH, 128, 512]),
        ("a_k_k", [DEPTH, 512]), ("a_k_a", [DEPTH, 512]), ("a_r_k", [DEPTH, 512]), ("a_gn_g", [DEPTH, 512]),
        ("a_gn_b", [DEPTH, 512]), ("b_lb_logits", [DEPTH, 512]), ("b_gn_g", [DEPTH, 512]),
        ("w_branch", [DEPTH, 3, 512, D]), ("w_o", [DEPTH, D, D]), ("norm_ffn_g", [DEPTH, D]),
        ("w_up", [DEPTH, D, 2 * DFF]), ("conv_w", [DEPTH, 3, 2 * DFF]), ("conv_b", [DEPTH, 2 * DFF]),
        ("w_down", [DEPTH, DFF, D]), ("norm_final_g", [D]),
    ]:
        P[name] = din(name, shape)
    c_ident = din("c_ident", [128, 128])
    out = nc.dram_tensor("out", [S, D], F32, kind="ExternalOutput").ap()
    h_dram = nc.dram_tensor("h_scr", [128, KC, S], F32, kind="Internal").ap()
    yT_dram = nc.dram_tensor("yT_scr", [3, 128, 4, S], BF16, kind="ExternalOutput" if dbg.get("dump_yT") else "Internal").ap()
    c_tri = din("c_tri", [7, 128, 128])
    c_rope = din("c_rope", [2, 128, S])
    yT_src = [[yT_dram[n] for n in range(3)] for l in range(DEPTH)]
    for n in range(3):
        if ("yT_in%d" % n) in dbg:
            t = din("yT_in%d" % n, [DEPTH, 128, 4, S], BF16)
            for l in range(DEPTH):
                yT_src[l][n] = t[l]

    ident = nc.alloc_sbuf_tensor("ident", [128, 128], F32)
    identb = nc.alloc_sbuf_tensor("identb", [128, 128], BF16)
    ones = nc.alloc_sbuf_tensor("ones", [128, 128], F32)
    epsc = nc.alloc_sbuf_tensor("epsc", [128, 1], F32)
    uT = nc.alloc_sbuf_tensor("uT", [128, KC, S + 2], BF16)
    gvec = nc.alloc_sbuf_tensor("gvec", [128, 2 * DEPTH + 1, KC], F32)
    psr = Ring([nc.alloc_psum_tensor("ps%d" % i, [128, 512], F32) for i in range(6)], "ps")
    psacc = Ring([nc.alloc_psum_tensor("psacc%d" % i, [128, 512], F32) for i in range(2)], "psacc")

    tri = nc.alloc_sbuf_tensor("tri", [128, 7, 128], F32)
    k.dma(tri.ap(), c_tri.rearrange("a p f -> p a f"), writes=["tri"])
    k.dma(ident.ap(), c_ident, writes=["ident"])
    op("dve", lambda e: e.tensor_copy(out=identb.ap(), in_=ident.ap()), reads=["ident"], writes=["identb"])
    op("pool", lambda e: e.memset(ones.ap(), 1.0), writes=["ones"])
    op("pool", lambda e: e.memset(epsc.ap(), 1e-6), writes=["epsc"])
    op("pool", lambda e: e.memset(uT.ap()[:, :, 0:2], 0.0), writes=["uTpad"])
    for l in range(DEPTH):
        k.dma(gvec.ap()[:, 2 * l, :], P["norm_mix_g"][l].rearrange("(c p) -> p c", p=128), writes=["gvec"],
              allow_slow_non_contiguous=True)
        k.dma(gvec.ap()[:, 2 * l + 1, :], P["norm_ffn_g"][l].rearrange("(c p) -> p c", p=128), writes=["gvec"],
              allow_slow_non_contiguous=True)
    k.dma(gvec.ap()[:, 2 * DEPTH, :], P["norm_final_g"].rearrange("(c p) -> p c", p=128), writes=["gvec"],
          allow_slow_non_contiguous=True)

    def hkeys(kc=None, tb=None):
        return [("hT", a, b) for a in (range(KC) if kc is None else [kc]) for b in (range(NTB) if tb is None else [tb])]

    def ukeys(kc=None, tb=None):
        return [("uT", a, b) for a in (range(KC) if kc is None else [kc]) for b in (range(NTB) if tb is None else [tb])]

    def tsl(tb):
        return slice(tb * TB, (tb + 1) * TB)

    def usl(tb, shift=0):
        return slice(2 + tb * TB - shift, 2 + (tb + 1) * TB - shift)

    def rmsnorm(hT, gi, dst_fn, sqr, tmpr):
        for tb in range(NTB):
            ps, pk = psr.next()
            for kc in range(KC):
                sq, sk = sqr.next()
                op("act", lambda e: e.activation(out=sq.ap(), in_=hT.ap()[:, kc, tsl(tb)], func=AF.Square),
                   reads=hkeys(kc, tb), writes=[sk])
                op("pe", lambda e: e.matmul(ps.ap(), lhsT=ones.ap(), rhs=sq.ap(), start=(kc == 0), stop=(kc == KC - 1)),
                   reads=[sk, "ones"], writes=[pk])
            sd, sdk = tmpr.next()
            op("act", lambda e: e.activation(out=sd.ap(), in_=ps.ap(), func=AF.Ln, bias=epsc.ap(), scale=1.0 / D),
               reads=[pk, "epsc"], writes=[sdk])
            rs, rsk = tmpr.next()
            op("act", lambda e: e.activation(out=rs.ap(), in_=sd.ap(), func=AF.Exp, scale=-0.5), reads=[sdk], writes=[rsk])
            for kc in range(KC):
                dst, dk = dst_fn(kc, tb)
                op("dve", lambda e: e.scalar_tensor_tensor(out=dst, in0=hT.ap()[:, kc, tsl(tb)],
                                                           scalar=gvec.ap()[:, gi, kc:kc + 1], in1=rs.ap(),
                                                           op0=ALU.mult, op1=ALU.mult),
                   reads=hkeys(kc, tb) + [rsk, "gvec"], writes=dk)

    def u_dst(kc, tb):
        return uT.ap()[:, kc, usl(tb)], ukeys(kc, tb)

    castsel = [0]

    def cast(dst, src, reads, writes, eng=None):
        ce = eng or "act"
        castsel[0] += 1
        if ce == "act":
            op("act", lambda e: e.copy(out=dst, in_=src), reads=reads, writes=writes)
        else:
            op(ce, lambda e: e.tensor_copy(out=dst, in_=src), reads=reads, writes=writes)

    def wload(src, nk, ncol, stg_ring, w_ring, q="sp"):
        stg, sk = stg_ring.next()
        wt, wk = w_ring.next()
        k.dma(stg.ap()[:, 0:nk, 0:ncol], src.rearrange("(c p) n -> p c n", p=128), writes=[sk], q=q)
        cast(wt.ap()[:, 0:nk, 0:ncol], stg.ap()[:, 0:nk, 0:ncol], [sk], [wk])
        return wt, wk

    def load_h(hT):
        for kc in range(KC):
            k.dma(hT.ap()[:, kc, :], h_dram[:, kc, :], reads=["h_dram"], writes=hkeys(kc))

    def store_h(hT):
        for kc in range(KC):
            k.dma(h_dram[:, kc, :], hT.ap()[:, kc, :], reads=hkeys(kc), writes=[("h_dram", kc)])

    with Scope(k) as sc:
        xs = sc.ring("xs", [128, D], F32, 2)
        xt = sc.ring("xt", [128, KC, 128], F32, 2)
        for tt in range(S // 128):
            a, ak = xs.next()
            b, bk = xt.next()
            k.dma(a.ap(), x[tt * 128:(tt + 1) * 128, :], writes=[ak])
            for half in range(2):
                ps, pk = psr.next()
                for j in range(4):
                    kc = half * 4 + j
                    op("pe", lambda e: e.transpose(out=ps.ap()[:, j * 128:(j + 1) * 128], in_=a.ap()[:, kc * 128:(kc + 1) * 128],
                                                   identity=ident.ap()), reads=[ak, "ident"], writes=[pk])
                cast(b.ap()[:, half * 4:half * 4 + 4, :], ps.ap().rearrange("p (a b) -> p a b", a=4), [pk], [(bk, half)],
                     eng=("dve", "act")[half])
            k.dma(h_dram[:, :, tt * 128:(tt + 1) * 128], b.ap(), reads=[(bk, 0), (bk, 1)], writes=[("h_dram", "x", tt)])


    def proj_fm(l, c0, ncol, sc_rings, consume, shift_w=None):
        stg_r, w_r = sc_rings
        wt, wk = wload(P["w_in"][l][:, c0:c0 + ncol], KC, ncol, stg_r, w_r)
        for tb in range(NTB):
            ps, pk = psr.next()
            for kc in range(KC):
                op("pe", lambda e: e.matmul(ps.ap()[0:ncol, :], lhsT=wt.ap()[:, kc, 0:ncol], rhs=uT.ap()[:, kc, usl(tb)],
                                            start=(kc == 0), stop=(kc == KC - 1)),
                   reads=[wk] + ukeys(kc, tb), writes=[pk])
            consume(tb, ps, pk)

    def mixer_hgrn(l):
        C = 64; NCH = S // C
        with Scope(k) as sc:
            stg_r = sc.ring("hstg", [128, KC, 128], F32, 2)
            w_r = sc.ring("hw", [128, KC, 128], BF16, 2)
            rings = (stg_r, w_r)
            lbt = sc.sb("lbt", [128, 2, 4], F32)
            lb = sc.sb("lb", [128, 4], F32)
            oml = sc.sb("oml", [128, 4], F32)
            noml = sc.sb("noml", [128, 4], F32)
            gng = sc.sb("gng", [128, 4], F32)
            onesS = sc.sb("onesS", [128, S], F32)
            Bc = sc.sb("Bc", [128, S], F32)
            sig = sc.sb("sig", [128, S], F32)
            bp = sc.sb("bp", [128, S], F32)
            Ex = sc.sb("Ex", [128, S], F32)
            ktl = sc.sb("ktl", [128, S], F32)
            qt = sc.sb("qt", [128, S], BF16)
            kt = sc.sb("kt", [128, S], BF16)
            qh = sc.sb("qh", [128, S], BF16)
            kcf = sc.sb("kcf", [128, S], F32)
            vtok = sc.sb("vtok", [64, NCH, 128], BF16)
            ysb = sc.sb("ysb", [128, S], F32)
            gs = sc.sb("gs", [128, S], F32)
            yo = sc.sb("yo", [128, S], BF16)
            dcol = sc.sb("dcol", [128, 4, NCH], F32)
            st = sc.sb("st", [128, 128], F32)
            stb_r = sc.ring("stb", [128, 128], BF16, 2)
            pt_r = sc.ring("pt", [64, 64], BF16, 4)
            kct_r = sc.ring("kct", [64, 128], BF16, 4)
            sq_r = sc.ring("hsq", [128, TB], F32, 2)
            nt_r = sc.ring("hnt", [128, TB], F32, 2)
            op("pool", lambda e: e.memset(onesS.ap(), 1.0), writes=["onesS"])
            k.dma(lbt.ap(), P["b_lb_logits"].rearrange("l (h p) -> p l h", p=128), writes=["lbt"], allow_slow_non_contiguous=True)
            k.dma(gng.ap(), P["b_gn_g"][l].rearrange("(h p) -> p h", p=128), writes=["gng"], allow_slow_non_contiguous=True)
            if l == 0:
                op("dve", lambda e: e.tensor_tensor(out=lb.ap(), in0=lbt.ap()[:, 0, :], in1=lbt.ap()[:, 0, :], op=ALU.subtract),
                   reads=["lbt"], writes=["lb"])
            else:
                op("dve", lambda e: e.tensor_tensor(out=lb.ap(), in0=lbt.ap()[:, 1, :], in1=lbt.ap()[:, 0, :], op=ALU.subtract),
                   reads=["lbt"], writes=["lb"])
                op("act", lambda e: e.activation(out=lb.ap(), in_=lb.ap(), func=AF.Sigmoid), reads=["lb"], writes=["lb"])
            op("dve", lambda e: e.tensor_scalar(out=oml.ap(), in0=lb.ap(), scalar1=-1.0, scalar2=1.0, op0=ALU.mult, op1=ALU.add),
               reads=["lb"], writes=["oml"])
            op("dve", lambda e: e.tensor_scalar(out=noml.ap(), in0=oml.ap(), scalar1=-1.0, scalar2=None, op0=ALU.mult),
               reads=["oml"], writes=["noml"])
            for h in range(4):
                cq = B_OFF + h * 128; cf = B_OFF + 512 + h * 128; ci = B_OFF + 1024 + h * 128; cg = B_OFF + 1536 + h * 128
                def f_consume(tb, ps, pk):
                    op("act", lambda e: e.activation(out=sig.ap()[:, tsl(tb)], in_=ps.ap(), func=AF.Sigmoid), reads=[pk], writes=[("sig", tb)])
                proj_fm(l, cf, 128, rings, f_consume)
                allsig = [("sig", tb) for tb in range(NTB)]
                op("dve", lambda e: e.tensor_scalar(out=ktl.ap(), in0=sig.ap(), scalar1=noml.ap()[:, h:h + 1], scalar2=oml.ap()[:, h:h + 1],
                                                    op0=ALU.mult, op1=ALU.add), reads=allsig + ["noml", "oml"], writes=["ktl"])
                op("dve", lambda e: e.tensor_scalar(out=sig.ap(), in0=sig.ap(), scalar1=oml.ap()[:, h:h + 1], scalar2=lb.ap()[:, h:h + 1],
                                                    op0=ALU.mult, op1=ALU.add), reads=allsig + ["oml", "lb"], writes=allsig)
                op("act", lambda e: e.activation(out=sig.ap(), in_=sig.ap(), func=AF.Ln), reads=allsig, writes=allsig)
                op("dve", lambda e: e.tensor_tensor_scan(out=Bc.ap(), data0=onesS.ap(), data1=sig.ap(), initial=0.0, op0=ALU.mult, op1=ALU.add),
                   reads=allsig + ["onesS"], writes=["Bc"])
                B3 = Bc.ap().rearrange("p (c t) -> p c t", t=C)
                op("dve", lambda e: e.tensor_tensor(out=bp.ap().rearrange("p (c t) -> p c t", t=C), in0=B3,
                                                    in1=B3[:, :, C // 2 - 1:C // 2].to_broadcast([128, NCH, C]), op=ALU.subtract),
                   reads=["Bc"], writes=["bp"] + [("bpq", tb) for tb in range(NTB)])
                op("dve", lambda e: e.tensor_tensor(out=dcol.ap()[:, 0, :], in0=B3[:, :, C - 1], in1=B3[:, :, C // 2 - 1], op=ALU.subtract),
                   reads=["Bc"], writes=[("dcol", 0)])
                op("dve", lambda e: e.tensor_copy(out=dcol.ap()[:, 1, 0:1], in_=B3[:, 0, C // 2 - 1:C // 2]), reads=["Bc"], writes=[("dcol", 1)])
                op("dve", lambda e: e.tensor_tensor(out=dcol.ap()[:, 1, 1:NCH], in0=B3[:, 1:NCH, C // 2 - 1], in1=B3[:, 0:NCH - 1, C - 1], op=ALU.subtract),
                   reads=["Bc", ("dcol", 1)], writes=[("dcol", 1)])
                op("dve", lambda e: e.tensor_tensor(out=dcol.ap()[:, 2, :], in0=dcol.ap()[:, 0, :], in1=dcol.ap()[:, 1, :], op=ALU.add),
                   reads=[("dcol", 0), ("dcol", 1)], writes=[("dcol", 2)])
                op("act", lambda e: e.activation(out=dcol.ap()[:, 0:3, :], in_=dcol.ap()[:, 0:3, :], func=AF.Exp),
                   reads=[("dcol", 0), ("dcol", 1), ("dcol", 2)], writes=[("dcol", 0), ("dcol", 1), ("dcol", 2)])
                op("act", lambda e: e.activation(out=Ex.ap(), in_=bp.ap(), func=AF.Exp, scale=-1.0), reads=["bp"], writes=["Ex"])
                op("dve", lambda e: e.tensor_tensor(out=ktl.ap(), in0=ktl.ap(), in1=Ex.ap(), op=ALU.mult), reads=["ktl", "Ex"], writes=["ktl"])
                op("act", lambda e: e.copy(out=kt.ap(), in_=ktl.ap()), reads=["ktl"], writes=["kt"])
                op("dve", lambda e: e.tensor_tensor(out=kcf.ap().rearrange("p (c t) -> p c t", t=C), in0=ktl.ap().rearrange("p (c t) -> p c t", t=C),
                                                    in1=dcol.ap()[:, 0, :].unsqueeze(2).to_broadcast([128, NCH, C]), op=ALU.mult),
                   reads=["ktl", ("dcol", 0)], writes=["kcf"])
                op("act", lambda e: e.activation(out=Ex.ap(), in_=bp.ap(), func=AF.Exp), reads=["bp"], writes=["Ex"])

                def q_consume(tb, ps, pk):
                    op("dve", lambda e: e.tensor_tensor(out=bp.ap()[:, tsl(tb)], in0=Ex.ap()[:, tsl(tb)], in1=ps.ap(), op=ALU.mult),
                       reads=["Ex", pk, "bp"], writes=[("bpq", tb)])
                proj_fm(l, cq, 128, rings, q_consume)
                allq = [("bpq", tb) for tb in range(NTB)]
                op("act", lambda e: e.copy(out=qt.ap(), in_=bp.ap()), reads=allq, writes=["qt"])
                op("dve", lambda e: e.tensor_tensor(out=qh.ap().rearrange("p (c t) -> p c t", t=C), in0=bp.ap().rearrange("p (c t) -> p c t", t=C),
                                                    in1=dcol.ap()[:, 1, :].unsqueeze(2).to_broadcast([128, NCH, C]), op=ALU.mult),
                   reads=allq + [("dcol", 1)], writes=["qh"])

                def g_consume(tb, ps, pk):
                    op("act", lambda e: e.activation(out=gs.ap()[:, tsl(tb)], in_=ps.ap(), func=AF.Silu), reads=[pk], writes=[("gs", tb)])
                proj_fm(l, cg, 128, rings, g_consume)
                wt, wk = wload(P["w_in"][l][:, ci:ci + 128], KC, 128, stg_r, w_r)
                for c4 in range(NCH // 4):
                    ps, pk = psr.next()
                    for j in range(4):
                        c = c4 * 4 + j
                        for kc in range(KC):
                            op("pe", lambda e: e.matmul(ps.ap()[0:C, j * 128:(j + 1) * 128], lhsT=uT.ap()[:, kc, 2 + c * C: 2 + (c + 1) * C],
                                                        rhs=wt.ap()[:, kc, :], start=(kc == 0), stop=(kc == KC - 1)),
                               reads=[wk] + ukeys(kc, (c * C) // TB), writes=[pk])
                    cast(vtok.ap()[:, c4 * 4:(c4 + 1) * 4, :], ps.ap()[0:C, :].rearrange("p (a b) -> p a b", a=4), [pk], [("vtok", c4)], eng="act")
                def pre(c):
                    cs = slice(c * C, (c + 1) * C)
                    ps_s, pk_s = psr.next()
                    op("pe", lambda e: e.matmul(ps_s.ap()[0:C, 0:C], lhsT=kt.ap()[:, cs], rhs=qt.ap()[:, cs], start=True, stop=True),
                       reads=["kt", "qt"], writes=[pk_s])
                    pt, ptk = pt_r.next()
                    op("dve", lambda e: e.tensor_tensor(out=pt.ap(), in0=ps_s.ap()[0:C, 0:C], in1=tri.ap()[0:C, 0, 0:C], op=ALU.mult),
                       reads=[pk_s, "tri"], writes=[ptk])
                    ps_t, pk_t = psr.next()
                    op("pe", lambda e: e.transpose(out=ps_t.ap()[0:C, 0:128], in_=kcf.ap()[:, cs], identity=ident.ap()),
                       reads=["kcf", "ident"], writes=[pk_t])
                    kct, kctk = kct_r.next()
                    op("act", lambda e: e.copy(out=kct.ap(), in_=ps_t.ap()[0:C, 0:128]), reads=[pk_t], writes=[kctk])
                    return pt, ptk, kct, kctk
                stb = None
                nxt_pre = pre(0)
                for c in range(NCH):
                    cs = slice(c * C, (c + 1) * C)
                    pt, ptk, kct, kctk = nxt_pre
                    if c + 1 < NCH:
                        nxt_pre = pre(c + 1)
                    if c < NCH - 1:
                        ps_d, pk_d = psacc.next()
                        op("pe", lambda e: e.matmul(ps_d.ap()[:, 0:128], lhsT=kct.ap(), rhs=vtok.ap()[:, c, :], start=True, stop=True),
                           reads=[kctk, ("vtok", c // 4)], writes=[pk_d])
                    ps_y, pk_y = psr.next()
                    if c > 0:
                        op("pe", lambda e: e.matmul(ps_y.ap()[:, 0:C], lhsT=stb[0].ap(), rhs=qh.ap()[:, cs], start=True, stop=False),
                           reads=[stb[1], "qh"], writes=[pk_y])
                    op("pe", lambda e: e.matmul(ps_y.ap()[:, 0:C], lhsT=vtok.ap()[:, c, :], rhs=pt.ap(), start=(c == 0), stop=True),
                       reads=[("vtok", c // 4), ptk], writes=[pk_y])
                    op("act", lambda e: e.copy(out=ysb.ap()[:, cs], in_=ps_y.ap()[:, 0:C]), reads=[pk_y], writes=[("ysb", c // 8)])
                    if c < NCH - 1:
                        if c == 0:
                            op("dve", lambda e: e.tensor_copy(out=st.ap(), in_=ps_d.ap()[:, 0:128]), reads=[pk_d], writes=["st"])
                        else:
                            op("dve", lambda e: e.scalar_tensor_tensor(out=st.ap(), in0=st.ap(), scalar=dcol.ap()[:, 2, c:c + 1], in1=ps_d.ap()[:, 0:128],
                                                                       op0=ALU.mult, op1=ALU.add), reads=["st", pk_d, ("dcol", 2)], writes=["st"])
                        stb = stb_r.next()
                        op("act", lambda e: e.copy(out=stb[0].ap(), in_=st.ap()), reads=["st"], writes=[stb[1]])
                for tb in range(NTB):
                    sq, sk = sq_r.next()
                    yk = [("ysb", tb * 2), ("ysb", tb * 2 + 1)]
                    op("act", lambda e: e.activation(out=sq.ap(), in_=ysb.ap()[:, tsl(tb)], func=AF.Square), reads=yk, writes=[sk])
                    ps, pk = psr.next()
                    op("pe", lambda e: e.matmul(ps.ap(), lhsT=ones.ap(), rhs=sq.ap(), start=True, stop=True), reads=[sk, "ones"], writes=[pk])
                    sd, sdk = nt_r.next()
                    op("act", lambda e: e.activation(out=sd.ap(), in_=ps.ap(), func=AF.Ln, bias=epsc.ap(), scale=1.0 / 128), reads=[pk, "epsc"], writes=[sdk])
                    op("act", lambda e: e.activation(out=sd.ap(), in_=sd.ap(), func=AF.Exp, scale=-0.5), reads=[sdk], writes=[sdk])
                    op("dve", lambda e: e.scalar_tensor_tensor(out=sd.ap(), in0=ysb.ap()[:, tsl(tb)], scalar=gng.ap()[:, h:h + 1], in1=sd.ap(),
                                                               op0=ALU.mult, op1=ALU.mult), reads=yk + [sdk, "gng"], writes=[sdk])
                    op("dve", lambda e: e.tensor_tensor(out=yo.ap()[:, tsl(tb)], in0=sd.ap(), in1=gs.ap()[:, tsl(tb)], op=ALU.mult),
                       reads=[sdk, ("gs", tb)], writes=[("yo", tb)])
                k.dma(yT_dram[1, :, h, :], yo.ap(), reads=[("yo", tb) for tb in range(NTB)], writes=[("yT_dram", 1, h)])


    def mixer_dsa(l):
        NQ = S // 128
        with Scope(k) as sc:
            qT = sc.sb("qT", [128, 4, S], BF16)
            kTz = sc.sb("kTz", [128, 4, S], BF16)
            qiT = sc.sb("qiT", [128, 2, S], BF16)
            kiTz = sc.sb("kiTz", [128, 2, S], BF16)
            vtok = sc.sb("cvtok", [128, 2, NQ, 128], BF16)
            wi = sc.sb("wi", [128, NQ, 4], F32)
            wabs = sc.sb("wabs", [128, NQ, 4], F32)
            wsg = sc.sb("wsg", [128, NQ, 4], F32)
            yo = sc.sb("cyo", [128, 4, S], BF16)
            onesb = sc.sb("onesb", [128, 128], BF16)
            rt_r = sc.ring("rt", [128, TB], F32, 3)
            op("pool", lambda e: e.memset(onesb.ap(), 1.0), writes=["onesb"])
            with Scope(k) as s2:
                stg_r = s2.ring("cstg", [128, KC, 128], F32, 2)
                w_r = s2.ring("cw", [128, KC, 128], BF16, 2)
                wsw_r = s2.ring("cwsw", [128, KC, 128], BF16, 2)
                rope = s2.sb("rope", [128, 2, S], F32)
                k.dma(rope.ap(), c_rope.rearrange("a p s -> p a s"), writes=["rope"])

                def wload_cols(col_list, swap):
                    stg, sk = stg_r.next()
                    wt, wk = w_r.next()
                    o = 0
                    for (c0, n) in col_list:
                        if c0 is None:
                            op("pool", lambda e: e.memset(stg.ap()[:, :, o:o + n], 0.0), writes=[(sk, o), sk])
                        else:
                            k.dma(stg.ap()[:, :, o:o + n], P["w_in"][l][:, c0:c0 + n].rearrange("(c p) n -> p c n", p=128), writes=[(sk, o), sk])
                        o += n
                    rk = [(sk, oo) for oo in np.cumsum([0] + [n for _, n in col_list[:-1]]).tolist()] + [sk]
                    cast(wt.ap()[:, :, 0:o], stg.ap()[:, :, 0:o], rk, [wk])
                    if not swap:
                        return wt, wk, None, None
                    ws, wsk = wsw_r.next()
                    for b0 in range(0, o, 64):
                        cast(ws.ap()[:, :, b0:b0 + 32], stg.ap()[:, :, b0 + 32:b0 + 64], rk, [(wsk, b0), wsk])
                        cast(ws.ap()[:, :, b0 + 32:b0 + 64], stg.ap()[:, :, b0:b0 + 32], rk, [(wsk, b0 + 32), wsk])
                    return wt, wk, ws, [(wsk, b0) for b0 in range(0, o, 32)] + [wsk]

                def rope_proj(col_list, dst_fn, dkey):
                    wt, wk, ws, wsk = wload_cols(col_list, True)
                    for tb in range(NTB):
                        p1, pk1 = psr.next()
                        p2, pk2 = psr.next()
                        for kc in range(KC):
                            op("pe", lambda e: e.matmul(p1.ap(), lhsT=wt.ap()[:, kc, :], rhs=uT.ap()[:, kc, usl(tb)], start=(kc == 0), stop=(kc == KC - 1)),
                               reads=[wk] + ukeys(kc, tb), writes=[pk1])
                        for kc in range(KC):
                            op("pe", lambda e: e.matmul(p2.ap(), lhsT=ws.ap()[:, kc, :], rhs=uT.ap()[:, kc, usl(tb)], start=(kc == 0), stop=(kc == KC - 1)),
                               reads=wsk + ukeys(kc, tb), writes=[pk2])
                        t1, tk1 = rt_r.next()
                        t2, tk2 = rt_r.next()
                        op("dve", lambda e: e.tensor_tensor(out=t1.ap(), in0=rope.ap()[:, 0, tsl(tb)], in1=p1.ap(), op=ALU.mult), reads=["rope", pk1], writes=[tk1])
                        op("dve", lambda e: e.tensor_tensor(out=t2.ap(), in0=rope.ap()[:, 1, tsl(tb)], in1=p2.ap(), op=ALU.mult), reads=["rope", pk2], writes=[tk2])
                        op("pool", lambda e: e.tensor_tensor(out=dst_fn(tb), in0=t1.ap(), in1=t2.ap(), op=ALU.add), reads=[tk1, tk2], writes=[(dkey, tb)])

                co = C_OFF
                for ch in range(4):
                    rope_proj([(co + ch * 128, 128)], lambda tb: qT.ap()[:, ch, tsl(tb)], ("qT", ch))
                for c in range(2):
                    rope_proj([(co + 512 + c * 64, 64), (None, 64)], lambda tb: kTz.ap()[:, c * 2, tsl(tb)], ("kTz", c * 2))
                    rope_proj([(None, 64), (co + 512 + c * 64, 64)], lambda tb: kTz.ap()[:, c * 2 + 1, tsl(tb)], ("kTz", c * 2 + 1))
                for ch in range(2):
                    rope_proj([(co + 768 + ch * 128, 128)], lambda tb: qiT.ap()[:, ch, tsl(tb)], ("qiT", ch))
                rope_proj([(co + 1024, 64), (None, 64)], lambda tb: kiTz.ap()[:, 0, tsl(tb)], ("kiTz", 0))
                rope_proj([(None, 64), (co + 1024, 64)], lambda tb: kiTz.ap()[:, 1, tsl(tb)], ("kiTz", 1))
                for c in range(2):
                    wt, wk, _, _ = wload_cols([(co + 640 + c * 64, 64), (co + 640 + c * 64, 64)], False)
                    for s4 in range(NQ // 4):
                        ps, pk = psr.next()
                        for j in range(4):
                            sb = s4 * 4 + j
                            for kc in range(KC):
                                op("pe", lambda e: e.matmul(ps.ap()[:, j * 128:(j + 1) * 128], lhsT=uT.ap()[:, kc, 2 + sb * 128: 2 + (sb + 1) * 128],
                                                            rhs=wt.ap()[:, kc, :], start=(kc == 0), stop=(kc == KC - 1)),
                                   reads=[wk] + ukeys(kc, sb // 4), writes=[pk])
                        cast(vtok.ap()[:, c, s4 * 4:(s4 + 1) * 4, :], ps.ap().rearrange("p (a b) -> p a b", a=4), [pk], [("cvtok", c, s4)], eng="act")
                wt, wk, _, _ = wload_cols([(co + 1088, 4)], False)
                ps, pk = psr.next()
                for qb in range(NQ):
                    for kc in range(KC):
                        op("pe", lambda e: e.matmul(ps.ap()[:, qb * 4:(qb + 1) * 4], lhsT=uT.ap()[:, kc, 2 + qb * 128: 2 + (qb + 1) * 128],
                                                    rhs=wt.ap()[:, kc, 0:4], start=(kc == 0), stop=(kc == KC - 1)),
                           reads=[wk] + ukeys(kc, qb // 4), writes=[pk])
                op("dve", lambda e: e.tensor_copy(out=wi.ap(), in_=ps.ap()[:, 0:NQ * 4].rearrange("p (a b) -> p a b", b=4)), reads=[pk], writes=["wi"])
                op("act", lambda e: e.activation(out=wabs.ap(), in_=wi.ap(), func=AF.Abs), reads=["wi"], writes=["wabs"])
                op("act", lambda e: e.activation(out=wsg.ap(), in_=wi.ap(), func=AF.Sign), reads=["wi"], writes=["wsg"])
            NCHN = 4
            score2 = [sc.sb("score%d" % i, [128, S], F32) for i in range(NCHN)]
            Mf2 = [sc.sb("Mf%d" % i, [128, S], BF16) for i in range(NCHN)]
            MT2 = [sc.sb("MT%d" % i, [128, NQ, 128], BF16) for i in range(NCHN)]
            m82 = [sc.sb("m8_%d" % i, [128, 8], F32) for i in range(NCHN)]
            e_r = sc.ring("cE", [128, 4, 128], BF16, 3)
            p_r = sc.ring("cP", [128, 4, 128], BF16, 3)
            rec_r = sc.ring("crec", [128, TB], F32, 2)
            SENT = -3e30

            def scores(qb, i):
                score = score2[i]
                ncol = (qb + 1) * 128
                qs = slice(qb * 128, (qb + 1) * 128)
                for g0 in range(0, ncol, TB):
                    gn = min(TB, ncol - g0)
                    for hi in range(4):
                        ps, pk = psr.next()
                        op("pe", lambda e: e.matmul(ps.ap()[:, 0:gn], lhsT=qiT.ap()[:, hi // 2, qs], rhs=kiTz.ap()[:, hi % 2, g0:g0 + gn], start=True, stop=True),
                           reads=[(("qiT", hi // 2), qb // 4)] + [(("kiTz", hi % 2), tb) for tb in range(g0 // TB, (g0 + gn - 1) // TB + 1)], writes=[pk])
                        r, rk = rt_r.next()
                        op("act", lambda e: e.activation(out=r.ap()[:, 0:gn], in_=ps.ap()[:, 0:gn], func=AF.Relu, scale=wabs.ap()[:, qb, hi:hi + 1]),
                           reads=[pk, "wabs"], writes=[rk])
                        if hi == 0:
                            op("dve", lambda e: e.tensor_scalar(out=score.ap()[:, g0:g0 + gn], in0=r.ap()[:, 0:gn], scalar1=wsg.ap()[:, qb, hi:hi + 1],
                                                                scalar2=None, op0=ALU.mult), reads=[rk, "wsg"], writes=[("score", i, g0)])
                        else:
                            op("dve", lambda e: e.scalar_tensor_tensor(out=score.ap()[:, g0:g0 + gn], in0=r.ap()[:, 0:gn], scalar=wsg.ap()[:, qb, hi:hi + 1],
                                                                       in1=score.ap()[:, g0:g0 + gn], op0=ALU.mult, op1=ALU.add),
                               reads=[rk, "wsg", ("score", i, g0)], writes=[("score", i, g0)])
                sk_all = [("score", i, g0) for g0 in range(0, S, TB)]
                op("dve", lambda e: e.tensor_tensor(out=score.ap()[:, qs], in0=score.ap()[:, qs], in1=tri.ap()[:, 2, :], op=ALU.mult), reads=sk_all + ["tri"], writes=sk_all)
                op("dve", lambda e: e.tensor_tensor(out=score.ap()[:, qs], in0=score.ap()[:, qs], in1=tri.ap()[:, 3, :], op=ALU.add), reads=sk_all + ["tri"], writes=sk_all)

            NIT = 28
            W0 = 8192.0
            bs_S = [sc.sb("bsS%d" % i, [128, 1], F32) for i in range(NCHN)]
            bs_inc = [sc.sb("bsI%d" % i, [128, 1], F32) for i in range(NCHN)]
            bs_mid = [sc.sb("bsM%d" % i, [128, 1], F32) for i in range(NCHN)]
            junkA = sc.sb("junkA", [128, S], BF16)
            junkD = sc.sb("junkD", [128, S], BF16)

            def topk_gen(blocks):
                sk_all = lambda i: [("score", i, g0) for g0 in range(0, S, TB)]
                act = [(i, qb, (qb + 1) * 128) for i, qb in enumerate(blocks) if (qb + 1) * 128 > 256]
                on_dve = lambda i: (i == NCHN - 1)
                for i, qb, ncol in act:
                    op("dve", lambda e: e.memset(bs_mid[i].ap(), 0.0), writes=[("bsM", i)])
                w = W0
                for it in range(NIT if act else 0):
                    for i, qb, ncol in act:
                        if on_dve(i):
                            op("dve", lambda e: e.tensor_scalar(out=junkD.ap()[:, 0:ncol], in0=score2[i].ap()[:, 0:ncol], scalar1=bs_mid[i].ap(), scalar2=None,
                                                                op0=ALU.is_ge, op1=ALU.add, accum_out=bs_S[i].ap()),
                               reads=sk_all(i) + [("bsM", i)], writes=[("bsS", i)])
                        else:
                            op("act", lambda e: e.activation(out=junkA.ap()[:, 0:ncol], in_=score2[i].ap()[:, 0:ncol], func=AF.Sign, bias=bs_mid[i].ap(), scale=1.0,
                                                             accum_out=bs_S[i].ap()),
                               reads=sk_all(i) + [("bsM", i)], writes=[("bsS", i)])
                    for i, qb, ncol in act:
                        kthr = 256.0 if on_dve(i) else float(512 - ncol)
                        op("dve", lambda e: e.tensor_scalar(out=bs_inc[i].ap(), in0=bs_S[i].ap(), scalar1=kthr, scalar2=w / 2, op0=ALU.is_ge, op1=ALU.mult),
                           reads=[("bsS", i)], writes=[("bsI", i)])
                        if on_dve(i):
                            op("dve", lambda e: e.scalar_tensor_tensor(out=bs_mid[i].ap(), in0=bs_mid[i].ap(), scalar=-w / 4, in1=bs_inc[i].ap(), op0=ALU.add, op1=ALU.add),
                               reads=[("bsM", i), ("bsI", i)], writes=[("bsM", i)])
                        else:
                            op("dve", lambda e: e.scalar_tensor_tensor(out=bs_mid[i].ap(), in0=bs_mid[i].ap(), scalar=w / 4, in1=bs_inc[i].ap(), op0=ALU.add, op1=ALU.subtract),
                               reads=[("bsM", i), ("bsI", i)], writes=[("bsM", i)])
                    w = w / 2
                    yield
                for i, qb in enumerate(blocks):
                    ncol = (qb + 1) * 128
                    if ncol > 256:
                        if on_dve(i):
                            op("dve", lambda e: e.tensor_scalar(out=bs_inc[i].ap(), in0=bs_mid[i].ap(), scalar1=-w / 2, scalar2=None, op0=ALU.add), reads=[("bsM", i)], writes=[("bsI", i)])
                        else:
                            op("dve", lambda e: e.tensor_scalar(out=bs_inc[i].ap(), in0=bs_mid[i].ap(), scalar1=-1.0, scalar2=-w / 2, op0=ALU.mult, op1=ALU.add), reads=[("bsM", i)], writes=[("bsI", i)])
                        op("dve", lambda e: e.tensor_scalar(out=Mf2[i].ap()[:, 0:ncol], in0=score2[i].ap()[:, 0:ncol], scalar1=bs_inc[i].ap(), scalar2=None, op0=ALU.is_ge),
                           reads=sk_all(i) + [("bsI", i)], writes=[("Mf", i)])
                    else:
                        op("dve", lambda e: e.tensor_scalar(out=Mf2[i].ap()[:, 0:ncol], in0=score2[i].ap()[:, 0:ncol], scalar1=-1e29, scalar2=None, op0=ALU.is_ge),
                           reads=sk_all(i), writes=[("Mf", i)])
                yield

            def maskT(qb, i):
                for s4 in range(0, qb + 1, 4):
                    nb = min(4, qb + 1 - s4)
                    ps, pk = psr.next()
                    for j in range(nb):
                        sb = s4 + j
                        op("pe", lambda e: e.transpose(out=ps.ap().bitcast(BF16)[:, j * 128:(j + 1) * 128], in_=Mf2[i].ap()[:, sb * 128:(sb + 1) * 128], identity=identb.ap()),
                           reads=[("Mf", i), "identb"], writes=[pk])
                    op("act", lambda e: e.copy(out=MT2[i].ap()[:, s4:s4 + nb, :], in_=ps.ap().bitcast(BF16)[:, 0:nb * 128].rearrange("p (a b) -> p a b", b=128)),
                       reads=[pk], writes=[("MT", i, s4)])

            def attn_gen(qb, i):
                qs = slice(qb * 128, (qb + 1) * 128)
                MT = MT2[i]
                for c in range(2):
                    ps_o, pk_o = psacc.next()
                    ps_d, pk_d = psacc.next()

                    def front(sb):
                        ps, pk = psr.next()
                        for j in range(4):
                            hq = c * 4 + j
                            op("pe", lambda e: e.matmul(ps.ap()[:, j * 128:(j + 1) * 128], lhsT=kTz.ap()[:, c * 2 + hq % 2, sb * 128:(sb + 1) * 128],
                                                        rhs=qT.ap()[:, hq // 2, qs], start=True, stop=True),
                               reads=[(("kTz", c * 2 + hq % 2), sb // 4), (("qT", hq // 2), qb // 4)], writes=[pk])
                        E, Ek = e_r.next()
                        op("act", lambda e: e.activation(out=E.ap(), in_=ps.ap().rearrange("p (a b) -> p a b", a=4), func=AF.Exp, scale=0.125), reads=[pk], writes=[Ek])
                        Pm, Pk = p_r.next()
                        me = "pool" if (sb % 3) else "dve"
                        op(me, lambda e: e.tensor_tensor(out=Pm.ap(), in0=E.ap(), in1=MT.ap()[:, sb:sb + 1, :].to_broadcast([128, 4, 128]), op=ALU.mult),
                           reads=[Ek, ("MT", i, (sb // 4) * 4)], writes=[Pk])
                        return Pm, Pk
                    nxt_p = front(0)
                    for sb in range(qb + 1):
                        Pm, Pk = nxt_p
                        if sb + 1 <= qb:
                            nxt_p = front(sb + 1)
                        op("pe", lambda e: e.matmul(ps_o.ap(), lhsT=vtok.ap()[:, c, sb, :], rhs=Pm.ap().rearrange("p a b -> p (a b)"), start=(sb == 0), stop=(sb == qb)),
                           reads=[("cvtok", c, sb // 4), Pk], writes=[pk_o])
                        op("pe", lambda e: e.matmul(ps_d.ap(), lhsT=onesb.ap(), rhs=Pm.ap().rearrange("p a b -> p (a b)"), start=(sb == 0), stop=(sb == qb)),
                           reads=["onesb", Pk], writes=[pk_d])
                        yield
                    rec, reck = rec_r.next()
                    op("dve", lambda e: e.reciprocal(out=rec.ap(), in_=ps_d.ap()), reads=[pk_d], writes=[reck])
                    for j in range(4):
                        hq = c * 4 + j
                        hb = (hq % 2) * 64
                        op("dve", lambda e: e.tensor_tensor(out=yo.ap()[hb:hb + 64, hq // 2, qs], in0=rec.ap()[hb:hb + 64, j * 128:(j + 1) * 128],
                                                            in1=ps_o.ap()[hb:hb + 64, j * 128:(j + 1) * 128], op=ALU.mult),
                           reads=[reck, pk_o], writes=[("cyo", hq // 2, hq % 2)])

            def run(g):
                for _ in g:
                    pass

            def zipgens(gt, ga_list, n_t, n_a):
                ga = (x for g in ga_list for x in g)
                per = max(1, -(-n_a // max(1, n_t)))
                done_a = False
                for _ in gt:
                    for _j in range(per):
                        if next(ga, "END") == "END":
                            done_a = True
                            break
                if not done_a:
                    for _ in ga:
                        pass

            grps = [tuple(range(g * NCHN, (g + 1) * NCHN)) for g in range(NQ // NCHN)]
            for i_, qb_ in enumerate(grps[0]):
                scores(qb_, i_)
            run(topk_gen(grps[0]))
            for i_, qb_ in enumerate(grps[0]):
                maskT(qb_, i_)
            for p in range(len(grps)):
                cur = grps[p]
                if p + 1 < len(grps):
                    nxt = grps[p + 1]
                    for i_, qb_ in enumerate(nxt):
                        scores(qb_, i_)
                    n_a = sum(2 * (q_ + 1) for q_ in cur)
                    zipgens(topk_gen(nxt), [attn_gen(q_, i_) for i_, q_ in enumerate(cur)], NIT + 1, n_a)
                    for i_, qb_ in enumerate(nxt):
                        maskT(qb_, i_)
                else:
                    for i_, q_ in enumerate(cur):
                        run(attn_gen(q_, i_))
            for ch in range(4):
                k.dma(yT_dram[2, :, ch, :], yo.ap()[:, ch, :], reads=[("cyo", ch, 0), ("cyo", ch, 1)], writes=[("yT_dram", 2, ch)])

    def mixer_rwkv(l):
        C = 128; NCH = S // C
        with Scope(k) as sc:
            stg_r = sc.ring("astg", [128, KC, 128], F32, 1)
            stm_r = sc.ring("astm", [128, KC, 128], F32, 1)
            w1_r = sc.ring("aw1", [128, KC, 128], BF16, 1)
            wm_r = sc.ring("awm", [128, KC, 128], BF16, 1)
            mub_r = sc.ring("amub", [128, 128], F32, 2)
            lrb = sc.sb("lrb", [128, 3, 512], BF16)
            with Scope(k) as s2:
                lrs = s2.sb("lrs", [128, 3, 512], F32)
                k.dma(lrs.ap()[0:64, 0, :], P["a_w_up"][l], writes=["lrs0"]); k.dma(lrs.ap()[0:64, 1, :], P["a_a_up"][l], writes=["lrs1"])
                k.dma(lrs.ap()[:, 2, :], P["a_g_up"][l], writes=["lrs2"])
                op("dve", lambda e: e.tensor_copy(out=lrb.ap()[0:64, 0:2, :], in_=lrs.ap()[0:64, 0:2, :]), reads=["lrs0", "lrs1"], writes=["lrb01"])
                op("dve", lambda e: e.tensor_copy(out=lrb.ap()[:, 2, :], in_=lrs.ap()[:, 2, :]), reads=["lrs2"], writes=["lrb2"])

            def proj_shift(c0, ncol, consume, token_major=None):
                stg, sk = stg_r.next(); stm, smk = stm_r.next(); w1, w1k = w1_r.next(); wm, wmk = wm_r.next(); mub, mk = mub_r.next()
                k.dma(stg.ap()[:, :, 0:ncol], P["w_in"][l][:, c0:c0 + ncol].rearrange("(c p) n -> p c n", p=128), writes=[sk])
                k.dma(mub.ap()[:, 0:ncol], P["a_mu"][l][c0 - A_OFF:c0 - A_OFF + ncol].partition_broadcast(128), writes=[mk])
                op("dve", lambda e: e.tensor_tensor(out=stm.ap()[:, :, 0:ncol], in0=stg.ap()[:, :, 0:ncol],
                                                    in1=mub.ap()[:, 0:ncol].unsqueeze(1).to_broadcast([128, KC, ncol]), op=ALU.mult),
                   reads=[sk, mk], writes=[smk])
                op("act", lambda e: e.copy(out=wm.ap()[:, :, 0:ncol], in_=stm.ap()[:, :, 0:ncol]), reads=[smk], writes=[wmk])
                op("dve", lambda e: e.tensor_tensor(out=w1.ap()[:, :, 0:ncol], in0=stg.ap()[:, :, 0:ncol], in1=stm.ap()[:, :, 0:ncol], op=ALU.subtract),
                   reads=[sk, smk], writes=[w1k])
                if token_major is not None:
                    token_major(w1, w1k, wm, wmk)
                    return
                for tb in range(NTB):
                    ps, pk = psr.next()
                    for kc in range(KC):
                        op("pe", lambda e: e.matmul(ps.ap()[0:ncol, :], lhsT=w1.ap()[:, kc, 0:ncol], rhs=uT.ap()[:, kc, usl(tb)], start=(kc == 0), stop=False),
                           reads=[w1k] + ukeys(kc, tb), writes=[pk])
                    for kc in range(KC):
                        op("pe", lambda e: e.matmul(ps.ap()[0:ncol, :], lhsT=wm.ap()[:, kc, 0:ncol], rhs=uT.ap()[:, kc, usl(tb, 1)], start=False, stop=(kc == KC - 1)),
                           reads=[wmk] + ukeys(kc, tb) + (ukeys(kc, tb - 1) if tb else []) + ["uTpad"], writes=[pk])
                    consume(tb, ps, pk)

            tw = sc.sb("tw", [64, S], BF16); adT = sc.sb("adT", [64, S], BF16); sgT = sc.sb("sgT", [128, S], BF16)
            proj_shift(A_OFF + 1536, 64, lambda tb, ps, pk: op("act", lambda e: e.activation(out=tw.ap()[:, tsl(tb)], in_=ps.ap()[0:64, :], func=AF.Tanh), reads=[pk], writes=[("tw", tb)]))
            proj_shift(A_OFF + 1600, 64, lambda tb, ps, pk: op("act", lambda e: e.copy(out=adT.ap()[:, tsl(tb)], in_=ps.ap()[0:64, :]), reads=[pk], writes=[("adT", tb)]))
            proj_shift(A_OFF + 1664, 128, lambda tb, ps, pk: op("act", lambda e: e.activation(out=sgT.ap()[:, tsl(tb)], in_=ps.ap(), func=AF.Sigmoid), reads=[pk], writes=[("sgT", tb)]))
            pc = sc.sb("pc", [128, 8, 4], F32)
            for i_, nm in enumerate(["a_w0", "a_a0", "a_k_k", "a_k_a", "a_r_k", "a_gn_g", "a_gn_b"]):
                k.dma(pc.ap()[:, i_, :], P[nm][l].rearrange("(h p) -> p h", p=128), writes=[("pc", i_)], allow_slow_non_contiguous=True)
            op("dve", lambda e: e.tensor_scalar(out=pc.ap()[:, 7, :], in0=pc.ap()[:, 3, :], scalar1=-1.0, scalar2=1.0, op0=ALU.mult, op1=ALU.add), reads=[("pc", 3)], writes=[("pc", 7)])
            pcall = [("pc", i_) for i_ in range(8)]
            gnc = sc.sb("gnc", [128, 1], F32)
            op("pool", lambda e: e.memset(gnc.ap(), 64e-5), writes=["gnc"])
            F = {n_: sc.sb("af_" + n_, [128, S], F32) for n_ in ["r", "k", "v", "a", "L", "lw", "t1", "t2"]}
            F["g"] = sc.sb("af_g", [128, S], BF16)
            Bh = {n_: [sc.sb("ab_%s%d" % (n_, h), [128, S], BF16) for h in range(2)] for n_ in ["bt", "kt", "bc", "kc"]}
            Bh["rh"] = [sc.sb("ab_rh", [128, S], BF16)] * 2
            Bh["ah"] = [sc.sb("ab_ah", [128, S], BF16)] * 2
            AR = sc.sb("AR", [128, NCH, 2, C], BF16)
            dcol = sc.sb("adcol", [128, 3, NCH], F32)
            vtok = sc.sb("avtok", [128, NCH, 128], BF16)
            vpad = [sc.sb("avpad%d" % h, [128, NCH, 128], BF16) for h in range(2)]
            H = [sc.sb("aH%d" % h, [128, 128], F32) for h in range(2)]
            Hb_r = [sc.ring("aHb%d" % h, [128, 128], BF16, 2) for h in range(2)]
            upad = [sc.ring("aupad%d" % h, [128, 128], BF16, 2) for h in range(2)]
            mt_r = sc.ring("amt", [128, 128], BF16, 8)
            kp_r = sc.ring("akp", [128, 128], BF16, 12)
            wu_r = sc.ring("awu", [128, 128], BF16, 2)
            tk_r = sc.ring("atk", [128, 128], BF16, 6)
            ya = F["t1"]
            yo = Bh["rh"][0]
            sq_r = sc.ring("asq", [128, TB], F32, 2)
            hm = lambda h: tri.ap()[:, 6, h:h + 1]
            for h in range(2):
                for r_ in upad[h].t:
                    op("pool", lambda e: e.memset(r_.ap(), 0.0), writes=[("upad0", h, r_.name)])
            for hp in range(4):
                co = A_OFF + hp * 128
                prm = lambda i_: pc.ap()[:, i_, hp:hp + 1]
                for nm, cc in (("r", co), ("k", co + 512), ("v", co + 1024)):
                    proj_shift(cc, 128, lambda tb, ps, pk: op("act", lambda e: e.copy(out=F[nm].ap()[:, tsl(tb)], in_=ps.ap()), reads=[pk], writes=[(nm, tb)]))
                allk = lambda nm: [(nm, tb) for tb in range(NTB)]

                def vtm(w1, w1k, wm, wmk):
                    for c4 in range(NCH // 4):
                        ps, pk = psr.next()
                        for j in range(4):
                            c = c4 * 4 + j
                            for kc in range(KC):
                                op("pe", lambda e: e.matmul(ps.ap()[:, j * 128:(j + 1) * 128], lhsT=uT.ap()[:, kc, 2 + c * C:2 + (c + 1) * C], rhs=w1.ap()[:, kc, :], start=(kc == 0), stop=False),
                                   reads=[w1k] + ukeys(kc, c // 4), writes=[pk])
                            for kc in range(KC):
                                op("pe", lambda e: e.matmul(ps.ap()[:, j * 128:(j + 1) * 128], lhsT=uT.ap()[:, kc, 1 + c * C:1 + (c + 1) * C], rhs=wm.ap()[:, kc, :], start=False, stop=(kc == KC - 1)),
                                   reads=[wmk] + ukeys(kc, c // 4) + (ukeys(kc, c // 4 - 1) if c >= 4 else []) + ["uTpad"], writes=[pk])
                        cast(vtok.ap()[:, c4 * 4:(c4 + 1) * 4, :], ps.ap().rearrange("p (a b) -> p a b", a=4), [pk], [("avtok", c4)], eng="act")
                proj_shift(co + 1024, 128, None, token_major=vtm)
                vk = [("avtok", c4) for c4 in range(NCH // 4)]
                for h in range(2):
                    op("pool", lambda e: e.tensor_tensor(out=vpad[h].ap(), in0=vtok.ap(), in1=tri.ap()[:, 5, h * 64:h * 64 + 1].to_broadcast([128, NCH, 128]) if False else
                                                         tri.ap()[0:1, 5, :].partition_broadcast(128).unsqueeze(1).to_broadcast([128, NCH, 128]) if False else vtok.ap(), op=ALU.mult) if False else
                       e.memset(vpad[h].ap(), 0.0), reads=vk, writes=[("vpad", h)])
                    op("act", lambda e: e.copy(out=vpad[h].ap()[:, :, h * 64:(h + 1) * 64], in_=vtok.ap()[:, :, h * 64:(h + 1) * 64]), reads=vk + [("vpad", h)], writes=[("vpad", h)])
                for tb in range(NTB):
                    ps, pk = psr.next()
                    op("pe", lambda e: e.matmul(ps.ap(), lhsT=lrb.ap()[0:64, 0, hp * 128:(hp + 1) * 128], rhs=tw.ap()[:, tsl(tb)], start=True, stop=True), reads=["lrb01", ("tw", tb)], writes=[pk])
                    op("act", lambda e: e.activation(out=F["lw"].ap()[:, tsl(tb)], in_=ps.ap(), func=AF.Sigmoid, bias=prm(0)), reads=[pk] + pcall, writes=[("lw", tb)])
                    ps, pk = psr.next()
                    op("pe", lambda e: e.matmul(ps.ap(), lhsT=lrb.ap()[0:64, 1, hp * 128:(hp + 1) * 128], rhs=adT.ap()[:, tsl(tb)], start=True, stop=True), reads=["lrb01", ("adT", tb)], writes=[pk])
                    op("act", lambda e: e.activation(out=F["a"].ap()[:, tsl(tb)], in_=ps.ap(), func=AF.Sigmoid, bias=prm(1)), reads=[pk] + pcall, writes=[("a", tb)])
                    ps, pk = psr.next()
                    op("pe", lambda e: e.matmul(ps.ap(), lhsT=lrb.ap()[:, 2, hp * 128:(hp + 1) * 128], rhs=sgT.ap()[:, tsl(tb)], start=True, stop=True), reads=["lrb2", ("sgT", tb)], writes=[pk])
                    op("act", lambda e: e.copy(out=F["g"].ap()[:, tsl(tb)], in_=ps.ap()), reads=[pk], writes=[("g", tb)])
                op("dve", lambda e: e.tensor_scalar(out=F["lw"].ap(), in0=F["lw"].ap(), scalar1=-0.6065306597126334, scalar2=None, op0=ALU.mult), reads=allk("lw"), writes=allk("lw"))
                op("dve", lambda e: e.tensor_tensor_scan(out=F["L"].ap(), data0=ones.ap()[:, 0:1].to_broadcast([128, S]), data1=F["lw"].ap(), initial=0.0, op0=ALU.mult, op1=ALU.add), reads=allk("lw") + ["ones"], writes=["L"])
                L3 = F["L"].ap().rearrange("p (c t) -> p c t", t=C)
                op("dve", lambda e: e.tensor_tensor(out=dcol.ap()[:, 0, :], in0=L3[:, :, C - 1], in1=L3[:, :, C // 2 - 1], op=ALU.subtract), reads=["L"], writes=[("dc", 0)])
                op("dve", lambda e: e.tensor_copy(out=dcol.ap()[:, 1, 0:1], in_=L3[:, 0, C // 2 - 1:C // 2]), reads=["L"], writes=[("dc", 1)])
                op("dve", lambda e: e.tensor_tensor(out=dcol.ap()[:, 1, 1:NCH], in0=L3[:, 1:NCH, C // 2 - 1], in1=L3[:, 0:NCH - 1, C - 1], op=ALU.subtract), reads=["L", ("dc", 1)], writes=[("dc", 1)])
                op("dve", lambda e: e.tensor_tensor(out=dcol.ap()[:, 2, :], in0=dcol.ap()[:, 0, :], in1=dcol.ap()[:, 1, :], op=ALU.add), reads=[("dc", 0), ("dc", 1)], writes=[("dc", 2)])
                dck = [("dc", 0), ("dc", 1), ("dc", 2)]
                op("act", lambda e: e.activation(out=dcol.ap(), in_=dcol.ap(), func=AF.Exp), reads=dck, writes=dck)
                t1, t2 = F["t1"], F["t2"]
                op("dve", lambda e: e.tensor_scalar(out=t1.ap(), in0=F["k"].ap(), scalar1=prm(2), scalar2=None, op0=ALU.mult), reads=allk("k") + pcall, writes=["t1"])
                op("act", lambda e: e.activation(out=t2.ap(), in_=t1.ap(), func=AF.Square), reads=["t1"], writes=["t2"])
                pss = [psr.next() for _ in range(NTB)]
                for tb in range(NTB):
                    op("pe", lambda e: e.matmul(pss[tb][0].ap(), lhsT=tri.ap()[:, 5, :], rhs=t2.ap()[:, tsl(tb)], start=True, stop=True), reads=["t2", "tri"], writes=[pss[tb][1]])
                for tb in range(NTB):
                    op("dve", lambda e: e.tensor_scalar(out=t2.ap()[:, tsl(tb)], in0=pss[tb][0].ap(), scalar1=1e-19, scalar2=None, op0=ALU.max), reads=[pss[tb][1], "t2"], writes=["t2"])
                op("act", lambda e: e.activation(out=t2.ap(), in_=t2.ap(), func=AF.Ln), reads=["t2"], writes=["t2"])
                op("act", lambda e: e.activation(out=t2.ap(), in_=t2.ap(), func=AF.Exp, scale=-0.5), reads=["t2"], writes=["t2"])
                op("dve", lambda e: e.tensor_tensor(out=t1.ap(), in0=t1.ap(), in1=t2.ap(), op=ALU.mult), reads=["t1", "t2"], writes=["t1"])
                op("dve", lambda e: e.tensor_scalar(out=t2.ap(), in0=F["a"].ap(), scalar1=prm(3), scalar2=prm(7), op0=ALU.mult, op1=ALU.add), reads=allk("a") + pcall, writes=["t2"])
                op("dve", lambda e: e.tensor_tensor(out=F["k"].ap(), in0=F["k"].ap(), in1=t2.ap(), op=ALU.mult), reads=allk("k") + ["t2"], writes=allk("k"))
                op("dve", lambda e: e.tensor_tensor(out=F["a"].ap(), in0=F["a"].ap(), in1=t1.ap(), op=ALU.mult), reads=allk("a") + ["t1"], writes=allk("a"))
                op("dve", lambda e: e.scalar_tensor_tensor(out=t2.ap(), in0=F["r"].ap(), scalar=prm(4), in1=F["k"].ap(), op0=ALU.mult, op1=ALU.mult), reads=allk("r") + allk("k") + pcall, writes=["t2"])
                pss = [psr.next() for _ in range(NTB)]
                for tb in range(NTB):
                    op("pe", lambda e: e.matmul(pss[tb][0].ap(), lhsT=tri.ap()[:, 5, :], rhs=t2.ap()[:, tsl(tb)], start=True, stop=True), reads=["t2", "tri"], writes=[pss[tb][1]])
                for tb in range(NTB):
                    op("dve", lambda e: e.tensor_tensor(out=F["v"].ap()[:, tsl(tb)], in0=F["v"].ap()[:, tsl(tb)], in1=pss[tb][0].ap(), op=ALU.mult), reads=[("v", tb), pss[tb][1]], writes=[("v", tb)])
                m_b = L3[:, :, C // 2 - 1:C // 2].to_broadcast([128, NCH, C])
                v3 = lambda t_: t_.ap().rearrange("p (c t) -> p c t", t=C)
                op("dve", lambda e: e.tensor_tensor(out=v3(t2), in0=L3, in1=m_b, op=ALU.subtract), reads=["L"], writes=["t2"])
                op("dve", lambda e: e.tensor_tensor(out=F["lw"].ap(), in0=t2.ap(), in1=F["lw"].ap(), op=ALU.subtract), reads=["t2"] + allk("lw"), writes=allk("lw"))
                op("act", lambda e: e.activation(out=F["lw"].ap(), in_=F["lw"].ap(), func=AF.Exp), reads=allk("lw"), writes=allk("lw"))
                op("act", lambda e: e.activation(out=F["L"].ap(), in_=t2.ap(), func=AF.Exp, scale=-1.0), reads=["t2"], writes=["L"])
                op("act", lambda e: e.activation(out=t2.ap(), in_=t2.ap(), func=AF.Exp), reads=["t2"], writes=["t2"])
                op("dve", lambda e: e.tensor_tensor(out=F["r"].ap(), in0=F["r"].ap(), in1=t2.ap(), op=ALU.mult), reads=allk("r") + ["t2"], writes=allk("r"))
                op("act", lambda e: e.copy(out=AR.ap()[:, :, 1, :], in_=v3(F["r"])), reads=allk("r"), writes=["AR1"])
                dB = dcol.ap()[:, 1, :].unsqueeze(2).to_broadcast([128, NCH, C]); dA = dcol.ap()[:, 0, :].unsqueeze(2).to_broadcast([128, NCH, C])
                op("dve", lambda e: e.tensor_tensor(out=v3(t2), in0=v3(F["r"]), in1=dB, op=ALU.mult), reads=allk("r") + dck, writes=["t2"])
                op("act", lambda e: e.copy(out=Bh["rh"][0].ap(), in_=t2.ap()), reads=["t2"], writes=[("rh", 0), ("rh", 1)] + [("yo", tb) for tb in range(NTB)])
                op("dve", lambda e: e.scalar_tensor_tensor(out=F["lw"].ap(), in0=t1.ap(), scalar=-1.0, in1=F["lw"].ap(), op0=ALU.mult, op1=ALU.mult), reads=["t1"] + allk("lw"), writes=allk("lw"))
                op("act", lambda e: e.copy(out=AR.ap()[:, :, 0, :], in_=v3(F["lw"])), reads=allk("lw"), writes=["AR0"])
                op("dve", lambda e: e.tensor_tensor(out=v3(t2), in0=v3(F["lw"]), in1=dB, op=ALU.mult), reads=allk("lw") + dck + [("rh", 0), ("rh", 1)], writes=["t2"])
                op("act", lambda e: e.copy(out=Bh["ah"][0].ap(), in_=t2.ap()), reads=["t2"], writes=[("ah", 0), ("ah", 1)])
                for src, nt, ncn in ((F["a"], "bt", "bc"), (F["k"], "kt", "kc")):
                    sk_ = allk("a") if nt == "bt" else allk("k")
                    op("dve", lambda e: e.tensor_tensor(out=src.ap(), in0=src.ap(), in1=F["L"].ap(), op=ALU.mult), reads=sk_ + ["L"], writes=sk_)
                    op("dve", lambda e: e.tensor_tensor(out=v3(t1) if nt == "kt" else v3(t2), in0=v3(src), in1=dA, op=ALU.mult), reads=sk_ + dck + [("ah", 0), ("ah", 1)], writes=["t1" if nt == "kt" else "t2"])
                    tt_ = t1 if nt == "kt" else t2
                    for h in range(2):
                        op("act", lambda e: e.activation(out=Bh[nt][h].ap(), in_=src.ap(), func=AF.Copy, scale=hm(h)), reads=sk_ + ["tri"], writes=[(nt, h)])
                        op("act", lambda e: e.activation(out=Bh[ncn][h].ap(), in_=tt_.ap(), func=AF.Copy, scale=hm(h)), reads=["t1" if nt == "kt" else "t2", "tri"], writes=[(ncn, h)])
                k.phase_barrier()
                pool_t = []
                for nm_ in ("r", "k", "a", "L", "lw", "t2"):
                    vb = F[nm_].ap().bitcast(BF16)
                    pool_t += [vb[:, j * 128:(j + 1) * 128] for j in range(32)]
                RES = {}
                it_ = iter(pool_t)
                for c_ in range(NCH):
                    for h_ in range(2):
                        for nm_ in ("TT", "rb", "ak", "rk", "bcT", "kcT"):
                            RES[(nm_, h_, c_)] = next(it_)
                vt_b = vtok.ap().rearrange("p a b -> p (a b)")
                tmp_t = [t_.ap() for t_ in mt_r.t] + [t_.ap() for t_ in tk_r.t] + [t_.ap() for t_ in kp_r.t] + [vt_b[:, j * 128:(j + 1) * 128] for j in range(16)]
                GCH = 8
                assert len(tmp_t) >= 5 * GCH, len(tmp_t)

                def A_gen(group):
                    st = {}
                    for gi_, (h, c) in enumerate(group):
                        cs = slice(c * C, (c + 1) * C)
                        T5 = tmp_t[gi_ * 5:(gi_ + 1) * 5]
                        tk_ = lambda j: ("tmp", gi_, j)
                        ps, pk = psr.next()
                        op("pe", lambda e: e.matmul(ps.ap()[:, 0:256], lhsT=Bh["bt"][h].ap()[:, cs], rhs=AR.ap()[:, c].rearrange("p a b -> p (a b)"), start=True, stop=True), reads=[("bt", h), "AR0", "AR1"], writes=[pk])
                        op("dve", lambda e: e.tensor_tensor(out=T5[1], in0=ps.ap()[:, 0:128], in1=tri.ap()[:, 1, :], op=ALU.mult), reads=[pk, "tri"], writes=[tk_(1)])
                        op("dve", lambda e: e.tensor_tensor(out=RES[("rb", h, c)], in0=ps.ap()[:, 128:256], in1=tri.ap()[:, 0, :], op=ALU.mult), reads=[pk, "tri"], writes=[("res", "rb", h, c)])
                        ps, pk = psr.next()
                        op("pe", lambda e: e.matmul(ps.ap()[:, 0:256], lhsT=Bh["kt"][h].ap()[:, cs], rhs=AR.ap()[:, c].rearrange("p a b -> p (a b)"), start=True, stop=True), reads=[("kt", h), "AR0", "AR1"], writes=[pk])
                        op("dve", lambda e: e.tensor_tensor(out=RES[("ak", h, c)], in0=ps.ap()[:, 0:128], in1=tri.ap()[:, 1, :], op=ALU.mult), reads=[pk, "tri"], writes=[("res", "ak", h, c)])
                        op("dve", lambda e: e.tensor_tensor(out=RES[("rk", h, c)], in0=ps.ap()[:, 128:256], in1=tri.ap()[:, 0, :], op=ALU.mult), reads=[pk, "tri"], writes=[("res", "rk", h, c)])
                        ps, pk = psr.next()
                        op("pe", lambda e: e.matmul(ps.ap()[:, 0:128], lhsT=AR.ap()[:, c, 0, :], rhs=Bh["bt"][h].ap()[:, cs], start=True, stop=True), reads=["AR0", ("bt", h)], writes=[pk])
                        op("dve", lambda e: e.tensor_tensor(out=T5[0], in0=ps.ap()[:, 0:128], in1=tri.ap()[:, 4, :], op=ALU.mult), reads=[pk, "tri"], writes=[tk_(0)])
                        op("dve", lambda e: e.tensor_tensor(out=T5[4], in0=T5[1], in1=identb.ap(), op=ALU.add), reads=[tk_(1), "identb"], writes=[tk_(4)])
                        pst, pkt = psr.next()
                        op("pe", lambda e: e.transpose(out=pst.ap().bitcast(BF16)[:, 0:128], in_=Bh["bc"][h].ap()[:, cs], identity=identb.ap()), reads=[("bc", h), "identb"], writes=[pkt])
                        op("pe", lambda e: e.transpose(out=pst.ap().bitcast(BF16)[:, 128:256], in_=Bh["kc"][h].ap()[:, cs], identity=identb.ap()), reads=[("kc", h), "identb"], writes=[pkt])
                        op("act", lambda e: e.copy(out=RES[("bcT", h, c)], in_=pst.ap().bitcast(BF16)[:, 0:128]), reads=[pkt], writes=[("res", "bcT", h, c)])
                        op("act", lambda e: e.copy(out=RES[("kcT", h, c)], in_=pst.ap().bitcast(BF16)[:, 128:256]), reads=[pkt], writes=[("res", "kcT", h, c)])
                        st[gi_] = [0, 1, 2, 3]
                    yield
                    for lev in range(6):
                        for gi_, (h, c) in enumerate(group):
                            T5 = tmp_t[gi_ * 5:(gi_ + 1) * 5]
                            tk_ = lambda j: ("tmp", gi_, j)
                            iX, iXT, iXn, iXTn = st[gi_]
                            psx, pkx = psr.next()
                            op("pe", lambda e: e.matmul(psx.ap()[:, 0:128], lhsT=T5[iXT], rhs=T5[iX], start=True, stop=True), reads=[tk_(iXT), tk_(iX)], writes=[pkx])
                            if lev < 5:
                                op("pe", lambda e: e.matmul(psx.ap()[:, 128:256], lhsT=T5[iX], rhs=T5[iXT], start=True, stop=True), reads=[tk_(iXT), tk_(iX)], writes=[pkx])
                            op("act", lambda e: e.copy(out=T5[iXn], in_=psx.ap()[:, 0:128]), reads=[pkx], writes=[tk_(iXn)])
                            if lev < 5:
                                op("act", lambda e: e.copy(out=T5[iXTn], in_=psx.ap()[:, 128:256]), reads=[pkx], writes=[tk_(iXTn)])
                            st[gi_] = [iXn, iXTn, iX, iXT]
                        yield
                        for gi_, (h, c) in enumerate(group):
                            T5 = tmp_t[gi_ * 5:(gi_ + 1) * 5]
                            tk_ = lambda j: ("tmp", gi_, j)
                            iX = st[gi_][0]
                            psq, pkq = psr.next()
                            op("pe", lambda e: e.matmul(psq.ap()[:, 0:128], lhsT=T5[iX], rhs=T5[4], start=True, stop=True), reads=[tk_(iX), tk_(4)], writes=[pkq])
                            if lev < 5:
                                op("dve", lambda e: e.tensor_tensor(out=T5[4], in0=T5[4], in1=psq.ap()[:, 0:128], op=ALU.add), reads=[tk_(4), pkq], writes=[tk_(4)])
                            else:
                                op("dve", lambda e: e.tensor_tensor(out=RES[("TT", h, c)], in0=T5[4], in1=psq.ap()[:, 0:128], op=ALU.add), reads=[tk_(4), pkq], writes=[("res", "TT", h, c)])
                        yield

                Hb = [None, None]

                def B_gen(chunks):
                    for c in chunks:
                        cs = slice(c * C, (c + 1) * C)
                        rk_ = lambda nm_, h: ("res", nm_, h, c)
                        wus = []
                        for h in range(2):
                            psw, pkw = psr.next()
                            if c > 0:
                                op("pe", lambda e: e.matmul(psw.ap()[:, 0:128], lhsT=Bh["ah"][h].ap()[:, cs], rhs=Hb[h][0].ap(), start=True, stop=False), reads=[("ah", h), Hb[h][1]], writes=[pkw])
                            op("pe", lambda e: e.matmul(psw.ap()[:, 0:128], lhsT=RES[("ak", h, c)], rhs=vpad[h].ap()[:, c, :], start=(c == 0), stop=True), reads=[rk_("ak", h), ("vpad", h)], writes=[pkw])
                            wu, wuk = wu_r.next()
                            op("act", lambda e: e.copy(out=wu.ap(), in_=psw.ap()[:, 0:128]), reads=[pkw], writes=[wuk])
                            wus.append((wu, wuk))
                        yield
                        Ups = []
                        for h in range(2):
                            psu, pku = psr.next()
                            op("pe", lambda e: e.matmul(psu.ap()[:, 0:128], lhsT=RES[("TT", h, c)], rhs=wus[h][0].ap(), start=True, stop=True), reads=[rk_("TT", h), wus[h][1]], writes=[pku])
                            up_, upk = upad[h].next()
                            op("dve", lambda e: e.tensor_copy(out=up_.ap(), in_=psu.ap()[:, 0:128]), reads=[pku, ("upad0", h, up_.name)], writes=[upk])
                            Ups.append((up_, upk))
                        yield
                        if c < NCH - 1:
                            for h in range(2):
                                psh, pkh = psr.next()
                                op("pe", lambda e: e.matmul(psh.ap()[:, 0:128], lhsT=RES[("bcT", h, c)], rhs=Ups[h][0].ap(), start=True, stop=False), reads=[rk_("bcT", h), Ups[h][1]], writes=[pkh])
                                op("pe", lambda e: e.matmul(psh.ap()[:, 0:128], lhsT=RES[("kcT", h, c)], rhs=vpad[h].ap()[:, c, :], start=False, stop=True), reads=[rk_("kcT", h), ("vpad", h)], writes=[pkh])
                                if c == 0:
                                    op("dve", lambda e: e.tensor_copy(out=H[h].ap(), in_=psh.ap()[:, 0:128]), reads=[pkh], writes=[("H", h)])
                                else:
                                    op("dve", lambda e: e.scalar_tensor_tensor(out=H[h].ap(), in0=H[h].ap(), scalar=dcol.ap()[:, 2, c:c + 1], in1=psh.ap()[:, 0:128], op0=ALU.mult, op1=ALU.add),
                                       reads=[("H", h), pkh] + dck, writes=[("H", h)])
                        psy, pky = psr.next()
                        first = True
                        for h in range(2):
                            if c > 0:
                                op("pe", lambda e: e.matmul(psy.ap()[:, 0:128], lhsT=Hb[h][0].ap(), rhs=Bh["rh"][h].ap()[:, cs], start=first, stop=False), reads=[Hb[h][1], ("rh", h)], writes=[pky]); first = False
                            op("pe", lambda e: e.matmul(psy.ap()[:, 0:128], lhsT=Ups[h][0].ap(), rhs=RES[("rb", h, c)], start=first, stop=False), reads=[Ups[h][1], rk_("rb", h)], writes=[pky]); first = False
                            op("pe", lambda e: e.matmul(psy.ap()[:, 0:128], lhsT=vpad[h].ap()[:, c, :], rhs=RES[("rk", h, c)], start=False, stop=(h == 1)), reads=[("vpad", h), rk_("rk", h)], writes=[pky])
                        op("act", lambda e: e.copy(out=ya.ap()[:, cs], in_=psy.ap()[:, 0:128]), reads=[pky], writes=[("ya", c // 4)])
                        if c < NCH - 1:
                            for h in range(2):
                                Hb[h] = Hb_r[h].next()
                                op("act", lambda e: e.copy(out=Hb[h][0].ap(), in_=H[h].ap()), reads=[("H", h)], writes=[Hb[h][1]])
                        yield

                def zip2(ga, gb):
                    da = db = False
                    while not (da and db):
                        if not da and next(ga, "END") == "END":
                            da = True
                        if not db and next(gb, "END") == "END":
                            db = True

                groups = [[(h, c) for c in range(g * 4, g * 4 + 4) for h in range(2)] for g in range(NCH // 4)]
                for _ in A_gen(groups[0]):
                    pass
                for g in range(len(groups)):
                    bg = B_gen(range(g * 4, g * 4 + 4))
                    if g + 1 < len(groups):
                        zip2(A_gen(groups[g + 1]), bg)
                    else:
                        for _ in bg:
                            pass
                k.phase_barrier()
                gA, gB = F["r"], F["k"]
                yall = [("ya", tb) for tb in range(NTB)]
                pss = [psr.next() for _ in range(NTB)]
                for tb in range(NTB):
                    op("pe", lambda e: e.matmul(pss[tb][0].ap(), lhsT=tri.ap()[:, 5, :], rhs=ya.ap()[:, tsl(tb)], start=True, stop=True), reads=[("ya", tb), "tri"], writes=[pss[tb][1]])
                for tb in range(NTB):
                    op("dve", lambda e: e.scalar_tensor_tensor(out=gA.ap()[:, tsl(tb)], in0=pss[tb][0].ap(), scalar=-1.0 / 64, in1=ya.ap()[:, tsl(tb)], op0=ALU.mult, op1=ALU.add),
                       reads=[pss[tb][1], ("ya", tb)], writes=[("r", tb)])
                op("act", lambda e: e.activation(out=gB.ap(), in_=gA.ap(), func=AF.Square), reads=allk("r"), writes=allk("k"))
                pss = [psr.next() for _ in range(NTB)]
                for tb in range(NTB):
                    op("pe", lambda e: e.matmul(pss[tb][0].ap(), lhsT=tri.ap()[:, 5, :], rhs=gB.ap()[:, tsl(tb)], start=True, stop=True), reads=[("k", tb), "tri"], writes=[pss[tb][1]])
                for tb in range(NTB):
                    op("act", lambda e: e.activation(out=gB.ap()[:, tsl(tb)], in_=pss[tb][0].ap(), func=AF.Ln, bias=gnc.ap(), scale=1.0 / 64), reads=[pss[tb][1], "gnc"], writes=[("k", tb)])
                op("act", lambda e: e.activation(out=gB.ap(), in_=gB.ap(), func=AF.Exp, scale=-0.5), reads=allk("k"), writes=allk("k"))
                op("dve", lambda e: e.tensor_tensor(out=gA.ap(), in0=gA.ap(), in1=gB.ap(), op=ALU.mult), reads=allk("r") + allk("k"), writes=allk("r"))
                op("dve", lambda e: e.tensor_scalar(out=gA.ap(), in0=gA.ap(), scalar1=prm(5), scalar2=prm(6), op0=ALU.mult, op1=ALU.add), reads=allk("r") + pcall, writes=allk("r"))
                op("dve", lambda e: e.tensor_tensor(out=gA.ap(), in0=gA.ap(), in1=F["v"].ap(), op=ALU.add), reads=allk("r") + allk("v"), writes=allk("r"))
                op("dve", lambda e: e.tensor_tensor(out=yo.ap(), in0=gA.ap(), in1=F["g"].ap(), op=ALU.mult), reads=allk("r") + allk("g"), writes=[("yo", tb) for tb in range(NTB)])
                k.dma(yT_dram[0, :, hp, :], yo.ap(), reads=[("yo", tb) for tb in range(NTB)], writes=[("yT_dram", 0, hp)])

    for l in range(DEPTH):
        with Scope(k) as sc:
            hT = sc.sb("hT", [128, KC, S], F32)
            sqr = sc.ring("sq", [128, TB], F32, 3)
            tmpr = sc.ring("nt", [128, TB], F32, 4)
            load_h(hT)
            rmsnorm(hT, 2 * l, u_dst, sqr, tmpr)
        if "yT_in0" not in dbg:
            mixer_rwkv(l)
        if "yT_in1" not in dbg:
            mixer_hgrn(l)
        if "yT_in2" not in dbg:
            mixer_dsa(l)
        with Scope(k) as sc:
            yT = sc.sb("yT", [128, 3, 4, S], BF16)
            mT = sc.sb("mT", [128, KC, S], BF16)
            stg_g = sc.ring("stg_g", [128, 3, KC, 128], F32, 2)
            wg_r = sc.ring("wg", [128, 3, KC, 128], BF16, 2)
            stg_b = sc.ring("stg_b", [128, 3, 4, 128], F32, 2)
            wb_r = sc.ring("wb", [128, 3, 4, 128], BF16, 2)
            sgr = sc.ring("sg", [128, TB], F32, 3)
            accr = sc.ring("acc", [128, TB], F32, 2)
            for n in range(3):
                k.dma(yT.ap()[:, n], yT_src[l][n], writes=[("yT", n)])
            for dc in range(KC):
                sg_t, sgk = stg_g.next(); wg, wgk = wg_r.next()
                sb_t, sbk = stg_b.next(); wb, wbk = wb_r.next()
                for n in range(3):
                    c0 = G_OFF + n * D + dc * 128
                    k.dma(sg_t.ap()[:, n], P["w_in"][l][:, c0:c0 + 128].rearrange("(c p) n -> p c n", p=128), writes=[(sgk, n)])
                    k.dma(sb_t.ap()[:, n], P["w_branch"][l, n][:, dc * 128:(dc + 1) * 128].rearrange("(c p) n -> p c n", p=128),
                          writes=[(sbk, n)])
                cast(wg.ap(), sg_t.ap(), [(sgk, n) for n in range(3)], [wgk])
                cast(wb.ap(), sb_t.ap(), [(sbk, n) for n in range(3)], [wbk])
                for tb in range(NTB):
                    acc, acck = accr.next()
                    for n in range(3):
                        pg, pgk = psr.next()
                        for kc in range(KC):
                            op("pe", lambda e: e.matmul(pg.ap(), lhsT=wg.ap()[:, n, kc, :], rhs=uT.ap()[:, kc, usl(tb)],
                                                        start=(kc == 0), stop=(kc == KC - 1)),
                               reads=[wgk] + ukeys(kc, tb), writes=[pgk])
                        pb, pbk = psr.next()
                        for c in range(4):
                            op("pe", lambda e: e.matmul(pb.ap(), lhsT=wb.ap()[:, n, c, :], rhs=yT.ap()[:, n, c, tsl(tb)],
                                                        start=(c == 0), stop=(c == 3)),
                               reads=[wbk, ("yT", n)], writes=[pbk])
                        sg, sgk2 = sgr.next()
                        op("act", lambda e: e.activation(out=sg.ap(), in_=pg.ap(), func=AF.Sigmoid), reads=[pgk], writes=[sgk2])
                        if n == 0:
                            op("dve", lambda e: e.tensor_tensor(out=acc.ap(), in0=sg.ap(), in1=pb.ap(), op=ALU.mult),
                               reads=[sgk2, pbk], writes=[acck])
                        else:
                            op("dve", lambda e: e.tensor_tensor(out=sg.ap(), in0=sg.ap(), in1=pb.ap(), op=ALU.mult),
                               reads=[sgk2, pbk], writes=[sgk2])
                            dstap = mT.ap()[:, dc, tsl(tb)] if n == 2 else acc.ap()
                            op("pool" if n == 1 else "dve", lambda e: e.tensor_tensor(out=dstap, in0=sg.ap(), in1=acc.ap(), op=ALU.add),
                               reads=[sgk2, acck], writes=[("mT", dc, tb)] if n == 2 else [acck])
            stg_o = sc.ring("stg_o", [128, KC, 128], F32, 2)
            wo_r = sc.ring("wo", [128, KC, 128], BF16, 2)
            hr = sc.ring("hr", [128, TB], F32, 3)
            for dc in range(KC):
                wo, wok = wload(P["w_o"][l][:, dc * 128:(dc + 1) * 128], KC, 128, stg_o, wo_r)
                for tb in range(NTB):
                    ht, htk = hr.next()
                    k.dma(ht.ap(), h_dram[:, dc, tsl(tb)], writes=[htk])
                    ps, pk = psr.next()
                    for c in range(KC):
                        op("pe", lambda e: e.matmul(ps.ap(), lhsT=wo.ap()[:, c, :], rhs=mT.ap()[:, c, tsl(tb)],
                                                    start=(c == 0), stop=(c == KC - 1)),
                           reads=[wok, ("mT", c, tb)], writes=[pk])
                    op("dve", lambda e: e.tensor_tensor(out=ht.ap(), in0=ht.ap(), in1=ps.ap(), op=ALU.add),
                       reads=[htk, pk], writes=[htk])
                    k.dma(h_dram[:, dc, tsl(tb)], ht.ap(), reads=[htk], writes=[("h_dram", dc, tb)])
        with Scope(k) as sc:
            hT = sc.sb("hT", [128, KC, S], F32)
            sc2 = Scope(k)
            sc_outer = sc; sc = sc2
            sqr = sc.ring("sq", [128, TB], F32, 2)
            tmpr = sc.ring("nt", [128, TB], F32, 3)
            load_h(hT)
            rmsnorm(hT, 2 * l + 1, u_dst, sqr, tmpr)
            cw = sc.sb("cw", [128, 3, 2 * NFF], F32)
            cb = sc.sb("cb", [128, 2 * NFF], F32)
            for j in range(3):
                k.dma(cw.ap()[:, j, :], P["conv_w"][l, j].rearrange("(c p) -> p c", p=128), writes=["cw"], allow_slow_non_contiguous=True)
            k.dma(cb.ap(), P["conv_b"][l].rearrange("(c p) -> p c", p=128), writes=["cb"], allow_slow_non_contiguous=True)
            G = 2
            stg_u = sc.ring("stg_u", [128, 2, KC, 128], F32, 2)
            wu_r = sc.ring("wu", [128, 2, KC, 128], BF16, 2)
            stg_d = sc.ring("stg_d", [128, G, D], F32, 1)
            wd_r = sc.ring("wd", [128, G, D], BF16, 2)
            upr = sc.ring("up", [128, 2, S + 2], F32, 1)
            cr = sc.ring("cc", [128, 2, TB], F32, 2)
            actr = sc.ring("actT", [128, G, S], BF16, 2)
            up, upk = upr.next()
            op("pool", lambda e: e.memset(up.ap()[:, :, 0:2], 0.0), writes=["uppad"])
            for g0 in range(0, NFF, G):
                actT, actk = actr.next()
                wd, wdk = wload(P["w_down"][l][g0 * 128:(g0 + G) * 128, :], G, D, stg_d, wd_r)
                for gi in range(G):
                    j = g0 + gi
                    su, suk = stg_u.next(); wu, wuk = wu_r.next()
                    for hv in range(2):
                        c0 = hv * DFF + j * 128
                        k.dma(su.ap()[:, hv], P["w_up"][l][:, c0:c0 + 128].rearrange("(c p) n -> p c n", p=128), writes=[(suk, hv)])
                    cast(wu.ap(), su.ap(), [(suk, 0), (suk, 1)], [wuk])
                    for tb in range(NTB):
                        cc, cck = cr.next()
                        for hv in range(2):
                            ps, pk = psr.next()
                            for kc in range(KC):
                                op("pe", lambda e: e.matmul(ps.ap(), lhsT=wu.ap()[:, hv, kc, :], rhs=uT.ap()[:, kc, usl(tb)],
                                                            start=(kc == 0), stop=(kc == KC - 1)),
                                   reads=[wuk] + ukeys(kc, tb), writes=[pk])
                            ch = hv * NFF + j
                            op("act", lambda e: e.copy(out=up.ap()[:, hv, usl(tb)], in_=ps.ap()), reads=[pk, "uppad"], writes=[("up", hv, tb)])
                            op("act", lambda e: e.activation(out=cc.ap()[:, hv, :], in_=ps.ap(), func=AF.Identity,
                                                             scale=cw.ap()[:, 2, ch:ch + 1], bias=cb.ap()[:, ch:ch + 1]),
                               reads=[pk, "cw", "cb"], writes=[(cck, hv)])
                            for sh in (1, 2):
                                rk = [("up", hv, tb)] + ([("up", hv, tb - 1)] if tb > 0 else [])
                                op("dve", lambda e: e.scalar_tensor_tensor(out=cc.ap()[:, hv, :], in0=up.ap()[:, hv, usl(tb, sh)],
                                                                           scalar=cw.ap()[:, 2 - sh, ch:ch + 1], in1=cc.ap()[:, hv, :],
                                                                           op0=ALU.mult, op1=ALU.add),
                                   reads=rk + ["cw", (cck, hv)], writes=[(cck, hv)])
                        op("act", lambda e: e.activation(out=cc.ap()[:, 0, :], in_=cc.ap()[:, 0, :], func=AF.Silu),
                           reads=[(cck, 0)], writes=[(cck, 0)])
                        op("pool", lambda e: e.tensor_tensor(out=actT.ap()[:, gi, tsl(tb)], in0=cc.ap()[:, 0, :], in1=cc.ap()[:, 1, :], op=ALU.mult),
                           reads=[(cck, 0), (cck, 1)], writes=[(actk, gi, tb)])
                for dc in range(KC):
                    for tb in range(NTB):
                        ps, pk = psr.next()
                        for gi in range(G):
                            op("pe", lambda e: e.matmul(ps.ap(), lhsT=wd.ap()[:, gi, dc * 128:(dc + 1) * 128], rhs=actT.ap()[:, gi, tsl(tb)],
                                                        start=(gi == 0), stop=(gi == G - 1)),
                               reads=[wdk, (actk, gi, tb)], writes=[pk])
                        op("dve", lambda e: e.tensor_tensor(out=hT.ap()[:, dc, tsl(tb)], in0=hT.ap()[:, dc, tsl(tb)], in1=ps.ap(), op=ALU.add),
                           reads=hkeys(dc, tb) + [pk], writes=hkeys(dc, tb))
            sc2.__exit__(None, None, None)
            sc = Scope(k)
            sqr = sc.ring("sq", [128, TB], F32, 2)
            tmpr = sc.ring("nt", [128, TB], F32, 3)
            if l < DEPTH - 1:
                store_h(hT)
            else:
                fo = sc.ring("fo", [128, KC, TB], F32, 1)
                ot = sc.ring("ot", [128, D], F32, 2)
                fT, fk = fo.next()
                cur = [0]

                def f_dst(kc, tb):
                    return fT.ap()[:, kc, :], [("fT", kc)]
                for tb in range(NTB):
                    ps, pk = psr.next()
                    for kc in range(KC):
                        sq, sk = sqr.next()
                        op("act", lambda e: e.activation(out=sq.ap(), in_=hT.ap()[:, kc, tsl(tb)], func=AF.Square),
                           reads=hkeys(kc, tb), writes=[sk])
                        op("pe", lambda e: e.matmul(ps.ap(), lhsT=ones.ap(), rhs=sq.ap(), start=(kc == 0), stop=(kc == KC - 1)),
                           reads=[sk, "ones"], writes=[pk])
                    sd, sdk = tmpr.next()
                    op("act", lambda e: e.activation(out=sd.ap(), in_=ps.ap(), func=AF.Ln, bias=epsc.ap(), scale=1.0 / D),
                       reads=[pk, "epsc"], writes=[sdk])
                    rs, rsk = tmpr.next()
                    op("act", lambda e: e.activation(out=rs.ap(), in_=sd.ap(), func=AF.Exp, scale=-0.5), reads=[sdk], writes=[rsk])
                    for kc in range(KC):
                        op("dve", lambda e: e.scalar_tensor_tensor(out=fT.ap()[:, kc, :], in0=hT.ap()[:, kc, tsl(tb)],
                                                                   scalar=gvec.ap()[:, 2 * DEPTH, kc:kc + 1], in1=rs.ap(),
                                                                   op0=ALU.mult, op1=ALU.mult),
                           reads=hkeys(kc, tb) + [rsk, "gvec"], writes=[("fT", kc)])
                    for t4 in range(4):
                        o, okk = ot.next()
                        for half in range(2):
                            ps2, pk2 = psr.next()
                            for j in range(4):
                                kc = half * 4 + j
                                op("pe", lambda e: e.transpose(out=ps2.ap()[:, j * 128:(j + 1) * 128],
                                                               in_=fT.ap()[:, kc, t4 * 128:(t4 + 1) * 128], identity=ident.ap()),
                                   reads=[("fT", kc), "ident"], writes=[pk2])
                            cast(o.ap()[:, half * 512:(half + 1) * 512], ps2.ap(), [pk2], [(okk, half)], eng=("dve", "act")[half])
                        r0 = tb * TB + t4 * 128
                        k.dma(out[r0:r0 + 128, :], o.ap(), reads=[(okk, 0), (okk, 1)], writes=[("out", r0)])
            sc.__exit__(None, None, None)
    k.drain("sp")
    return k


PARAM_NAMES = ["norm_mix_g", "w_in", "a_mu", "a_w0", "a_w_up", "a_a0", "a_a_up", "a_g_up", "a_k_k", "a_k_a", "a_r_k",
               "a_gn_g", "a_gn_b", "b_lb_logits", "b_gn_g", "w_branch", "w_o", "norm_ffn_g", "w_up", "conv_w", "conv_b",
               "w_down", "norm_final_g"]


def make_consts():
    i = np.arange(128)
    low = (i[:, None] >= i[None, :])
    bd = ((i[:, None] // 64) == (i[None, :] // 64))
    hm = np.zeros((128, 128)); hm[:64, 0] = 1; hm[64:, 1] = 1
    tri = np.stack([(i[:, None] <= i[None, :]), (i[:, None] < i[None, :]), low, np.where(low, 0.0, -1e30),
                    (i[:, None] > i[None, :]), bd, hm]).astype(np.float32)
    inv_freq = (np.float32(10000.0) ** (-np.arange(32, dtype=np.float32) * np.float32(2.0 / 64))).astype(np.float32)
    ang = (np.arange(S, dtype=np.float32)[:, None] * inv_freq[None, :]).astype(np.float32)
    cos = np.cos(ang).astype(np.float32).T; sin = np.sin(ang).astype(np.float32).T
    cosF = np.concatenate([cos, cos, cos, cos], axis=0)
    sinF = np.concatenate([-sin, sin, -sin, sin], axis=0)
    return {"c_ident": np.eye(128, dtype=np.float32), "c_tri": tri, "c_rope": np.stack([cosF, sinF]).astype(np.float32)}


def make_in_map(inputs, b, consts):
    m = {"x": np.ascontiguousarray(inputs["x"][b])}
    for n in PARAM_NAMES:
        a = np.ascontiguousarray(np.asarray(inputs[n], dtype=np.float32))
        if n == "a_r_k":
            a = a.reshape(DEPTH, 512)
        m[n] = a
    m.update(consts)
    return m


def kernel(**inputs):
    k = build()
    consts = make_consts()
    in_maps = [make_in_map(inputs, b, consts) for b in range(NB)]
    res = run_bass_kernel_spmd(k.nc, in_maps, core_ids=list(range(NB)))
    return np.stack([np.asarray(r["out"], dtype=np.float32) for r in res.results], axis=0)
```

```python
import numpy as np
import ml_dtypes
import concourse.bass as bass
import concourse.mybir as mybir
from concourse.bass_utils import run_bass_kernel_spmd

F32 = mybir.dt.float32
BF16 = mybir.dt.bfloat16
AF = mybir.ActivationFunctionType
ALU = mybir.AluOpType
AX = mybir.AxisListType

S = 2048; D = 1024; DEPTH = 2; NB = 8
TB = 512; NTB = S // TB; KC = D // 128
A_COLS = 1792; B_COLS = 2048; C_COLS = 1092; G_COLS = 3072; IN_COLS = 8004
A_OFF = 0; B_OFF = A_COLS; C_OFF = A_COLS + B_COLS; G_OFF = C_OFF + C_COLS
DFF = 2816; NFF = DFF // 128
NDMA = 24
NOSYNC_SAME = {"pe"}


class K:
    def __init__(self):
        self.nc = nc = bass.Bass("TRN2", target_bir_lowering=False)
        self.eng = {"pe": nc.tensor, "act": nc.scalar, "dve": nc.vector, "pool": nc.gpsimd, "sp": nc.sync}
        self.sem = {}
        self.cnt = {}
        for e in self.eng:
            self.sem[e] = nc.alloc_semaphore("s_" + e)
            self.cnt[e] = 0
        self.seen = {e: {} for e in self.eng}
        self.lastw = {}
        self.readers = {}
        self.dsem = [nc.alloc_semaphore("d%d" % i) for i in range(NDMA)]
        self.dval = [0] * NDMA
        self.di = 0
        self.nwait = 0
        self.ninst = 0

    def _wait(self, e, ev):
        name, sem, val, src = ev
        if src == e and e in NOSYNC_SAME:
            return
        if self.seen[e].get(name, 0) >= val:
            return
        self.eng[e].wait_ge(sem, val)
        self.nwait += 1
        self.seen[e][name] = val

    def _deps(self, e, reads, writes):
        for r in reads:
            ev = self.lastw.get(r)
            if ev is not None:
                self._wait(e, ev)
        for w in writes:
            ev = self.lastw.get(w)
            if ev is not None:
                self._wait(e, ev)
            for ev in self.readers.get(w, ()):
                self._wait(e, ev)

    def _record(self, ev, reads, writes):
        for w in writes:
            self.lastw[w] = ev
            self.readers[w] = []
        for r in reads:
            if r in writes:
                continue
            lst = self.readers.setdefault(r, [])
            lst[:] = [x for x in lst if x[0] != ev[0]]
            lst.append(ev)

    def op(self, e, fn, reads=(), writes=()):
        self._deps(e, reads, writes)
        ins = fn(self.eng[e])
        self.cnt[e] += 1
        ins.then_inc(self.sem[e], 1)
        ev = ("E" + e, self.sem[e], self.cnt[e], e)
        self.seen[e]["E" + e] = max(self.seen[e].get("E" + e, 0), 0)
        self._record(ev, reads, writes)
        self.ninst += 1
        return ins

    def dma(self, out, in_, reads=(), writes=(), q="sp", **kw):
        self._deps(q, reads, writes)
        j = self.di % NDMA
        self.di += 1
        if self.dval[j] > 0:
            self._wait(q, ("D%d" % j, self.dsem[j], self.dval[j], None))
        self.dval[j] += 16
        self.eng[q].dma_start(out=out, in_=in_, **kw).then_inc(self.dsem[j], 16)
        ev = ("D%d" % j, self.dsem[j], self.dval[j], None)
        self._record(ev, reads, writes)
        self.ninst += 1

    def drain(self, e="sp"):
        for e2 in self.eng:
            if e2 != e and self.cnt[e2] > 0:
                self._wait(e, ("E" + e2, self.sem[e2], self.cnt[e2], e2))
        for j in range(NDMA):
            if self.dval[j] > 0:
                self._wait(e, ("D%d" % j, self.dsem[j], self.dval[j], None))


    def phase_barrier(self):
        self.drain("sp")
        self.nc.all_engine_barrier()
        self.lastw.clear()
        self.readers.clear()


class Scope:
    def __init__(self, k):
        self.k = k
        self.stack = []

    def __enter__(self):
        return self

    _uid = [0]

    def sb(self, name, shape, dtype):
        Scope._uid[0] += 1
        g = self.k.nc.sbuf_tensor("%s_u%d" % (name, Scope._uid[0]), list(shape), dtype)
        t = g.__enter__()
        self.stack.append(g)
        return t

    def ring(self, name, shape, dtype, n):
        return Ring([self.sb("%s_%d" % (name, i), shape, dtype) for i in range(n)], name)

    def __exit__(self, *a):
        self.k.phase_barrier()
        for g in reversed(self.stack):
            g.__exit__(None, None, None)
        return False


class Ring:
    def __init__(self, tensors, name):
        self.t = tensors
        self.name = name
        self.i = 0

    def next(self):
        j = self.i % len(self.t)
        self.i += 1
        return self.t[j], (self.name, j)


def build(dbg=None):
    dbg = dbg or {}
    k = K()
    nc = k.nc
    op = k.op
    uid = [0]

    def din(name, shape, dt=F32):
        return nc.dram_tensor(name, list(shape), dt, kind="ExternalInput").ap()

    x = din("x", [S, D])
    P = {}
    for name, shape in [
        ("norm_mix_g", [DEPTH, D]), ("w_in", [DEPTH, D, IN_COLS]), ("a_mu", [DEPTH, A_COLS]), ("a_w0", [DEPTH, 512]),
        ("a_w_up", [DEPTH, 64, 512]), ("a_a0", [DEPTH, 512]), ("a_a_up", [DEPTH, 64, 512]), ("a_g_up", [DEPTH, 128, 512]),
        ("a_k_k", [DEPTH, 512]), ("a_k_a", [DEPTH, 512]), ("a_r_k", [DEPTH, 512]), ("a_gn_g", [DEPTH, 512]),
        ("a_gn_b", [DEPTH, 512]), ("b_lb_logits", [DEPTH, 512]), ("b_gn_g", [DEPTH, 512]),
        ("w_branch", [DEPTH, 3, 512, D]), ("w_o", [DEPTH, D, D]), ("norm_ffn_g", [DEPTH, D]),
        ("w_up", [DEPTH, D, 2 * DFF]), ("conv_w", [DEPTH, 3, 2 * DFF]), ("conv_b", [DEPTH, 2 * DFF]),
        ("w_down", [DEPTH, DFF, D]), ("norm_final_g", [D]),
    ]:
        P[name] = din(name, shape)
    c_ident = din("c_ident", [128, 128])
    out = nc.dram_tensor("out", [S, D], F32, kind="ExternalOutput").ap()
    h_dram = nc.dram_tensor("h_scr", [128, KC, S], F32, kind="Internal").ap()
    yT_dram = nc.dram_tensor("yT_scr", [3, 128, 4, S], BF16, kind="ExternalOutput" if dbg.get("dump_yT") else "Internal").ap()
    c_tri = din("c_tri", [7, 128, 128])
    c_rope = din("c_rope", [2, 128, S])
    yT_src = [[yT_dram[n] for n in range(3)] for l in range(DEPTH)]
    for n in range(3):
        if ("yT_in%d" % n) in dbg:
            t = din("yT_in%d" % n, [DEPTH, 128, 4, S], BF16)
            for l in range(DEPTH):
                yT_src[l][n] = t[l]

    ident = nc.alloc_sbuf_tensor("ident", [128, 128], F32)
    identb = nc.alloc_sbuf_tensor("identb", [128, 128], BF16)
    ones = nc.alloc_sbuf_tensor("ones", [128, 128], F32)
    epsc = nc.alloc_sbuf_tensor("epsc", [128, 1], F32)
    uT = nc.alloc_sbuf_tensor("uT", [128, KC, S + 2], BF16)
    gvec = nc.alloc_sbuf_tensor("gvec", [128, 2 * DEPTH + 1, KC], F32)
    psr = Ring([nc.alloc_psum_tensor("ps%d" % i, [128, 512], F32) for i in range(6)], "ps")
    psacc = Ring([nc.alloc_psum_tensor("psacc%d" % i, [128, 512], F32) for i in range(2)], "psacc")

    tri = nc.alloc_sbuf_tensor("tri", [128, 7, 128], F32)
    k.dma(tri.ap(), c_tri.rearrange("a p f -> p a f"), writes=["tri"])
    k.dma(ident.ap(), c_ident, writes=["ident"])
    pv_ring = Ring([nc.alloc_sbuf_tensor("pvstg%d" % i, [128, 128], F32) for i in range(2)], "pvstg")

    def load_pvec(dst, src1d, n, wkeys):
        stg, sk = pv_ring.next()
        k.dma(stg.ap()[0:n, :], src1d.rearrange("(c p) -> c p", p=128), writes=[sk])
        ps, pk = psr.next()
        op("pe", lambda e: e.transpose(out=ps.ap()[:, 0:n], in_=stg.ap()[0:n, :], identity=ident.ap()[0:n, 0:n]), reads=[sk, "ident"], writes=[pk])
        op("dve", lambda e: e.tensor_copy(out=dst, in_=ps.ap()[:, 0:n]), reads=[pk], writes=wkeys)

    op("dve", lambda e: e.tensor_copy(out=identb.ap(), in_=ident.ap()), reads=["ident"], writes=["identb"])
    op("pool", lambda e: e.memset(ones.ap(), 1.0), writes=["ones"])
    op("pool", lambda e: e.memset(epsc.ap(), 1e-6), writes=["epsc"])
    op("pool", lambda e: e.memset(uT.ap()[:, :, 0:2], 0.0), writes=["uTpad"])
    for l in range(DEPTH):
        load_pvec(gvec.ap()[:, 2 * l, :], P["norm_mix_g"][l], KC, ["gvec"])
        load_pvec(gvec.ap()[:, 2 * l + 1, :], P["norm_ffn_g"][l], KC, ["gvec"])
    load_pvec(gvec.ap()[:, 2 * DEPTH, :], P["norm_final_g"], KC, ["gvec"])

    def hkeys(kc=None, tb=None):
        return [("hT", a, b) for a in (range(KC) if kc is None else [kc]) for b in (range(NTB) if tb is None else [tb])]

    def ukeys(kc=None, tb=None):
        return [("uT", a, b) for a in (range(KC) if kc is None else [kc]) for b in (range(NTB) if tb is None else [tb])]

    def tsl(tb):
        return slice(tb * TB, (tb + 1) * TB)

    def usl(tb, shift=0):
        return slice(2 + tb * TB - shift, 2 + (tb + 1) * TB - shift)

    def rmsnorm(hT, gi, dst_fn, sqr, tmpr):
        for tb in range(NTB):
            ps, pk = psr.next()
            for kc in range(KC):
                sq, sk = sqr.next()
                op("act", lambda e: e.activation(out=sq.ap(), in_=hT.ap()[:, kc, tsl(tb)], func=AF.Square),
                   reads=hkeys(kc, tb), writes=[sk])
                op("pe", lambda e: e.matmul(ps.ap(), lhsT=ones.ap(), rhs=sq.ap(), start=(kc == 0), stop=(kc == KC - 1)),
                   reads=[sk, "ones"], writes=[pk])
            sd, sdk = tmpr.next()
            op("act", lambda e: e.activation(out=sd.ap(), in_=ps.ap(), func=AF.Ln, bias=epsc.ap(), scale=1.0 / D),
               reads=[pk, "epsc"], writes=[sdk])
            rs, rsk = tmpr.next()
            op("act", lambda e: e.activation(out=rs.ap(), in_=sd.ap(), func=AF.Exp, scale=-0.5), reads=[sdk], writes=[rsk])
            for kc in range(KC):
                dst, dk = dst_fn(kc, tb)
                op("dve", lambda e: e.scalar_tensor_tensor(out=dst, in0=hT.ap()[:, kc, tsl(tb)],
                                                           scalar=gvec.ap()[:, gi, kc:kc + 1], in1=rs.ap(),
                                                           op0=ALU.mult, op1=ALU.mult),
                   reads=hkeys(kc, tb) + [rsk, "gvec"], writes=dk)

    def u_dst(kc, tb):
        return uT.ap()[:, kc, usl(tb)], ukeys(kc, tb)

    castsel = [0]

    def cast(dst, src, reads, writes, eng=None):
        ce = eng or "act"
        castsel[0] += 1
        if ce == "act":
            op("act", lambda e: e.copy(out=dst, in_=src), reads=reads, writes=writes)
        else:
            op(ce, lambda e: e.tensor_copy(out=dst, in_=src), reads=reads, writes=writes)

    def wload(src, nk, ncol, stg_ring, w_ring, q="sp"):
        stg, sk = stg_ring.next()
        wt, wk = w_ring.next()
        k.dma(stg.ap()[:, 0:nk, 0:ncol], src.rearrange("(c p) n -> p c n", p=128), writes=[sk], q=q)
        cast(wt.ap()[:, 0:nk, 0:ncol], stg.ap()[:, 0:nk, 0:ncol], [sk], [wk])
        return wt, wk

    def load_h(hT):
        for kc in range(KC):
            k.dma(hT.ap()[:, kc, :], h_dram[:, kc, :], reads=["h_dram"], writes=hkeys(kc))

    def store_h(hT):
        for kc in range(KC):
            k.dma(h_dram[:, kc, :], hT.ap()[:, kc, :], reads=hkeys(kc), writes=[("h_dram", kc)])

    with Scope(k) as sc:
        xs = sc.ring("xs", [128, D], F32, 2)
        xt = sc.ring("xt", [128, KC, 128], F32, 2)
        for tt in range(S // 128):
            a, ak = xs.next()
            b, bk = xt.next()
            k.dma(a.ap(), x[tt * 128:(tt + 1) * 128, :], writes=[ak])
            for half in range(2):
                ps, pk = psr.next()
                for j in range(4):
                    kc = half * 4 + j
                    op("pe", lambda e: e.transpose(out=ps.ap()[:, j * 128:(j + 1) * 128], in_=a.ap()[:, kc * 128:(kc + 1) * 128],
                                                   identity=ident.ap()), reads=[ak, "ident"], writes=[pk])
                cast(b.ap()[:, half * 4:half * 4 + 4, :], ps.ap().rearrange("p (a b) -> p a b", a=4), [pk], [(bk, half)],
                     eng=("dve", "act")[half])
            k.dma(h_dram[:, :, tt * 128:(tt + 1) * 128], b.ap(), reads=[(bk, 0), (bk, 1)], writes=[("h_dram", "x", tt)])


    def proj_fm(l, c0, ncol, sc_rings, consume, shift_w=None):
        stg_r, w_r = sc_rings
        wt, wk = wload(P["w_in"][l][:, c0:c0 + ncol], KC, ncol, stg_r, w_r)
        for tb in range(NTB):
            ps, pk = psr.next()
            for kc in range(KC):
                op("pe", lambda e: e.matmul(ps.ap()[0:ncol, :], lhsT=wt.ap()[:, kc, 0:ncol], rhs=uT.ap()[:, kc, usl(tb)],
                                            start=(kc == 0), stop=(kc == KC - 1)),
                   reads=[wk] + ukeys(kc, tb), writes=[pk])
            consume(tb, ps, pk)

    def mixer_hgrn(l):
        C = 64; NCH = S // C
        with Scope(k) as sc:
            stg_r = sc.ring("hstg", [128, KC, 128], F32, 2)
            w_r = sc.ring("hw", [128, KC, 128], BF16, 2)
            rings = (stg_r, w_r)
            lbt = sc.sb("lbt", [128, 2, 4], F32)
            lb = sc.sb("lb", [128, 4], F32)
            oml = sc.sb("oml", [128, 4], F32)
            noml = sc.sb("noml", [128, 4], F32)
            gng = sc.sb("gng", [128, 4], F32)
            onesS = sc.sb("onesS", [128, S], F32)
            Bc = sc.sb("Bc", [128, S], F32)
            sig = sc.sb("sig", [128, S], F32)
            bp = sc.sb("bp", [128, S], F32)
            Ex = sc.sb("Ex", [128, S], F32)
            ktl = sc.sb("ktl", [128, S], F32)
            qt = sc.sb("qt", [128, S], BF16)
            kt = sc.sb("kt", [128, S], BF16)
            qh = sc.sb("qh", [128, S], BF16)
            kcf = sc.sb("kcf", [128, S], F32)
            vtok = sc.sb("vtok", [64, NCH, 128], BF16)
            ysb = sc.sb("ysb", [128, S], F32)
            gs = sc.sb("gs", [128, S], F32)
            yo = sc.sb("yo", [128, S], BF16)
            dcol = sc.sb("dcol", [128, 4, NCH], F32)
            st = sc.sb("st", [128, 128], F32)
            stb_r = sc.ring("stb", [128, 128], BF16, 2)
            pt_r = sc.ring("pt", [64, 64], BF16, 4)
            kct_r = sc.ring("kct", [64, 128], BF16, 4)
            sq_r = sc.ring("hsq", [128, TB], F32, 2)
            nt_r = sc.ring("hnt", [128, TB], F32, 2)
            op("pool", lambda e: e.memset(onesS.ap(), 1.0), writes=["onesS"])
            for l_ in range(DEPTH):
                load_pvec(lbt.ap()[:, l_, :], P["b_lb_logits"][l_], 4, ["lbt"])
            load_pvec(gng.ap(), P["b_gn_g"][l], 4, ["gng"])
            if l == 0:
                op("dve", lambda e: e.tensor_tensor(out=lb.ap(), in0=lbt.ap()[:, 0, :], in1=lbt.ap()[:, 0, :], op=ALU.subtract),
                   reads=["lbt"], writes=["lb"])
            else:
                op("dve", lambda e: e.tensor_tensor(out=lb.ap(), in0=lbt.ap()[:, 1, :], in1=lbt.ap()[:, 0, :], op=ALU.subtract),
                   reads=["lbt"], writes=["lb"])
                op("act", lambda e: e.activation(out=lb.ap(), in_=lb.ap(), func=AF.Sigmoid), reads=["lb"], writes=["lb"])
            op("dve", lambda e: e.tensor_scalar(out=oml.ap(), in0=lb.ap(), scalar1=-1.0, scalar2=1.0, op0=ALU.mult, op1=ALU.add),
               reads=["lb"], writes=["oml"])
            op("dve", lambda e: e.tensor_scalar(out=noml.ap(), in0=oml.ap(), scalar1=-1.0, scalar2=None, op0=ALU.mult),
               reads=["oml"], writes=["noml"])
            for h in range(4):
                cq = B_OFF + h * 128; cf = B_OFF + 512 + h * 128; ci = B_OFF + 1024 + h * 128; cg = B_OFF + 1536 + h * 128
                def f_consume(tb, ps, pk):
                    op("act", lambda e: e.activation(out=sig.ap()[:, tsl(tb)], in_=ps.ap(), func=AF.Sigmoid), reads=[pk], writes=[("sig", tb)])
                proj_fm(l, cf, 128, rings, f_consume)
                allsig = [("sig", tb) for tb in range(NTB)]
                op("dve", lambda e: e.tensor_scalar(out=ktl.ap(), in0=sig.ap(), scalar1=noml.ap()[:, h:h + 1], scalar2=oml.ap()[:, h:h + 1],
                                                    op0=ALU.mult, op1=ALU.add), reads=allsig + ["noml", "oml"], writes=["ktl"])
                op("dve", lambda e: e.tensor_scalar(out=sig.ap(), in0=sig.ap(), scalar1=oml.ap()[:, h:h + 1], scalar2=lb.ap()[:, h:h + 1],
                                                    op0=ALU.mult, op1=ALU.add), reads=allsig + ["oml", "lb"], writes=allsig)
                op("act", lambda e: e.activation(out=sig.ap(), in_=sig.ap(), func=AF.Ln), reads=allsig, writes=allsig)
                op("dve", lambda e: e.tensor_tensor_scan(out=Bc.ap(), data0=onesS.ap(), data1=sig.ap(), initial=0.0, op0=ALU.mult, op1=ALU.add),
                   reads=allsig + ["onesS"], writes=["Bc"])
                B3 = Bc.ap().rearrange("p (c t) -> p c t", t=C)
                op("dve", lambda e: e.tensor_tensor(out=bp.ap().rearrange("p (c t) -> p c t", t=C), in0=B3,
                                                    in1=B3[:, :, C // 2 - 1:C // 2].to_broadcast([128, NCH, C]), op=ALU.subtract),
                   reads=["Bc"], writes=["bp"] + [("bpq", tb) for tb in range(NTB)])
                op("dve", lambda e: e.tensor_tensor(out=dcol.ap()[:, 0, :], in0=B3[:, :, C - 1], in1=B3[:, :, C // 2 - 1], op=ALU.subtract),
                   reads=["Bc"], writes=[("dcol", 0)])
                op("dve", lambda e: e.tensor_copy(out=dcol.ap()[:, 1, 0:1], in_=B3[:, 0, C // 2 - 1:C // 2]), reads=["Bc"], writes=[("dcol", 1)])
                op("dve", lambda e: e.tensor_tensor(out=dcol.ap()[:, 1, 1:NCH], in0=B3[:, 1:NCH, C // 2 - 1], in1=B3[:, 0:NCH - 1, C - 1], op=ALU.subtract),
                   reads=["Bc", ("dcol", 1)], writes=[("dcol", 1)])
                op("dve", lambda e: e.tensor_tensor(out=dcol.ap()[:, 2, :], in0=dcol.ap()[:, 0, :], in1=dcol.ap()[:, 1, :], op=ALU.add),
                   reads=[("dcol", 0), ("dcol", 1)], writes=[("dcol", 2)])
                op("act", lambda e: e.activation(out=dcol.ap()[:, 0:3, :], in_=dcol.ap()[:, 0:3, :], func=AF.Exp),
                   reads=[("dcol", 0), ("dcol", 1), ("dcol", 2)], writes=[("dcol", 0), ("dcol", 1), ("dcol", 2)])
                op("act", lambda e: e.activation(out=Ex.ap(), in_=bp.ap(), func=AF.Exp, scale=-1.0), reads=["bp"], writes=["Ex"])
                op("dve", lambda e: e.tensor_tensor(out=ktl.ap(), in0=ktl.ap(), in1=Ex.ap(), op=ALU.mult), reads=["ktl", "Ex"], writes=["ktl"])
                op("act", lambda e: e.copy(out=kt.ap(), in_=ktl.ap()), reads=["ktl"], writes=["kt"])
                op("dve", lambda e: e.tensor_tensor(out=kcf.ap().rearrange("p (c t) -> p c t", t=C), in0=ktl.ap().rearrange("p (c t) -> p c t", t=C),
                                                    in1=dcol.ap()[:, 0, :].unsqueeze(2).to_broadcast([128, NCH, C]), op=ALU.mult),
                   reads=["ktl", ("dcol", 0)], writes=["kcf"])
                op("act", lambda e: e.activation(out=Ex.ap(), in_=bp.ap(), func=AF.Exp), reads=["bp"], writes=["Ex"])

                def q_consume(tb, ps, pk):
                    op("dve", lambda e: e.tensor_tensor(out=bp.ap()[:, tsl(tb)], in0=Ex.ap()[:, tsl(tb)], in1=ps.ap(), op=ALU.mult),
                       reads=["Ex", pk, "bp"], writes=[("bpq", tb)])
                proj_fm(l, cq, 128, rings, q_consume)
                allq = [("bpq", tb) for tb in range(NTB)]
                op("act", lambda e: e.copy(out=qt.ap(), in_=bp.ap()), reads=allq, writes=["qt"])
                op("dve", lambda e: e.tensor_tensor(out=qh.ap().rearrange("p (c t) -> p c t", t=C), in0=bp.ap().rearrange("p (c t) -> p c t", t=C),
                                                    in1=dcol.ap()[:, 1, :].unsqueeze(2).to_broadcast([128, NCH, C]), op=ALU.mult),
                   reads=allq + [("dcol", 1)], writes=["qh"])

                def g_consume(tb, ps, pk):
                    op("act", lambda e: e.activation(out=gs.ap()[:, tsl(tb)], in_=ps.ap(), func=AF.Silu), reads=[pk], writes=[("gs", tb)])
                proj_fm(l, cg, 128, rings, g_consume)
                wt, wk = wload(P["w_in"][l][:, ci:ci + 128], KC, 128, stg_r, w_r)
                for c4 in range(NCH // 4):
                    ps, pk = psr.next()
                    for j in range(4):
                        c = c4 * 4 + j
                        for kc in range(KC):
                            op("pe", lambda e: e.matmul(ps.ap()[0:C, j * 128:(j + 1) * 128], lhsT=uT.ap()[:, kc, 2 + c * C: 2 + (c + 1) * C],
                                                        rhs=wt.ap()[:, kc, :], start=(kc == 0), stop=(kc == KC - 1)),
                               reads=[wk] + ukeys(kc, (c * C) // TB), writes=[pk])
                    cast(vtok.ap()[:, c4 * 4:(c4 + 1) * 4, :], ps.ap()[0:C, :].rearrange("p (a b) -> p a b", a=4), [pk], [("vtok", c4)], eng="act")
                def pre(c):
                    cs = slice(c * C, (c + 1) * C)
                    ps_s, pk_s = psr.next()
                    op("pe", lambda e: e.matmul(ps_s.ap()[0:C, 0:C], lhsT=kt.ap()[:, cs], rhs=qt.ap()[:, cs], start=True, stop=True),
                       reads=["kt", "qt"], writes=[pk_s])
                    pt, ptk = pt_r.next()
                    op("dve", lambda e: e.tensor_tensor(out=pt.ap(), in0=ps_s.ap()[0:C, 0:C], in1=tri.ap()[0:C, 0, 0:C], op=ALU.mult),
                       reads=[pk_s, "tri"], writes=[ptk])
                    ps_t, pk_t = psr.next()
                    op("pe", lambda e: e.transpose(out=ps_t.ap()[0:C, 0:128], in_=kcf.ap()[:, cs], identity=ident.ap()),
                       reads=["kcf", "ident"], writes=[pk_t])
                    kct, kctk = kct_r.next()
                    op("act", lambda e: e.copy(out=kct.ap(), in_=ps_t.ap()[0:C, 0:128]), reads=[pk_t], writes=[kctk])
                    return pt, ptk, kct, kctk
                stb = None
                nxt_pre = pre(0)
                for c in range(NCH):
                    cs = slice(c * C, (c + 1) * C)
                    pt, ptk, kct, kctk = nxt_pre
                    if c + 1 < NCH:
                        nxt_pre = pre(c + 1)
                    if c < NCH - 1:
                        ps_d, pk_d = psacc.next()
                        op("pe", lambda e: e.matmul(ps_d.ap()[:, 0:128], lhsT=kct.ap(), rhs=vtok.ap()[:, c, :], start=True, stop=True),
                           reads=[kctk, ("vtok", c // 4)], writes=[pk_d])
                    ps_y, pk_y = psr.next()
                    if c > 0:
                        op("pe", lambda e: e.matmul(ps_y.ap()[:, 0:C], lhsT=stb[0].ap(), rhs=qh.ap()[:, cs], start=True, stop=False),
                           reads=[stb[1], "qh"], writes=[pk_y])
                    op("pe", lambda e: e.matmul(ps_y.ap()[:, 0:C], lhsT=vtok.ap()[:, c, :], rhs=pt.ap(), start=(c == 0), stop=True),
                       reads=[("vtok", c // 4), ptk], writes=[pk_y])
                    op("act", lambda e: e.copy(out=ysb.ap()[:, cs], in_=ps_y.ap()[:, 0:C]), reads=[pk_y], writes=[("ysb", c // 8)])
                    if c < NCH - 1:
                        if c == 0:
                            op("dve", lambda e: e.tensor_copy(out=st.ap(), in_=ps_d.ap()[:, 0:128]), reads=[pk_d], writes=["st"])
                        else:
                            op("dve", lambda e: e.scalar_tensor_tensor(out=st.ap(), in0=st.ap(), scalar=dcol.ap()[:, 2, c:c + 1], in1=ps_d.ap()[:, 0:128],
                                                                       op0=ALU.mult, op1=ALU.add), reads=["st", pk_d, ("dcol", 2)], writes=["st"])
                        stb = stb_r.next()
                        op("act", lambda e: e.copy(out=stb[0].ap(), in_=st.ap()), reads=["st"], writes=[stb[1]])
                for tb in range(NTB):
                    sq, sk = sq_r.next()
                    yk = [("ysb", tb * 2), ("ysb", tb * 2 + 1)]
                    op("act", lambda e: e.activation(out=sq.ap(), in_=ysb.ap()[:, tsl(tb)], func=AF.Square), reads=yk, writes=[sk])
                    ps, pk = psr.next()
                    op("pe", lambda e: e.matmul(ps.ap(), lhsT=ones.ap(), rhs=sq.ap(), start=True, stop=True), reads=[sk, "ones"], writes=[pk])
                    sd, sdk = nt_r.next()
                    op("act", lambda e: e.activation(out=sd.ap(), in_=ps.ap(), func=AF.Ln, bias=epsc.ap(), scale=1.0 / 128), reads=[pk, "epsc"], writes=[sdk])
                    op("act", lambda e: e.activation(out=sd.ap(), in_=sd.ap(), func=AF.Exp, scale=-0.5), reads=[sdk], writes=[sdk])
                    op("dve", lambda e: e.scalar_tensor_tensor(out=sd.ap(), in0=ysb.ap()[:, tsl(tb)], scalar=gng.ap()[:, h:h + 1], in1=sd.ap(),
                                                               op0=ALU.mult, op1=ALU.mult), reads=yk + [sdk, "gng"], writes=[sdk])
                    op("dve", lambda e: e.tensor_tensor(out=yo.ap()[:, tsl(tb)], in0=sd.ap(), in1=gs.ap()[:, tsl(tb)], op=ALU.mult),
                       reads=[sdk, ("gs", tb)], writes=[("yo", tb)])
                k.dma(yT_dram[1, :, h, :], yo.ap(), reads=[("yo", tb) for tb in range(NTB)], writes=[("yT_dram", 1, h)])


    def mixer_dsa(l):
        NQ = S // 128
        with Scope(k) as sc:
            qT = sc.sb("qT", [128, 4, S], BF16)
            kTz = sc.sb("kTz", [128, 4, S], BF16)
            qiT = sc.sb("qiT", [128, 2, S], BF16)
            kiTz = sc.sb("kiTz", [128, 2, S], BF16)
            vtok = sc.sb("cvtok", [128, 2, NQ, 128], BF16)
            wi = sc.sb("wi", [128, NQ, 4], F32)
            wabs = sc.sb("wabs", [128, NQ, 4], F32)
            wsg = sc.sb("wsg", [128, NQ, 4], F32)
            yo = sc.sb("cyo", [128, 4, S], BF16)
            onesb = sc.sb("onesb", [128, 128], BF16)
            rt_r = sc.ring("rt", [128, TB], F32, 3)
            op("pool", lambda e: e.memset(onesb.ap(), 1.0), writes=["onesb"])
            with Scope(k) as s2:
                stg_r = s2.ring("cstg", [128, KC, 128], F32, 2)
                w_r = s2.ring("cw", [128, KC, 128], BF16, 2)
                wsw_r = s2.ring("cwsw", [128, KC, 128], BF16, 2)
                rope = s2.sb("rope", [128, 2, S], F32)
                k.dma(rope.ap(), c_rope.rearrange("a p s -> p a s"), writes=["rope"])

                def wload_cols(col_list, swap):
                    stg, sk = stg_r.next()
                    wt, wk = w_r.next()
                    o = 0
                    for (c0, n) in col_list:
                        if c0 is None:
                            op("pool", lambda e: e.memset(stg.ap()[:, :, o:o + n], 0.0), writes=[(sk, o), sk])
                        else:
                            k.dma(stg.ap()[:, :, o:o + n], P["w_in"][l][:, c0:c0 + n].rearrange("(c p) n -> p c n", p=128), writes=[(sk, o), sk])
                        o += n
                    rk = [(sk, oo) for oo in np.cumsum([0] + [n for _, n in col_list[:-1]]).tolist()] + [sk]
                    cast(wt.ap()[:, :, 0:o], stg.ap()[:, :, 0:o], rk, [wk])
                    if not swap:
                        return wt, wk, None, None
                    ws, wsk = wsw_r.next()
                    for b0 in range(0, o, 64):
                        cast(ws.ap()[:, :, b0:b0 + 32], stg.ap()[:, :, b0 + 32:b0 + 64], rk, [(wsk, b0), wsk])
                        cast(ws.ap()[:, :, b0 + 32:b0 + 64], stg.ap()[:, :, b0:b0 + 32], rk, [(wsk, b0 + 32), wsk])
                    return wt, wk, ws, [(wsk, b0) for b0 in range(0, o, 32)] + [wsk]

                def rope_proj(col_list, dst_fn, dkey):
                    wt, wk, ws, wsk = wload_cols(col_list, True)
                    for tb in range(NTB):
                        p1, pk1 = psr.next()
                        p2, pk2 = psr.next()
                        for kc in range(KC):
                            op("pe", lambda e: e.matmul(p1.ap(), lhsT=wt.ap()[:, kc, :], rhs=uT.ap()[:, kc, usl(tb)], start=(kc == 0), stop=(kc == KC - 1)),
                               reads=[wk] + ukeys(kc, tb), writes=[pk1])
                        for kc in range(KC):
                            op("pe", lambda e: e.matmul(p2.ap(), lhsT=ws.ap()[:, kc, :], rhs=uT.ap()[:, kc, usl(tb)], start=(kc == 0), stop=(kc == KC - 1)),
                               reads=wsk + ukeys(kc, tb), writes=[pk2])
                        t1, tk1 = rt_r.next()
                        t2, tk2 = rt_r.next()
                        op("dve", lambda e: e.tensor_tensor(out=t1.ap(), in0=rope.ap()[:, 0, tsl(tb)], in1=p1.ap(), op=ALU.mult), reads=["rope", pk1], writes=[tk1])
                        op("dve", lambda e: e.tensor_tensor(out=t2.ap(), in0=rope.ap()[:, 1, tsl(tb)], in1=p2.ap(), op=ALU.mult), reads=["rope", pk2], writes=[tk2])
                        op("pool", lambda e: e.tensor_tensor(out=dst_fn(tb), in0=t1.ap(), in1=t2.ap(), op=ALU.add), reads=[tk1, tk2], writes=[(dkey, tb)])

                co = C_OFF
                for ch in range(4):
                    rope_proj([(co + ch * 128, 128)], lambda tb: qT.ap()[:, ch, tsl(tb)], ("qT", ch))
                for c in range(2):
                    rope_proj([(co + 512 + c * 64, 64), (None, 64)], lambda tb: kTz.ap()[:, c * 2, tsl(tb)], ("kTz", c * 2))
                    rope_proj([(None, 64), (co + 512 + c * 64, 64)], lambda tb: kTz.ap()[:, c * 2 + 1, tsl(tb)], ("kTz", c * 2 + 1))
                for ch in range(2):
                    rope_proj([(co + 768 + ch * 128, 128)], lambda tb: qiT.ap()[:, ch, tsl(tb)], ("qiT", ch))
                rope_proj([(co + 1024, 64), (None, 64)], lambda tb: kiTz.ap()[:, 0, tsl(tb)], ("kiTz", 0))
                rope_proj([(None, 64), (co + 1024, 64)], lambda tb: kiTz.ap()[:, 1, tsl(tb)], ("kiTz", 1))
                for c in range(2):
                    wt, wk, _, _ = wload_cols([(co + 640 + c * 64, 64), (co + 640 + c * 64, 64)], False)
                    for s4 in range(NQ // 4):
                        ps, pk = psr.next()
                        for j in range(4):
                            sb = s4 * 4 + j
                            for kc in range(KC):
                                op("pe", lambda e: e.matmul(ps.ap()[:, j * 128:(j + 1) * 128], lhsT=uT.ap()[:, kc, 2 + sb * 128: 2 + (sb + 1) * 128],
                                                            rhs=wt.ap()[:, kc, :], start=(kc == 0), stop=(kc == KC - 1)),
                                   reads=[wk] + ukeys(kc, sb // 4), writes=[pk])
                        cast(vtok.ap()[:, c, s4 * 4:(s4 + 1) * 4, :], ps.ap().rearrange("p (a b) -> p a b", a=4), [pk], [("cvtok", c, s4)], eng="act")
                wt, wk, _, _ = wload_cols([(co + 1088, 4)], False)
                ps, pk = psr.next()
                for qb in range(NQ):
                    for kc in range(KC):
                        op("pe", lambda e: e.matmul(ps.ap()[:, qb * 4:(qb + 1) * 4], lhsT=uT.ap()[:, kc, 2 + qb * 128: 2 + (qb + 1) * 128],
                                                    rhs=wt.ap()[:, kc, 0:4], start=(kc == 0), stop=(kc == KC - 1)),
                           reads=[wk] + ukeys(kc, qb // 4), writes=[pk])
                op("dve", lambda e: e.tensor_copy(out=wi.ap(), in_=ps.ap()[:, 0:NQ * 4].rearrange("p (a b) -> p a b", b=4)), reads=[pk], writes=["wi"])
                op("act", lambda e: e.activation(out=wabs.ap(), in_=wi.ap(), func=AF.Abs), reads=["wi"], writes=["wabs"])
                op("act", lambda e: e.activation(out=wsg.ap(), in_=wi.ap(), func=AF.Sign), reads=["wi"], writes=["wsg"])
            NCHN = 4
            score2 = [sc.sb("score%d" % i, [128, S], F32) for i in range(NCHN)]
            Mf2 = [sc.sb("Mf%d" % i, [128, S], BF16) for i in range(NCHN)]
            MT2 = [sc.sb("MT%d" % i, [128, NQ, 128], BF16) for i in range(NCHN)]
            m82 = [sc.sb("m8_%d" % i, [128, 8], F32) for i in range(NCHN)]
            e_r = sc.ring("cE", [128, 4, 128], BF16, 3)
            p_r = sc.ring("cP", [128, 4, 128], BF16, 3)
            rec_r = sc.ring("crec", [128, TB], F32, 2)
            SENT = -3e30

            def scores(qb, i):
                score = score2[i]
                ncol = (qb + 1) * 128
                qs = slice(qb * 128, (qb + 1) * 128)
                for g0 in range(0, ncol, TB):
                    gn = min(TB, ncol - g0)
                    for hi in range(4):
                        ps, pk = psr.next()
                        op("pe", lambda e: e.matmul(ps.ap()[:, 0:gn], lhsT=qiT.ap()[:, hi // 2, qs], rhs=kiTz.ap()[:, hi % 2, g0:g0 + gn], start=True, stop=True),
                           reads=[(("qiT", hi // 2), qb // 4)] + [(("kiTz", hi % 2), tb) for tb in range(g0 // TB, (g0 + gn - 1) // TB + 1)], writes=[pk])
                        r, rk = rt_r.next()
                        op("act", lambda e: e.activation(out=r.ap()[:, 0:gn], in_=ps.ap()[:, 0:gn], func=AF.Relu, scale=wabs.ap()[:, qb, hi:hi + 1]),
                           reads=[pk, "wabs"], writes=[rk])
                        if hi == 0:
                            op("dve", lambda e: e.tensor_scalar(out=score.ap()[:, g0:g0 + gn], in0=r.ap()[:, 0:gn], scalar1=wsg.ap()[:, qb, hi:hi + 1],
                                                                scalar2=None, op0=ALU.mult), reads=[rk, "wsg"], writes=[("score", i, g0)])
                        else:
                            op("dve", lambda e: e.scalar_tensor_tensor(out=score.ap()[:, g0:g0 + gn], in0=r.ap()[:, 0:gn], scalar=wsg.ap()[:, qb, hi:hi + 1],
                                                                       in1=score.ap()[:, g0:g0 + gn], op0=ALU.mult, op1=ALU.add),
                               reads=[rk, "wsg", ("score", i, g0)], writes=[("score", i, g0)])
                sk_all = [("score", i, g0) for g0 in range(0, S, TB)]
                op("dve", lambda e: e.tensor_tensor(out=score.ap()[:, qs], in0=score.ap()[:, qs], in1=tri.ap()[:, 2, :], op=ALU.mult), reads=sk_all + ["tri"], writes=sk_all)
                op("dve", lambda e: e.tensor_tensor(out=score.ap()[:, qs], in0=score.ap()[:, qs], in1=tri.ap()[:, 3, :], op=ALU.add), reads=sk_all + ["tri"], writes=sk_all)

            NIT = 28
            W0 = 8192.0
            bs_S = [sc.sb("bsS%d" % i, [128, 1], F32) for i in range(NCHN)]
            bs_inc = [sc.sb("bsI%d" % i, [128, 1], F32) for i in range(NCHN)]
            bs_mid = [sc.sb("bsM%d" % i, [128, 1], F32) for i in range(NCHN)]
            junkA = sc.sb("junkA", [128, S], BF16)
            junkD = sc.sb("junkD", [128, S], BF16)

            def topk_gen(blocks):
                sk_all = lambda i: [("score", i, g0) for g0 in range(0, S, TB)]
                act = [(i, qb, (qb + 1) * 128) for i, qb in enumerate(blocks) if (qb + 1) * 128 > 256]
                on_dve = lambda i: (i == NCHN - 1)
                for i, qb, ncol in act:
                    op("dve", lambda e: e.memset(bs_mid[i].ap(), 0.0), writes=[("bsM", i)])
                w = W0
                for it in range(NIT if act else 0):
                    for i, qb, ncol in act:
                        if on_dve(i):
                            op("dve", lambda e: e.tensor_scalar(out=junkD.ap()[:, 0:ncol], in0=score2[i].ap()[:, 0:ncol], scalar1=bs_mid[i].ap(), scalar2=None,
                                                                op0=ALU.is_ge, op1=ALU.add, accum_out=bs_S[i].ap()),
                               reads=sk_all(i) + [("bsM", i)], writes=[("bsS", i)])
                        else:
                            op("act", lambda e: e.activation(out=junkA.ap()[:, 0:ncol], in_=score2[i].ap()[:, 0:ncol], func=AF.Sign, bias=bs_mid[i].ap(), scale=1.0,
                                                             accum_out=bs_S[i].ap()),
                               reads=sk_all(i) + [("bsM", i)], writes=[("bsS", i)])
                    for i, qb, ncol in act:
                        kthr = 256.0 if on_dve(i) else float(512 - ncol)
                        op("dve", lambda e: e.tensor_scalar(out=bs_inc[i].ap(), in0=bs_S[i].ap(), scalar1=kthr, scalar2=w / 2, op0=ALU.is_ge, op1=ALU.mult),
                           reads=[("bsS", i)], writes=[("bsI", i)])
                        if on_dve(i):
                            op("dve", lambda e: e.scalar_tensor_tensor(out=bs_mid[i].ap(), in0=bs_mid[i].ap(), scalar=-w / 4, in1=bs_inc[i].ap(), op0=ALU.add, op1=ALU.add),
                               reads=[("bsM", i), ("bsI", i)], writes=[("bsM", i)])
                        else:
                            op("dve", lambda e: e.scalar_tensor_tensor(out=bs_mid[i].ap(), in0=bs_mid[i].ap(), scalar=w / 4, in1=bs_inc[i].ap(), op0=ALU.add, op1=ALU.subtract),
                               reads=[("bsM", i), ("bsI", i)], writes=[("bsM", i)])
                    w = w / 2
                    yield
                for i, qb in enumerate(blocks):
                    ncol = (qb + 1) * 128
                    if ncol > 256:
                        if on_dve(i):
                            op("dve", lambda e: e.tensor_scalar(out=bs_inc[i].ap(), in0=bs_mid[i].ap(), scalar1=-w / 2, scalar2=None, op0=ALU.add), reads=[("bsM", i)], writes=[("bsI", i)])
                        else:
                            op("dve", lambda e: e.tensor_scalar(out=bs_inc[i].ap(), in0=bs_mid[i].ap(), scalar1=-1.0, scalar2=-w / 2, op0=ALU.mult, op1=ALU.add), reads=[("bsM", i)], writes=[("bsI", i)])
                        op("dve", lambda e: e.tensor_scalar(out=Mf2[i].ap()[:, 0:ncol], in0=score2[i].ap()[:, 0:ncol], scalar1=bs_inc[i].ap(), scalar2=None, op0=ALU.is_ge),
                           reads=sk_all(i) + [("bsI", i)], writes=[("Mf", i)])
                    else:
                        op("dve", lambda e: e.tensor_scalar(out=Mf2[i].ap()[:, 0:ncol], in0=score2[i].ap()[:, 0:ncol], scalar1=-1e29, scalar2=None, op0=ALU.is_ge),
                           reads=sk_all(i), writes=[("Mf", i)])
                yield

            def maskT(qb, i):
                for s4 in range(0, qb + 1, 4):
                    nb = min(4, qb + 1 - s4)
                    ps, pk = psr.next()
                    for j in range(nb):
                        sb = s4 + j
                        op("pe", lambda e: e.transpose(out=ps.ap().bitcast(BF16)[:, j * 128:(j + 1) * 128], in_=Mf2[i].ap()[:, sb * 128:(sb + 1) * 128], identity=identb.ap()),
                           reads=[("Mf", i), "identb"], writes=[pk])
                    op("act", lambda e: e.copy(out=MT2[i].ap()[:, s4:s4 + nb, :], in_=ps.ap().bitcast(BF16)[:, 0:nb * 128].rearrange("p (a b) -> p a b", b=128)),
                       reads=[pk], writes=[("MT", i, s4)])

            def attn_gen(qb, i):
                qs = slice(qb * 128, (qb + 1) * 128)
                MT = MT2[i]
                for c in range(2):
                    ps_o, pk_o = psacc.next()
                    ps_d, pk_d = psacc.next()

                    def front(sb):
                        ps, pk = psr.next()
                        for j in range(4):
                            hq = c * 4 + j
                            op("pe", lambda e: e.matmul(ps.ap()[:, j * 128:(j + 1) * 128], lhsT=kTz.ap()[:, c * 2 + hq % 2, sb * 128:(sb + 1) * 128],
                                                        rhs=qT.ap()[:, hq // 2, qs], start=True, stop=True),
                               reads=[(("kTz", c * 2 + hq % 2), sb // 4), (("qT", hq // 2), qb // 4)], writes=[pk])
                        E, Ek = e_r.next()
                        op("act", lambda e: e.activation(out=E.ap(), in_=ps.ap().rearrange("p (a b) -> p a b", a=4), func=AF.Exp, scale=0.125), reads=[pk], writes=[Ek])
                        Pm, Pk = p_r.next()
                        me = "pool" if (sb % 3) else "dve"
                        op(me, lambda e: e.tensor_tensor(out=Pm.ap(), in0=E.ap(), in1=MT.ap()[:, sb:sb + 1, :].to_broadcast([128, 4, 128]), op=ALU.mult),
                           reads=[Ek, ("MT", i, (sb // 4) * 4)], writes=[Pk])
                        return Pm, Pk
                    nxt_p = front(0)
                    for sb in range(qb + 1):
                        Pm, Pk = nxt_p
                        if sb + 1 <= qb:
                            nxt_p = front(sb + 1)
                        op("pe", lambda e: e.matmul(ps_o.ap(), lhsT=vtok.ap()[:, c, sb, :], rhs=Pm.ap().rearrange("p a b -> p (a b)"), start=(sb == 0), stop=(sb == qb)),
                           reads=[("cvtok", c, sb // 4), Pk], writes=[pk_o])
                        op("pe", lambda e: e.matmul(ps_d.ap(), lhsT=onesb.ap(), rhs=Pm.ap().rearrange("p a b -> p (a b)"), start=(sb == 0), stop=(sb == qb)),
                           reads=["onesb", Pk], writes=[pk_d])
                        yield
                    rec, reck = rec_r.next()
                    op("dve", lambda e: e.reciprocal(out=rec.ap(), in_=ps_d.ap()), reads=[pk_d], writes=[reck])
                    for j in range(4):
                        hq = c * 4 + j
                        hb = (hq % 2) * 64
                        op("dve", lambda e: e.tensor_tensor(out=yo.ap()[hb:hb + 64, hq // 2, qs], in0=rec.ap()[hb:hb + 64, j * 128:(j + 1) * 128],
                                                            in1=ps_o.ap()[hb:hb + 64, j * 128:(j + 1) * 128], op=ALU.mult),
                           reads=[reck, pk_o], writes=[("cyo", hq // 2, hq % 2)])

            def run(g):
                for _ in g:
                    pass

            def zipgens(gt, ga_list, n_t, n_a):
                ga = (x for g in ga_list for x in g)
                per = max(1, -(-n_a // max(1, n_t)))
                done_a = False
                for _ in gt:
                    for _j in range(per):
                        if next(ga, "END") == "END":
                            done_a = True
                            break
                if not done_a:
                    for _ in ga:
                        pass

            grps = [tuple(range(g * NCHN, (g + 1) * NCHN)) for g in range(NQ // NCHN)]
            for i_, qb_ in enumerate(grps[0]):
                scores(qb_, i_)
            run(topk_gen(grps[0]))
            for i_, qb_ in enumerate(grps[0]):
                maskT(qb_, i_)
            for p in range(len(grps)):
                cur = grps[p]
                if p + 1 < len(grps):
                    nxt = grps[p + 1]
                    for i_, qb_ in enumerate(nxt):
                        scores(qb_, i_)
                    n_a = sum(2 * (q_ + 1) for q_ in cur)
                    zipgens(topk_gen(nxt), [attn_gen(q_, i_) for i_, q_ in enumerate(cur)], NIT + 1, n_a)
                    for i_, qb_ in enumerate(nxt):
                        maskT(qb_, i_)
                else:
                    for i_, q_ in enumerate(cur):
                        run(attn_gen(q_, i_))
            for ch in range(4):
                k.dma(yT_dram[2, :, ch, :], yo.ap()[:, ch, :], reads=[("cyo", ch, 0), ("cyo", ch, 1)], writes=[("yT_dram", 2, ch)])

    def mixer_rwkv(l):
        C = 128; NCH = S // C
        with Scope(k) as sc:
            stg_r = sc.ring("astg", [128, KC, 128], F32, 1)
            stm_r = sc.ring("astm", [128, KC, 128], F32, 1)
            w1_r = sc.ring("aw1", [128, KC, 128], BF16, 1)
            wm_r = sc.ring("awm", [128, KC, 128], BF16, 1)
            mub_r = sc.ring("amub", [128, 128], F32, 2)
            lrb = sc.sb("lrb", [128, 3, 512], BF16)
            with Scope(k) as s2:
                lrs = s2.sb("lrs", [128, 3, 512], F32)
                k.dma(lrs.ap()[0:64, 0, :], P["a_w_up"][l], writes=["lrs0"]); k.dma(lrs.ap()[0:64, 1, :], P["a_a_up"][l], writes=["lrs1"])
                k.dma(lrs.ap()[:, 2, :], P["a_g_up"][l], writes=["lrs2"])
                op("dve", lambda e: e.tensor_copy(out=lrb.ap()[0:64, 0:2, :], in_=lrs.ap()[0:64, 0:2, :]), reads=["lrs0", "lrs1"], writes=["lrb01"])
                op("dve", lambda e: e.tensor_copy(out=lrb.ap()[:, 2, :], in_=lrs.ap()[:, 2, :]), reads=["lrs2"], writes=["lrb2"])

            def proj_shift(c0, ncol, consume, token_major=None):
                stg, sk = stg_r.next(); stm, smk = stm_r.next(); w1, w1k = w1_r.next(); wm, wmk = wm_r.next(); mub, mk = mub_r.next()
                k.dma(stg.ap()[:, :, 0:ncol], P["w_in"][l][:, c0:c0 + ncol].rearrange("(c p) n -> p c n", p=128), writes=[sk])
                k.dma(mub.ap()[:, 0:ncol], P["a_mu"][l][c0 - A_OFF:c0 - A_OFF + ncol].partition_broadcast(128), writes=[mk])
                op("dve", lambda e: e.tensor_tensor(out=stm.ap()[:, :, 0:ncol], in0=stg.ap()[:, :, 0:ncol],
                                                    in1=mub.ap()[:, 0:ncol].unsqueeze(1).to_broadcast([128, KC, ncol]), op=ALU.mult),
                   reads=[sk, mk], writes=[smk])
                op("act", lambda e: e.copy(out=wm.ap()[:, :, 0:ncol], in_=stm.ap()[:, :, 0:ncol]), reads=[smk], writes=[wmk])
                op("dve", lambda e: e.tensor_tensor(out=w1.ap()[:, :, 0:ncol], in0=stg.ap()[:, :, 0:ncol], in1=stm.ap()[:, :, 0:ncol], op=ALU.subtract),
                   reads=[sk, smk], writes=[w1k])
                if token_major is not None:
                    token_major(w1, w1k, wm, wmk)
                    return
                for tb in range(NTB):
                    ps, pk = psr.next()
                    for kc in range(KC):
                        op("pe", lambda e: e.matmul(ps.ap()[0:ncol, :], lhsT=w1.ap()[:, kc, 0:ncol], rhs=uT.ap()[:, kc, usl(tb)], start=(kc == 0), stop=False),
                           reads=[w1k] + ukeys(kc, tb), writes=[pk])
                    for kc in range(KC):
                        op("pe", lambda e: e.matmul(ps.ap()[0:ncol, :], lhsT=wm.ap()[:, kc, 0:ncol], rhs=uT.ap()[:, kc, usl(tb, 1)], start=False, stop=(kc == KC - 1)),
                           reads=[wmk] + ukeys(kc, tb) + (ukeys(kc, tb - 1) if tb else []) + ["uTpad"], writes=[pk])
                    consume(tb, ps, pk)

            tw = sc.sb("tw", [64, S], BF16); adT = sc.sb("adT", [64, S], BF16); sgT = sc.sb("sgT", [128, S], BF16)
            proj_shift(A_OFF + 1536, 64, lambda tb, ps, pk: op("act", lambda e: e.activation(out=tw.ap()[:, tsl(tb)], in_=ps.ap()[0:64, :], func=AF.Tanh), reads=[pk], writes=[("tw", tb)]))
            proj_shift(A_OFF + 1600, 64, lambda tb, ps, pk: op("act", lambda e: e.copy(out=adT.ap()[:, tsl(tb)], in_=ps.ap()[0:64, :]), reads=[pk], writes=[("adT", tb)]))
            proj_shift(A_OFF + 1664, 128, lambda tb, ps, pk: op("act", lambda e: e.activation(out=sgT.ap()[:, tsl(tb)], in_=ps.ap(), func=AF.Sigmoid), reads=[pk], writes=[("sgT", tb)]))
            pc = sc.sb("pc", [128, 8, 4], F32)
            for i_, nm in enumerate(["a_w0", "a_a0", "a_k_k", "a_k_a", "a_r_k", "a_gn_g", "a_gn_b"]):
                load_pvec(pc.ap()[:, i_, :], P[nm][l], 4, [("pc", i_)])
            op("dve", lambda e: e.tensor_scalar(out=pc.ap()[:, 7, :], in0=pc.ap()[:, 3, :], scalar1=-1.0, scalar2=1.0, op0=ALU.mult, op1=ALU.add), reads=[("pc", 3)], writes=[("pc", 7)])
            pcall = [("pc", i_) for i_ in range(8)]
            gnc = sc.sb("gnc", [128, 1], F32)
            op("pool", lambda e: e.memset(gnc.ap(), 64e-5), writes=["gnc"])
            F = {n_: sc.sb("af_" + n_, [128, S], F32) for n_ in ["r", "k", "v", "a", "L", "lw", "t1", "t2"]}
            F["g"] = sc.sb("af_g", [128, S], BF16)
            Bh = {n_: [sc.sb("ab_%s%d" % (n_, h), [128, S], BF16) for h in range(2)] for n_ in ["bt", "kt", "bc", "kc"]}
            Bh["rh"] = [sc.sb("ab_rh", [128, S], BF16)] * 2
            Bh["ah"] = [sc.sb("ab_ah", [128, S], BF16)] * 2
            AR = sc.sb("AR", [128, NCH, 2, C], BF16)
            dcol = sc.sb("adcol", [128, 3, NCH], F32)
            vtok = sc.sb("avtok", [128, NCH, 128], BF16)
            vpad = [sc.sb("avpad%d" % h, [128, NCH, 128], BF16) for h in range(2)]
            H = [sc.sb("aH%d" % h, [128, 128], F32) for h in range(2)]
            Hb_r = [sc.ring("aHb%d" % h, [128, 128], BF16, 2) for h in range(2)]
            upad = [sc.ring("aupad%d" % h, [128, 128], BF16, 2) for h in range(2)]
            mt_r = sc.ring("amt", [128, 128], BF16, 8)
            kp_r = sc.ring("akp", [128, 128], BF16, 12)
            wu_r = sc.ring("awu", [128, 128], BF16, 2)
            tk_r = sc.ring("atk", [128, 128], BF16, 6)
            ya = F["t1"]
            yo = Bh["rh"][0]
            hm = lambda h: tri.ap()[:, 6, h:h + 1]
            for h in range(2):
                for r_ in upad[h].t:
                    op("pool", lambda e: e.memset(r_.ap(), 0.0), writes=[("upad0", h, r_.name)])
            for hp in range(4):
                co = A_OFF + hp * 128
                prm = lambda i_: pc.ap()[:, i_, hp:hp + 1]
                for nm, cc in (("r", co), ("k", co + 512), ("v", co + 1024)):
                    proj_shift(cc, 128, lambda tb, ps, pk: op("act", lambda e: e.copy(out=F[nm].ap()[:, tsl(tb)], in_=ps.ap()), reads=[pk], writes=[(nm, tb)]))
                allk = lambda nm: [(nm, tb) for tb in range(NTB)]

                def vtm(w1, w1k, wm, wmk):
                    for c4 in range(NCH // 4):
                        ps, pk = psr.next()
                        for j in range(4):
                            c = c4 * 4 + j
                            for kc in range(KC):
                                op("pe", lambda e: e.matmul(ps.ap()[:, j * 128:(j + 1) * 128], lhsT=uT.ap()[:, kc, 2 + c * C:2 + (c + 1) * C], rhs=w1.ap()[:, kc, :], start=(kc == 0), stop=False),
                                   reads=[w1k] + ukeys(kc, c // 4), writes=[pk])
                            for kc in range(KC):
                                op("pe", lambda e: e.matmul(ps.ap()[:, j * 128:(j + 1) * 128], lhsT=uT.ap()[:, kc, 1 + c * C:1 + (c + 1) * C], rhs=wm.ap()[:, kc, :], start=False, stop=(kc == KC - 1)),
                                   reads=[wmk] + ukeys(kc, c // 4) + (ukeys(kc, c // 4 - 1) if c >= 4 else []) + ["uTpad"], writes=[pk])
                        cast(vtok.ap()[:, c4 * 4:(c4 + 1) * 4, :], ps.ap().rearrange("p (a b) -> p a b", a=4), [pk], [("avtok", c4)], eng="act")
                proj_shift(co + 1024, 128, None, token_major=vtm)
                vk = [("avtok", c4) for c4 in range(NCH // 4)]
                for h in range(2):
                    op("pool", lambda e: e.tensor_tensor(out=vpad[h].ap(), in0=vtok.ap(), in1=tri.ap()[:, 5, h * 64:h * 64 + 1].to_broadcast([128, NCH, 128]) if False else
                                                         tri.ap()[0:1, 5, :].partition_broadcast(128).unsqueeze(1).to_broadcast([128, NCH, 128]) if False else vtok.ap(), op=ALU.mult) if False else
                       e.memset(vpad[h].ap(), 0.0), reads=vk, writes=[("vpad", h)])
                    op("act", lambda e: e.copy(out=vpad[h].ap()[:, :, h * 64:(h + 1) * 64], in_=vtok.ap()[:, :, h * 64:(h + 1) * 64]), reads=vk + [("vpad", h)], writes=[("vpad", h)])
                for tb in range(NTB):
                    ps, pk = psr.next()
                    op("pe", lambda e: e.matmul(ps.ap(), lhsT=lrb.ap()[0:64, 0, hp * 128:(hp + 1) * 128], rhs=tw.ap()[:, tsl(tb)], start=True, stop=True), reads=["lrb01", ("tw", tb)], writes=[pk])
                    op("act", lambda e: e.activation(out=F["lw"].ap()[:, tsl(tb)], in_=ps.ap(), func=AF.Sigmoid, bias=prm(0)), reads=[pk] + pcall, writes=[("lw", tb)])
                    ps, pk = psr.next()
                    op("pe", lambda e: e.matmul(ps.ap(), lhsT=lrb.ap()[0:64, 1, hp * 128:(hp + 1) * 128], rhs=adT.ap()[:, tsl(tb)], start=True, stop=True), reads=["lrb01", ("adT", tb)], writes=[pk])
                    op("act", lambda e: e.activation(out=F["a"].ap()[:, tsl(tb)], in_=ps.ap(), func=AF.Sigmoid, bias=prm(1)), reads=[pk] + pcall, writes=[("a", tb)])
                    ps, pk = psr.next()
                    op("pe", lambda e: e.matmul(ps.ap(), lhsT=lrb.ap()[:, 2, hp * 128:(hp + 1) * 128], rhs=sgT.ap()[:, tsl(tb)], start=True, stop=True), reads=["lrb2", ("sgT", tb)], writes=[pk])
                    op("act", lambda e: e.copy(out=F["g"].ap()[:, tsl(tb)], in_=ps.ap()), reads=[pk], writes=[("g", tb)])
                op("dve", lambda e: e.tensor_scalar(out=F["lw"].ap(), in0=F["lw"].ap(), scalar1=-0.6065306597126334, scalar2=None, op0=ALU.mult), reads=allk("lw"), writes=allk("lw"))
                op("dve", lambda e: e.tensor_tensor_scan(out=F["L"].ap(), data0=ones.ap()[:, 0:1].to_broadcast([128, S]), data1=F["lw"].ap(), initial=0.0, op0=ALU.mult, op1=ALU.add), reads=allk("lw") + ["ones"], writes=["L"])
                L3 = F["L"].ap().rearrange("p (c t) -> p c t", t=C)
                op("dve", lambda e: e.tensor_tensor(out=dcol.ap()[:, 0, :], in0=L3[:, :, C - 1], in1=L3[:, :, C // 2 - 1], op=ALU.subtract), reads=["L"], writes=[("dc", 0)])
                op("dve", lambda e: e.tensor_copy(out=dcol.ap()[:, 1, 0:1], in_=L3[:, 0, C // 2 - 1:C // 2]), reads=["L"], writes=[("dc", 1)])
                op("dve", lambda e: e.tensor_tensor(out=dcol.ap()[:, 1, 1:NCH], in0=L3[:, 1:NCH, C // 2 - 1], in1=L3[:, 0:NCH - 1, C - 1], op=ALU.subtract), reads=["L", ("dc", 1)], writes=[("dc", 1)])
                op("dve", lambda e: e.tensor_tensor(out=dcol.ap()[:, 2, :], in0=dcol.ap()[:, 0, :], in1=dcol.ap()[:, 1, :], op=ALU.add), reads=[("dc", 0), ("dc", 1)], writes=[("dc", 2)])
                dck = [("dc", 0), ("dc", 1), ("dc", 2)]
                op("act", lambda e: e.activation(out=dcol.ap(), in_=dcol.ap(), func=AF.Exp), reads=dck, writes=dck)
                t1, t2 = F["t1"], F["t2"]
                op("dve", lambda e: e.tensor_scalar(out=t1.ap(), in0=F["k"].ap(), scalar1=prm(2), scalar2=None, op0=ALU.mult), reads=allk("k") + pcall, writes=["t1"])
                op("act", lambda e: e.activation(out=t2.ap(), in_=t1.ap(), func=AF.Square), reads=["t1"], writes=["t2"])
                pss = [psr.next() for _ in range(NTB)]
                for tb in range(NTB):
                    op("pe", lambda e: e.matmul(pss[tb][0].ap(), lhsT=tri.ap()[:, 5, :], rhs=t2.ap()[:, tsl(tb)], start=True, stop=True), reads=["t2", "tri"], writes=[pss[tb][1]])
                for tb in range(NTB):
                    op("dve", lambda e: e.tensor_scalar(out=t2.ap()[:, tsl(tb)], in0=pss[tb][0].ap(), scalar1=1e-19, scalar2=None, op0=ALU.max), reads=[pss[tb][1], "t2"], writes=["t2"])
                op("act", lambda e: e.activation(out=t2.ap(), in_=t2.ap(), func=AF.Ln), reads=["t2"], writes=["t2"])
                op("act", lambda e: e.activation(out=t2.ap(), in_=t2.ap(), func=AF.Exp, scale=-0.5), reads=["t2"], writes=["t2"])
                op("dve", lambda e: e.tensor_tensor(out=t1.ap(), in0=t1.ap(), in1=t2.ap(), op=ALU.mult), reads=["t1", "t2"], writes=["t1"])
                op("dve", lambda e: e.tensor_scalar(out=t2.ap(), in0=F["a"].ap(), scalar1=prm(3), scalar2=prm(7), op0=ALU.mult, op1=ALU.add), reads=allk("a") + pcall, writes=["t2"])
                op("dve", lambda e: e.tensor_tensor(out=F["k"].ap(), in0=F["k"].ap(), in1=t2.ap(), op=ALU.mult), reads=allk("k") + ["t2"], writes=allk("k"))
                op("dve", lambda e: e.tensor_tensor(out=F["a"].ap(), in0=F["a"].ap(), in1=t1.ap(), op=ALU.mult), reads=allk("a") + ["t1"], writes=allk("a"))
                op("dve", lambda e: e.scalar_tensor_tensor(out=t2.ap(), in0=F["r"].ap(), scalar=prm(4), in1=F["k"].ap(), op0=ALU.mult, op1=ALU.mult), reads=allk("r") + allk("k") + pcall, writes=["t2"])
                pss = [psr.next() for _ in range(NTB)]
                for tb in range(NTB):
                    op("pe", lambda e: e.matmul(pss[tb][0].ap(), lhsT=tri.ap()[:, 5, :], rhs=t2.ap()[:, tsl(tb)], start=True, stop=True), reads=["t2", "tri"], writes=[pss[tb][1]])
                for tb in range(NTB):
                    op("dve", lambda e: e.tensor_tensor(out=F["v"].ap()[:, tsl(tb)], in0=F["v"].ap()[:, tsl(tb)], in1=pss[tb][0].ap(), op=ALU.mult), reads=[("v", tb), pss[tb][1]], writes=[("v", tb)])
                m_b = L3[:, :, C // 2 - 1:C // 2].to_broadcast([128, NCH, C])
                v3 = lambda t_: t_.ap().rearrange("p (c t) -> p c t", t=C)
                op("dve", lambda e: e.tensor_tensor(out=v3(t2), in0=L3, in1=m_b, op=ALU.subtract), reads=["L"], writes=["t2"])
                op("dve", lambda e: e.tensor_tensor(out=F["lw"].ap(), in0=t2.ap(), in1=F["lw"].ap(), op=ALU.subtract), reads=["t2"] + allk("lw"), writes=allk("lw"))
                op("act", lambda e: e.activation(out=F["lw"].ap(), in_=F["lw"].ap(), func=AF.Exp), reads=allk("lw"), writes=allk("lw"))
                op("act", lambda e: e.activation(out=F["L"].ap(), in_=t2.ap(), func=AF.Exp, scale=-1.0), reads=["t2"], writes=["L"])
                op("act", lambda e: e.activation(out=t2.ap(), in_=t2.ap(), func=AF.Exp), reads=["t2"], writes=["t2"])
                op("dve", lambda e: e.tensor_tensor(out=F["r"].ap(), in0=F["r"].ap(), in1=t2.ap(), op=ALU.mult), reads=allk("r") + ["t2"], writes=allk("r"))
                op("act", lambda e: e.copy(out=AR.ap()[:, :, 1, :], in_=v3(F["r"])), reads=allk("r"), writes=["AR1"])
                dB = dcol.ap()[:, 1, :].unsqueeze(2).to_broadcast([128, NCH, C]); dA = dcol.ap()[:, 0, :].unsqueeze(2).to_broadcast([128, NCH, C])
                op("dve", lambda e: e.tensor_tensor(out=v3(t2), in0=v3(F["r"]), in1=dB, op=ALU.mult), reads=allk("r") + dck, writes=["t2"])
                op("act", lambda e: e.copy(out=Bh["rh"][0].ap(), in_=t2.ap()), reads=["t2"], writes=[("rh", 0), ("rh", 1)] + [("yo", tb) for tb in range(NTB)])
                op("dve", lambda e: e.scalar_tensor_tensor(out=F["lw"].ap(), in0=t1.ap(), scalar=-1.0, in1=F["lw"].ap(), op0=ALU.mult, op1=ALU.mult), reads=["t1"] + allk("lw"), writes=allk("lw"))
                op("act", lambda e: e.copy(out=AR.ap()[:, :, 0, :], in_=v3(F["lw"])), reads=allk("lw"), writes=["AR0"])
                op("dve", lambda e: e.tensor_tensor(out=v3(t2), in0=v3(F["lw"]), in1=dB, op=ALU.mult), reads=allk("lw") + dck + [("rh", 0), ("rh", 1)], writes=["t2"])
                op("act", lambda e: e.copy(out=Bh["ah"][0].ap(), in_=t2.ap()), reads=["t2"], writes=[("ah", 0), ("ah", 1)])
                for src, nt, ncn in ((F["a"], "bt", "bc"), (F["k"], "kt", "kc")):
                    sk_ = allk("a") if nt == "bt" else allk("k")
                    op("dve", lambda e: e.tensor_tensor(out=src.ap(), in0=src.ap(), in1=F["L"].ap(), op=ALU.mult), reads=sk_ + ["L"], writes=sk_)
                    op("dve", lambda e: e.tensor_tensor(out=v3(t1) if nt == "kt" else v3(t2), in0=v3(src), in1=dA, op=ALU.mult), reads=sk_ + dck + [("ah", 0), ("ah", 1)], writes=["t1" if nt == "kt" else "t2"])
                    tt_ = t1 if nt == "kt" else t2
                    for h in range(2):
                        op("act", lambda e: e.activation(out=Bh[nt][h].ap(), in_=src.ap(), func=AF.Copy, scale=hm(h)), reads=sk_ + ["tri"], writes=[(nt, h)])
                        op("act", lambda e: e.activation(out=Bh[ncn][h].ap(), in_=tt_.ap(), func=AF.Copy, scale=hm(h)), reads=["t1" if nt == "kt" else "t2", "tri"], writes=[(ncn, h)])
                k.phase_barrier()
                pool_t = []
                for nm_ in ("r", "k", "a", "L", "lw", "t2"):
                    vb = F[nm_].ap().bitcast(BF16)
                    pool_t += [vb[:, j * 128:(j + 1) * 128] for j in range(32)]
                RES = {}
                it_ = iter(pool_t)
                for c_ in range(NCH):
                    for h_ in range(2):
                        for nm_ in ("TT", "rb", "ak", "rk", "bcT", "kcT"):
                            RES[(nm_, h_, c_)] = next(it_)
                vt_b = vtok.ap().rearrange("p a b -> p (a b)")
                tmp_t = [t_.ap() for t_ in mt_r.t] + [t_.ap() for t_ in tk_r.t] + [t_.ap() for t_ in kp_r.t] + [vt_b[:, j * 128:(j + 1) * 128] for j in range(16)]
                GCH = 8
                assert len(tmp_t) >= 5 * GCH, len(tmp_t)

                def A_gen(group):
                    st = {}
                    for gi_, (h, c) in enumerate(group):
                        cs = slice(c * C, (c + 1) * C)
                        T5 = tmp_t[gi_ * 5:(gi_ + 1) * 5]
                        tk_ = lambda j: ("tmp", gi_, j)
                        ps, pk = psr.next()
                        op("pe", lambda e: e.matmul(ps.ap()[:, 0:256], lhsT=Bh["bt"][h].ap()[:, cs], rhs=AR.ap()[:, c].rearrange("p a b -> p (a b)"), start=True, stop=True), reads=[("bt", h), "AR0", "AR1"], writes=[pk])
                        op("dve", lambda e: e.tensor_tensor(out=T5[1], in0=ps.ap()[:, 0:128], in1=tri.ap()[:, 1, :], op=ALU.mult), reads=[pk, "tri"], writes=[tk_(1)])
                        op("dve", lambda e: e.tensor_tensor(out=RES[("rb", h, c)], in0=ps.ap()[:, 128:256], in1=tri.ap()[:, 0, :], op=ALU.mult), reads=[pk, "tri"], writes=[("res", "rb", h, c)])
                        ps, pk = psr.next()
                        op("pe", lambda e: e.matmul(ps.ap()[:, 0:256], lhsT=Bh["kt"][h].ap()[:, cs], rhs=AR.ap()[:, c].rearrange("p a b -> p (a b)"), start=True, stop=True), reads=[("kt", h), "AR0", "AR1"], writes=[pk])
                        op("dve", lambda e: e.tensor_tensor(out=RES[("ak", h, c)], in0=ps.ap()[:, 0:128], in1=tri.ap()[:, 1, :], op=ALU.mult), reads=[pk, "tri"], writes=[("res", "ak", h, c)])
                        op("dve", lambda e: e.tensor_tensor(out=RES[("rk", h, c)], in0=ps.ap()[:, 128:256], in1=tri.ap()[:, 0, :], op=ALU.mult), reads=[pk, "tri"], writes=[("res", "rk", h, c)])
                        ps, pk = psr.next()
                        op("pe", lambda e: e.matmul(ps.ap()[:, 0:128], lhsT=AR.ap()[:, c, 0, :], rhs=Bh["bt"][h].ap()[:, cs], start=True, stop=True), reads=["AR0", ("bt", h)], writes=[pk])
                        op("dve", lambda e: e.tensor_tensor(out=T5[0], in0=ps.ap()[:, 0:128], in1=tri.ap()[:, 4, :], op=ALU.mult), reads=[pk, "tri"], writes=[tk_(0)])
                        op("dve", lambda e: e.tensor_tensor(out=T5[4], in0=T5[1], in1=identb.ap(), op=ALU.add), reads=[tk_(1), "identb"], writes=[tk_(4)])
                        pst, pkt = psr.next()
                        op("pe", lambda e: e.transpose(out=pst.ap().bitcast(BF16)[:, 0:128], in_=Bh["bc"][h].ap()[:, cs], identity=identb.ap()), reads=[("bc", h), "identb"], writes=[pkt])
                        op("pe", lambda e: e.transpose(out=pst.ap().bitcast(BF16)[:, 128:256], in_=Bh["kc"][h].ap()[:, cs], identity=identb.ap()), reads=[("kc", h), "identb"], writes=[pkt])
                        op("act", lambda e: e.copy(out=RES[("bcT", h, c)], in_=pst.ap().bitcast(BF16)[:, 0:128]), reads=[pkt], writes=[("res", "bcT", h, c)])
                        op("act", lambda e: e.copy(out=RES[("kcT", h, c)], in_=pst.ap().bitcast(BF16)[:, 128:256]), reads=[pkt], writes=[("res", "kcT", h, c)])
                        st[gi_] = [0, 1, 2, 3]
                    yield
                    for lev in range(6):
                        for gi_, (h, c) in enumerate(group):
                            T5 = tmp_t[gi_ * 5:(gi_ + 1) * 5]
                            tk_ = lambda j: ("tmp", gi_, j)
                            iX, iXT, iXn, iXTn = st[gi_]
                            psx, pkx = psr.next()
                            op("pe", lambda e: e.matmul(psx.ap()[:, 0:128], lhsT=T5[iXT], rhs=T5[iX], start=True, stop=True), reads=[tk_(iXT), tk_(iX)], writes=[pkx])
                            if lev < 5:
                                op("pe", lambda e: e.matmul(psx.ap()[:, 128:256], lhsT=T5[iX], rhs=T5[iXT], start=True, stop=True), reads=[tk_(iXT), tk_(iX)], writes=[pkx])
                            op("act", lambda e: e.copy(out=T5[iXn], in_=psx.ap()[:, 0:128]), reads=[pkx], writes=[tk_(iXn)])
                            if lev < 5:
                                op("act", lambda e: e.copy(out=T5[iXTn], in_=psx.ap()[:, 128:256]), reads=[pkx], writes=[tk_(iXTn)])
                            st[gi_] = [iXn, iXTn, iX, iXT]
                        yield
                        for gi_, (h, c) in enumerate(group):
                            T5 = tmp_t[gi_ * 5:(gi_ + 1) * 5]
                            tk_ = lambda j: ("tmp", gi_, j)
                            iX = st[gi_][0]
                            psq, pkq = psr.next()
                            op("pe", lambda e: e.matmul(psq.ap()[:, 0:128], lhsT=T5[iX], rhs=T5[4], start=True, stop=True), reads=[tk_(iX), tk_(4)], writes=[pkq])
                            if lev < 5:
                                op("dve", lambda e: e.tensor_tensor(out=T5[4], in0=T5[4], in1=psq.ap()[:, 0:128], op=ALU.add), reads=[tk_(4), pkq], writes=[tk_(4)])
                            else:
                                op("dve", lambda e: e.tensor_tensor(out=RES[("TT", h, c)], in0=T5[4], in1=psq.ap()[:, 0:128], op=ALU.add), reads=[tk_(4), pkq], writes=[("res", "TT", h, c)])
                        yield

                Hb = [None, None]

                def B_gen(chunks):
                    for c in chunks:
                        cs = slice(c * C, (c + 1) * C)
                        rk_ = lambda nm_, h: ("res", nm_, h, c)
                        wus = []
                        for h in range(2):
                            psw, pkw = psr.next()
                            if c > 0:
                                op("pe", lambda e: e.matmul(psw.ap()[:, 0:128], lhsT=Bh["ah"][h].ap()[:, cs], rhs=Hb[h][0].ap(), start=True, stop=False), reads=[("ah", h), Hb[h][1]], writes=[pkw])
                            op("pe", lambda e: e.matmul(psw.ap()[:, 0:128], lhsT=RES[("ak", h, c)], rhs=vpad[h].ap()[:, c, :], start=(c == 0), stop=True), reads=[rk_("ak", h), ("vpad", h)], writes=[pkw])
                            wu, wuk = wu_r.next()
                            op("act", lambda e: e.copy(out=wu.ap(), in_=psw.ap()[:, 0:128]), reads=[pkw], writes=[wuk])
                            wus.append((wu, wuk))
                        yield
                        Ups = []
                        for h in range(2):
                            psu, pku = psr.next()
                            op("pe", lambda e: e.matmul(psu.ap()[:, 0:128], lhsT=RES[("TT", h, c)], rhs=wus[h][0].ap(), start=True, stop=True), reads=[rk_("TT", h), wus[h][1]], writes=[pku])
                            up_, upk = upad[h].next()
                            op("dve", lambda e: e.tensor_copy(out=up_.ap(), in_=psu.ap()[:, 0:128]), reads=[pku, ("upad0", h, up_.name)], writes=[upk])
                            Ups.append((up_, upk))
                        yield
                        if c < NCH - 1:
                            for h in range(2):
                                psh, pkh = psr.next()
                                op("pe", lambda e: e.matmul(psh.ap()[:, 0:128], lhsT=RES[("bcT", h, c)], rhs=Ups[h][0].ap(), start=True, stop=False), reads=[rk_("bcT", h), Ups[h][1]], writes=[pkh])
                                op("pe", lambda e: e.matmul(psh.ap()[:, 0:128], lhsT=RES[("kcT", h, c)], rhs=vpad[h].ap()[:, c, :], start=False, stop=True), reads=[rk_("kcT", h), ("vpad", h)], writes=[pkh])
                                if c == 0:
                                    op("dve", lambda e: e.tensor_copy(out=H[h].ap(), in_=psh.ap()[:, 0:128]), reads=[pkh], writes=[("H", h)])
                                else:
                                    op("dve", lambda e: e.scalar_tensor_tensor(out=H[h].ap(), in0=H[h].ap(), scalar=dcol.ap()[:, 2, c:c + 1], in1=psh.ap()[:, 0:128], op0=ALU.mult, op1=ALU.add),
                                       reads=[("H", h), pkh] + dck, writes=[("H", h)])
                        psy, pky = psr.next()
                        first = True
                        for h in range(2):
                            if c > 0:
                                op("pe", lambda e: e.matmul(psy.ap()[:, 0:128], lhsT=Hb[h][0].ap(), rhs=Bh["rh"][h].ap()[:, cs], start=first, stop=False), reads=[Hb[h][1], ("rh", h)], writes=[pky]); first = False
                            op("pe", lambda e: e.matmul(psy.ap()[:, 0:128], lhsT=Ups[h][0].ap(), rhs=RES[("rb", h, c)], start=first, stop=False), reads=[Ups[h][1], rk_("rb", h)], writes=[pky]); first = False
                            op("pe", lambda e: e.matmul(psy.ap()[:, 0:128], lhsT=vpad[h].ap()[:, c, :], rhs=RES[("rk", h, c)], start=False, stop=(h == 1)), reads=[("vpad", h), rk_("rk", h)], writes=[pky])
                        op("act", lambda e: e.copy(out=ya.ap()[:, cs], in_=psy.ap()[:, 0:128]), reads=[pky], writes=[("ya", c // 4)])
                        if c < NCH - 1:
                            for h in range(2):
                                Hb[h] = Hb_r[h].next()
                                op("act", lambda e: e.copy(out=Hb[h][0].ap(), in_=H[h].ap()), reads=[("H", h)], writes=[Hb[h][1]])
                        yield

                def zip2(ga, gb):
                    da = db = False
                    while not (da and db):
                        if not da and next(ga, "END") == "END":
                            da = True
                        if not db and next(gb, "END") == "END":
                            db = True

                groups = [[(h, c) for c in range(g * 4, g * 4 + 4) for h in range(2)] for g in range(NCH // 4)]
                for _ in A_gen(groups[0]):
                    pass
                for g in range(len(groups)):
                    bg = B_gen(range(g * 4, g * 4 + 4))
                    if g + 1 < len(groups):
                        zip2(A_gen(groups[g + 1]), bg)
                    else:
                        for _ in bg:
                            pass
                k.phase_barrier()
                gA, gB = F["r"], F["k"]
                yall = [("ya", tb) for tb in range(NTB)]
                pss = [psr.next() for _ in range(NTB)]
                for tb in range(NTB):
                    op("pe", lambda e: e.matmul(pss[tb][0].ap(), lhsT=tri.ap()[:, 5, :], rhs=ya.ap()[:, tsl(tb)], start=True, stop=True), reads=[("ya", tb), "tri"], writes=[pss[tb][1]])
                for tb in range(NTB):
                    op("dve", lambda e: e.scalar_tensor_tensor(out=gA.ap()[:, tsl(tb)], in0=pss[tb][0].ap(), scalar=-1.0 / 64, in1=ya.ap()[:, tsl(tb)], op0=ALU.mult, op1=ALU.add),
                       reads=[pss[tb][1], ("ya", tb)], writes=[("r", tb)])
                op("act", lambda e: e.activation(out=gB.ap(), in_=gA.ap(), func=AF.Square), reads=allk("r"), writes=allk("k"))
                pss = [psr.next() for _ in range(NTB)]
                for tb in range(NTB):
                    op("pe", lambda e: e.matmul(pss[tb][0].ap(), lhsT=tri.ap()[:, 5, :], rhs=gB.ap()[:, tsl(tb)], start=True, stop=True), reads=[("k", tb), "tri"], writes=[pss[tb][1]])
                for tb in range(NTB):
                    op("act", lambda e: e.activation(out=gB.ap()[:, tsl(tb)], in_=pss[tb][0].ap(), func=AF.Ln, bias=gnc.ap(), scale=1.0 / 64), reads=[pss[tb][1], "gnc"], writes=[("k", tb)])
                op("act", lambda e: e.activation(out=gB.ap(), in_=gB.ap(), func=AF.Exp, scale=-0.5), reads=allk("k"), writes=allk("k"))
                op("dve", lambda e: e.tensor_tensor(out=gA.ap(), in0=gA.ap(), in1=gB.ap(), op=ALU.mult), reads=allk("r") + allk("k"), writes=allk("r"))
                op("dve", lambda e: e.tensor_scalar(out=gA.ap(), in0=gA.ap(), scalar1=prm(5), scalar2=prm(6), op0=ALU.mult, op1=ALU.add), reads=allk("r") + pcall, writes=allk("r"))
                op("dve", lambda e: e.tensor_tensor(out=gA.ap(), in0=gA.ap(), in1=F["v"].ap(), op=ALU.add), reads=allk("r") + allk("v"), writes=allk("r"))
                op("dve", lambda e: e.tensor_tensor(out=yo.ap(), in0=gA.ap(), in1=F["g"].ap(), op=ALU.mult), reads=allk("r") + allk("g"), writes=[("yo", tb) for tb in range(NTB)])
                k.dma(yT_dram[0, :, hp, :], yo.ap(), reads=[("yo", tb) for tb in range(NTB)], writes=[("yT_dram", 0, hp)])

    for l in range(DEPTH):
        with Scope(k) as sc:
            hT = sc.sb("hT", [128, KC, S], F32)
            sqr = sc.ring("sq", [128, TB], F32, 3)
            tmpr = sc.ring("nt", [128, TB], F32, 4)
            load_h(hT)
            rmsnorm(hT, 2 * l, u_dst, sqr, tmpr)
        if "yT_in0" not in dbg:
            mixer_rwkv(l)
        if "yT_in1" not in dbg:
            mixer_hgrn(l)
        if "yT_in2" not in dbg:
            mixer_dsa(l)
        with Scope(k) as sc:
            yT = sc.sb("yT", [128, 3, 4, S], BF16)
            mT = sc.sb("mT", [128, KC, S], BF16)
            stg_g = sc.ring("stg_g", [128, 3, KC, 128], F32, 2)
            wg_r = sc.ring("wg", [128, 3, KC, 128], BF16, 2)
            stg_b = sc.ring("stg_b", [128, 3, 4, 128], F32, 2)
            wb_r = sc.ring("wb", [128, 3, 4, 128], BF16, 2)
            sgr = sc.ring("sg", [128, TB], F32, 3)
            accr = sc.ring("acc", [128, TB], F32, 2)
            for n in range(3):
                k.dma(yT.ap()[:, n], yT_src[l][n], writes=[("yT", n)])
            for dc in range(KC):
                sg_t, sgk = stg_g.next(); wg, wgk = wg_r.next()
                sb_t, sbk = stg_b.next(); wb, wbk = wb_r.next()
                for n in range(3):
                    c0 = G_OFF + n * D + dc * 128
                    k.dma(sg_t.ap()[:, n], P["w_in"][l][:, c0:c0 + 128].rearrange("(c p) n -> p c n", p=128), writes=[(sgk, n)])
                    k.dma(sb_t.ap()[:, n], P["w_branch"][l, n][:, dc * 128:(dc + 1) * 128].rearrange("(c p) n -> p c n", p=128),
                          writes=[(sbk, n)])
                cast(wg.ap(), sg_t.ap(), [(sgk, n) for n in range(3)], [wgk])
                cast(wb.ap(), sb_t.ap(), [(sbk, n) for n in range(3)], [wbk])
                for tb in range(NTB):
                    acc, acck = accr.next()
                    for n in range(3):
                        pg, pgk = psr.next()
                        for kc in range(KC):
                            op("pe", lambda e: e.matmul(pg.ap(), lhsT=wg.ap()[:, n, kc, :], rhs=uT.ap()[:, kc, usl(tb)],
                                                        start=(kc == 0), stop=(kc == KC - 1)),
                               reads=[wgk] + ukeys(kc, tb), writes=[pgk])
                        pb, pbk = psr.next()
                        for c in range(4):
                            op("pe", lambda e: e.matmul(pb.ap(), lhsT=wb.ap()[:, n, c, :], rhs=yT.ap()[:, n, c, tsl(tb)],
                                                        start=(c == 0), stop=(c == 3)),
                               reads=[wbk, ("yT", n)], writes=[pbk])
                        sg, sgk2 = sgr.next()
                        op("act", lambda e: e.activation(out=sg.ap(), in_=pg.ap(), func=AF.Sigmoid), reads=[pgk], writes=[sgk2])
                        if n == 0:
                            op("dve", lambda e: e.tensor_tensor(out=acc.ap(), in0=sg.ap(), in1=pb.ap(), op=ALU.mult),
                               reads=[sgk2, pbk], writes=[acck])
                        else:
                            op("dve", lambda e: e.tensor_tensor(out=sg.ap(), in0=sg.ap(), in1=pb.ap(), op=ALU.mult),
                               reads=[sgk2, pbk], writes=[sgk2])
                            dstap = mT.ap()[:, dc, tsl(tb)] if n == 2 else acc.ap()
                            op("pool" if n == 1 else "dve", lambda e: e.tensor_tensor(out=dstap, in0=sg.ap(), in1=acc.ap(), op=ALU.add),
                               reads=[sgk2, acck], writes=[("mT", dc, tb)] if n == 2 else [acck])
            stg_o = sc.ring("stg_o", [128, KC, 128], F32, 2)
            wo_r = sc.ring("wo", [128, KC, 128], BF16, 2)
            hr = sc.ring("hr", [128, TB], F32, 3)
            for dc in range(KC):
                wo, wok = wload(P["w_o"][l][:, dc * 128:(dc + 1) * 128], KC, 128, stg_o, wo_r)
                for tb in range(NTB):
                    ht, htk = hr.next()
                    k.dma(ht.ap(), h_dram[:, dc, tsl(tb)], writes=[htk])
                    ps, pk = psr.next()
                    for c in range(KC):
                        op("pe", lambda e: e.matmul(ps.ap(), lhsT=wo.ap()[:, c, :], rhs=mT.ap()[:, c, tsl(tb)],
                                                    start=(c == 0), stop=(c == KC - 1)),
                           reads=[wok, ("mT", c, tb)], writes=[pk])
                    op("dve", lambda e: e.tensor_tensor(out=ht.ap(), in0=ht.ap(), in1=ps.ap(), op=ALU.add),
                       reads=[htk, pk], writes=[htk])
                    k.dma(h_dram[:, dc, tsl(tb)], ht.ap(), reads=[htk], writes=[("h_dram", dc, tb)])
        with Scope(k) as sc:
            hT = sc.sb("hT", [128, KC, S], F32)
            sc2 = Scope(k)
            sc_outer = sc; sc = sc2
            sqr = sc.ring("sq", [128, TB], F32, 2)
            tmpr = sc.ring("nt", [128, TB], F32, 3)
            load_h(hT)
            rmsnorm(hT, 2 * l + 1, u_dst, sqr, tmpr)
            cw = sc.sb("cw", [128, 3, 2 * NFF], F32)
            cb = sc.sb("cb", [128, 2 * NFF], F32)
            for j in range(3):
                load_pvec(cw.ap()[:, j, :], P["conv_w"][l, j], 2 * NFF, ["cw"])
            load_pvec(cb.ap(), P["conv_b"][l], 2 * NFF, ["cb"])
            G = 2
            stg_u = sc.ring("stg_u", [128, 2, KC, 128], F32, 2)
            wu_r = sc.ring("wu", [128, 2, KC, 128], BF16, 2)
            stg_d = sc.ring("stg_d", [128, G, D], F32, 1)
            wd_r = sc.ring("wd", [128, G, D], BF16, 2)
            upr = sc.ring("up", [128, 2, S + 2], F32, 1)
            cr = sc.ring("cc", [128, 2, TB], F32, 2)
            actr = sc.ring("actT", [128, G, S], BF16, 2)
            up, upk = upr.next()
            op("pool", lambda e: e.memset(up.ap()[:, :, 0:2], 0.0), writes=["uppad"])
            for g0 in range(0, NFF, G):
                actT, actk = actr.next()
                wd, wdk = wload(P["w_down"][l][g0 * 128:(g0 + G) * 128, :], G, D, stg_d, wd_r)
                for gi in range(G):
                    j = g0 + gi
                    su, suk = stg_u.next(); wu, wuk = wu_r.next()
                    for hv in range(2):
                        c0 = hv * DFF + j * 128
                        k.dma(su.ap()[:, hv], P["w_up"][l][:, c0:c0 + 128].rearrange("(c p) n -> p c n", p=128), writes=[(suk, hv)])
                    cast(wu.ap(), su.ap(), [(suk, 0), (suk, 1)], [wuk])
                    for tb in range(NTB):
                        cc, cck = cr.next()
                        for hv in range(2):
                            ps, pk = psr.next()
                            for kc in range(KC):
                                op("pe", lambda e: e.matmul(ps.ap(), lhsT=wu.ap()[:, hv, kc, :], rhs=uT.ap()[:, kc, usl(tb)],
                                                            start=(kc == 0), stop=(kc == KC - 1)),
                                   reads=[wuk] + ukeys(kc, tb), writes=[pk])
                            ch = hv * NFF + j
                            op("act", lambda e: e.copy(out=up.ap()[:, hv, usl(tb)], in_=ps.ap()), reads=[pk, "uppad"], writes=[("up", hv, tb)])
                            op("act", lambda e: e.activation(out=cc.ap()[:, hv, :], in_=ps.ap(), func=AF.Identity,
                                                             scale=cw.ap()[:, 2, ch:ch + 1], bias=cb.ap()[:, ch:ch + 1]),
                               reads=[pk, "cw", "cb"], writes=[(cck, hv)])
                            for sh in (1, 2):
                                rk = [("up", hv, tb)] + ([("up", hv, tb - 1)] if tb > 0 else [])
                                op("dve", lambda e: e.scalar_tensor_tensor(out=cc.ap()[:, hv, :], in0=up.ap()[:, hv, usl(tb, sh)],
                                                                           scalar=cw.ap()[:, 2 - sh, ch:ch + 1], in1=cc.ap()[:, hv, :],
                                                                           op0=ALU.mult, op1=ALU.add),
                                   reads=rk + ["cw", (cck, hv)], writes=[(cck, hv)])
                        op("act", lambda e: e.activation(out=cc.ap()[:, 0, :], in_=cc.ap()[:, 0, :], func=AF.Silu),
                           reads=[(cck, 0)], writes=[(cck, 0)])
                        op("pool", lambda e: e.tensor_tensor(out=actT.ap()[:, gi, tsl(tb)], in0=cc.ap()[:, 0, :], in1=cc.ap()[:, 1, :], op=ALU.mult),
                           reads=[(cck, 0), (cck, 1)], writes=[(actk, gi, tb)])
                for dc in range(KC):
                    for tb in range(NTB):
                        ps, pk = psr.next()
                        for gi in range(G):
                            op("pe", lambda e: e.matmul(ps.ap(), lhsT=wd.ap()[:, gi, dc * 128:(dc + 1) * 128], rhs=actT.ap()[:, gi, tsl(tb)],
                                                        start=(gi == 0), stop=(gi == G - 1)),
                               reads=[wdk, (actk, gi, tb)], writes=[pk])
                        op("dve", lambda e: e.tensor_tensor(out=hT.ap()[:, dc, tsl(tb)], in0=hT.ap()[:, dc, tsl(tb)], in1=ps.ap(), op=ALU.add),
                           reads=hkeys(dc, tb) + [pk], writes=hkeys(dc, tb))
            sc2.__exit__(None, None, None)
            sc = Scope(k)
            sqr = sc.ring("sq", [128, TB], F32, 2)
            tmpr = sc.ring("nt", [128, TB], F32, 3)
            if l < DEPTH - 1:
                store_h(hT)
            else:
                fo = sc.ring("fo", [128, KC, TB], F32, 1)
                ot = sc.ring("ot", [128, D], F32, 2)
                fT, fk = fo.next()
                cur = [0]

                def f_dst(kc, tb):
                    return fT.ap()[:, kc, :], [("fT", kc)]
                for tb in range(NTB):
                    ps, pk = psr.next()
                    for kc in range(KC):
                        sq, sk = sqr.next()
                        op("act", lambda e: e.activation(out=sq.ap(), in_=hT.ap()[:, kc, tsl(tb)], func=AF.Square),
                           reads=hkeys(kc, tb), writes=[sk])
                        op("pe", lambda e: e.matmul(ps.ap(), lhsT=ones.ap(), rhs=sq.ap(), start=(kc == 0), stop=(kc == KC - 1)),
                           reads=[sk, "ones"], writes=[pk])
                    sd, sdk = tmpr.next()
                    op("act", lambda e: e.activation(out=sd.ap(), in_=ps.ap(), func=AF.Ln, bias=epsc.ap(), scale=1.0 / D),
                       reads=[pk, "epsc"], writes=[sdk])
                    rs, rsk = tmpr.next()
                    op("act", lambda e: e.activation(out=rs.ap(), in_=sd.ap(), func=AF.Exp, scale=-0.5), reads=[sdk], writes=[rsk])
                    for kc in range(KC):
                        op("dve", lambda e: e.scalar_tensor_tensor(out=fT.ap()[:, kc, :], in0=hT.ap()[:, kc, tsl(tb)],
                                                                   scalar=gvec.ap()[:, 2 * DEPTH, kc:kc + 1], in1=rs.ap(),
                                                                   op0=ALU.mult, op1=ALU.mult),
                           reads=hkeys(kc, tb) + [rsk, "gvec"], writes=[("fT", kc)])
                    for t4 in range(4):
                        o, okk = ot.next()
                        for half in range(2):
                            ps2, pk2 = psr.next()
                            for j in range(4):
                                kc = half * 4 + j
                                op("pe", lambda e: e.transpose(out=ps2.ap()[:, j * 128:(j + 1) * 128],
                                                               in_=fT.ap()[:, kc, t4 * 128:(t4 + 1) * 128], identity=ident.ap()),
                                   reads=[("fT", kc), "ident"], writes=[pk2])
                            cast(o.ap()[:, half * 512:(half + 1) * 512], ps2.ap(), [pk2], [(okk, half)], eng=("dve", "act")[half])
                        r0 = tb * TB + t4 * 128
                        k.dma(out[r0:r0 + 128, :], o.ap(), reads=[(okk, 0), (okk, 1)], writes=[("out", r0)])
            sc.__exit__(None, None, None)
    k.drain("sp")
    return k


PARAM_NAMES = ["norm_mix_g", "w_in", "a_mu", "a_w0", "a_w_up", "a_a0", "a_a_up", "a_g_up", "a_k_k", "a_k_a", "a_r_k",
               "a_gn_g", "a_gn_b", "b_lb_logits", "b_gn_g", "w_branch", "w_o", "norm_ffn_g", "w_up", "conv_w", "conv_b",
               "w_down", "norm_final_g"]


def make_consts():
    i = np.arange(128)
    low = (i[:, None] >= i[None, :])
    bd = ((i[:, None] // 64) == (i[None, :] // 64))
    hm = np.zeros((128, 128)); hm[:64, 0] = 1; hm[64:, 1] = 1
    tri = np.stack([(i[:, None] <= i[None, :]), (i[:, None] < i[None, :]), low, np.where(low, 0.0, -1e30),
                    (i[:, None] > i[None, :]), bd, hm]).astype(np.float32)
    inv_freq = (np.float32(10000.0) ** (-np.arange(32, dtype=np.float32) * np.float32(2.0 / 64))).astype(np.float32)
    ang = (np.arange(S, dtype=np.float32)[:, None] * inv_freq[None, :]).astype(np.float32)
    cos = np.cos(ang).astype(np.float32).T; sin = np.sin(ang).astype(np.float32).T
    cosF = np.concatenate([cos, cos, cos, cos], axis=0)
    sinF = np.concatenate([-sin, sin, -sin, sin], axis=0)
    return {"c_ident": np.eye(128, dtype=np.float32), "c_tri": tri, "c_rope": np.stack([cosF, sinF]).astype(np.float32)}


def make_in_map(inputs, b, consts):
    m = {"x": np.ascontiguousarray(inputs["x"][b])}
    for n in PARAM_NAMES:
        a = np.ascontiguousarray(np.asarray(inputs[n], dtype=np.float32))
        if n == "a_r_k":
            a = a.reshape(DEPTH, 512)
        m[n] = a
    m.update(consts)
    return m


def kernel(**inputs):
    k = build()
    consts = make_consts()
    in_maps = [make_in_map(inputs, b, consts) for b in range(NB)]
    res = run_bass_kernel_spmd(k.nc, in_maps, core_ids=list(range(NB)))
    return np.stack([np.asarray(r["out"], dtype=np.float32) for r in res.results], axis=0)
```

```python
import numpy as np
import ml_dtypes
import concourse.bass as bass
import concourse.mybir as mybir
from concourse.bass_utils import run_bass_kernel_spmd

F32 = mybir.dt.float32
BF16 = mybir.dt.bfloat16
AF = mybir.ActivationFunctionType
ALU = mybir.AluOpType
AX = mybir.AxisListType

S = 2048; D = 1024; DEPTH = 2; NB = 8
TB = 512; NTB = S // TB; KC = D // 128
A_COLS = 1792; B_COLS = 2048; C_COLS = 1092; G_COLS = 3072; IN_COLS = 8004
A_OFF = 0; B_OFF = A_COLS; C_OFF = A_COLS + B_COLS; G_OFF = C_OFF + C_COLS
DFF = 2816; NFF = DFF // 128
NDMA = 24
NOSYNC_SAME = {"pe"}


class K:
    def __init__(self):
        self.nc = nc = bass.Bass("TRN2", target_bir_lowering=False)
        self.eng = {"pe": nc.tensor, "act": nc.scalar, "dve": nc.vector, "pool": nc.gpsimd, "sp": nc.sync}
        self.sem = {}
        self.cnt = {}
        for e in self.eng:
            self.sem[e] = nc.alloc_semaphore("s_" + e)
            self.cnt[e] = 0
        self.seen = {e: {} for e in self.eng}
        self.lastw = {}
        self.readers = {}
        self.dsem = [nc.alloc_semaphore("d%d" % i) for i in range(NDMA)]
        self.dval = [0] * NDMA
        self.di = 0
        self.nwait = 0
        self.ninst = 0

    def _wait(self, e, ev):
        name, sem, val, src = ev
        if src == e and e in NOSYNC_SAME:
            return
        if self.seen[e].get(name, 0) >= val:
            return
        self.eng[e].wait_ge(sem, val)
        self.nwait += 1
        self.seen[e][name] = val

    def _deps(self, e, reads, writes):
        for r in reads:
            ev = self.lastw.get(r)
            if ev is not None:
                self._wait(e, ev)
        for w in writes:
            ev = self.lastw.get(w)
            if ev is not None:
                self._wait(e, ev)
            for ev in self.readers.get(w, ()):
                self._wait(e, ev)

    def _record(self, ev, reads, writes):
        for w in writes:
            self.lastw[w] = ev
            self.readers[w] = []
        for r in reads:
            if r in writes:
                continue
            lst = self.readers.setdefault(r, [])
            lst[:] = [x for x in lst if x[0] != ev[0]]
            lst.append(ev)

    def op(self, e, fn, reads=(), writes=()):
        self._deps(e, reads, writes)
        ins = fn(self.eng[e])
        self.cnt[e] += 1
        ins.then_inc(self.sem[e], 1)
        ev = ("E" + e, self.sem[e], self.cnt[e], e)
        self.seen[e]["E" + e] = max(self.seen[e].get("E" + e, 0), 0)
        self._record(ev, reads, writes)
        self.ninst += 1
        return ins

    def dma(self, out, in_, reads=(), writes=(), q="sp", **kw):
        self._deps(q, reads, writes)
        j = self.di % NDMA
        self.di += 1
        if self.dval[j] > 0:
            self._wait(q, ("D%d" % j, self.dsem[j], self.dval[j], None))
        self.dval[j] += 16
        self.eng[q].dma_start(out=out, in_=in_, **kw).then_inc(self.dsem[j], 16)
        ev = ("D%d" % j, self.dsem[j], self.dval[j], None)
        self._record(ev, reads, writes)
        self.ninst += 1

    def drain(self, e="sp"):
        for e2 in self.eng:
            if e2 != e and self.cnt[e2] > 0:
                self._wait(e, ("E" + e2, self.sem[e2], self.cnt[e2], e2))
        for j in range(NDMA):
            if self.dval[j] > 0:
                self._wait(e, ("D%d" % j, self.dsem[j], self.dval[j], None))


    def phase_barrier(self):
        self.drain("sp")
        self.nc.all_engine_barrier()
        self.lastw.clear()
        self.readers.clear()


class Scope:
    def __init__(self, k):
        self.k = k
        self.stack = []

    def __enter__(self):
        return self

    _uid = [0]

    def sb(self, name, shape, dtype):
        Scope._uid[0] += 1
        g = self.k.nc.sbuf_tensor("%s_u%d" % (name, Scope._uid[0]), list(shape), dtype)
        t = g.__enter__()
        self.stack.append(g)
        return t

    def ring(self, name, shape, dtype, n):
        return Ring([self.sb("%s_%d" % (name, i), shape, dtype) for i in range(n)], name)

    def __exit__(self, *a):
        self.k.phase_barrier()
        for g in reversed(self.stack):
            g.__exit__(None, None, None)
        return False


class Ring:
    def __init__(self, tensors, name):
        self.t = tensors
        self.name = name
        self.i = 0

    def next(self):
        j = self.i % len(self.t)
        self.i += 1
        return self.t[j], (self.name, j)


def build(dbg=None):
    dbg = dbg or {}
    k = K()
    nc = k.nc
    op = k.op
    uid = [0]

    def din(name, shape, dt=F32):
        return nc.dram_tensor(name, list(shape), dt, kind="ExternalInput").ap()

    x = din("x", [S, D])
    P = {}
    for name, shape in [
        ("norm_mix_g", [DEPTH, D]), ("w_in", [DEPTH, D, IN_COLS]), ("a_mu", [DEPTH, A_COLS]), ("a_w0", [DEPTH, 512]),
        ("a_w_up", [DEPTH, 64, 512]), ("a_a0", [DEPTH, 512]), ("a_a_up", [DEPTH, 64, 512]), ("a_g_up", [DEPTH, 128, 512]),
        ("a_k_k", [DEPTH, 512]), ("a_k_a", [DEPTH, 512]), ("a_r_k", [DEPTH, 512]), ("a_gn_g", [DEPTH, 512]),
        ("a_gn_b", [DEPTH, 512]), ("b_lb_logits", [DEPTH, 512]), ("b_gn_g", [DEPTH, 512]),
        ("w_branch", [DEPTH, 3, 512, D]), ("w_o", [DEPTH, D, D]), ("norm_ffn_g", [DEPTH, D]),
        ("w_up", [DEPTH, D, 2 * DFF]), ("conv_w", [DEPTH, 3, 2 * DFF]), ("conv_b", [DEPTH, 2 * DFF]),
        ("w_down", [DEPTH, DFF, D]), ("norm_final_g", [D]),
    ]:
        P[name] = din(name, shape)
    c_ident = din("c_ident", [128, 128])
    out = nc.dram_tensor("out", [S, D], F32, kind="ExternalOutput").ap()
    h_dram = nc.dram_tensor("h_scr", [128, KC, S], F32, kind="Internal").ap()
    yT_dram = nc.dram_tensor("yT_scr", [3, 128, 4, S], BF16, kind="ExternalOutput" if dbg.get("dump_yT") else "Internal").ap()
    c_tri = din("c_tri", [7, 128, 128])
    c_rope = din("c_rope", [2, 128, S])
    yT_src = [[yT_dram[n] for n in range(3)] for l in range(DEPTH)]
    for n in range(3):
        if ("yT_in%d" % n) in dbg:
            t = din("yT_in%d" % n, [DEPTH, 128, 4, S], BF16)
            for l in range(DEPTH):
                yT_src[l][n] = t[l]

    ident = nc.alloc_sbuf_tensor("ident", [128, 128], F32)
    identb = nc.alloc_sbuf_tensor("identb", [128, 128], BF16)
    ones = nc.alloc_sbuf_tensor("ones", [128, 128], F32)
    epsc = nc.alloc_sbuf_tensor("epsc", [128, 1], F32)
    uT = nc.alloc_sbuf_tensor("uT", [128, KC, S + 2], BF16)
    gvec = nc.alloc_sbuf_tensor("gvec", [128, 2 * DEPTH + 1, KC], F32)
    psr = Ring([nc.alloc_psum_tensor("ps%d" % i, [128, 512], F32) for i in range(6)], "ps")
    psacc = Ring([nc.alloc_psum_tensor("psacc%d" % i, [128, 512], F32) for i in range(2)], "psacc")

    tri = nc.alloc_sbuf_tensor("tri", [128, 7, 128], F32)
    k.dma(tri.ap(), c_tri.rearrange("a p f -> p a f"), writes=["tri"])
    k.dma(ident.ap(), c_ident, writes=["ident"])
    pv_ring = Ring([nc.alloc_sbuf_tensor("pvstg%d" % i, [128, 128], F32) for i in range(2)], "pvstg")

    def load_pvec(dst, src1d, n, wkeys):
        stg, sk = pv_ring.next()
        k.dma(stg.ap()[0:n, :], src1d.rearrange("(c p) -> c p", p=128), writes=[sk])
        ps, pk = psr.next()
        op("pe", lambda e: e.transpose(out=ps.ap()[:, 0:n], in_=stg.ap()[0:n, :], identity=ident.ap()[0:n, 0:n]), reads=[sk, "ident"], writes=[pk])
        op("dve", lambda e: e.tensor_copy(out=dst, in_=ps.ap()[:, 0:n]), reads=[pk], writes=wkeys)

    op("dve", lambda e: e.tensor_copy(out=identb.ap(), in_=ident.ap()), reads=["ident"], writes=["identb"])
    op("pool", lambda e: e.memset(ones.ap(), 1.0), writes=["ones"])
    ones16 = nc.alloc_sbuf_tensor("ones16", [128, 128], BF16)
    op("pool", lambda e: e.memset(ones16.ap(), 1.0), writes=["ones16"])
    op("pool", lambda e: e.memset(epsc.ap(), 1e-6), writes=["epsc"])
    op("pool", lambda e: e.memset(uT.ap()[:, :, 0:2], 0.0), writes=["uTpad"])
    for l in range(DEPTH):
        load_pvec(gvec.ap()[:, 2 * l, :], P["norm_mix_g"][l], KC, ["gvec"])
        load_pvec(gvec.ap()[:, 2 * l + 1, :], P["norm_ffn_g"][l], KC, ["gvec"])
    load_pvec(gvec.ap()[:, 2 * DEPTH, :], P["norm_final_g"], KC, ["gvec"])

    def hkeys(kc=None, tb=None):
        return [("hT", a, b) for a in (range(KC) if kc is None else [kc]) for b in (range(NTB) if tb is None else [tb])]

    def ukeys(kc=None, tb=None):
        return [("uT", a, b) for a in (range(KC) if kc is None else [kc]) for b in (range(NTB) if tb is None else [tb])]

    def tsl(tb):
        return slice(tb * TB, (tb + 1) * TB)

    def usl(tb, shift=0):
        return slice(2 + tb * TB - shift, 2 + (tb + 1) * TB - shift)

    def rmsnorm(hT, gi, dst_fn, sqr, tmpr):
        for tb in range(NTB):
            ps, pk = psr.next()
            for kc in range(KC):
                sq, sk = sqr.next()
                op("act", lambda e: e.activation(out=sq.ap(), in_=hT.ap()[:, kc, tsl(tb)], func=AF.Square),
                   reads=hkeys(kc, tb), writes=[sk])
                op("pe", lambda e: e.matmul(ps.ap(), lhsT=ones16.ap(), rhs=sq.ap(), start=(kc == 0), stop=(kc == KC - 1)),
                   reads=[sk, "ones16"], writes=[pk])
            sd, sdk = tmpr.next()
            op("act", lambda e: e.activation(out=sd.ap(), in_=ps.ap(), func=AF.Ln, bias=epsc.ap(), scale=1.0 / D),
               reads=[pk, "epsc"], writes=[sdk])
            rs, rsk = tmpr.next()
            op("act", lambda e: e.activation(out=rs.ap(), in_=sd.ap(), func=AF.Exp, scale=-0.5), reads=[sdk], writes=[rsk])
            for kc in range(KC):
                dst, dk = dst_fn(kc, tb)
                op("dve", lambda e: e.scalar_tensor_tensor(out=dst, in0=hT.ap()[:, kc, tsl(tb)],
                                                           scalar=gvec.ap()[:, gi, kc:kc + 1], in1=rs.ap(),
                                                           op0=ALU.mult, op1=ALU.mult),
                   reads=hkeys(kc, tb) + [rsk, "gvec"], writes=dk)

    def u_dst(kc, tb):
        return uT.ap()[:, kc, usl(tb)], ukeys(kc, tb)

    castsel = [0]

    def cast(dst, src, reads, writes, eng=None):
        ce = eng or "act"
        castsel[0] += 1
        if ce == "act":
            op("act", lambda e: e.copy(out=dst, in_=src), reads=reads, writes=writes)
        else:
            op(ce, lambda e: e.tensor_copy(out=dst, in_=src), reads=reads, writes=writes)

    def wload(src, nk, ncol, stg_ring, w_ring, q="sp"):
        stg, sk = stg_ring.next()
        wt, wk = w_ring.next()
        k.dma(stg.ap()[:, 0:nk, 0:ncol], src.rearrange("(c p) n -> p c n", p=128), writes=[sk], q=q)
        cast(wt.ap()[:, 0:nk, 0:ncol], stg.ap()[:, 0:nk, 0:ncol], [sk], [wk])
        return wt, wk

    def load_h(hT):
        for kc in range(KC):
            k.dma(hT.ap()[:, kc, :], h_dram[:, kc, :], reads=["h_dram"], writes=hkeys(kc))

    def store_h(hT):
        for kc in range(KC):
            k.dma(h_dram[:, kc, :], hT.ap()[:, kc, :], reads=hkeys(kc), writes=[("h_dram", kc)])

    with Scope(k) as sc:
        xs = sc.ring("xs", [128, D], F32, 2)
        xt = sc.ring("xt", [128, KC, 128], F32, 2)
        for tt in range(S // 128):
            a, ak = xs.next()
            b, bk = xt.next()
            k.dma(a.ap(), x[tt * 128:(tt + 1) * 128, :], writes=[ak])
            for half in range(2):
                ps, pk = psr.next()
                for j in range(4):
                    kc = half * 4 + j
                    op("pe", lambda e: e.transpose(out=ps.ap()[:, j * 128:(j + 1) * 128], in_=a.ap()[:, kc * 128:(kc + 1) * 128],
                                                   identity=ident.ap()), reads=[ak, "ident"], writes=[pk])
                cast(b.ap()[:, half * 4:half * 4 + 4, :], ps.ap().rearrange("p (a b) -> p a b", a=4), [pk], [(bk, half)],
                     eng=("dve", "act")[half])
            k.dma(h_dram[:, :, tt * 128:(tt + 1) * 128], b.ap(), reads=[(bk, 0), (bk, 1)], writes=[("h_dram", "x", tt)])


    def proj_fm(l, c0, ncol, sc_rings, consume, shift_w=None):
        stg_r, w_r = sc_rings
        wt, wk = wload(P["w_in"][l][:, c0:c0 + ncol], KC, ncol, stg_r, w_r)
        for tb in range(NTB):
            ps, pk = psr.next()
            for kc in range(KC):
                op("pe", lambda e: e.matmul(ps.ap()[0:ncol, :], lhsT=wt.ap()[:, kc, 0:ncol], rhs=uT.ap()[:, kc, usl(tb)],
                                            start=(kc == 0), stop=(kc == KC - 1)),
                   reads=[wk] + ukeys(kc, tb), writes=[pk])
            consume(tb, ps, pk)

    def mixer_hgrn(l):
        C = 64; NCH = S // C
        with Scope(k) as sc:
            stg_r = sc.ring("hstg", [128, KC, 128], F32, 2)
            w_r = sc.ring("hw", [128, KC, 128], BF16, 2)
            rings = (stg_r, w_r)
            lbt = sc.sb("lbt", [128, 2, 4], F32)
            lb = sc.sb("lb", [128, 4], F32)
            oml = sc.sb("oml", [128, 4], F32)
            noml = sc.sb("noml", [128, 4], F32)
            gng = sc.sb("gng", [128, 4], F32)
            onesS = sc.sb("onesS", [128, S], F32)
            Bc = sc.sb("Bc", [128, S], F32)
            sig = sc.sb("sig", [128, S], F32)
            bp = sc.sb("bp", [128, S], F32)
            Ex = sc.sb("Ex", [128, S], F32)
            ktl = sc.sb("ktl", [128, S], F32)
            qt = sc.sb("qt", [128, S], BF16)
            kt = sc.sb("kt", [128, S], BF16)
            qh = sc.sb("qh", [128, S], BF16)
            kcf = sc.sb("kcf", [128, S], F32)
            vtok = sc.sb("vtok", [64, NCH, 128], BF16)
            ysb = sc.sb("ysb", [128, S], F32)
            gs = sc.sb("gs", [128, S], F32)
            yo = sc.sb("yo", [128, S], BF16)
            dcol = sc.sb("dcol", [128, 4, NCH], F32)
            st = sc.sb("st", [128, 128], F32)
            stb_r = sc.ring("stb", [128, 128], BF16, 2)
            pt_r = sc.ring("pt", [64, 64], BF16, 4)
            kct_r = sc.ring("kct", [64, 128], BF16, 4)
            sq_r = sc.ring("hsq", [128, TB], F32, 2)
            nt_r = sc.ring("hnt", [128, TB], F32, 2)
            op("pool", lambda e: e.memset(onesS.ap(), 1.0), writes=["onesS"])
            for l_ in range(DEPTH):
                load_pvec(lbt.ap()[:, l_, :], P["b_lb_logits"][l_], 4, ["lbt"])
            load_pvec(gng.ap(), P["b_gn_g"][l], 4, ["gng"])
            if l == 0:
                op("dve", lambda e: e.tensor_tensor(out=lb.ap(), in0=lbt.ap()[:, 0, :], in1=lbt.ap()[:, 0, :], op=ALU.subtract),
                   reads=["lbt"], writes=["lb"])
            else:
                op("dve", lambda e: e.tensor_tensor(out=lb.ap(), in0=lbt.ap()[:, 1, :], in1=lbt.ap()[:, 0, :], op=ALU.subtract),
                   reads=["lbt"], writes=["lb"])
                op("act", lambda e: e.activation(out=lb.ap(), in_=lb.ap(), func=AF.Sigmoid), reads=["lb"], writes=["lb"])
            op("dve", lambda e: e.tensor_scalar(out=oml.ap(), in0=lb.ap(), scalar1=-1.0, scalar2=1.0, op0=ALU.mult, op1=ALU.add),
               reads=["lb"], writes=["oml"])
            op("dve", lambda e: e.tensor_scalar(out=noml.ap(), in0=oml.ap(), scalar1=-1.0, scalar2=None, op0=ALU.mult),
               reads=["oml"], writes=["noml"])
            for h in range(4):
                cq = B_OFF + h * 128; cf = B_OFF + 512 + h * 128; ci = B_OFF + 1024 + h * 128; cg = B_OFF + 1536 + h * 128
                def f_consume(tb, ps, pk):
                    op("act", lambda e: e.activation(out=sig.ap()[:, tsl(tb)], in_=ps.ap(), func=AF.Sigmoid), reads=[pk], writes=[("sig", tb)])
                proj_fm(l, cf, 128, rings, f_consume)
                allsig = [("sig", tb) for tb in range(NTB)]
                op("dve", lambda e: e.tensor_scalar(out=ktl.ap(), in0=sig.ap(), scalar1=noml.ap()[:, h:h + 1], scalar2=oml.ap()[:, h:h + 1],
                                                    op0=ALU.mult, op1=ALU.add), reads=allsig + ["noml", "oml"], writes=["ktl"])
                op("dve", lambda e: e.tensor_scalar(out=sig.ap(), in0=sig.ap(), scalar1=oml.ap()[:, h:h + 1], scalar2=lb.ap()[:, h:h + 1],
                                                    op0=ALU.mult, op1=ALU.add), reads=allsig + ["oml", "lb"], writes=allsig)
                op("act", lambda e: e.activation(out=sig.ap(), in_=sig.ap(), func=AF.Ln), reads=allsig, writes=allsig)
                op("dve", lambda e: e.tensor_tensor_scan(out=Bc.ap(), data0=onesS.ap(), data1=sig.ap(), initial=0.0, op0=ALU.mult, op1=ALU.add),
                   reads=allsig + ["onesS"], writes=["Bc"])
                B3 = Bc.ap().rearrange("p (c t) -> p c t", t=C)
                op("dve", lambda e: e.tensor_tensor(out=bp.ap().rearrange("p (c t) -> p c t", t=C), in0=B3,
                                                    in1=B3[:, :, C // 2 - 1:C // 2].to_broadcast([128, NCH, C]), op=ALU.subtract),
                   reads=["Bc"], writes=["bp"] + [("bpq", tb) for tb in range(NTB)])
                op("dve", lambda e: e.tensor_tensor(out=dcol.ap()[:, 0, :], in0=B3[:, :, C - 1], in1=B3[:, :, C // 2 - 1], op=ALU.subtract),
                   reads=["Bc"], writes=[("dcol", 0)])
                op("dve", lambda e: e.tensor_copy(out=dcol.ap()[:, 1, 0:1], in_=B3[:, 0, C // 2 - 1:C // 2]), reads=["Bc"], writes=[("dcol", 1)])
                op("dve", lambda e: e.tensor_tensor(out=dcol.ap()[:, 1, 1:NCH], in0=B3[:, 1:NCH, C // 2 - 1], in1=B3[:, 0:NCH - 1, C - 1], op=ALU.subtract),
                   reads=["Bc", ("dcol", 1)], writes=[("dcol", 1)])
                op("dve", lambda e: e.tensor_tensor(out=dcol.ap()[:, 2, :], in0=dcol.ap()[:, 0, :], in1=dcol.ap()[:, 1, :], op=ALU.add),
                   reads=[("dcol", 0), ("dcol", 1)], writes=[("dcol", 2)])
                op("act", lambda e: e.activation(out=dcol.ap()[:, 0:3, :], in_=dcol.ap()[:, 0:3, :], func=AF.Exp),
                   reads=[("dcol", 0), ("dcol", 1), ("dcol", 2)], writes=[("dcol", 0), ("dcol", 1), ("dcol", 2)])
                op("act", lambda e: e.activation(out=Ex.ap(), in_=bp.ap(), func=AF.Exp, scale=-1.0), reads=["bp"], writes=["Ex"])
                op("dve", lambda e: e.tensor_tensor(out=ktl.ap(), in0=ktl.ap(), in1=Ex.ap(), op=ALU.mult), reads=["ktl", "Ex"], writes=["ktl"])
                op("act", lambda e: e.copy(out=kt.ap(), in_=ktl.ap()), reads=["ktl"], writes=["kt"])
                op("dve", lambda e: e.tensor_tensor(out=kcf.ap().rearrange("p (c t) -> p c t", t=C), in0=ktl.ap().rearrange("p (c t) -> p c t", t=C),
                                                    in1=dcol.ap()[:, 0, :].unsqueeze(2).to_broadcast([128, NCH, C]), op=ALU.mult),
                   reads=["ktl", ("dcol", 0)], writes=["kcf"])
                op("act", lambda e: e.activation(out=Ex.ap(), in_=bp.ap(), func=AF.Exp), reads=["bp"], writes=["Ex"])

                def q_consume(tb, ps, pk):
                    op("dve", lambda e: e.tensor_tensor(out=bp.ap()[:, tsl(tb)], in0=Ex.ap()[:, tsl(tb)], in1=ps.ap(), op=ALU.mult),
                       reads=["Ex", pk, "bp"], writes=[("bpq", tb)])
                proj_fm(l, cq, 128, rings, q_consume)
                allq = [("bpq", tb) for tb in range(NTB)]
                op("act", lambda e: e.copy(out=qt.ap(), in_=bp.ap()), reads=allq, writes=["qt"])
                op("dve", lambda e: e.tensor_tensor(out=qh.ap().rearrange("p (c t) -> p c t", t=C), in0=bp.ap().rearrange("p (c t) -> p c t", t=C),
                                                    in1=dcol.ap()[:, 1, :].unsqueeze(2).to_broadcast([128, NCH, C]), op=ALU.mult),
                   reads=allq + [("dcol", 1)], writes=["qh"])

                def g_consume(tb, ps, pk):
                    op("act", lambda e: e.activation(out=gs.ap()[:, tsl(tb)], in_=ps.ap(), func=AF.Silu), reads=[pk], writes=[("gs", tb)])
                proj_fm(l, cg, 128, rings, g_consume)
                wt, wk = wload(P["w_in"][l][:, ci:ci + 128], KC, 128, stg_r, w_r)
                for c4 in range(NCH // 4):
                    ps, pk = psr.next()
                    for j in range(4):
                        c = c4 * 4 + j
                        for kc in range(KC):
                            op("pe", lambda e: e.matmul(ps.ap()[0:C, j * 128:(j + 1) * 128], lhsT=uT.ap()[:, kc, 2 + c * C: 2 + (c + 1) * C],
                                                        rhs=wt.ap()[:, kc, :], start=(kc == 0), stop=(kc == KC - 1)),
                               reads=[wk] + ukeys(kc, (c * C) // TB), writes=[pk])
                    cast(vtok.ap()[:, c4 * 4:(c4 + 1) * 4, :], ps.ap()[0:C, :].rearrange("p (a b) -> p a b", a=4), [pk], [("vtok", c4)], eng="act")
                def pre(c):
                    cs = slice(c * C, (c + 1) * C)
                    ps_s, pk_s = psr.next()
                    op("pe", lambda e: e.matmul(ps_s.ap()[0:C, 0:C], lhsT=kt.ap()[:, cs], rhs=qt.ap()[:, cs], start=True, stop=True),
                       reads=["kt", "qt"], writes=[pk_s])
                    pt, ptk = pt_r.next()
                    op("dve", lambda e: e.tensor_tensor(out=pt.ap(), in0=ps_s.ap()[0:C, 0:C], in1=tri.ap()[0:C, 0, 0:C], op=ALU.mult),
                       reads=[pk_s, "tri"], writes=[ptk])
                    ps_t, pk_t = psr.next()
                    op("pe", lambda e: e.transpose(out=ps_t.ap()[0:C, 0:128], in_=kcf.ap()[:, cs], identity=ident.ap()),
                       reads=["kcf", "ident"], writes=[pk_t])
                    kct, kctk = kct_r.next()
                    op("act", lambda e: e.copy(out=kct.ap(), in_=ps_t.ap()[0:C, 0:128]), reads=[pk_t], writes=[kctk])
                    return pt, ptk, kct, kctk
                stb = None
                nxt_pre = pre(0)
                for c in range(NCH):
                    cs = slice(c * C, (c + 1) * C)
                    pt, ptk, kct, kctk = nxt_pre
                    if c + 1 < NCH:
                        nxt_pre = pre(c + 1)
                    if c < NCH - 1:
                        ps_d, pk_d = psacc.next()
                        op("pe", lambda e: e.matmul(ps_d.ap()[:, 0:128], lhsT=kct.ap(), rhs=vtok.ap()[:, c, :], start=True, stop=True),
                           reads=[kctk, ("vtok", c // 4)], writes=[pk_d])
                    ps_y, pk_y = psr.next()
                    if c > 0:
                        op("pe", lambda e: e.matmul(ps_y.ap()[:, 0:C], lhsT=stb[0].ap(), rhs=qh.ap()[:, cs], start=True, stop=False),
                           reads=[stb[1], "qh"], writes=[pk_y])
                    op("pe", lambda e: e.matmul(ps_y.ap()[:, 0:C], lhsT=vtok.ap()[:, c, :], rhs=pt.ap(), start=(c == 0), stop=True),
                       reads=[("vtok", c // 4), ptk], writes=[pk_y])
                    op("act", lambda e: e.copy(out=ysb.ap()[:, cs], in_=ps_y.ap()[:, 0:C]), reads=[pk_y], writes=[("ysb", c // 8)])
                    if c < NCH - 1:
                        if c == 0:
                            op("dve", lambda e: e.tensor_copy(out=st.ap(), in_=ps_d.ap()[:, 0:128]), reads=[pk_d], writes=["st"])
                        else:
                            op("dve", lambda e: e.scalar_tensor_tensor(out=st.ap(), in0=st.ap(), scalar=dcol.ap()[:, 2, c:c + 1], in1=ps_d.ap()[:, 0:128],
                                                                       op0=ALU.mult, op1=ALU.add), reads=["st", pk_d, ("dcol", 2)], writes=["st"])
                        stb = stb_r.next()
                        op("act", lambda e: e.copy(out=stb[0].ap(), in_=st.ap()), reads=["st"], writes=[stb[1]])
                for tb in range(NTB):
                    sq, sk = sq_r.next()
                    yk = [("ysb", tb * 2), ("ysb", tb * 2 + 1)]
                    op("act", lambda e: e.activation(out=sq.ap(), in_=ysb.ap()[:, tsl(tb)], func=AF.Square), reads=yk, writes=[sk])
                    ps, pk = psr.next()
                    op("pe", lambda e: e.matmul(ps.ap(), lhsT=ones.ap(), rhs=sq.ap(), start=True, stop=True), reads=[sk, "ones"], writes=[pk])
                    sd, sdk = nt_r.next()
                    op("act", lambda e: e.activation(out=sd.ap(), in_=ps.ap(), func=AF.Ln, bias=epsc.ap(), scale=1.0 / 128), reads=[pk, "epsc"], writes=[sdk])
                    op("act", lambda e: e.activation(out=sd.ap(), in_=sd.ap(), func=AF.Exp, scale=-0.5), reads=[sdk], writes=[sdk])
                    op("dve", lambda e: e.scalar_tensor_tensor(out=sd.ap(), in0=ysb.ap()[:, tsl(tb)], scalar=gng.ap()[:, h:h + 1], in1=sd.ap(),
                                                               op0=ALU.mult, op1=ALU.mult), reads=yk + [sdk, "gng"], writes=[sdk])
                    op("dve", lambda e: e.tensor_tensor(out=yo.ap()[:, tsl(tb)], in0=sd.ap(), in1=gs.ap()[:, tsl(tb)], op=ALU.mult),
                       reads=[sdk, ("gs", tb)], writes=[("yo", tb)])
                k.dma(yT_dram[1, :, h, :], yo.ap(), reads=[("yo", tb) for tb in range(NTB)], writes=[("yT_dram", 1, h)])


    def mixer_dsa(l):
        NQ = S // 128
        with Scope(k) as sc:
            qT = sc.sb("qT", [128, 4, S], BF16)
            kTz = sc.sb("kTz", [128, 4, S], BF16)
            qiT = sc.sb("qiT", [128, 2, S], BF16)
            kiTz = sc.sb("kiTz", [128, 2, S], BF16)
            vtok = sc.sb("cvtok", [128, 2, NQ, 128], BF16)
            wi = sc.sb("wi", [128, NQ, 4], F32)
            wabs = sc.sb("wabs", [128, NQ, 4], F32)
            wsg = sc.sb("wsg", [128, NQ, 4], F32)
            yo = sc.sb("cyo", [128, 4, S], BF16)
            onesb = sc.sb("onesb", [128, 128], BF16)
            rt_r = sc.ring("rt", [128, TB], F32, 3)
            op("pool", lambda e: e.memset(onesb.ap(), 1.0), writes=["onesb"])
            with Scope(k) as s2:
                stg_r = s2.ring("cstg", [128, KC, 128], F32, 2)
                w_r = s2.ring("cw", [128, KC, 128], BF16, 2)
                wsw_r = s2.ring("cwsw", [128, KC, 128], BF16, 2)
                rope = s2.sb("rope", [128, 2, S], F32)
                k.dma(rope.ap(), c_rope.rearrange("a p s -> p a s"), writes=["rope"])

                def wload_cols(col_list, swap):
                    stg, sk = stg_r.next()
                    wt, wk = w_r.next()
                    o = 0
                    for (c0, n) in col_list:
                        if c0 is None:
                            op("pool", lambda e: e.memset(stg.ap()[:, :, o:o + n], 0.0), writes=[(sk, o), sk])
                        else:
                            k.dma(stg.ap()[:, :, o:o + n], P["w_in"][l][:, c0:c0 + n].rearrange("(c p) n -> p c n", p=128), writes=[(sk, o), sk])
                        o += n
                    rk = [(sk, oo) for oo in np.cumsum([0] + [n for _, n in col_list[:-1]]).tolist()] + [sk]
                    cast(wt.ap()[:, :, 0:o], stg.ap()[:, :, 0:o], rk, [wk])
                    if not swap:
                        return wt, wk, None, None
                    ws, wsk = wsw_r.next()
                    for b0 in range(0, o, 64):
                        cast(ws.ap()[:, :, b0:b0 + 32], stg.ap()[:, :, b0 + 32:b0 + 64], rk, [(wsk, b0), wsk])
                        cast(ws.ap()[:, :, b0 + 32:b0 + 64], stg.ap()[:, :, b0:b0 + 32], rk, [(wsk, b0 + 32), wsk])
                    return wt, wk, ws, [(wsk, b0) for b0 in range(0, o, 32)] + [wsk]

                def rope_proj(col_list, dst_fn, dkey):
                    wt, wk, ws, wsk = wload_cols(col_list, True)
                    for tb in range(NTB):
                        p1, pk1 = psr.next()
                        p2, pk2 = psr.next()
                        for kc in range(KC):
                            op("pe", lambda e: e.matmul(p1.ap(), lhsT=wt.ap()[:, kc, :], rhs=uT.ap()[:, kc, usl(tb)], start=(kc == 0), stop=(kc == KC - 1)),
                               reads=[wk] + ukeys(kc, tb), writes=[pk1])
                        for kc in range(KC):
                            op("pe", lambda e: e.matmul(p2.ap(), lhsT=ws.ap()[:, kc, :], rhs=uT.ap()[:, kc, usl(tb)], start=(kc == 0), stop=(kc == KC - 1)),
                               reads=wsk + ukeys(kc, tb), writes=[pk2])
                        t1, tk1 = rt_r.next()
                        t2, tk2 = rt_r.next()
                        op("dve", lambda e: e.tensor_tensor(out=t1.ap(), in0=rope.ap()[:, 0, tsl(tb)], in1=p1.ap(), op=ALU.mult), reads=["rope", pk1], writes=[tk1])
                        op("dve", lambda e: e.tensor_tensor(out=t2.ap(), in0=rope.ap()[:, 1, tsl(tb)], in1=p2.ap(), op=ALU.mult), reads=["rope", pk2], writes=[tk2])
                        op("pool", lambda e: e.tensor_tensor(out=dst_fn(tb), in0=t1.ap(), in1=t2.ap(), op=ALU.add), reads=[tk1, tk2], writes=[(dkey, tb)])

                co = C_OFF
                for ch in range(4):
                    rope_proj([(co + ch * 128, 128)], lambda tb: qT.ap()[:, ch, tsl(tb)], ("qT", ch))
                for c in range(2):
                    rope_proj([(co + 512 + c * 64, 64), (None, 64)], lambda tb: kTz.ap()[:, c * 2, tsl(tb)], ("kTz", c * 2))
                    rope_proj([(None, 64), (co + 512 + c * 64, 64)], lambda tb: kTz.ap()[:, c * 2 + 1, tsl(tb)], ("kTz", c * 2 + 1))
                for ch in range(2):
                    rope_proj([(co + 768 + ch * 128, 128)], lambda tb: qiT.ap()[:, ch, tsl(tb)], ("qiT", ch))
                rope_proj([(co + 1024, 64), (None, 64)], lambda tb: kiTz.ap()[:, 0, tsl(tb)], ("kiTz", 0))
                rope_proj([(None, 64), (co + 1024, 64)], lambda tb: kiTz.ap()[:, 1, tsl(tb)], ("kiTz", 1))
                for c in range(2):
                    wt, wk, _, _ = wload_cols([(co + 640 + c * 64, 64), (co + 640 + c * 64, 64)], False)
                    for s4 in range(NQ // 4):
                        ps, pk = psr.next()
                        for j in range(4):
                            sb = s4 * 4 + j
                            for kc in range(KC):
                                op("pe", lambda e: e.matmul(ps.ap()[:, j * 128:(j + 1) * 128], lhsT=uT.ap()[:, kc, 2 + sb * 128: 2 + (sb + 1) * 128],
                                                            rhs=wt.ap()[:, kc, :], start=(kc == 0), stop=(kc == KC - 1)),
                                   reads=[wk] + ukeys(kc, sb // 4), writes=[pk])
                        cast(vtok.ap()[:, c, s4 * 4:(s4 + 1) * 4, :], ps.ap().rearrange("p (a b) -> p a b", a=4), [pk], [("cvtok", c, s4)], eng="act")
                wt, wk, _, _ = wload_cols([(co + 1088, 4)], False)
                ps, pk = psr.next()
                for qb in range(NQ):
                    for kc in range(KC):
                        op("pe", lambda e: e.matmul(ps.ap()[:, qb * 4:(qb + 1) * 4], lhsT=uT.ap()[:, kc, 2 + qb * 128: 2 + (qb + 1) * 128],
                                                    rhs=wt.ap()[:, kc, 0:4], start=(kc == 0), stop=(kc == KC - 1)),
                           reads=[wk] + ukeys(kc, qb // 4), writes=[pk])
                op("dve", lambda e: e.tensor_copy(out=wi.ap(), in_=ps.ap()[:, 0:NQ * 4].rearrange("p (a b) -> p a b", b=4)), reads=[pk], writes=["wi"])
                op("act", lambda e: e.activation(out=wabs.ap(), in_=wi.ap(), func=AF.Abs), reads=["wi"], writes=["wabs"])
                op("act", lambda e: e.activation(out=wsg.ap(), in_=wi.ap(), func=AF.Sign), reads=["wi"], writes=["wsg"])
            NCHN = 4
            score2 = [sc.sb("score%d" % i, [128, S], F32) for i in range(NCHN)]
            Mf2 = [sc.sb("Mf%d" % i, [128, S], BF16) for i in range(NCHN)]
            MT2 = [sc.sb("MT%d" % i, [128, NQ, 128], BF16) for i in range(NCHN)]
            m82 = [sc.sb("m8_%d" % i, [128, 8], F32) for i in range(NCHN)]
            e_r = sc.ring("cE", [128, 4, 128], BF16, 3)
            p_r = sc.ring("cP", [128, 4, 128], BF16, 3)
            rec_r = sc.ring("crec", [128, TB], F32, 2)
            SENT = -3e30

            def scores(qb, i):
                score = score2[i]
                ncol = (qb + 1) * 128
                qs = slice(qb * 128, (qb + 1) * 128)
                for g0 in range(0, ncol, TB):
                    gn = min(TB, ncol - g0)
                    for hi in range(4):
                        ps, pk = psr.next()
                        op("pe", lambda e: e.matmul(ps.ap()[:, 0:gn], lhsT=qiT.ap()[:, hi // 2, qs], rhs=kiTz.ap()[:, hi % 2, g0:g0 + gn], start=True, stop=True),
                           reads=[(("qiT", hi // 2), qb // 4)] + [(("kiTz", hi % 2), tb) for tb in range(g0 // TB, (g0 + gn - 1) // TB + 1)], writes=[pk])
                        r, rk = rt_r.next()
                        op("act", lambda e: e.activation(out=r.ap()[:, 0:gn], in_=ps.ap()[:, 0:gn], func=AF.Relu, scale=wabs.ap()[:, qb, hi:hi + 1]),
                           reads=[pk, "wabs"], writes=[rk])
                        if hi == 0:
                            op("dve", lambda e: e.tensor_scalar(out=score.ap()[:, g0:g0 + gn], in0=r.ap()[:, 0:gn], scalar1=wsg.ap()[:, qb, hi:hi + 1],
                                                                scalar2=None, op0=ALU.mult), reads=[rk, "wsg"], writes=[("score", i, g0)])
                        else:
                            op("dve", lambda e: e.scalar_tensor_tensor(out=score.ap()[:, g0:g0 + gn], in0=r.ap()[:, 0:gn], scalar=wsg.ap()[:, qb, hi:hi + 1],
                                                                       in1=score.ap()[:, g0:g0 + gn], op0=ALU.mult, op1=ALU.add),
                               reads=[rk, "wsg", ("score", i, g0)], writes=[("score", i, g0)])
                sk_all = [("score", i, g0) for g0 in range(0, S, TB)]
                op("dve", lambda e: e.tensor_tensor(out=score.ap()[:, qs], in0=score.ap()[:, qs], in1=tri.ap()[:, 2, :], op=ALU.mult), reads=sk_all + ["tri"], writes=sk_all)
                op("dve", lambda e: e.tensor_tensor(out=score.ap()[:, qs], in0=score.ap()[:, qs], in1=tri.ap()[:, 3, :], op=ALU.add), reads=sk_all + ["tri"], writes=sk_all)

            NIT = 28
            W0 = 8192.0
            bs_S = [sc.sb("bsS%d" % i, [128, 1], F32) for i in range(NCHN)]
            bs_inc = [sc.sb("bsI%d" % i, [128, 1], F32) for i in range(NCHN)]
            bs_mid = [sc.sb("bsM%d" % i, [128, 1], F32) for i in range(NCHN)]
            junkA = sc.sb("junkA", [128, S], BF16)
            junkD = sc.sb("junkD", [128, S], BF16)

            def topk_gen(blocks):
                sk_all = lambda i: [("score", i, g0) for g0 in range(0, S, TB)]
                act = [(i, qb, (qb + 1) * 128) for i, qb in enumerate(blocks) if (qb + 1) * 128 > 256]
                on_dve = lambda i: (i == NCHN - 1)
                for i, qb, ncol in act:
                    op("dve", lambda e: e.memset(bs_mid[i].ap(), 0.0), writes=[("bsM", i)])
                w = W0
                for it in range(NIT if act else 0):
                    for i, qb, ncol in act:
                        if on_dve(i):
                            op("dve", lambda e: e.tensor_scalar(out=junkD.ap()[:, 0:ncol], in0=score2[i].ap()[:, 0:ncol], scalar1=bs_mid[i].ap(), scalar2=None,
                                                                op0=ALU.is_ge, op1=ALU.add, accum_out=bs_S[i].ap()),
                               reads=sk_all(i) + [("bsM", i)], writes=[("bsS", i)])
                        else:
                            op("act", lambda e: e.activation(out=junkA.ap()[:, 0:ncol], in_=score2[i].ap()[:, 0:ncol], func=AF.Sign, bias=bs_mid[i].ap(), scale=1.0,
                                                             accum_out=bs_S[i].ap()),
                               reads=sk_all(i) + [("bsM", i)], writes=[("bsS", i)])
                    for i, qb, ncol in act:
                        kthr = 256.0 if on_dve(i) else float(512 - ncol)
                        op("dve", lambda e: e.tensor_scalar(out=bs_inc[i].ap(), in0=bs_S[i].ap(), scalar1=kthr, scalar2=w / 2, op0=ALU.is_ge, op1=ALU.mult),
                           reads=[("bsS", i)], writes=[("bsI", i)])
                        if on_dve(i):
                            op("dve", lambda e: e.scalar_tensor_tensor(out=bs_mid[i].ap(), in0=bs_mid[i].ap(), scalar=-w / 4, in1=bs_inc[i].ap(), op0=ALU.add, op1=ALU.add),
                               reads=[("bsM", i), ("bsI", i)], writes=[("bsM", i)])
                        else:
                            op("dve", lambda e: e.scalar_tensor_tensor(out=bs_mid[i].ap(), in0=bs_mid[i].ap(), scalar=w / 4, in1=bs_inc[i].ap(), op0=ALU.add, op1=ALU.subtract),
                               reads=[("bsM", i), ("bsI", i)], writes=[("bsM", i)])
                    w = w / 2
                    yield
                for i, qb in enumerate(blocks):
                    ncol = (qb + 1) * 128
                    if ncol > 256:
                        if on_dve(i):
                            op("dve", lambda e: e.tensor_scalar(out=bs_inc[i].ap(), in0=bs_mid[i].ap(), scalar1=-w / 2, scalar2=None, op0=ALU.add), reads=[("bsM", i)], writes=[("bsI", i)])
                        else:
                            op("dve", lambda e: e.tensor_scalar(out=bs_inc[i].ap(), in0=bs_mid[i].ap(), scalar1=-1.0, scalar2=-w / 2, op0=ALU.mult, op1=ALU.add), reads=[("bsM", i)], writes=[("bsI", i)])
                        op("dve", lambda e: e.tensor_scalar(out=Mf2[i].ap()[:, 0:ncol], in0=score2[i].ap()[:, 0:ncol], scalar1=bs_inc[i].ap(), scalar2=None, op0=ALU.is_ge),
                           reads=sk_all(i) + [("bsI", i)], writes=[("Mf", i)])
                    else:
                        op("dve", lambda e: e.tensor_scalar(out=Mf2[i].ap()[:, 0:ncol], in0=score2[i].ap()[:, 0:ncol], scalar1=-1e29, scalar2=None, op0=ALU.is_ge),
                           reads=sk_all(i), writes=[("Mf", i)])
                yield

            def maskT(qb, i):
                for s4 in range(0, qb + 1, 4):
                    nb = min(4, qb + 1 - s4)
                    ps, pk = psr.next()
                    for j in range(nb):
                        sb = s4 + j
                        op("pe", lambda e: e.transpose(out=ps.ap().bitcast(BF16)[:, j * 128:(j + 1) * 128], in_=Mf2[i].ap()[:, sb * 128:(sb + 1) * 128], identity=identb.ap()),
                           reads=[("Mf", i), "identb"], writes=[pk])
                    op("act", lambda e: e.copy(out=MT2[i].ap()[:, s4:s4 + nb, :], in_=ps.ap().bitcast(BF16)[:, 0:nb * 128].rearrange("p (a b) -> p a b", b=128)),
                       reads=[pk], writes=[("MT", i, s4)])

            def attn_gen(qb, i):
                qs = slice(qb * 128, (qb + 1) * 128)
                MT = MT2[i]
                for c in range(2):
                    ps_o, pk_o = psacc.next()
                    ps_d, pk_d = psacc.next()

                    def front(sb):
                        ps, pk = psr.next()
                        for j in range(4):
                            hq = c * 4 + j
                            op("pe", lambda e: e.matmul(ps.ap()[:, j * 128:(j + 1) * 128], lhsT=kTz.ap()[:, c * 2 + hq % 2, sb * 128:(sb + 1) * 128],
                                                        rhs=qT.ap()[:, hq // 2, qs], start=True, stop=True),
                               reads=[(("kTz", c * 2 + hq % 2), sb // 4), (("qT", hq // 2), qb // 4)], writes=[pk])
                        E, Ek = e_r.next()
                        op("act", lambda e: e.activation(out=E.ap(), in_=ps.ap().rearrange("p (a b) -> p a b", a=4), func=AF.Exp, scale=0.125), reads=[pk], writes=[Ek])
                        Pm, Pk = p_r.next()
                        me = "pool" if (sb % 3) else "dve"
                        op(me, lambda e: e.tensor_tensor(out=Pm.ap(), in0=E.ap(), in1=MT.ap()[:, sb:sb + 1, :].to_broadcast([128, 4, 128]), op=ALU.mult),
                           reads=[Ek, ("MT", i, (sb // 4) * 4)], writes=[Pk])
                        return Pm, Pk
                    nxt_p = front(0)
                    for sb in range(qb + 1):
                        Pm, Pk = nxt_p
                        if sb + 1 <= qb:
                            nxt_p = front(sb + 1)
                        op("pe", lambda e: e.matmul(ps_o.ap(), lhsT=vtok.ap()[:, c, sb, :], rhs=Pm.ap().rearrange("p a b -> p (a b)"), start=(sb == 0), stop=(sb == qb)),
                           reads=[("cvtok", c, sb // 4), Pk], writes=[pk_o])
                        op("pe", lambda e: e.matmul(ps_d.ap(), lhsT=onesb.ap(), rhs=Pm.ap().rearrange("p a b -> p (a b)"), start=(sb == 0), stop=(sb == qb)),
                           reads=["onesb", Pk], writes=[pk_d])
                        yield
                    rec, reck = rec_r.next()
                    op("dve", lambda e: e.reciprocal(out=rec.ap(), in_=ps_d.ap()), reads=[pk_d], writes=[reck])
                    for j in range(4):
                        hq = c * 4 + j
                        hb = (hq % 2) * 64
                        op("dve", lambda e: e.tensor_tensor(out=yo.ap()[hb:hb + 64, hq // 2, qs], in0=rec.ap()[hb:hb + 64, j * 128:(j + 1) * 128],
                                                            in1=ps_o.ap()[hb:hb + 64, j * 128:(j + 1) * 128], op=ALU.mult),
                           reads=[reck, pk_o], writes=[("cyo", hq // 2, hq % 2)])

            def run(g):
                for _ in g:
                    pass

            def zipgens(gt, ga_list, n_t, n_a):
                ga = (x for g in ga_list for x in g)
                per = max(1, -(-n_a // max(1, n_t)))
                done_a = False
                for _ in gt:
                    for _j in range(per):
                        if next(ga, "END") == "END":
                            done_a = True
                            break
                if not done_a:
                    for _ in ga:
                        pass

            grps = [tuple(range(g * NCHN, (g + 1) * NCHN)) for g in range(NQ // NCHN)]
            for i_, qb_ in enumerate(grps[0]):
                scores(qb_, i_)
            run(topk_gen(grps[0]))
            for i_, qb_ in enumerate(grps[0]):
                maskT(qb_, i_)
            for p in range(len(grps)):
                cur = grps[p]
                if p + 1 < len(grps):
                    nxt = grps[p + 1]
                    for i_, qb_ in enumerate(nxt):
                        scores(qb_, i_)
                    n_a = sum(2 * (q_ + 1) for q_ in cur)
                    zipgens(topk_gen(nxt), [attn_gen(q_, i_) for i_, q_ in enumerate(cur)], NIT + 1, n_a)
                    for i_, qb_ in enumerate(nxt):
                        maskT(qb_, i_)
                else:
                    for i_, q_ in enumerate(cur):
                        run(attn_gen(q_, i_))
            for ch in range(4):
                k.dma(yT_dram[2, :, ch, :], yo.ap()[:, ch, :], reads=[("cyo", ch, 0), ("cyo", ch, 1)], writes=[("yT_dram", 2, ch)])

    def mixer_rwkv(l):
        C = 128; NCH = S // C
        with Scope(k) as sc:
            stg_r = sc.ring("astg", [128, KC, 128], F32, 1)
            stm_r = sc.ring("astm", [128, KC, 128], F32, 1)
            w1_r = sc.ring("aw1", [128, KC, 128], BF16, 1)
            wm_r = sc.ring("awm", [128, KC, 128], BF16, 1)
            mub_r = sc.ring("amub", [128, 128], F32, 2)
            lrb = sc.sb("lrb", [128, 3, 512], BF16)
            with Scope(k) as s2:
                lrs = s2.sb("lrs", [128, 3, 512], F32)
                k.dma(lrs.ap()[0:64, 0, :], P["a_w_up"][l], writes=["lrs0"]); k.dma(lrs.ap()[0:64, 1, :], P["a_a_up"][l], writes=["lrs1"])
                k.dma(lrs.ap()[:, 2, :], P["a_g_up"][l], writes=["lrs2"])
                op("dve", lambda e: e.tensor_copy(out=lrb.ap()[0:64, 0:2, :], in_=lrs.ap()[0:64, 0:2, :]), reads=["lrs0", "lrs1"], writes=["lrb01"])
                op("dve", lambda e: e.tensor_copy(out=lrb.ap()[:, 2, :], in_=lrs.ap()[:, 2, :]), reads=["lrs2"], writes=["lrb2"])

            def proj_shift(c0, ncol, consume, token_major=None):
                stg, sk = stg_r.next(); stm, smk = stm_r.next(); w1, w1k = w1_r.next(); wm, wmk = wm_r.next(); mub, mk = mub_r.next()
                k.dma(stg.ap()[:, :, 0:ncol], P["w_in"][l][:, c0:c0 + ncol].rearrange("(c p) n -> p c n", p=128), writes=[sk])
                k.dma(mub.ap()[:, 0:ncol], P["a_mu"][l][c0 - A_OFF:c0 - A_OFF + ncol].partition_broadcast(128), writes=[mk])
                op("dve", lambda e: e.tensor_tensor(out=stm.ap()[:, :, 0:ncol], in0=stg.ap()[:, :, 0:ncol],
                                                    in1=mub.ap()[:, 0:ncol].unsqueeze(1).to_broadcast([128, KC, ncol]), op=ALU.mult),
                   reads=[sk, mk], writes=[smk])
                op("act", lambda e: e.copy(out=wm.ap()[:, :, 0:ncol], in_=stm.ap()[:, :, 0:ncol]), reads=[smk], writes=[wmk])
                op("dve", lambda e: e.tensor_tensor(out=w1.ap()[:, :, 0:ncol], in0=stg.ap()[:, :, 0:ncol], in1=stm.ap()[:, :, 0:ncol], op=ALU.subtract),
                   reads=[sk, smk], writes=[w1k])
                if token_major is not None:
                    token_major(w1, w1k, wm, wmk)
                    return
                for tb in range(NTB):
                    ps, pk = psr.next()
                    for kc in range(KC):
                        op("pe", lambda e: e.matmul(ps.ap()[0:ncol, :], lhsT=w1.ap()[:, kc, 0:ncol], rhs=uT.ap()[:, kc, usl(tb)], start=(kc == 0), stop=False),
                           reads=[w1k] + ukeys(kc, tb), writes=[pk])
                    for kc in range(KC):
                        op("pe", lambda e: e.matmul(ps.ap()[0:ncol, :], lhsT=wm.ap()[:, kc, 0:ncol], rhs=uT.ap()[:, kc, usl(tb, 1)], start=False, stop=(kc == KC - 1)),
                           reads=[wmk] + ukeys(kc, tb) + (ukeys(kc, tb - 1) if tb else []) + ["uTpad"], writes=[pk])
                    consume(tb, ps, pk)

            tw = sc.sb("tw", [64, S], BF16); adT = sc.sb("adT", [64, S], BF16); sgT = sc.sb("sgT", [128, S], BF16)
            proj_shift(A_OFF + 1536, 64, lambda tb, ps, pk: op("act", lambda e: e.activation(out=tw.ap()[:, tsl(tb)], in_=ps.ap()[0:64, :], func=AF.Tanh), reads=[pk], writes=[("tw", tb)]))
            proj_shift(A_OFF + 1600, 64, lambda tb, ps, pk: op("act", lambda e: e.copy(out=adT.ap()[:, tsl(tb)], in_=ps.ap()[0:64, :]), reads=[pk], writes=[("adT", tb)]))
            proj_shift(A_OFF + 1664, 128, lambda tb, ps, pk: op("act", lambda e: e.activation(out=sgT.ap()[:, tsl(tb)], in_=ps.ap(), func=AF.Sigmoid), reads=[pk], writes=[("sgT", tb)]))
            pc = sc.sb("pc", [128, 8, 4], F32)
            for i_, nm in enumerate(["a_w0", "a_a0", "a_k_k", "a_k_a", "a_r_k", "a_gn_g", "a_gn_b"]):
                load_pvec(pc.ap()[:, i_, :], P[nm][l], 4, [("pc", i_)])
            op("dve", lambda e: e.tensor_scalar(out=pc.ap()[:, 7, :], in0=pc.ap()[:, 3, :], scalar1=-1.0, scalar2=1.0, op0=ALU.mult, op1=ALU.add), reads=[("pc", 3)], writes=[("pc", 7)])
            pcall = [("pc", i_) for i_ in range(8)]
            gnc = sc.sb("gnc", [128, 1], F32)
            op("pool", lambda e: e.memset(gnc.ap(), 64e-5), writes=["gnc"])
            F = {n_: sc.sb("af_" + n_, [128, S], F32) for n_ in ["r", "k", "v", "a", "L", "lw", "t1", "t2"]}
            F["g"] = sc.sb("af_g", [128, S], BF16)
            Bh = {n_: [sc.sb("ab_%s%d" % (n_, h), [128, S], BF16) for h in range(2)] for n_ in ["bt", "kt", "bc", "kc"]}
            Bh["rh"] = [sc.sb("ab_rh", [128, S], BF16)] * 2
            Bh["ah"] = [sc.sb("ab_ah", [128, S], BF16)] * 2
            AR = sc.sb("AR", [128, NCH, 2, C], BF16)
            dcol = sc.sb("adcol", [128, 3, NCH], F32)
            vtok = sc.sb("avtok", [128, NCH, 128], BF16)
            vpad = [sc.sb("avpad%d" % h, [128, NCH, 128], BF16) for h in range(2)]
            H = [sc.sb("aH%d" % h, [128, 128], F32) for h in range(2)]
            Hb_r = [sc.ring("aHb%d" % h, [128, 128], BF16, 2) for h in range(2)]
            upad = [sc.ring("aupad%d" % h, [128, 128], BF16, 2) for h in range(2)]
            mt_r = sc.ring("amt", [128, 128], BF16, 8)
            kp_r = sc.ring("akp", [128, 128], BF16, 12)
            wu_r = sc.ring("awu", [128, 128], BF16, 2)
            tk_r = sc.ring("atk", [128, 128], BF16, 6)
            ya = F["t1"]
            yo = Bh["rh"][0]
            hm = lambda h: tri.ap()[:, 6, h:h + 1]
            for h in range(2):
                for r_ in upad[h].t:
                    op("pool", lambda e: e.memset(r_.ap(), 0.0), writes=[("upad0", h, r_.name)])
            for hp in range(4):
                co = A_OFF + hp * 128
                prm = lambda i_: pc.ap()[:, i_, hp:hp + 1]
                for nm, cc in (("r", co), ("k", co + 512), ("v", co + 1024)):
                    proj_shift(cc, 128, lambda tb, ps, pk: op("act", lambda e: e.copy(out=F[nm].ap()[:, tsl(tb)], in_=ps.ap()), reads=[pk], writes=[(nm, tb)]))
                allk = lambda nm: [(nm, tb) for tb in range(NTB)]

                def vtm(w1, w1k, wm, wmk):
                    for c4 in range(NCH // 4):
                        ps, pk = psr.next()
                        for j in range(4):
                            c = c4 * 4 + j
                            for kc in range(KC):
                                op("pe", lambda e: e.matmul(ps.ap()[:, j * 128:(j + 1) * 128], lhsT=uT.ap()[:, kc, 2 + c * C:2 + (c + 1) * C], rhs=w1.ap()[:, kc, :], start=(kc == 0), stop=False),
                                   reads=[w1k] + ukeys(kc, c // 4), writes=[pk])
                            for kc in range(KC):
                                op("pe", lambda e: e.matmul(ps.ap()[:, j * 128:(j + 1) * 128], lhsT=uT.ap()[:, kc, 1 + c * C:1 + (c + 1) * C], rhs=wm.ap()[:, kc, :], start=False, stop=(kc == KC - 1)),
                                   reads=[wmk] + ukeys(kc, c // 4) + (ukeys(kc, c // 4 - 1) if c >= 4 else []) + ["uTpad"], writes=[pk])
                        cast(vtok.ap()[:, c4 * 4:(c4 + 1) * 4, :], ps.ap().rearrange("p (a b) -> p a b", a=4), [pk], [("avtok", c4)], eng="act")
                proj_shift(co + 1024, 128, None, token_major=vtm)
                vk = [("avtok", c4) for c4 in range(NCH // 4)]
                for h in range(2):
                    op("pool", lambda e: e.tensor_tensor(out=vpad[h].ap(), in0=vtok.ap(), in1=tri.ap()[:, 5, h * 64:h * 64 + 1].to_broadcast([128, NCH, 128]) if False else
                                                         tri.ap()[0:1, 5, :].partition_broadcast(128).unsqueeze(1).to_broadcast([128, NCH, 128]) if False else vtok.ap(), op=ALU.mult) if False else
                       e.memset(vpad[h].ap(), 0.0), reads=vk, writes=[("vpad", h)])
                    op("act", lambda e: e.copy(out=vpad[h].ap()[:, :, h * 64:(h + 1) * 64], in_=vtok.ap()[:, :, h * 64:(h + 1) * 64]), reads=vk + [("vpad", h)], writes=[("vpad", h)])
                for tb in range(NTB):
                    ps, pk = psr.next()
                    op("pe", lambda e: e.matmul(ps.ap(), lhsT=lrb.ap()[0:64, 0, hp * 128:(hp + 1) * 128], rhs=tw.ap()[:, tsl(tb)], start=True, stop=True), reads=["lrb01", ("tw", tb)], writes=[pk])
                    op("act", lambda e: e.activation(out=F["lw"].ap()[:, tsl(tb)], in_=ps.ap(), func=AF.Sigmoid, bias=prm(0)), reads=[pk] + pcall, writes=[("lw", tb)])
                    ps, pk = psr.next()
                    op("pe", lambda e: e.matmul(ps.ap(), lhsT=lrb.ap()[0:64, 1, hp * 128:(hp + 1) * 128], rhs=adT.ap()[:, tsl(tb)], start=True, stop=True), reads=["lrb01", ("adT", tb)], writes=[pk])
                    op("act", lambda e: e.activation(out=F["a"].ap()[:, tsl(tb)], in_=ps.ap(), func=AF.Sigmoid, bias=prm(1)), reads=[pk] + pcall, writes=[("a", tb)])
                    ps, pk = psr.next()
                    op("pe", lambda e: e.matmul(ps.ap(), lhsT=lrb.ap()[:, 2, hp * 128:(hp + 1) * 128], rhs=sgT.ap()[:, tsl(tb)], start=True, stop=True), reads=["lrb2", ("sgT", tb)], writes=[pk])
                    op("act", lambda e: e.copy(out=F["g"].ap()[:, tsl(tb)], in_=ps.ap()), reads=[pk], writes=[("g", tb)])
                op("dve", lambda e: e.tensor_scalar(out=F["lw"].ap(), in0=F["lw"].ap(), scalar1=-0.6065306597126334, scalar2=None, op0=ALU.mult), reads=allk("lw"), writes=allk("lw"))
                op("dve", lambda e: e.tensor_tensor_scan(out=F["L"].ap(), data0=ones.ap()[:, 0:1].to_broadcast([128, S]), data1=F["lw"].ap(), initial=0.0, op0=ALU.mult, op1=ALU.add), reads=allk("lw") + ["ones"], writes=["L"])
                L3 = F["L"].ap().rearrange("p (c t) -> p c t", t=C)
                op("dve", lambda e: e.tensor_tensor(out=dcol.ap()[:, 0, :], in0=L3[:, :, C - 1], in1=L3[:, :, C // 2 - 1], op=ALU.subtract), reads=["L"], writes=[("dc", 0)])
                op("dve", lambda e: e.tensor_copy(out=dcol.ap()[:, 1, 0:1], in_=L3[:, 0, C // 2 - 1:C // 2]), reads=["L"], writes=[("dc", 1)])
                op("dve", lambda e: e.tensor_tensor(out=dcol.ap()[:, 1, 1:NCH], in0=L3[:, 1:NCH, C // 2 - 1], in1=L3[:, 0:NCH - 1, C - 1], op=ALU.subtract), reads=["L", ("dc", 1)], writes=[("dc", 1)])
                op("dve", lambda e: e.tensor_tensor(out=dcol.ap()[:, 2, :], in0=dcol.ap()[:, 0, :], in1=dcol.ap()[:, 1, :], op=ALU.add), reads=[("dc", 0), ("dc", 1)], writes=[("dc", 2)])
                dck = [("dc", 0), ("dc", 1), ("dc", 2)]
                op("act", lambda e: e.activation(out=dcol.ap(), in_=dcol.ap(), func=AF.Exp), reads=dck, writes=dck)
                t1, t2 = F["t1"], F["t2"]
                op("dve", lambda e: e.tensor_scalar(out=t1.ap(), in0=F["k"].ap(), scalar1=prm(2), scalar2=None, op0=ALU.mult), reads=allk("k") + pcall, writes=["t1"])
                op("act", lambda e: e.activation(out=t2.ap(), in_=t1.ap(), func=AF.Square), reads=["t1"], writes=["t2"])
                pss = [psr.next() for _ in range(NTB)]
                for tb in range(NTB):
                    op("pe", lambda e: e.matmul(pss[tb][0].ap(), lhsT=tri.ap()[:, 5, :], rhs=t2.ap()[:, tsl(tb)], start=True, stop=True), reads=["t2", "tri"], writes=[pss[tb][1]])
                for tb in range(NTB):
                    op("dve", lambda e: e.tensor_scalar(out=t2.ap()[:, tsl(tb)], in0=pss[tb][0].ap(), scalar1=1e-19, scalar2=None, op0=ALU.max), reads=[pss[tb][1], "t2"], writes=["t2"])
                op("act", lambda e: e.activation(out=t2.ap(), in_=t2.ap(), func=AF.Ln), reads=["t2"], writes=["t2"])
                op("act", lambda e: e.activation(out=t2.ap(), in_=t2.ap(), func=AF.Exp, scale=-0.5), reads=["t2"], writes=["t2"])
                op("dve", lambda e: e.tensor_tensor(out=t1.ap(), in0=t1.ap(), in1=t2.ap(), op=ALU.mult), reads=["t1", "t2"], writes=["t1"])
                op("dve", lambda e: e.tensor_scalar(out=t2.ap(), in0=F["a"].ap(), scalar1=prm(3), scalar2=prm(7), op0=ALU.mult, op1=ALU.add), reads=allk("a") + pcall, writes=["t2"])
                op("dve", lambda e: e.tensor_tensor(out=F["k"].ap(), in0=F["k"].ap(), in1=t2.ap(), op=ALU.mult), reads=allk("k") + ["t2"], writes=allk("k"))
                op("dve", lambda e: e.tensor_tensor(out=F["a"].ap(), in0=F["a"].ap(), in1=t1.ap(), op=ALU.mult), reads=allk("a") + ["t1"], writes=allk("a"))
                op("dve", lambda e: e.scalar_tensor_tensor(out=t2.ap(), in0=F["r"].ap(), scalar=prm(4), in1=F["k"].ap(), op0=ALU.mult, op1=ALU.mult), reads=allk("r") + allk("k") + pcall, writes=["t2"])
                pss = [psr.next() for _ in range(NTB)]
                for tb in range(NTB):
                    op("pe", lambda e: e.matmul(pss[tb][0].ap(), lhsT=tri.ap()[:, 5, :], rhs=t2.ap()[:, tsl(tb)], start=True, stop=True), reads=["t2", "tri"], writes=[pss[tb][1]])
                for tb in range(NTB):
                    op("dve", lambda e: e.tensor_tensor(out=F["v"].ap()[:, tsl(tb)], in0=F["v"].ap()[:, tsl(tb)], in1=pss[tb][0].ap(), op=ALU.mult), reads=[("v", tb), pss[tb][1]], writes=[("v", tb)])
                m_b = L3[:, :, C // 2 - 1:C // 2].to_broadcast([128, NCH, C])
                v3 = lambda t_: t_.ap().rearrange("p (c t) -> p c t", t=C)
                op("dve", lambda e: e.tensor_tensor(out=v3(t2), in0=L3, in1=m_b, op=ALU.subtract), reads=["L"], writes=["t2"])
                op("dve", lambda e: e.tensor_tensor(out=F["lw"].ap(), in0=t2.ap(), in1=F["lw"].ap(), op=ALU.subtract), reads=["t2"] + allk("lw"), writes=allk("lw"))
                op("act", lambda e: e.activation(out=F["lw"].ap(), in_=F["lw"].ap(), func=AF.Exp), reads=allk("lw"), writes=allk("lw"))
                op("act", lambda e: e.activation(out=F["L"].ap(), in_=t2.ap(), func=AF.Exp, scale=-1.0), reads=["t2"], writes=["L"])
                op("act", lambda e: e.activation(out=t2.ap(), in_=t2.ap(), func=AF.Exp), reads=["t2"], writes=["t2"])
                op("dve", lambda e: e.tensor_tensor(out=F["r"].ap(), in0=F["r"].ap(), in1=t2.ap(), op=ALU.mult), reads=allk("r") + ["t2"], writes=allk("r"))
                op("act", lambda e: e.copy(out=AR.ap()[:, :, 1, :], in_=v3(F["r"])), reads=allk("r"), writes=["AR1"])
                dB = dcol.ap()[:, 1, :].unsqueeze(2).to_broadcast([128, NCH, C]); dA = dcol.ap()[:, 0, :].unsqueeze(2).to_broadcast([128, NCH, C])
                op("dve", lambda e: e.tensor_tensor(out=v3(t2), in0=v3(F["r"]), in1=dB, op=ALU.mult), reads=allk("r") + dck, writes=["t2"])
                op("act", lambda e: e.copy(out=Bh["rh"][0].ap(), in_=t2.ap()), reads=["t2"], writes=[("rh", 0), ("rh", 1)] + [("yo", tb) for tb in range(NTB)])
                op("dve", lambda e: e.scalar_tensor_tensor(out=F["lw"].ap(), in0=t1.ap(), scalar=-1.0, in1=F["lw"].ap(), op0=ALU.mult, op1=ALU.mult), reads=["t1"] + allk("lw"), writes=allk("lw"))
                op("act", lambda e: e.copy(out=AR.ap()[:, :, 0, :], in_=v3(F["lw"])), reads=allk("lw"), writes=["AR0"])
                op("dve", lambda e: e.tensor_tensor(out=v3(t2), in0=v3(F["lw"]), in1=dB, op=ALU.mult), reads=allk("lw") + dck + [("rh", 0), ("rh", 1)], writes=["t2"])
                op("act", lambda e: e.copy(out=Bh["ah"][0].ap(), in_=t2.ap()), reads=["t2"], writes=[("ah", 0), ("ah", 1)])
                for src, nt, ncn in ((F["a"], "bt", "bc"), (F["k"], "kt", "kc")):
                    sk_ = allk("a") if nt == "bt" else allk("k")
                    op("dve", lambda e: e.tensor_tensor(out=src.ap(), in0=src.ap(), in1=F["L"].ap(), op=ALU.mult), reads=sk_ + ["L"], writes=sk_)
                    op("dve", lambda e: e.tensor_tensor(out=v3(t1) if nt == "kt" else v3(t2), in0=v3(src), in1=dA, op=ALU.mult), reads=sk_ + dck + [("ah", 0), ("ah", 1)], writes=["t1" if nt == "kt" else "t2"])
                    tt_ = t1 if nt == "kt" else t2
                    for h in range(2):
                        op("act", lambda e: e.activation(out=Bh[nt][h].ap(), in_=src.ap(), func=AF.Copy, scale=hm(h)), reads=sk_ + ["tri"], writes=[(nt, h)])
                        op("act", lambda e: e.activation(out=Bh[ncn][h].ap(), in_=tt_.ap(), func=AF.Copy, scale=hm(h)), reads=["t1" if nt == "kt" else "t2", "tri"], writes=[(ncn, h)])
                k.phase_barrier()
                pool_t = []
                for nm_ in ("r", "k", "a", "L", "lw", "t2"):
                    vb = F[nm_].ap().bitcast(BF16)
                    pool_t += [vb[:, j * 128:(j + 1) * 128] for j in range(32)]
                RES = {}
                it_ = iter(pool_t)
                for c_ in range(NCH):
                    for h_ in range(2):
                        for nm_ in ("TT", "rb", "ak", "rk", "bcT", "kcT"):
                            RES[(nm_, h_, c_)] = next(it_)
                vt_b = vtok.ap().rearrange("p a b -> p (a b)")
                tmp_t = [t_.ap() for t_ in mt_r.t] + [t_.ap() for t_ in tk_r.t] + [t_.ap() for t_ in kp_r.t] + [vt_b[:, j * 128:(j + 1) * 128] for j in range(16)]
                GCH = 8
                assert len(tmp_t) >= 5 * GCH, len(tmp_t)

                def A_gen(group):
                    st = {}
                    for gi_, (h, c) in enumerate(group):
                        cs = slice(c * C, (c + 1) * C)
                        T5 = tmp_t[gi_ * 5:(gi_ + 1) * 5]
                        tk_ = lambda j: ("tmp", gi_, j)
                        ps, pk = psr.next()
                        op("pe", lambda e: e.matmul(ps.ap()[:, 0:256], lhsT=Bh["bt"][h].ap()[:, cs], rhs=AR.ap()[:, c].rearrange("p a b -> p (a b)"), start=True, stop=True), reads=[("bt", h), "AR0", "AR1"], writes=[pk])
                        op("dve", lambda e: e.tensor_tensor(out=T5[1], in0=ps.ap()[:, 0:128], in1=tri.ap()[:, 1, :], op=ALU.mult), reads=[pk, "tri"], writes=[tk_(1)])
                        op("dve", lambda e: e.tensor_tensor(out=RES[("rb", h, c)], in0=ps.ap()[:, 128:256], in1=tri.ap()[:, 0, :], op=ALU.mult), reads=[pk, "tri"], writes=[("res", "rb", h, c)])
                        ps, pk = psr.next()
                        op("pe", lambda e: e.matmul(ps.ap()[:, 0:256], lhsT=Bh["kt"][h].ap()[:, cs], rhs=AR.ap()[:, c].rearrange("p a b -> p (a b)"), start=True, stop=True), reads=[("kt", h), "AR0", "AR1"], writes=[pk])
                        op("dve", lambda e: e.tensor_tensor(out=RES[("ak", h, c)], in0=ps.ap()[:, 0:128], in1=tri.ap()[:, 1, :], op=ALU.mult), reads=[pk, "tri"], writes=[("res", "ak", h, c)])
                        op("dve", lambda e: e.tensor_tensor(out=RES[("rk", h, c)], in0=ps.ap()[:, 128:256], in1=tri.ap()[:, 0, :], op=ALU.mult), reads=[pk, "tri"], writes=[("res", "rk", h, c)])
                        ps, pk = psr.next()
                        op("pe", lambda e: e.matmul(ps.ap()[:, 0:128], lhsT=AR.ap()[:, c, 0, :], rhs=Bh["bt"][h].ap()[:, cs], start=True, stop=True), reads=["AR0", ("bt", h)], writes=[pk])
                        op("dve", lambda e: e.tensor_tensor(out=T5[0], in0=ps.ap()[:, 0:128], in1=tri.ap()[:, 4, :], op=ALU.mult), reads=[pk, "tri"], writes=[tk_(0)])
                        op("dve", lambda e: e.tensor_tensor(out=T5[4], in0=T5[1], in1=identb.ap(), op=ALU.add), reads=[tk_(1), "identb"], writes=[tk_(4)])
                        pst, pkt = psr.next()
                        op("pe", lambda e: e.transpose(out=pst.ap().bitcast(BF16)[:, 0:128], in_=Bh["bc"][h].ap()[:, cs], identity=identb.ap()), reads=[("bc", h), "identb"], writes=[pkt])
                        op("pe", lambda e: e.transpose(out=pst.ap().bitcast(BF16)[:, 128:256], in_=Bh["kc"][h].ap()[:, cs], identity=identb.ap()), reads=[("kc", h), "identb"], writes=[pkt])
                        op("act", lambda e: e.copy(out=RES[("bcT", h, c)], in_=pst.ap().bitcast(BF16)[:, 0:128]), reads=[pkt], writes=[("res", "bcT", h, c)])
                        op("act", lambda e: e.copy(out=RES[("kcT", h, c)], in_=pst.ap().bitcast(BF16)[:, 128:256]), reads=[pkt], writes=[("res", "kcT", h, c)])
                        st[gi_] = [0, 1, 2, 3]
                    yield
                    for lev in range(6):
                        for gi_, (h, c) in enumerate(group):
                            T5 = tmp_t[gi_ * 5:(gi_ + 1) * 5]
                            tk_ = lambda j: ("tmp", gi_, j)
                            iX, iXT, iXn, iXTn = st[gi_]
                            psx, pkx = psr.next()
                            op("pe", lambda e: e.matmul(psx.ap()[:, 0:128], lhsT=T5[iXT], rhs=T5[iX], start=True, stop=True), reads=[tk_(iXT), tk_(iX)], writes=[pkx])
                            if lev < 5:
                                op("pe", lambda e: e.matmul(psx.ap()[:, 128:256], lhsT=T5[iX], rhs=T5[iXT], start=True, stop=True), reads=[tk_(iXT), tk_(iX)], writes=[pkx])
                            op("act", lambda e: e.copy(out=T5[iXn], in_=psx.ap()[:, 0:128]), reads=[pkx], writes=[tk_(iXn)])
                            if lev < 5:
                                op("act", lambda e: e.copy(out=T5[iXTn], in_=psx.ap()[:, 128:256]), reads=[pkx], writes=[tk_(iXTn)])
                            st[gi_] = [iXn, iXTn, iX, iXT]
                        yield
                        for gi_, (h, c) in enumerate(group):
                            T5 = tmp_t[gi_ * 5:(gi_ + 1) * 5]
                            tk_ = lambda j: ("tmp", gi_, j)
                            iX = st[gi_][0]
                            psq, pkq = psr.next()
                            op("pe", lambda e: e.matmul(psq.ap()[:, 0:128], lhsT=T5[iX], rhs=T5[4], start=True, stop=True), reads=[tk_(iX), tk_(4)], writes=[pkq])
                            if lev < 5:
                                op("dve", lambda e: e.tensor_tensor(out=T5[4], in0=T5[4], in1=psq.ap()[:, 0:128], op=ALU.add), reads=[tk_(4), pkq], writes=[tk_(4)])
                            else:
                                op("dve", lambda e: e.tensor_tensor(out=RES[("TT", h, c)], in0=T5[4], in1=psq.ap()[:, 0:128], op=ALU.add), reads=[tk_(4), pkq], writes=[("res", "TT", h, c)])
                        yield

                Hb = [None, None]

                def B_gen(chunks):
                    for c in chunks:
                        cs = slice(c * C, (c + 1) * C)
                        rk_ = lambda nm_, h: ("res", nm_, h, c)
                        wus = []
                        for h in range(2):
                            psw, pkw = psr.next()
                            if c > 0:
                                op("pe", lambda e: e.matmul(psw.ap()[:, 0:128], lhsT=Bh["ah"][h].ap()[:, cs], rhs=Hb[h][0].ap(), start=True, stop=False), reads=[("ah", h), Hb[h][1]], writes=[pkw])
                            op("pe", lambda e: e.matmul(psw.ap()[:, 0:128], lhsT=RES[("ak", h, c)], rhs=vpad[h].ap()[:, c, :], start=(c == 0), stop=True), reads=[rk_("ak", h), ("vpad", h)], writes=[pkw])
                            wu, wuk = wu_r.next()
                            op("act", lambda e: e.copy(out=wu.ap(), in_=psw.ap()[:, 0:128]), reads=[pkw], writes=[wuk])
                            wus.append((wu, wuk))
                        yield
                        Ups = []
                        for h in range(2):
                            psu, pku = psr.next()
                            op("pe", lambda e: e.matmul(psu.ap()[:, 0:128], lhsT=RES[("TT", h, c)], rhs=wus[h][0].ap(), start=True, stop=True), reads=[rk_("TT", h), wus[h][1]], writes=[pku])
                            up_, upk = upad[h].next()
                            op("dve", lambda e: e.tensor_copy(out=up_.ap(), in_=psu.ap()[:, 0:128]), reads=[pku, ("upad0", h, up_.name)], writes=[upk])
                            Ups.append((up_, upk))
                        yield
                        if c < NCH - 1:
                            for h in range(2):
                                psh, pkh = psr.next()
                                op("pe", lambda e: e.matmul(psh.ap()[:, 0:128], lhsT=RES[("bcT", h, c)], rhs=Ups[h][0].ap(), start=True, stop=False), reads=[rk_("bcT", h), Ups[h][1]], writes=[pkh])
                                op("pe", lambda e: e.matmul(psh.ap()[:, 0:128], lhsT=RES[("kcT", h, c)], rhs=vpad[h].ap()[:, c, :], start=False, stop=True), reads=[rk_("kcT", h), ("vpad", h)], writes=[pkh])
                                if c == 0:
                                    op("dve", lambda e: e.tensor_copy(out=H[h].ap(), in_=psh.ap()[:, 0:128]), reads=[pkh], writes=[("H", h)])
                                else:
                                    op("dve", lambda e: e.scalar_tensor_tensor(out=H[h].ap(), in0=H[h].ap(), scalar=dcol.ap()[:, 2, c:c + 1], in1=psh.ap()[:, 0:128], op0=ALU.mult, op1=ALU.add),
                                       reads=[("H", h), pkh] + dck, writes=[("H", h)])
                        psy, pky = psr.next()
                        first = True
                        for h in range(2):
                            if c > 0:
                                op("pe", lambda e: e.matmul(psy.ap()[:, 0:128], lhsT=Hb[h][0].ap(), rhs=Bh["rh"][h].ap()[:, cs], start=first, stop=False), reads=[Hb[h][1], ("rh", h)], writes=[pky]); first = False
                            op("pe", lambda e: e.matmul(psy.ap()[:, 0:128], lhsT=Ups[h][0].ap(), rhs=RES[("rb", h, c)], start=first, stop=False), reads=[Ups[h][1], rk_("rb", h)], writes=[pky]); first = False
                            op("pe", lambda e: e.matmul(psy.ap()[:, 0:128], lhsT=vpad[h].ap()[:, c, :], rhs=RES[("rk", h, c)], start=False, stop=(h == 1)), reads=[("vpad", h), rk_("rk", h)], writes=[pky])
                        op("act", lambda e: e.copy(out=ya.ap()[:, cs], in_=psy.ap()[:, 0:128]), reads=[pky], writes=[("ya", c // 4)])
                        if c < NCH - 1:
                            for h in range(2):
                                Hb[h] = Hb_r[h].next()
                                op("act", lambda e: e.copy(out=Hb[h][0].ap(), in_=H[h].ap()), reads=[("H", h)], writes=[Hb[h][1]])
                        yield

                def zip2(ga, gb):
                    da = db = False
                    while not (da and db):
                        if not da and next(ga, "END") == "END":
                            da = True
                        if not db and next(gb, "END") == "END":
                            db = True

                groups = [[(h, c) for c in range(g * 4, g * 4 + 4) for h in range(2)] for g in range(NCH // 4)]
                for _ in A_gen(groups[0]):
                    pass
                for g in range(len(groups)):
                    bg = B_gen(range(g * 4, g * 4 + 4))
                    if g + 1 < len(groups):
                        zip2(A_gen(groups[g + 1]), bg)
                    else:
                        for _ in bg:
                            pass
                k.phase_barrier()
                gA, gB = F["r"], F["k"]
                yall = [("ya", tb) for tb in range(NTB)]
                pss = [psr.next() for _ in range(NTB)]
                for tb in range(NTB):
                    op("pe", lambda e: e.matmul(pss[tb][0].ap(), lhsT=tri.ap()[:, 5, :], rhs=ya.ap()[:, tsl(tb)], start=True, stop=True), reads=[("ya", tb), "tri"], writes=[pss[tb][1]])
                for tb in range(NTB):
                    op("dve", lambda e: e.scalar_tensor_tensor(out=gA.ap()[:, tsl(tb)], in0=pss[tb][0].ap(), scalar=-1.0 / 64, in1=ya.ap()[:, tsl(tb)], op0=ALU.mult, op1=ALU.add),
                       reads=[pss[tb][1], ("ya", tb)], writes=[("r", tb)])
                op("act", lambda e: e.activation(out=gB.ap(), in_=gA.ap(), func=AF.Square), reads=allk("r"), writes=allk("k"))
                pss = [psr.next() for _ in range(NTB)]
                for tb in range(NTB):
                    op("pe", lambda e: e.matmul(pss[tb][0].ap(), lhsT=tri.ap()[:, 5, :], rhs=gB.ap()[:, tsl(tb)], start=True, stop=True), reads=[("k", tb), "tri"], writes=[pss[tb][1]])
                for tb in range(NTB):
                    op("act", lambda e: e.activation(out=gB.ap()[:, tsl(tb)], in_=pss[tb][0].ap(), func=AF.Ln, bias=gnc.ap(), scale=1.0 / 64), reads=[pss[tb][1], "gnc"], writes=[("k", tb)])
                op("act", lambda e: e.activation(out=gB.ap(), in_=gB.ap(), func=AF.Exp, scale=-0.5), reads=allk("k"), writes=allk("k"))
                op("dve", lambda e: e.tensor_tensor(out=gA.ap(), in0=gA.ap(), in1=gB.ap(), op=ALU.mult), reads=allk("r") + allk("k"), writes=allk("r"))
                op("dve", lambda e: e.tensor_scalar(out=gA.ap(), in0=gA.ap(), scalar1=prm(5), scalar2=prm(6), op0=ALU.mult, op1=ALU.add), reads=allk("r") + pcall, writes=allk("r"))
                op("dve", lambda e: e.tensor_tensor(out=gA.ap(), in0=gA.ap(), in1=F["v"].ap(), op=ALU.add), reads=allk("r") + allk("v"), writes=allk("r"))
                op("dve", lambda e: e.tensor_tensor(out=yo.ap(), in0=gA.ap(), in1=F["g"].ap(), op=ALU.mult), reads=allk("r") + allk("g"), writes=[("yo", tb) for tb in range(NTB)])
                k.dma(yT_dram[0, :, hp, :], yo.ap(), reads=[("yo", tb) for tb in range(NTB)], writes=[("yT_dram", 0, hp)])

    for l in range(DEPTH):
        with Scope(k) as sc:
            hT = sc.sb("hT", [128, KC, S], F32)
            sqr = sc.ring("sq", [128, TB], BF16, 3)
            tmpr = sc.ring("nt", [128, TB], F32, 4)
            load_h(hT)
            rmsnorm(hT, 2 * l, u_dst, sqr, tmpr)
        if "yT_in0" not in dbg:
            mixer_rwkv(l)
        if "yT_in1" not in dbg:
            mixer_hgrn(l)
        if "yT_in2" not in dbg:
            mixer_dsa(l)
        with Scope(k) as sc:
            yT = sc.sb("yT", [128, 3, 4, S], BF16)
            mT = sc.sb("mT", [128, KC, S], BF16)
            stg_g = sc.ring("stg_g", [128, 3, KC, 128], F32, 2)
            wg_r = sc.ring("wg", [128, 3, KC, 128], BF16, 2)
            stg_b = sc.ring("stg_b", [128, 3, 4, 128], F32, 2)
            wb_r = sc.ring("wb", [128, 3, 4, 128], BF16, 2)
            sgr = sc.ring("sg", [128, TB], F32, 3)
            accr = sc.ring("acc", [128, TB], F32, 2)
            for n in range(3):
                k.dma(yT.ap()[:, n], yT_src[l][n], writes=[("yT", n)])
            for dc in range(KC):
                sg_t, sgk = stg_g.next(); wg, wgk = wg_r.next()
                sb_t, sbk = stg_b.next(); wb, wbk = wb_r.next()
                for n in range(3):
                    c0 = G_OFF + n * D + dc * 128
                    k.dma(sg_t.ap()[:, n], P["w_in"][l][:, c0:c0 + 128].rearrange("(c p) n -> p c n", p=128), writes=[(sgk, n)])
                    k.dma(sb_t.ap()[:, n], P["w_branch"][l, n][:, dc * 128:(dc + 1) * 128].rearrange("(c p) n -> p c n", p=128),
                          writes=[(sbk, n)])
                cast(wg.ap(), sg_t.ap(), [(sgk, n) for n in range(3)], [wgk])
                cast(wb.ap(), sb_t.ap(), [(sbk, n) for n in range(3)], [wbk])
                for tb in range(NTB):
                    acc, acck = accr.next()
                    for n in range(3):
                        pg, pgk = psr.next()
                        for kc in range(KC):
                            op("pe", lambda e: e.matmul(pg.ap(), lhsT=wg.ap()[:, n, kc, :], rhs=uT.ap()[:, kc, usl(tb)],
                                                        start=(kc == 0), stop=(kc == KC - 1)),
                               reads=[wgk] + ukeys(kc, tb), writes=[pgk])
                        pb, pbk = psr.next()
                        for c in range(4):
                            op("pe", lambda e: e.matmul(pb.ap(), lhsT=wb.ap()[:, n, c, :], rhs=yT.ap()[:, n, c, tsl(tb)],
                                                        start=(c == 0), stop=(c == 3)),
                               reads=[wbk, ("yT", n)], writes=[pbk])
                        sg, sgk2 = sgr.next()
                        op("act", lambda e: e.activation(out=sg.ap(), in_=pg.ap(), func=AF.Sigmoid), reads=[pgk], writes=[sgk2])
                        if n == 0:
                            op("dve", lambda e: e.tensor_tensor(out=acc.ap(), in0=sg.ap(), in1=pb.ap(), op=ALU.mult),
                               reads=[sgk2, pbk], writes=[acck])
                        else:
                            op("dve", lambda e: e.tensor_tensor(out=sg.ap(), in0=sg.ap(), in1=pb.ap(), op=ALU.mult),
                               reads=[sgk2, pbk], writes=[sgk2])
                            dstap = mT.ap()[:, dc, tsl(tb)] if n == 2 else acc.ap()
                            op("pool" if n == 1 else "dve", lambda e: e.tensor_tensor(out=dstap, in0=sg.ap(), in1=acc.ap(), op=ALU.add),
                               reads=[sgk2, acck], writes=[("mT", dc, tb)] if n == 2 else [acck])
            stg_o = sc.ring("stg_o", [128, KC, 128], F32, 2)
            wo_r = sc.ring("wo", [128, KC, 128], BF16, 2)
            hr = sc.ring("hr", [128, TB], F32, 3)
            for dc in range(KC):
                wo, wok = wload(P["w_o"][l][:, dc * 128:(dc + 1) * 128], KC, 128, stg_o, wo_r)
                for tb in range(NTB):
                    ht, htk = hr.next()
                    k.dma(ht.ap(), h_dram[:, dc, tsl(tb)], writes=[htk])
                    ps, pk = psr.next()
                    for c in range(KC):
                        op("pe", lambda e: e.matmul(ps.ap(), lhsT=wo.ap()[:, c, :], rhs=mT.ap()[:, c, tsl(tb)],
                                                    start=(c == 0), stop=(c == KC - 1)),
                           reads=[wok, ("mT", c, tb)], writes=[pk])
                    op("dve", lambda e: e.tensor_tensor(out=ht.ap(), in0=ht.ap(), in1=ps.ap(), op=ALU.add),
                       reads=[htk, pk], writes=[htk])
                    k.dma(h_dram[:, dc, tsl(tb)], ht.ap(), reads=[htk], writes=[("h_dram", dc, tb)])
        with Scope(k) as sc:
            hT = sc.sb("hT", [128, KC, S], F32)
            sc2 = Scope(k)
            sc_outer = sc; sc = sc2
            sqr = sc.ring("sq", [128, TB], BF16, 3)
            tmpr = sc.ring("nt", [128, TB], F32, 3)
            load_h(hT)
            rmsnorm(hT, 2 * l + 1, u_dst, sqr, tmpr)
            cw = sc.sb("cw", [128, 3, 2 * NFF], F32)
            cb = sc.sb("cb", [128, 2 * NFF], F32)
            for j in range(3):
                load_pvec(cw.ap()[:, j, :], P["conv_w"][l, j], 2 * NFF, ["cw"])
            load_pvec(cb.ap(), P["conv_b"][l], 2 * NFF, ["cb"])
            G = 2
            stg_u = sc.ring("stg_u", [128, 2, KC, 128], F32, 2)
            wu_r = sc.ring("wu", [128, 2, KC, 128], BF16, 2)
            stg_d = sc.ring("stg_d", [128, G, D], F32, 1)
            wd_r = sc.ring("wd", [128, G, D], BF16, 2)
            upr = sc.ring("up", [128, 2, S + 2], F32, 1)
            cr = sc.ring("cc", [128, 2, TB], F32, 2)
            actr = sc.ring("actT", [128, G, S], BF16, 2)
            up, upk = upr.next()
            op("pool", lambda e: e.memset(up.ap()[:, :, 0:2], 0.0), writes=["uppad"])
            for g0 in range(0, NFF, G):
                actT, actk = actr.next()
                wd, wdk = wload(P["w_down"][l][g0 * 128:(g0 + G) * 128, :], G, D, stg_d, wd_r)
                for gi in range(G):
                    j = g0 + gi
                    su, suk = stg_u.next(); wu, wuk = wu_r.next()
                    for hv in range(2):
                        c0 = hv * DFF + j * 128
                        k.dma(su.ap()[:, hv], P["w_up"][l][:, c0:c0 + 128].rearrange("(c p) n -> p c n", p=128), writes=[(suk, hv)])
                    cast(wu.ap(), su.ap(), [(suk, 0), (suk, 1)], [wuk])
                    for tb in range(NTB):
                        cc, cck = cr.next()
                        for hv in range(2):
                            ps, pk = psr.next()
                            for kc in range(KC):
                                op("pe", lambda e: e.matmul(ps.ap(), lhsT=wu.ap()[:, hv, kc, :], rhs=uT.ap()[:, kc, usl(tb)],
                                                            start=(kc == 0), stop=(kc == KC - 1)),
                                   reads=[wuk] + ukeys(kc, tb), writes=[pk])
                            ch = hv * NFF + j
                            op("act", lambda e: e.copy(out=up.ap()[:, hv, usl(tb)], in_=ps.ap()), reads=[pk, "uppad"], writes=[("up", hv, tb)])
                            op("act", lambda e: e.activation(out=cc.ap()[:, hv, :], in_=ps.ap(), func=AF.Identity,
                                                             scale=cw.ap()[:, 2, ch:ch + 1], bias=cb.ap()[:, ch:ch + 1]),
                               reads=[pk, "cw", "cb"], writes=[(cck, hv)])
                            for sh in (1, 2):
                                rk = [("up", hv, tb)] + ([("up", hv, tb - 1)] if tb > 0 else [])
                                op("dve", lambda e: e.scalar_tensor_tensor(out=cc.ap()[:, hv, :], in0=up.ap()[:, hv, usl(tb, sh)],
                                                                           scalar=cw.ap()[:, 2 - sh, ch:ch + 1], in1=cc.ap()[:, hv, :],
                                                                           op0=ALU.mult, op1=ALU.add),
                                   reads=rk + ["cw", (cck, hv)], writes=[(cck, hv)])
                        op("act", lambda e: e.activation(out=cc.ap()[:, 0, :], in_=cc.ap()[:, 0, :], func=AF.Silu),
                           reads=[(cck, 0)], writes=[(cck, 0)])
                        op("pool", lambda e: e.tensor_tensor(out=actT.ap()[:, gi, tsl(tb)], in0=cc.ap()[:, 0, :], in1=cc.ap()[:, 1, :], op=ALU.mult),
                           reads=[(cck, 0), (cck, 1)], writes=[(actk, gi, tb)])
                for dc in range(KC):
                    for tb in range(NTB):
                        ps, pk = psr.next()
                        for gi in range(G):
                            op("pe", lambda e: e.matmul(ps.ap(), lhsT=wd.ap()[:, gi, dc * 128:(dc + 1) * 128], rhs=actT.ap()[:, gi, tsl(tb)],
                                                        start=(gi == 0), stop=(gi == G - 1)),
                               reads=[wdk, (actk, gi, tb)], writes=[pk])
                        op("dve", lambda e: e.tensor_tensor(out=hT.ap()[:, dc, tsl(tb)], in0=hT.ap()[:, dc, tsl(tb)], in1=ps.ap(), op=ALU.add),
                           reads=hkeys(dc, tb) + [pk], writes=hkeys(dc, tb))
            sc2.__exit__(None, None, None)
            sc = Scope(k)
            sqr = sc.ring("sq", [128, TB], BF16, 3)
            tmpr = sc.ring("nt", [128, TB], F32, 3)
            if l < DEPTH - 1:
                store_h(hT)
            else:
                fo = sc.ring("fo", [128, KC, TB], F32, 1)
                ot = sc.ring("ot", [128, D], F32, 2)
                fT, fk = fo.next()
                cur = [0]

                def f_dst(kc, tb):
                    return fT.ap()[:, kc, :], [("fT", kc)]
                for tb in range(NTB):
                    ps, pk = psr.next()
                    for kc in range(KC):
                        sq, sk = sqr.next()
                        op("act", lambda e: e.activation(out=sq.ap(), in_=hT.ap()[:, kc, tsl(tb)], func=AF.Square),
                           reads=hkeys(kc, tb), writes=[sk])
                        op("pe", lambda e: e.matmul(ps.ap(), lhsT=ones16.ap(), rhs=sq.ap(), start=(kc == 0), stop=(kc == KC - 1)),
                           reads=[sk, "ones16"], writes=[pk])
                    sd, sdk = tmpr.next()
                    op("act", lambda e: e.activation(out=sd.ap(), in_=ps.ap(), func=AF.Ln, bias=epsc.ap(), scale=1.0 / D),
                       reads=[pk, "epsc"], writes=[sdk])
                    rs, rsk = tmpr.next()
                    op("act", lambda e: e.activation(out=rs.ap(), in_=sd.ap(), func=AF.Exp, scale=-0.5), reads=[sdk], writes=[rsk])
                    for kc in range(KC):
                        op("dve", lambda e: e.scalar_tensor_tensor(out=fT.ap()[:, kc, :], in0=hT.ap()[:, kc, tsl(tb)],
                                                                   scalar=gvec.ap()[:, 2 * DEPTH, kc:kc + 1], in1=rs.ap(),
                                                                   op0=ALU.mult, op1=ALU.mult),
                           reads=hkeys(kc, tb) + [rsk, "gvec"], writes=[("fT", kc)])
                    for t4 in range(4):
                        o, okk = ot.next()
                        for half in range(2):
                            ps2, pk2 = psr.next()
                            for j in range(4):
                                kc = half * 4 + j
                                op("pe", lambda e: e.transpose(out=ps2.ap()[:, j * 128:(j + 1) * 128],
                                                               in_=fT.ap()[:, kc, t4 * 128:(t4 + 1) * 128], identity=ident.ap()),
                                   reads=[("fT", kc), "ident"], writes=[pk2])
                            cast(o.ap()[:, half * 512:(half + 1) * 512], ps2.ap(), [pk2], [(okk, half)], eng=("dve", "act")[half])
                        r0 = tb * TB + t4 * 128
                        k.dma(out[r0:r0 + 128, :], o.ap(), reads=[(okk, 0), (okk, 1)], writes=[("out", r0)])
            sc.__exit__(None, None, None)
    k.drain("sp")
    return k


PARAM_NAMES = ["norm_mix_g", "w_in", "a_mu", "a_w0", "a_w_up", "a_a0", "a_a_up", "a_g_up", "a_k_k", "a_k_a", "a_r_k",
               "a_gn_g", "a_gn_b", "b_lb_logits", "b_gn_g", "w_branch", "w_o", "norm_ffn_g", "w_up", "conv_w", "conv_b",
               "w_down", "norm_final_g"]


def make_consts():
    i = np.arange(128)
    low = (i[:, None] >= i[None, :])
    bd = ((i[:, None] // 64) == (i[None, :] // 64))
    hm = np.zeros((128, 128)); hm[:64, 0] = 1; hm[64:, 1] = 1
    tri = np.stack([(i[:, None] <= i[None, :]), (i[:, None] < i[None, :]), low, np.where(low, 0.0, -1e30),
                    (i[:, None] > i[None, :]), bd, hm]).astype(np.float32)
    inv_freq = (np.float32(10000.0) ** (-np.arange(32, dtype=np.float32) * np.float32(2.0 / 64))).astype(np.float32)
    ang = (np.arange(S, dtype=np.float32)[:, None] * inv_freq[None, :]).astype(np.float32)
    cos = np.cos(ang).astype(np.float32).T; sin = np.sin(ang).astype(np.float32).T
    cosF = np.concatenate([cos, cos, cos, cos], axis=0)
    sinF = np.concatenate([-sin, sin, -sin, sin], axis=0)
    return {"c_ident": np.eye(128, dtype=np.float32), "c_tri": tri, "c_rope": np.stack([cosF, sinF]).astype(np.float32)}


def make_in_map(inputs, b, consts):
    m = {"x": np.ascontiguousarray(inputs["x"][b])}
    for n in PARAM_NAMES:
        a = np.ascontiguousarray(np.asarray(inputs[n], dtype=np.float32))
        if n == "a_r_k":
            a = a.reshape(DEPTH, 512)
        m[n] = a
    m.update(consts)
    return m


def kernel(**inputs):
    k = build()
    consts = make_consts()
    in_maps = [make_in_map(inputs, b, consts) for b in range(NB)]
    res = run_bass_kernel_spmd(k.nc, in_maps, core_ids=list(range(NB)))
    return np.stack([np.asarray(r["out"], dtype=np.float32) for r in res.results], axis=0)
```

```python
import numpy as np
import ml_dtypes
import concourse.bass as bass
import concourse.mybir as mybir
from concourse.bass_utils import run_bass_kernel_spmd

F32 = mybir.dt.float32
BF16 = mybir.dt.bfloat16
AF = mybir.ActivationFunctionType
ALU = mybir.AluOpType
AX = mybir.AxisListType

S = 2048; D = 1024; DEPTH = 2; NB = 8
TB = 512; NTB = S // TB; KC = D // 128
A_COLS = 1792; B_COLS = 2048; C_COLS = 1092; G_COLS = 3072; IN_COLS = 8004
A_OFF = 0; B_OFF = A_COLS; C_OFF = A_COLS + B_COLS; G_OFF = C_OFF + C_COLS
DFF = 2816; NFF = DFF // 128
NDMA = 24
NOSYNC_SAME = {"pe"}


class K:
    def __init__(self):
        self.nc = nc = bass.Bass("TRN2", target_bir_lowering=False)
        self.eng = {"pe": nc.tensor, "act": nc.scalar, "dve": nc.vector, "pool": nc.gpsimd, "sp": nc.sync}
        self.sem = {}
        self.cnt = {}
        for e in self.eng:
            self.sem[e] = nc.alloc_semaphore("s_" + e)
            self.cnt[e] = 0
        self.seen = {e: {} for e in self.eng}
        self.lastw = {}
        self.readers = {}
        self.dsem = [nc.alloc_semaphore("d%d" % i) for i in range(NDMA)]
        self.dval = [0] * NDMA
        self.di = 0
        self.nwait = 0
        self.ninst = 0

    def _wait(self, e, ev):
        name, sem, val, src = ev
        if src == e and e in NOSYNC_SAME:
            return
        if self.seen[e].get(name, 0) >= val:
            return
        self.eng[e].wait_ge(sem, val)
        self.nwait += 1
        self.seen[e][name] = val

    def _deps(self, e, reads, writes):
        for r in reads:
            ev = self.lastw.get(r)
            if ev is not None:
                self._wait(e, ev)
        for w in writes:
            ev = self.lastw.get(w)
            if ev is not None:
                self._wait(e, ev)
            for ev in self.readers.get(w, ()):
                self._wait(e, ev)

    def _record(self, ev, reads, writes):
        for w in writes:
            self.lastw[w] = ev
            self.readers[w] = []
        for r in reads:
            if r in writes:
                continue
            lst = self.readers.setdefault(r, [])
            lst[:] = [x for x in lst if x[0] != ev[0]]
            lst.append(ev)

    def op(self, e, fn, reads=(), writes=()):
        self._deps(e, reads, writes)
        ins = fn(self.eng[e])
        self.cnt[e] += 1
        ins.then_inc(self.sem[e], 1)
        ev = ("E" + e, self.sem[e], self.cnt[e], e)
        self.seen[e]["E" + e] = max(self.seen[e].get("E" + e, 0), 0)
        self._record(ev, reads, writes)
        self.ninst += 1
        return ins

    def dma(self, out, in_, reads=(), writes=(), q="sp", **kw):
        self._deps(q, reads, writes)
        j = self.di % NDMA
        self.di += 1
        if self.dval[j] > 0:
            self._wait(q, ("D%d" % j, self.dsem[j], self.dval[j], None))
        self.dval[j] += 16
        self.eng[q].dma_start(out=out, in_=in_, **kw).then_inc(self.dsem[j], 16)
        ev = ("D%d" % j, self.dsem[j], self.dval[j], None)
        self._record(ev, reads, writes)
        self.ninst += 1

    def drain(self, e="sp"):
        for e2 in self.eng:
            if e2 != e and self.cnt[e2] > 0:
                self._wait(e, ("E" + e2, self.sem[e2], self.cnt[e2], e2))
        for j in range(NDMA):
            if self.dval[j] > 0:
                self._wait(e, ("D%d" % j, self.dsem[j], self.dval[j], None))


    def phase_barrier(self):
        self.drain("sp")
        self.nc.all_engine_barrier()
        self.lastw.clear()
        self.readers.clear()


class Scope:
    def __init__(self, k):
        self.k = k
        self.stack = []

    def __enter__(self):
        return self

    _uid = [0]

    def sb(self, name, shape, dtype):
        Scope._uid[0] += 1
        g = self.k.nc.sbuf_tensor("%s_u%d" % (name, Scope._uid[0]), list(shape), dtype)
        t = g.__enter__()
        self.stack.append(g)
        return t

    def ring(self, name, shape, dtype, n):
        return Ring([self.sb("%s_%d" % (name, i), shape, dtype) for i in range(n)], name)

    def __exit__(self, *a):
        self.k.phase_barrier()
        for g in reversed(self.stack):
            g.__exit__(None, None, None)
        return False


class Ring:
    def __init__(self, tensors, name):
        self.t = tensors
        self.name = name
        self.i = 0

    def next(self):
        j = self.i % len(self.t)
        self.i += 1
        return self.t[j], (self.name, j)


def build(dbg=None):
    dbg = dbg or {}
    k = K()
    nc = k.nc
    op = k.op
    uid = [0]

    def din(name, shape, dt=F32):
        return nc.dram_tensor(name, list(shape), dt, kind="ExternalInput").ap()

    x = din("x", [S, D])
    P = {}
    for name, shape in [
        ("norm_mix_g", [DEPTH, D]), ("w_in", [DEPTH, D, IN_COLS]), ("a_mu", [DEPTH, A_COLS]), ("a_w0", [DEPTH, 512]),
        ("a_w_up", [DEPTH, 64, 512]), ("a_a0", [DEPTH, 512]), ("a_a_up", [DEPTH, 64, 512]), ("a_g_up", [DEPTH, 128, 512]),
        ("a_k_k", [DEPTH, 512]), ("a_k_a", [DEPTH, 512]), ("a_r_k", [DEPTH, 512]), ("a_gn_g", [DEPTH, 512]),
        ("a_gn_b", [DEPTH, 512]), ("b_lb_logits", [DEPTH, 512]), ("b_gn_g", [DEPTH, 512]),
        ("w_branch", [DEPTH, 3, 512, D]), ("w_o", [DEPTH, D, D]), ("norm_ffn_g", [DEPTH, D]),
        ("w_up", [DEPTH, D, 2 * DFF]), ("conv_w", [DEPTH, 3, 2 * DFF]), ("conv_b", [DEPTH, 2 * DFF]),
        ("w_down", [DEPTH, DFF, D]), ("norm_final_g", [D]),
    ]:
        P[name] = din(name, shape)
    c_ident = din("c_ident", [128, 128])
    out = nc.dram_tensor("out", [S, D], F32, kind="ExternalOutput").ap()
    h_dram = nc.dram_tensor("h_scr", [128, KC, S], F32, kind="Internal").ap()
    yT_dram = nc.dram_tensor("yT_scr", [3, 128, 4, S], BF16, kind="ExternalOutput" if dbg.get("dump_yT") else "Internal").ap()
    c_tri = din("c_tri", [7, 128, 128])
    c_rope = din("c_rope", [2, 128, S])
    yT_src = [[yT_dram[n] for n in range(3)] for l in range(DEPTH)]
    for n in range(3):
        if ("yT_in%d" % n) in dbg:
            t = din("yT_in%d" % n, [DEPTH, 128, 4, S], BF16)
            for l in range(DEPTH):
                yT_src[l][n] = t[l]

    ident = nc.alloc_sbuf_tensor("ident", [128, 128], F32)
    identb = nc.alloc_sbuf_tensor("identb", [128, 128], BF16)
    ones = nc.alloc_sbuf_tensor("ones", [128, 128], F32)
    epsc = nc.alloc_sbuf_tensor("epsc", [128, 1], F32)
    uT = nc.alloc_sbuf_tensor("uT", [128, KC, S + 2], BF16)
    gvec = nc.alloc_sbuf_tensor("gvec", [128, 2 * DEPTH + 1, KC], F32)
    psr = Ring([nc.alloc_psum_tensor("ps%d" % i, [128, 512], F32) for i in range(6)], "ps")
    psacc = Ring([nc.alloc_psum_tensor("psacc%d" % i, [128, 512], F32) for i in range(2)], "psacc")

    tri = nc.alloc_sbuf_tensor("tri", [128, 7, 128], F32)
    k.dma(tri.ap(), c_tri.rearrange("a p f -> p a f"), writes=["tri"])
    k.dma(ident.ap(), c_ident, writes=["ident"])
    pv_ring = Ring([nc.alloc_sbuf_tensor("pvstg%d" % i, [128, 128], F32) for i in range(2)], "pvstg")

    def load_pvec(dst, src1d, n, wkeys):
        stg, sk = pv_ring.next()
        k.dma(stg.ap()[0:n, :], src1d.rearrange("(c p) -> c p", p=128), writes=[sk])
        ps, pk = psr.next()
        op("pe", lambda e: e.transpose(out=ps.ap()[:, 0:n], in_=stg.ap()[0:n, :], identity=ident.ap()[0:n, 0:n]), reads=[sk, "ident"], writes=[pk])
        op("dve", lambda e: e.tensor_copy(out=dst, in_=ps.ap()[:, 0:n]), reads=[pk], writes=wkeys)

    op("dve", lambda e: e.tensor_copy(out=identb.ap(), in_=ident.ap()), reads=["ident"], writes=["identb"])
    op("pool", lambda e: e.memset(ones.ap(), 1.0), writes=["ones"])
    ones16 = nc.alloc_sbuf_tensor("ones16", [128, 128], BF16)
    op("pool", lambda e: e.memset(ones16.ap(), 1.0), writes=["ones16"])
    op("pool", lambda e: e.memset(epsc.ap(), 1e-6), writes=["epsc"])
    op("pool", lambda e: e.memset(uT.ap()[:, :, 0:2], 0.0), writes=["uTpad"])
    for l in range(DEPTH):
        load_pvec(gvec.ap()[:, 2 * l, :], P["norm_mix_g"][l], KC, ["gvec"])
        load_pvec(gvec.ap()[:, 2 * l + 1, :], P["norm_ffn_g"][l], KC, ["gvec"])
    load_pvec(gvec.ap()[:, 2 * DEPTH, :], P["norm_final_g"], KC, ["gvec"])

    def hkeys(kc=None, tb=None):
        return [("hT", a, b) for a in (range(KC) if kc is None else [kc]) for b in (range(NTB) if tb is None else [tb])]

    def ukeys(kc=None, tb=None):
        return [("uT", a, b) for a in (range(KC) if kc is None else [kc]) for b in (range(NTB) if tb is None else [tb])]

    def tsl(tb):
        return slice(tb * TB, (tb + 1) * TB)

    def usl(tb, shift=0):
        return slice(2 + tb * TB - shift, 2 + (tb + 1) * TB - shift)

    def rmsnorm(hT, gi, dst_fn, sqr, tmpr):
        for tb in range(NTB):
            ps, pk = psr.next()
            for kc in range(KC):
                sq, sk = sqr.next()
                op("act", lambda e: e.activation(out=sq.ap(), in_=hT.ap()[:, kc, tsl(tb)], func=AF.Square),
                   reads=hkeys(kc, tb), writes=[sk])
                op("pe", lambda e: e.matmul(ps.ap(), lhsT=ones16.ap(), rhs=sq.ap(), start=(kc == 0), stop=(kc == KC - 1)),
                   reads=[sk, "ones16"], writes=[pk])
            sd, sdk = tmpr.next()
            op("act", lambda e: e.activation(out=sd.ap(), in_=ps.ap(), func=AF.Ln, bias=epsc.ap(), scale=1.0 / D),
               reads=[pk, "epsc"], writes=[sdk])
            rs, rsk = tmpr.next()
            op("act", lambda e: e.activation(out=rs.ap(), in_=sd.ap(), func=AF.Exp, scale=-0.5), reads=[sdk], writes=[rsk])
            for kc in range(KC):
                dst, dk = dst_fn(kc, tb)
                op("dve", lambda e: e.scalar_tensor_tensor(out=dst, in0=hT.ap()[:, kc, tsl(tb)],
                                                           scalar=gvec.ap()[:, gi, kc:kc + 1], in1=rs.ap(),
                                                           op0=ALU.mult, op1=ALU.mult),
                   reads=hkeys(kc, tb) + [rsk, "gvec"], writes=dk)

    def u_dst(kc, tb):
        return uT.ap()[:, kc, usl(tb)], ukeys(kc, tb)

    castsel = [0]

    def cast(dst, src, reads, writes, eng=None):
        ce = eng or "act"
        castsel[0] += 1
        if ce == "act":
            op("act", lambda e: e.copy(out=dst, in_=src), reads=reads, writes=writes)
        else:
            op(ce, lambda e: e.tensor_copy(out=dst, in_=src), reads=reads, writes=writes)

    def wload(src, nk, ncol, stg_ring, w_ring, q="sp"):
        stg, sk = stg_ring.next()
        wt, wk = w_ring.next()
        k.dma(stg.ap()[:, 0:nk, 0:ncol], src.rearrange("(c p) n -> p c n", p=128), writes=[sk], q=q)
        cast(wt.ap()[:, 0:nk, 0:ncol], stg.ap()[:, 0:nk, 0:ncol], [sk], [wk])
        return wt, wk

    def load_h(hT):
        for kc in range(KC):
            k.dma(hT.ap()[:, kc, :], h_dram[:, kc, :], reads=["h_dram"], writes=hkeys(kc))

    def store_h(hT):
        for kc in range(KC):
            k.dma(h_dram[:, kc, :], hT.ap()[:, kc, :], reads=hkeys(kc), writes=[("h_dram", kc)])

    with Scope(k) as sc:
        xs = sc.ring("xs", [128, D], F32, 2)
        xt = sc.ring("xt", [128, KC, 128], F32, 2)
        for tt in range(S // 128):
            a, ak = xs.next()
            b, bk = xt.next()
            k.dma(a.ap(), x[tt * 128:(tt + 1) * 128, :], writes=[ak])
            for half in range(2):
                ps, pk = psr.next()
                for j in range(4):
                    kc = half * 4 + j
                    op("pe", lambda e: e.transpose(out=ps.ap()[:, j * 128:(j + 1) * 128], in_=a.ap()[:, kc * 128:(kc + 1) * 128],
                                                   identity=ident.ap()), reads=[ak, "ident"], writes=[pk])
                cast(b.ap()[:, half * 4:half * 4 + 4, :], ps.ap().rearrange("p (a b) -> p a b", a=4), [pk], [(bk, half)],
                     eng=("dve", "act")[half])
            k.dma(h_dram[:, :, tt * 128:(tt + 1) * 128], b.ap(), reads=[(bk, 0), (bk, 1)], writes=[("h_dram", "x", tt)])


    def proj_fm(l, c0, ncol, sc_rings, consume, shift_w=None):
        stg_r, w_r = sc_rings
        wt, wk = wload(P["w_in"][l][:, c0:c0 + ncol], KC, ncol, stg_r, w_r)
        for tb in range(NTB):
            ps, pk = psr.next()
            for kc in range(KC):
                op("pe", lambda e: e.matmul(ps.ap()[0:ncol, :], lhsT=wt.ap()[:, kc, 0:ncol], rhs=uT.ap()[:, kc, usl(tb)],
                                            start=(kc == 0), stop=(kc == KC - 1)),
                   reads=[wk] + ukeys(kc, tb), writes=[pk])
            consume(tb, ps, pk)

    def mixer_hgrn(l):
        C = 64; NCH = S // C
        with Scope(k) as sc:
            stg_r = sc.ring("hstg", [128, KC, 128], F32, 2)
            w_r = sc.ring("hw", [128, KC, 128], BF16, 2)
            rings = (stg_r, w_r)
            lbt = sc.sb("lbt", [128, 2, 4], F32)
            lb = sc.sb("lb", [128, 4], F32)
            oml = sc.sb("oml", [128, 4], F32)
            noml = sc.sb("noml", [128, 4], F32)
            gng = sc.sb("gng", [128, 4], F32)
            onesS = sc.sb("onesS", [128, S], F32)
            Bc = sc.sb("Bc", [128, S], F32)
            sig = sc.sb("sig", [128, S], F32)
            bp = sc.sb("bp", [128, S], F32)
            Ex = sc.sb("Ex", [128, S], F32)
            ktl = sc.sb("ktl", [128, S], F32)
            qt = sc.sb("qt", [128, S], BF16)
            kt = sc.sb("kt", [128, S], BF16)
            qh = sc.sb("qh", [128, S], BF16)
            kcf = sc.sb("kcf", [128, S], F32)
            vtok = sc.sb("vtok", [64, NCH, 128], BF16)
            ysb = sc.sb("ysb", [128, S], F32)
            gs = sc.sb("gs", [128, S], F32)
            yo = sc.sb("yo", [128, S], BF16)
            dcol = sc.sb("dcol", [128, 4, NCH], F32)
            st = sc.sb("st", [128, 128], F32)
            stb_r = sc.ring("stb", [128, 128], BF16, 2)
            pt_r = sc.ring("pt", [64, 64], BF16, 4)
            kct_r = sc.ring("kct", [64, 128], BF16, 4)
            sq_r = sc.ring("hsq", [128, TB], F32, 2)
            nt_r = sc.ring("hnt", [128, TB], F32, 2)
            op("pool", lambda e: e.memset(onesS.ap(), 1.0), writes=["onesS"])
            for l_ in range(DEPTH):
                load_pvec(lbt.ap()[:, l_, :], P["b_lb_logits"][l_], 4, ["lbt"])
            load_pvec(gng.ap(), P["b_gn_g"][l], 4, ["gng"])
            if l == 0:
                op("dve", lambda e: e.tensor_tensor(out=lb.ap(), in0=lbt.ap()[:, 0, :], in1=lbt.ap()[:, 0, :], op=ALU.subtract),
                   reads=["lbt"], writes=["lb"])
            else:
                op("dve", lambda e: e.tensor_tensor(out=lb.ap(), in0=lbt.ap()[:, 1, :], in1=lbt.ap()[:, 0, :], op=ALU.subtract),
                   reads=["lbt"], writes=["lb"])
                op("act", lambda e: e.activation(out=lb.ap(), in_=lb.ap(), func=AF.Sigmoid), reads=["lb"], writes=["lb"])
            op("dve", lambda e: e.tensor_scalar(out=oml.ap(), in0=lb.ap(), scalar1=-1.0, scalar2=1.0, op0=ALU.mult, op1=ALU.add),
               reads=["lb"], writes=["oml"])
            op("dve", lambda e: e.tensor_scalar(out=noml.ap(), in0=oml.ap(), scalar1=-1.0, scalar2=None, op0=ALU.mult),
               reads=["oml"], writes=["noml"])
            for h in range(4):
                cq = B_OFF + h * 128; cf = B_OFF + 512 + h * 128; ci = B_OFF + 1024 + h * 128; cg = B_OFF + 1536 + h * 128
                def f_consume(tb, ps, pk):
                    op("act", lambda e: e.activation(out=sig.ap()[:, tsl(tb)], in_=ps.ap(), func=AF.Sigmoid), reads=[pk], writes=[("sig", tb)])
                proj_fm(l, cf, 128, rings, f_consume)
                allsig = [("sig", tb) for tb in range(NTB)]
                op("dve", lambda e: e.tensor_scalar(out=ktl.ap(), in0=sig.ap(), scalar1=noml.ap()[:, h:h + 1], scalar2=oml.ap()[:, h:h + 1],
                                                    op0=ALU.mult, op1=ALU.add), reads=allsig + ["noml", "oml"], writes=["ktl"])
                op("dve", lambda e: e.tensor_scalar(out=sig.ap(), in0=sig.ap(), scalar1=oml.ap()[:, h:h + 1], scalar2=lb.ap()[:, h:h + 1],
                                                    op0=ALU.mult, op1=ALU.add), reads=allsig + ["oml", "lb"], writes=allsig)
                op("act", lambda e: e.activation(out=sig.ap(), in_=sig.ap(), func=AF.Ln), reads=allsig, writes=allsig)
                op("dve", lambda e: e.tensor_tensor_scan(out=Bc.ap(), data0=onesS.ap(), data1=sig.ap(), initial=0.0, op0=ALU.mult, op1=ALU.add),
                   reads=allsig + ["onesS"], writes=["Bc"])
                B3 = Bc.ap().rearrange("p (c t) -> p c t", t=C)
                op("dve", lambda e: e.tensor_tensor(out=bp.ap().rearrange("p (c t) -> p c t", t=C), in0=B3,
                                                    in1=B3[:, :, C // 2 - 1:C // 2].to_broadcast([128, NCH, C]), op=ALU.subtract),
                   reads=["Bc"], writes=["bp"] + [("bpq", tb) for tb in range(NTB)])
                op("dve", lambda e: e.tensor_tensor(out=dcol.ap()[:, 0, :], in0=B3[:, :, C - 1], in1=B3[:, :, C // 2 - 1], op=ALU.subtract),
                   reads=["Bc"], writes=[("dcol", 0)])
                op("dve", lambda e: e.tensor_copy(out=dcol.ap()[:, 1, 0:1], in_=B3[:, 0, C // 2 - 1:C // 2]), reads=["Bc"], writes=[("dcol", 1)])
                op("dve", lambda e: e.tensor_tensor(out=dcol.ap()[:, 1, 1:NCH], in0=B3[:, 1:NCH, C // 2 - 1], in1=B3[:, 0:NCH - 1, C - 1], op=ALU.subtract),
                   reads=["Bc", ("dcol", 1)], writes=[("dcol", 1)])
                op("dve", lambda e: e.tensor_tensor(out=dcol.ap()[:, 2, :], in0=dcol.ap()[:, 0, :], in1=dcol.ap()[:, 1, :], op=ALU.add),
                   reads=[("dcol", 0), ("dcol", 1)], writes=[("dcol", 2)])
                op("act", lambda e: e.activation(out=dcol.ap()[:, 0:3, :], in_=dcol.ap()[:, 0:3, :], func=AF.Exp),
                   reads=[("dcol", 0), ("dcol", 1), ("dcol", 2)], writes=[("dcol", 0), ("dcol", 1), ("dcol", 2)])
                op("act", lambda e: e.activation(out=Ex.ap(), in_=bp.ap(), func=AF.Exp, scale=-1.0), reads=["bp"], writes=["Ex"])
                op("dve", lambda e: e.tensor_tensor(out=ktl.ap(), in0=ktl.ap(), in1=Ex.ap(), op=ALU.mult), reads=["ktl", "Ex"], writes=["ktl"])
                op("act", lambda e: e.copy(out=kt.ap(), in_=ktl.ap()), reads=["ktl"], writes=["kt"])
                op("dve", lambda e: e.tensor_tensor(out=kcf.ap().rearrange("p (c t) -> p c t", t=C), in0=ktl.ap().rearrange("p (c t) -> p c t", t=C),
                                                    in1=dcol.ap()[:, 0, :].unsqueeze(2).to_broadcast([128, NCH, C]), op=ALU.mult),
                   reads=["ktl", ("dcol", 0)], writes=["kcf"])
                op("act", lambda e: e.activation(out=Ex.ap(), in_=bp.ap(), func=AF.Exp), reads=["bp"], writes=["Ex"])

                def q_consume(tb, ps, pk):
                    op("dve", lambda e: e.tensor_tensor(out=bp.ap()[:, tsl(tb)], in0=Ex.ap()[:, tsl(tb)], in1=ps.ap(), op=ALU.mult),
                       reads=["Ex", pk, "bp"], writes=[("bpq", tb)])
                proj_fm(l, cq, 128, rings, q_consume)
                allq = [("bpq", tb) for tb in range(NTB)]
                op("act", lambda e: e.copy(out=qt.ap(), in_=bp.ap()), reads=allq, writes=["qt"])
                op("dve", lambda e: e.tensor_tensor(out=qh.ap().rearrange("p (c t) -> p c t", t=C), in0=bp.ap().rearrange("p (c t) -> p c t", t=C),
                                                    in1=dcol.ap()[:, 1, :].unsqueeze(2).to_broadcast([128, NCH, C]), op=ALU.mult),
                   reads=allq + [("dcol", 1)], writes=["qh"])

                def g_consume(tb, ps, pk):
                    op("act", lambda e: e.activation(out=gs.ap()[:, tsl(tb)], in_=ps.ap(), func=AF.Silu), reads=[pk], writes=[("gs", tb)])
                proj_fm(l, cg, 128, rings, g_consume)
                wt, wk = wload(P["w_in"][l][:, ci:ci + 128], KC, 128, stg_r, w_r)
                for c4 in range(NCH // 4):
                    ps, pk = psr.next()
                    for j in range(4):
                        c = c4 * 4 + j
                        for kc in range(KC):
                            op("pe", lambda e: e.matmul(ps.ap()[0:C, j * 128:(j + 1) * 128], lhsT=uT.ap()[:, kc, 2 + c * C: 2 + (c + 1) * C],
                                                        rhs=wt.ap()[:, kc, :], start=(kc == 0), stop=(kc == KC - 1)),
                               reads=[wk] + ukeys(kc, (c * C) // TB), writes=[pk])
                    cast(vtok.ap()[:, c4 * 4:(c4 + 1) * 4, :], ps.ap()[0:C, :].rearrange("p (a b) -> p a b", a=4), [pk], [("vtok", c4)], eng="act")
                def pre(c):
                    cs = slice(c * C, (c + 1) * C)
                    ps_s, pk_s = psr.next()
                    op("pe", lambda e: e.matmul(ps_s.ap()[0:C, 0:C], lhsT=kt.ap()[:, cs], rhs=qt.ap()[:, cs], start=True, stop=True),
                       reads=["kt", "qt"], writes=[pk_s])
                    pt, ptk = pt_r.next()
                    op("dve", lambda e: e.tensor_tensor(out=pt.ap(), in0=ps_s.ap()[0:C, 0:C], in1=tri.ap()[0:C, 0, 0:C], op=ALU.mult),
                       reads=[pk_s, "tri"], writes=[ptk])
                    ps_t, pk_t = psr.next()
                    op("pe", lambda e: e.transpose(out=ps_t.ap()[0:C, 0:128], in_=kcf.ap()[:, cs], identity=ident.ap()),
                       reads=["kcf", "ident"], writes=[pk_t])
                    kct, kctk = kct_r.next()
                    op("act", lambda e: e.copy(out=kct.ap(), in_=ps_t.ap()[0:C, 0:128]), reads=[pk_t], writes=[kctk])
                    return pt, ptk, kct, kctk
                stb = None
                nxt_pre = pre(0)
                for c in range(NCH):
                    cs = slice(c * C, (c + 1) * C)
                    pt, ptk, kct, kctk = nxt_pre
                    if c + 1 < NCH:
                        nxt_pre = pre(c + 1)
                    if c < NCH - 1:
                        ps_d, pk_d = psacc.next()
                        op("pe", lambda e: e.matmul(ps_d.ap()[:, 0:128], lhsT=kct.ap(), rhs=vtok.ap()[:, c, :], start=True, stop=True),
                           reads=[kctk, ("vtok", c // 4)], writes=[pk_d])
                    ps_y, pk_y = psr.next()
                    if c > 0:
                        op("pe", lambda e: e.matmul(ps_y.ap()[:, 0:C], lhsT=stb[0].ap(), rhs=qh.ap()[:, cs], start=True, stop=False),
                           reads=[stb[1], "qh"], writes=[pk_y])
                    op("pe", lambda e: e.matmul(ps_y.ap()[:, 0:C], lhsT=vtok.ap()[:, c, :], rhs=pt.ap(), start=(c == 0), stop=True),
                       reads=[("vtok", c // 4), ptk], writes=[pk_y])
                    op("act", lambda e: e.copy(out=ysb.ap()[:, cs], in_=ps_y.ap()[:, 0:C]), reads=[pk_y], writes=[("ysb", c // 8)])
                    if c < NCH - 1:
                        if c == 0:
                            op("dve", lambda e: e.tensor_copy(out=st.ap(), in_=ps_d.ap()[:, 0:128]), reads=[pk_d], writes=["st"])
                        else:
                            op("dve", lambda e: e.scalar_tensor_tensor(out=st.ap(), in0=st.ap(), scalar=dcol.ap()[:, 2, c:c + 1], in1=ps_d.ap()[:, 0:128],
                                                                       op0=ALU.mult, op1=ALU.add), reads=["st", pk_d, ("dcol", 2)], writes=["st"])
                        stb = stb_r.next()
                        op("act", lambda e: e.copy(out=stb[0].ap(), in_=st.ap()), reads=["st"], writes=[stb[1]])
                for tb in range(NTB):
                    sq, sk = sq_r.next()
                    yk = [("ysb", tb * 2), ("ysb", tb * 2 + 1)]
                    op("act", lambda e: e.activation(out=sq.ap(), in_=ysb.ap()[:, tsl(tb)], func=AF.Square), reads=yk, writes=[sk])
                    ps, pk = psr.next()
                    op("pe", lambda e: e.matmul(ps.ap(), lhsT=ones.ap(), rhs=sq.ap(), start=True, stop=True), reads=[sk, "ones"], writes=[pk])
                    sd, sdk = nt_r.next()
                    op("act", lambda e: e.activation(out=sd.ap(), in_=ps.ap(), func=AF.Ln, bias=epsc.ap(), scale=1.0 / 128), reads=[pk, "epsc"], writes=[sdk])
                    op("act", lambda e: e.activation(out=sd.ap(), in_=sd.ap(), func=AF.Exp, scale=-0.5), reads=[sdk], writes=[sdk])
                    op("dve", lambda e: e.scalar_tensor_tensor(out=sd.ap(), in0=ysb.ap()[:, tsl(tb)], scalar=gng.ap()[:, h:h + 1], in1=sd.ap(),
                                                               op0=ALU.mult, op1=ALU.mult), reads=yk + [sdk, "gng"], writes=[sdk])
                    op("dve", lambda e: e.tensor_tensor(out=yo.ap()[:, tsl(tb)], in0=sd.ap(), in1=gs.ap()[:, tsl(tb)], op=ALU.mult),
                       reads=[sdk, ("gs", tb)], writes=[("yo", tb)])
                k.dma(yT_dram[1, :, h, :], yo.ap(), reads=[("yo", tb) for tb in range(NTB)], writes=[("yT_dram", 1, h)])


    def mixer_dsa(l):
        NQ = S // 128
        with Scope(k) as sc:
            qT = sc.sb("qT", [128, 4, S], BF16)
            kTz = sc.sb("kTz", [128, 4, S], BF16)
            qiT = sc.sb("qiT", [128, 2, S], BF16)
            kiTz = sc.sb("kiTz", [128, 2, S], BF16)
            vtok = sc.sb("cvtok", [128, 2, NQ, 128], BF16)
            wi = sc.sb("wi", [128, NQ, 4], F32)
            wabs = sc.sb("wabs", [128, NQ, 4], F32)
            wsg = sc.sb("wsg", [128, NQ, 4], F32)
            yo = sc.sb("cyo", [128, 4, S], BF16)
            onesb = sc.sb("onesb", [128, 128], BF16)
            rt_r = sc.ring("rt", [128, TB], F32, 3)
            op("pool", lambda e: e.memset(onesb.ap(), 1.0), writes=["onesb"])
            with Scope(k) as s2:
                stg_r = s2.ring("cstg", [128, KC, 128], F32, 2)
                w_r = s2.ring("cw", [128, KC, 128], BF16, 2)
                wsw_r = s2.ring("cwsw", [128, KC, 128], BF16, 2)
                rope = s2.sb("rope", [128, 2, S], F32)
                k.dma(rope.ap(), c_rope.rearrange("a p s -> p a s"), writes=["rope"])

                def wload_cols(col_list, swap):
                    stg, sk = stg_r.next()
                    wt, wk = w_r.next()
                    o = 0
                    for (c0, n) in col_list:
                        if c0 is None:
                            op("pool", lambda e: e.memset(stg.ap()[:, :, o:o + n], 0.0), writes=[(sk, o), sk])
                        else:
                            k.dma(stg.ap()[:, :, o:o + n], P["w_in"][l][:, c0:c0 + n].rearrange("(c p) n -> p c n", p=128), writes=[(sk, o), sk])
                        o += n
                    rk = [(sk, oo) for oo in np.cumsum([0] + [n for _, n in col_list[:-1]]).tolist()] + [sk]
                    cast(wt.ap()[:, :, 0:o], stg.ap()[:, :, 0:o], rk, [wk])
                    if not swap:
                        return wt, wk, None, None
                    ws, wsk = wsw_r.next()
                    for b0 in range(0, o, 64):
                        cast(ws.ap()[:, :, b0:b0 + 32], stg.ap()[:, :, b0 + 32:b0 + 64], rk, [(wsk, b0), wsk])
                        cast(ws.ap()[:, :, b0 + 32:b0 + 64], stg.ap()[:, :, b0:b0 + 32], rk, [(wsk, b0 + 32), wsk])
                    return wt, wk, ws, [(wsk, b0) for b0 in range(0, o, 32)] + [wsk]

                def rope_proj(col_list, dst_fn, dkey):
                    wt, wk, ws, wsk = wload_cols(col_list, True)
                    for tb in range(NTB):
                        p1, pk1 = psr.next()
                        p2, pk2 = psr.next()
                        for kc in range(KC):
                            op("pe", lambda e: e.matmul(p1.ap(), lhsT=wt.ap()[:, kc, :], rhs=uT.ap()[:, kc, usl(tb)], start=(kc == 0), stop=(kc == KC - 1)),
                               reads=[wk] + ukeys(kc, tb), writes=[pk1])
                        for kc in range(KC):
                            op("pe", lambda e: e.matmul(p2.ap(), lhsT=ws.ap()[:, kc, :], rhs=uT.ap()[:, kc, usl(tb)], start=(kc == 0), stop=(kc == KC - 1)),
                               reads=wsk + ukeys(kc, tb), writes=[pk2])
                        t1, tk1 = rt_r.next()
                        t2, tk2 = rt_r.next()
                        op("dve", lambda e: e.tensor_tensor(out=t1.ap(), in0=rope.ap()[:, 0, tsl(tb)], in1=p1.ap(), op=ALU.mult), reads=["rope", pk1], writes=[tk1])
                        op("dve", lambda e: e.tensor_tensor(out=t2.ap(), in0=rope.ap()[:, 1, tsl(tb)], in1=p2.ap(), op=ALU.mult), reads=["rope", pk2], writes=[tk2])
                        op("pool", lambda e: e.tensor_tensor(out=dst_fn(tb), in0=t1.ap(), in1=t2.ap(), op=ALU.add), reads=[tk1, tk2], writes=[(dkey, tb)])

                co = C_OFF
                for ch in range(4):
                    rope_proj([(co + ch * 128, 128)], lambda tb: qT.ap()[:, ch, tsl(tb)], ("qT", ch))
                for c in range(2):
                    rope_proj([(co + 512 + c * 64, 64), (None, 64)], lambda tb: kTz.ap()[:, c * 2, tsl(tb)], ("kTz", c * 2))
                    rope_proj([(None, 64), (co + 512 + c * 64, 64)], lambda tb: kTz.ap()[:, c * 2 + 1, tsl(tb)], ("kTz", c * 2 + 1))
                for ch in range(2):
                    rope_proj([(co + 768 + ch * 128, 128)], lambda tb: qiT.ap()[:, ch, tsl(tb)], ("qiT", ch))
                rope_proj([(co + 1024, 64), (None, 64)], lambda tb: kiTz.ap()[:, 0, tsl(tb)], ("kiTz", 0))
                rope_proj([(None, 64), (co + 1024, 64)], lambda tb: kiTz.ap()[:, 1, tsl(tb)], ("kiTz", 1))
                for c in range(2):
                    wt, wk, _, _ = wload_cols([(co + 640 + c * 64, 64), (co + 640 + c * 64, 64)], False)
                    for s4 in range(NQ // 4):
                        ps, pk = psr.next()
                        for j in range(4):
                            sb = s4 * 4 + j
                            for kc in range(KC):
                                op("pe", lambda e: e.matmul(ps.ap()[:, j * 128:(j + 1) * 128], lhsT=uT.ap()[:, kc, 2 + sb * 128: 2 + (sb + 1) * 128],
                                                            rhs=wt.ap()[:, kc, :], start=(kc == 0), stop=(kc == KC - 1)),
                                   reads=[wk] + ukeys(kc, sb // 4), writes=[pk])
                        cast(vtok.ap()[:, c, s4 * 4:(s4 + 1) * 4, :], ps.ap().rearrange("p (a b) -> p a b", a=4), [pk], [("cvtok", c, s4)], eng="act")
                wt, wk, _, _ = wload_cols([(co + 1088, 4)], False)
                ps, pk = psr.next()
                for qb in range(NQ):
                    for kc in range(KC):
                        op("pe", lambda e: e.matmul(ps.ap()[:, qb * 4:(qb + 1) * 4], lhsT=uT.ap()[:, kc, 2 + qb * 128: 2 + (qb + 1) * 128],
                                                    rhs=wt.ap()[:, kc, 0:4], start=(kc == 0), stop=(kc == KC - 1)),
                           reads=[wk] + ukeys(kc, qb // 4), writes=[pk])
                op("dve", lambda e: e.tensor_copy(out=wi.ap(), in_=ps.ap()[:, 0:NQ * 4].rearrange("p (a b) -> p a b", b=4)), reads=[pk], writes=["wi"])
                op("act", lambda e: e.activation(out=wabs.ap(), in_=wi.ap(), func=AF.Abs), reads=["wi"], writes=["wabs"])
                op("act", lambda e: e.activation(out=wsg.ap(), in_=wi.ap(), func=AF.Sign), reads=["wi"], writes=["wsg"])
            NCHN = 4
            score2 = [sc.sb("score%d" % i, [128, S], F32) for i in range(NCHN)]
            Mf2 = [sc.sb("Mf%d" % i, [128, S], BF16) for i in range(NCHN)]
            MT2 = [sc.sb("MT%d" % i, [128, NQ, 128], BF16) for i in range(NCHN)]
            m82 = [sc.sb("m8_%d" % i, [128, 8], F32) for i in range(NCHN)]
            e_r = sc.ring("cE", [128, 4, 128], BF16, 3)
            p_r = sc.ring("cP", [128, 4, 128], BF16, 3)
            rec_r = sc.ring("crec", [128, TB], F32, 2)
            SENT = -3e30

            def scores(qb, i):
                score = score2[i]
                ncol = (qb + 1) * 128
                qs = slice(qb * 128, (qb + 1) * 128)
                for g0 in range(0, ncol, TB):
                    gn = min(TB, ncol - g0)
                    for hi in range(4):
                        ps, pk = psr.next()
                        op("pe", lambda e: e.matmul(ps.ap()[:, 0:gn], lhsT=qiT.ap()[:, hi // 2, qs], rhs=kiTz.ap()[:, hi % 2, g0:g0 + gn], start=True, stop=True),
                           reads=[(("qiT", hi // 2), qb // 4)] + [(("kiTz", hi % 2), tb) for tb in range(g0 // TB, (g0 + gn - 1) // TB + 1)], writes=[pk])
                        r, rk = rt_r.next()
                        op("act", lambda e: e.activation(out=r.ap()[:, 0:gn], in_=ps.ap()[:, 0:gn], func=AF.Relu, scale=wabs.ap()[:, qb, hi:hi + 1]),
                           reads=[pk, "wabs"], writes=[rk])
                        if hi == 0:
                            op("dve", lambda e: e.tensor_scalar(out=score.ap()[:, g0:g0 + gn], in0=r.ap()[:, 0:gn], scalar1=wsg.ap()[:, qb, hi:hi + 1],
                                                                scalar2=None, op0=ALU.mult), reads=[rk, "wsg"], writes=[("score", i, g0)])
                        else:
                            op("dve", lambda e: e.scalar_tensor_tensor(out=score.ap()[:, g0:g0 + gn], in0=r.ap()[:, 0:gn], scalar=wsg.ap()[:, qb, hi:hi + 1],
                                                                       in1=score.ap()[:, g0:g0 + gn], op0=ALU.mult, op1=ALU.add),
                               reads=[rk, "wsg", ("score", i, g0)], writes=[("score", i, g0)])
                sk_all = [("score", i, g0) for g0 in range(0, S, TB)]
                op("dve", lambda e: e.tensor_tensor(out=score.ap()[:, qs], in0=score.ap()[:, qs], in1=tri.ap()[:, 2, :], op=ALU.mult), reads=sk_all + ["tri"], writes=sk_all)
                op("dve", lambda e: e.tensor_tensor(out=score.ap()[:, qs], in0=score.ap()[:, qs], in1=tri.ap()[:, 3, :], op=ALU.add), reads=sk_all + ["tri"], writes=sk_all)

            NIT = 28
            W0 = 8192.0
            bs_S = [sc.sb("bsS%d" % i, [128, 1], F32) for i in range(NCHN)]
            bs_inc = [sc.sb("bsI%d" % i, [128, 1], F32) for i in range(NCHN)]
            bs_mid = [sc.sb("bsM%d" % i, [128, 1], F32) for i in range(NCHN)]
            junkA = sc.sb("junkA", [128, S], BF16)
            junkD = sc.sb("junkD", [128, S], BF16)

            def topk_gen(blocks):
                sk_all = lambda i: [("score", i, g0) for g0 in range(0, S, TB)]
                act = [(i, qb, (qb + 1) * 128) for i, qb in enumerate(blocks) if (qb + 1) * 128 > 256]
                on_dve = lambda i: (i == NCHN - 1)
                for i, qb, ncol in act:
                    op("dve", lambda e: e.memset(bs_mid[i].ap(), 0.0), writes=[("bsM", i)])
                w = W0
                for it in range(NIT if act else 0):
                    for i, qb, ncol in act:
                        if on_dve(i):
                            op("dve", lambda e: e.tensor_scalar(out=junkD.ap()[:, 0:ncol], in0=score2[i].ap()[:, 0:ncol], scalar1=bs_mid[i].ap(), scalar2=None,
                                                                op0=ALU.is_ge, op1=ALU.add, accum_out=bs_S[i].ap()),
                               reads=sk_all(i) + [("bsM", i)], writes=[("bsS", i)])
                        else:
                            op("act", lambda e: e.activation(out=junkA.ap()[:, 0:ncol], in_=score2[i].ap()[:, 0:ncol], func=AF.Sign, bias=bs_mid[i].ap(), scale=1.0,
                                                             accum_out=bs_S[i].ap()),
                               reads=sk_all(i) + [("bsM", i)], writes=[("bsS", i)])
                    for i, qb, ncol in act:
                        kthr = 256.0 if on_dve(i) else float(512 - ncol)
                        op("dve", lambda e: e.tensor_scalar(out=bs_inc[i].ap(), in0=bs_S[i].ap(), scalar1=kthr, scalar2=w / 2, op0=ALU.is_ge, op1=ALU.mult),
                           reads=[("bsS", i)], writes=[("bsI", i)])
                        if on_dve(i):
                            op("dve", lambda e: e.scalar_tensor_tensor(out=bs_mid[i].ap(), in0=bs_mid[i].ap(), scalar=-w / 4, in1=bs_inc[i].ap(), op0=ALU.add, op1=ALU.add),
                               reads=[("bsM", i), ("bsI", i)], writes=[("bsM", i)])
                        else:
                            op("dve", lambda e: e.scalar_tensor_tensor(out=bs_mid[i].ap(), in0=bs_mid[i].ap(), scalar=w / 4, in1=bs_inc[i].ap(), op0=ALU.add, op1=ALU.subtract),
                               reads=[("bsM", i), ("bsI", i)], writes=[("bsM", i)])
                    w = w / 2
                    yield
                for i, qb in enumerate(blocks):
                    ncol = (qb + 1) * 128
                    if ncol > 256:
                        if on_dve(i):
                            op("dve", lambda e: e.tensor_scalar(out=bs_inc[i].ap(), in0=bs_mid[i].ap(), scalar1=-w / 2, scalar2=None, op0=ALU.add), reads=[("bsM", i)], writes=[("bsI", i)])
                        else:
                            op("dve", lambda e: e.tensor_scalar(out=bs_inc[i].ap(), in0=bs_mid[i].ap(), scalar1=-1.0, scalar2=-w / 2, op0=ALU.mult, op1=ALU.add), reads=[("bsM", i)], writes=[("bsI", i)])
                        op("dve", lambda e: e.tensor_scalar(out=Mf2[i].ap()[:, 0:ncol], in0=score2[i].ap()[:, 0:ncol], scalar1=bs_inc[i].ap(), scalar2=None, op0=ALU.is_ge),
                           reads=sk_all(i) + [("bsI", i)], writes=[("Mf", i)])
                    else:
                        op("dve", lambda e: e.tensor_scalar(out=Mf2[i].ap()[:, 0:ncol], in0=score2[i].ap()[:, 0:ncol], scalar1=-1e29, scalar2=None, op0=ALU.is_ge),
                           reads=sk_all(i), writes=[("Mf", i)])
                yield

            def maskT(qb, i):
                for s4 in range(0, qb + 1, 4):
                    nb = min(4, qb + 1 - s4)
                    ps, pk = psr.next()
                    for j in range(nb):
                        sb = s4 + j
                        op("pe", lambda e: e.transpose(out=ps.ap().bitcast(BF16)[:, j * 128:(j + 1) * 128], in_=Mf2[i].ap()[:, sb * 128:(sb + 1) * 128], identity=identb.ap()),
                           reads=[("Mf", i), "identb"], writes=[pk])
                    op("act", lambda e: e.copy(out=MT2[i].ap()[:, s4:s4 + nb, :], in_=ps.ap().bitcast(BF16)[:, 0:nb * 128].rearrange("p (a b) -> p a b", b=128)),
                       reads=[pk], writes=[("MT", i, s4)])

            def attn_gen(qb, i):
                qs = slice(qb * 128, (qb + 1) * 128)
                MT = MT2[i]
                for c in range(2):
                    ps_o, pk_o = psacc.next()
                    ps_d, pk_d = psacc.next()

                    def front(sb):
                        ps, pk = psr.next()
                        for j in range(4):
                            hq = c * 4 + j
                            op("pe", lambda e: e.matmul(ps.ap()[:, j * 128:(j + 1) * 128], lhsT=kTz.ap()[:, c * 2 + hq % 2, sb * 128:(sb + 1) * 128],
                                                        rhs=qT.ap()[:, hq // 2, qs], start=True, stop=True),
                               reads=[(("kTz", c * 2 + hq % 2), sb // 4), (("qT", hq // 2), qb // 4)], writes=[pk])
                        E, Ek = e_r.next()
                        op("act", lambda e: e.activation(out=E.ap(), in_=ps.ap().rearrange("p (a b) -> p a b", a=4), func=AF.Exp, scale=0.125), reads=[pk], writes=[Ek])
                        Pm, Pk = p_r.next()
                        me = "pool" if (sb % 3) else "dve"
                        op(me, lambda e: e.tensor_tensor(out=Pm.ap(), in0=E.ap(), in1=MT.ap()[:, sb:sb + 1, :].to_broadcast([128, 4, 128]), op=ALU.mult),
                           reads=[Ek, ("MT", i, (sb // 4) * 4)], writes=[Pk])
                        return Pm, Pk
                    nxt_p = front(0)
                    for sb in range(qb + 1):
                        Pm, Pk = nxt_p
                        if sb + 1 <= qb:
                            nxt_p = front(sb + 1)
                        op("pe", lambda e: e.matmul(ps_o.ap(), lhsT=vtok.ap()[:, c, sb, :], rhs=Pm.ap().rearrange("p a b -> p (a b)"), start=(sb == 0), stop=(sb == qb)),
                           reads=[("cvtok", c, sb // 4), Pk], writes=[pk_o])
                        op("pe", lambda e: e.matmul(ps_d.ap(), lhsT=onesb.ap(), rhs=Pm.ap().rearrange("p a b -> p (a b)"), start=(sb == 0), stop=(sb == qb)),
                           reads=["onesb", Pk], writes=[pk_d])
                        yield
                    rec, reck = rec_r.next()
                    op("dve", lambda e: e.reciprocal(out=rec.ap(), in_=ps_d.ap()), reads=[pk_d], writes=[reck])
                    for j in range(4):
                        hq = c * 4 + j
                        hb = (hq % 2) * 64
                        op("dve", lambda e: e.tensor_tensor(out=yo.ap()[hb:hb + 64, hq // 2, qs], in0=rec.ap()[hb:hb + 64, j * 128:(j + 1) * 128],
                                                            in1=ps_o.ap()[hb:hb + 64, j * 128:(j + 1) * 128], op=ALU.mult),
                           reads=[reck, pk_o], writes=[("cyo", hq // 2, hq % 2)])

            def run(g):
                for _ in g:
                    pass

            def zipgens(gt, ga_list, n_t, n_a):
                ga = (x for g in ga_list for x in g)
                per = max(1, -(-n_a // max(1, n_t)))
                done_a = False
                for _ in gt:
                    for _j in range(per):
                        if next(ga, "END") == "END":
                            done_a = True
                            break
                if not done_a:
                    for _ in ga:
                        pass

            grps = [tuple(range(g * NCHN, (g + 1) * NCHN)) for g in range(NQ // NCHN)]
            for i_, qb_ in enumerate(grps[0]):
                scores(qb_, i_)
            run(topk_gen(grps[0]))
            for i_, qb_ in enumerate(grps[0]):
                maskT(qb_, i_)
            for p in range(len(grps)):
                cur = grps[p]
                if p + 1 < len(grps):
                    nxt = grps[p + 1]
                    for i_, qb_ in enumerate(nxt):
                        scores(qb_, i_)
                    n_a = sum(2 * (q_ + 1) for q_ in cur)
                    zipgens(topk_gen(nxt), [attn_gen(q_, i_) for i_, q_ in enumerate(cur)], NIT + 1, n_a)
                    for i_, qb_ in enumerate(nxt):
                        maskT(qb_, i_)
                else:
                    for i_, q_ in enumerate(cur):
                        run(attn_gen(q_, i_))
            for ch in range(4):
                k.dma(yT_dram[2, :, ch, :], yo.ap()[:, ch, :], reads=[("cyo", ch, 0), ("cyo", ch, 1)], writes=[("yT_dram", 2, ch)])

    def mixer_rwkv(l):
        C = 128; NCH = S // C
        with Scope(k) as sc:
            stg_r = sc.ring("astg", [128, KC, 128], F32, 1)
            stm_r = sc.ring("astm", [128, KC, 128], F32, 1)
            w1_r = sc.ring("aw1", [128, KC, 128], BF16, 1)
            wm_r = sc.ring("awm", [128, KC, 128], BF16, 1)
            mub_r = sc.ring("amub", [128, 128], F32, 2)
            lrb = sc.sb("lrb", [128, 3, 512], BF16)
            with Scope(k) as s2:
                lrs = s2.sb("lrs", [128, 3, 512], F32)
                k.dma(lrs.ap()[0:64, 0, :], P["a_w_up"][l], writes=["lrs0"]); k.dma(lrs.ap()[0:64, 1, :], P["a_a_up"][l], writes=["lrs1"])
                k.dma(lrs.ap()[:, 2, :], P["a_g_up"][l], writes=["lrs2"])
                op("dve", lambda e: e.tensor_copy(out=lrb.ap()[0:64, 0:2, :], in_=lrs.ap()[0:64, 0:2, :]), reads=["lrs0", "lrs1"], writes=["lrb01"])
                op("dve", lambda e: e.tensor_copy(out=lrb.ap()[:, 2, :], in_=lrs.ap()[:, 2, :]), reads=["lrs2"], writes=["lrb2"])

            def proj_shift(c0, ncol, consume, token_major=None):
                stg, sk = stg_r.next(); stm, smk = stm_r.next(); w1, w1k = w1_r.next(); wm, wmk = wm_r.next(); mub, mk = mub_r.next()
                k.dma(stg.ap()[:, :, 0:ncol], P["w_in"][l][:, c0:c0 + ncol].rearrange("(c p) n -> p c n", p=128), writes=[sk])
                k.dma(mub.ap()[:, 0:ncol], P["a_mu"][l][c0 - A_OFF:c0 - A_OFF + ncol].partition_broadcast(128), writes=[mk])
                op("dve", lambda e: e.tensor_tensor(out=stm.ap()[:, :, 0:ncol], in0=stg.ap()[:, :, 0:ncol],
                                                    in1=mub.ap()[:, 0:ncol].unsqueeze(1).to_broadcast([128, KC, ncol]), op=ALU.mult),
                   reads=[sk, mk], writes=[smk])
                op("act", lambda e: e.copy(out=wm.ap()[:, :, 0:ncol], in_=stm.ap()[:, :, 0:ncol]), reads=[smk], writes=[wmk])
                op("dve", lambda e: e.tensor_tensor(out=w1.ap()[:, :, 0:ncol], in0=stg.ap()[:, :, 0:ncol], in1=stm.ap()[:, :, 0:ncol], op=ALU.subtract),
                   reads=[sk, smk], writes=[w1k])
                if token_major is not None:
                    token_major(w1, w1k, wm, wmk)
                    return
                for tb in range(NTB):
                    ps, pk = psr.next()
                    for kc in range(KC):
                        op("pe", lambda e: e.matmul(ps.ap()[0:ncol, :], lhsT=w1.ap()[:, kc, 0:ncol], rhs=uT.ap()[:, kc, usl(tb)], start=(kc == 0), stop=False),
                           reads=[w1k] + ukeys(kc, tb), writes=[pk])
                    for kc in range(KC):
                        op("pe", lambda e: e.matmul(ps.ap()[0:ncol, :], lhsT=wm.ap()[:, kc, 0:ncol], rhs=uT.ap()[:, kc, usl(tb, 1)], start=False, stop=(kc == KC - 1)),
                           reads=[wmk] + ukeys(kc, tb) + (ukeys(kc, tb - 1) if tb else []) + ["uTpad"], writes=[pk])
                    consume(tb, ps, pk)

            tw = sc.sb("tw", [64, S], BF16); adT = sc.sb("adT", [64, S], BF16); sgT = sc.sb("sgT", [128, S], BF16)
            proj_shift(A_OFF + 1536, 64, lambda tb, ps, pk: op("act", lambda e: e.activation(out=tw.ap()[:, tsl(tb)], in_=ps.ap()[0:64, :], func=AF.Tanh), reads=[pk], writes=[("tw", tb)]))
            proj_shift(A_OFF + 1600, 64, lambda tb, ps, pk: op("act", lambda e: e.copy(out=adT.ap()[:, tsl(tb)], in_=ps.ap()[0:64, :]), reads=[pk], writes=[("adT", tb)]))
            proj_shift(A_OFF + 1664, 128, lambda tb, ps, pk: op("act", lambda e: e.activation(out=sgT.ap()[:, tsl(tb)], in_=ps.ap(), func=AF.Sigmoid), reads=[pk], writes=[("sgT", tb)]))
            pc = sc.sb("pc", [128, 8, 4], F32)
            for i_, nm in enumerate(["a_w0", "a_a0", "a_k_k", "a_k_a", "a_r_k", "a_gn_g", "a_gn_b"]):
                load_pvec(pc.ap()[:, i_, :], P[nm][l], 4, [("pc", i_)])
            op("dve", lambda e: e.tensor_scalar(out=pc.ap()[:, 7, :], in0=pc.ap()[:, 3, :], scalar1=-1.0, scalar2=1.0, op0=ALU.mult, op1=ALU.add), reads=[("pc", 3)], writes=[("pc", 7)])
            pcall = [("pc", i_) for i_ in range(8)]
            mu_pc = sc.sb("mu_pc", [128, 14], F32)
            load_pvec(mu_pc.ap(), P["a_mu"][l], 14, ["mu_pc"])
            gnc = sc.sb("gnc", [128, 1], F32)
            op("pool", lambda e: e.memset(gnc.ap(), 64e-5), writes=["gnc"])
            F = {n_: sc.sb("af_" + n_, [128, S], F32) for n_ in ["r", "k", "v", "a", "L", "lw", "t1", "t2"]}
            F["g"] = sc.sb("af_g", [128, S], BF16)
            Bh = {n_: [sc.sb("ab_%s%d" % (n_, h), [128, S], BF16) for h in range(2)] for n_ in ["bt", "kt", "bc", "kc"]}
            Bh["rh"] = [sc.sb("ab_rh", [128, S], BF16)] * 2
            Bh["ah"] = [sc.sb("ab_ah", [128, S], BF16)] * 2
            AR = sc.sb("AR", [128, NCH, 2, C], BF16)
            dcol = sc.sb("adcol", [128, 3, NCH], F32)
            vtok = sc.sb("avtok", [128, NCH, 128], BF16)
            vpad = [sc.sb("avpad%d" % h, [128, NCH, 128], BF16) for h in range(2)]
            H = [sc.sb("aH%d" % h, [128, 128], F32) for h in range(2)]
            Hb_r = [sc.ring("aHb%d" % h, [128, 128], BF16, 2) for h in range(2)]
            upad = [sc.ring("aupad%d" % h, [128, 128], BF16, 2) for h in range(2)]
            mt_r = sc.ring("amt", [128, 128], BF16, 8)
            kp_r = sc.ring("akp", [128, 128], BF16, 12)
            wu_r = sc.ring("awu", [128, 128], BF16, 2)
            tk_r = sc.ring("atk", [128, 128], BF16, 6)
            ya = F["t1"]
            yo = Bh["rh"][0]
            hm = lambda h: tri.ap()[:, 6, h:h + 1]
            for h in range(2):
                for r_ in upad[h].t:
                    op("pool", lambda e: e.memset(r_.ap(), 0.0), writes=[("upad0", h, r_.name)])
            for hp in range(4):
                co = A_OFF + hp * 128
                prm = lambda i_: pc.ap()[:, i_, hp:hp + 1]
                allk = lambda nm: [(nm, tb) for tb in range(NTB)]
                for nm, cc in (("r", co), ("k", co + 512), ("v", co + 1024)):
                    stg, sk = stg_r.next(); w1, w1k = w1_r.next()
                    k.dma(stg.ap(), P["w_in"][l][:, cc:cc + 128].rearrange("(c p) n -> p c n", p=128), writes=[sk])
                    op("act", lambda e: e.copy(out=w1.ap(), in_=stg.ap()), reads=[sk], writes=[w1k])
                    for tb in range(NTB):
                        ps, pk = psr.next()
                        for kc in range(KC):
                            op("pe", lambda e: e.matmul(ps.ap(), lhsT=w1.ap()[:, kc, :], rhs=uT.ap()[:, kc, usl(tb)], start=(kc == 0), stop=(kc == KC - 1)),
                               reads=[w1k] + ukeys(kc, tb), writes=[pk])
                        op("act", lambda e: e.copy(out=F[nm].ap()[:, tsl(tb)], in_=ps.ap()), reads=[pk], writes=[(nm, tb)])
                    mucol = mu_pc.ap()[:, (cc - A_OFF) // 128:(cc - A_OFF) // 128 + 1]
                    op("dve", lambda e: e.tensor_tensor(out=F["t2"].ap()[:, 1:S], in0=F[nm].ap()[:, 0:S - 1], in1=F[nm].ap()[:, 1:S], op=ALU.subtract), reads=allk(nm), writes=["t2"])
                    op("dve", lambda e: e.tensor_scalar(out=F["t2"].ap()[:, 0:1], in0=F[nm].ap()[:, 0:1], scalar1=-1.0, scalar2=None, op0=ALU.mult), reads=allk(nm) + ["t2"], writes=["t2"])
                    op("dve", lambda e: e.scalar_tensor_tensor(out=F[nm].ap(), in0=F["t2"].ap(), scalar=mucol, in1=F[nm].ap(), op0=ALU.mult, op1=ALU.add),
                       reads=["t2", "mu_pc"] + allk(nm), writes=allk(nm))

                def vtm(w1, w1k, wm, wmk):
                    for c4 in range(NCH // 4):
                        ps, pk = psr.next()
                        for j in range(4):
                            c = c4 * 4 + j
                            for kc in range(KC):
                                op("pe", lambda e: e.matmul(ps.ap()[:, j * 128:(j + 1) * 128], lhsT=uT.ap()[:, kc, 2 + c * C:2 + (c + 1) * C], rhs=w1.ap()[:, kc, :], start=(kc == 0), stop=False),
                                   reads=[w1k] + ukeys(kc, c // 4), writes=[pk])
                            for kc in range(KC):
                                op("pe", lambda e: e.matmul(ps.ap()[:, j * 128:(j + 1) * 128], lhsT=uT.ap()[:, kc, 1 + c * C:1 + (c + 1) * C], rhs=wm.ap()[:, kc, :], start=False, stop=(kc == KC - 1)),
                                   reads=[wmk] + ukeys(kc, c // 4) + (ukeys(kc, c // 4 - 1) if c >= 4 else []) + ["uTpad"], writes=[pk])
                        cast(vtok.ap()[:, c4 * 4:(c4 + 1) * 4, :], ps.ap().rearrange("p (a b) -> p a b", a=4), [pk], [("avtok", c4)], eng="act")
                proj_shift(co + 1024, 128, None, token_major=vtm)
                vk = [("avtok", c4) for c4 in range(NCH // 4)]
                for h in range(2):
                    op("pool", lambda e: e.tensor_tensor(out=vpad[h].ap(), in0=vtok.ap(), in1=tri.ap()[:, 5, h * 64:h * 64 + 1].to_broadcast([128, NCH, 128]) if False else
                                                         tri.ap()[0:1, 5, :].partition_broadcast(128).unsqueeze(1).to_broadcast([128, NCH, 128]) if False else vtok.ap(), op=ALU.mult) if False else
                       e.memset(vpad[h].ap(), 0.0), reads=vk, writes=[("vpad", h)])
                    op("act", lambda e: e.copy(out=vpad[h].ap()[:, :, h * 64:(h + 1) * 64], in_=vtok.ap()[:, :, h * 64:(h + 1) * 64]), reads=vk + [("vpad", h)], writes=[("vpad", h)])
                for tb in range(NTB):
                    ps, pk = psr.next()
                    op("pe", lambda e: e.matmul(ps.ap(), lhsT=lrb.ap()[0:64, 0, hp * 128:(hp + 1) * 128], rhs=tw.ap()[:, tsl(tb)], start=True, stop=True), reads=["lrb01", ("tw", tb)], writes=[pk])
                    op("act", lambda e: e.activation(out=F["lw"].ap()[:, tsl(tb)], in_=ps.ap(), func=AF.Sigmoid, bias=prm(0)), reads=[pk] + pcall, writes=[("lw", tb)])
                    ps, pk = psr.next()
                    op("pe", lambda e: e.matmul(ps.ap(), lhsT=lrb.ap()[0:64, 1, hp * 128:(hp + 1) * 128], rhs=adT.ap()[:, tsl(tb)], start=True, stop=True), reads=["lrb01", ("adT", tb)], writes=[pk])
                    op("act", lambda e: e.activation(out=F["a"].ap()[:, tsl(tb)], in_=ps.ap(), func=AF.Sigmoid, bias=prm(1)), reads=[pk] + pcall, writes=[("a", tb)])
                    ps, pk = psr.next()
                    op("pe", lambda e: e.matmul(ps.ap(), lhsT=lrb.ap()[:, 2, hp * 128:(hp + 1) * 128], rhs=sgT.ap()[:, tsl(tb)], start=True, stop=True), reads=["lrb2", ("sgT", tb)], writes=[pk])
                    op("act", lambda e: e.copy(out=F["g"].ap()[:, tsl(tb)], in_=ps.ap()), reads=[pk], writes=[("g", tb)])
                op("dve", lambda e: e.tensor_scalar(out=F["lw"].ap(), in0=F["lw"].ap(), scalar1=-0.6065306597126334, scalar2=None, op0=ALU.mult), reads=allk("lw"), writes=allk("lw"))
                op("dve", lambda e: e.tensor_tensor_scan(out=F["L"].ap(), data0=ones.ap()[:, 0:1].to_broadcast([128, S]), data1=F["lw"].ap(), initial=0.0, op0=ALU.mult, op1=ALU.add), reads=allk("lw") + ["ones"], writes=["L"])
                L3 = F["L"].ap().rearrange("p (c t) -> p c t", t=C)
                op("dve", lambda e: e.tensor_tensor(out=dcol.ap()[:, 0, :], in0=L3[:, :, C - 1], in1=L3[:, :, C // 2 - 1], op=ALU.subtract), reads=["L"], writes=[("dc", 0)])
                op("dve", lambda e: e.tensor_copy(out=dcol.ap()[:, 1, 0:1], in_=L3[:, 0, C // 2 - 1:C // 2]), reads=["L"], writes=[("dc", 1)])
                op("dve", lambda e: e.tensor_tensor(out=dcol.ap()[:, 1, 1:NCH], in0=L3[:, 1:NCH, C // 2 - 1], in1=L3[:, 0:NCH - 1, C - 1], op=ALU.subtract), reads=["L", ("dc", 1)], writes=[("dc", 1)])
                op("dve", lambda e: e.tensor_tensor(out=dcol.ap()[:, 2, :], in0=dcol.ap()[:, 0, :], in1=dcol.ap()[:, 1, :], op=ALU.add), reads=[("dc", 0), ("dc", 1)], writes=[("dc", 2)])
                dck = [("dc", 0), ("dc", 1), ("dc", 2)]
                op("act", lambda e: e.activation(out=dcol.ap(), in_=dcol.ap(), func=AF.Exp), reads=dck, writes=dck)
                t1, t2 = F["t1"], F["t2"]
                op("dve", lambda e: e.tensor_scalar(out=t1.ap(), in0=F["k"].ap(), scalar1=prm(2), scalar2=None, op0=ALU.mult), reads=allk("k") + pcall, writes=["t1"])
                op("act", lambda e: e.activation(out=t2.ap(), in_=t1.ap(), func=AF.Square), reads=["t1"], writes=["t2"])
                pss = [psr.next() for _ in range(NTB)]
                for tb in range(NTB):
                    op("pe", lambda e: e.matmul(pss[tb][0].ap(), lhsT=tri.ap()[:, 5, :], rhs=t2.ap()[:, tsl(tb)], start=True, stop=True), reads=["t2", "tri"], writes=[pss[tb][1]])
                for tb in range(NTB):
                    op("dve", lambda e: e.tensor_scalar(out=t2.ap()[:, tsl(tb)], in0=pss[tb][0].ap(), scalar1=1e-19, scalar2=None, op0=ALU.max), reads=[pss[tb][1], "t2"], writes=["t2"])
                op("act", lambda e: e.activation(out=t2.ap(), in_=t2.ap(), func=AF.Ln), reads=["t2"], writes=["t2"])
                op("act", lambda e: e.activation(out=t2.ap(), in_=t2.ap(), func=AF.Exp, scale=-0.5), reads=["t2"], writes=["t2"])
                op("dve", lambda e: e.tensor_tensor(out=t1.ap(), in0=t1.ap(), in1=t2.ap(), op=ALU.mult), reads=["t1", "t2"], writes=["t1"])
                op("dve", lambda e: e.tensor_scalar(out=t2.ap(), in0=F["a"].ap(), scalar1=prm(3), scalar2=prm(7), op0=ALU.mult, op1=ALU.add), reads=allk("a") + pcall, writes=["t2"])
                op("dve", lambda e: e.tensor_tensor(out=F["k"].ap(), in0=F["k"].ap(), in1=t2.ap(), op=ALU.mult), reads=allk("k") + ["t2"], writes=allk("k"))
                op("dve", lambda e: e.tensor_tensor(out=F["a"].ap(), in0=F["a"].ap(), in1=t1.ap(), op=ALU.mult), reads=allk("a") + ["t1"], writes=allk("a"))
                op("dve", lambda e: e.scalar_tensor_tensor(out=t2.ap(), in0=F["r"].ap(), scalar=prm(4), in1=F["k"].ap(), op0=ALU.mult, op1=ALU.mult), reads=allk("r") + allk("k") + pcall, writes=["t2"])
                pss = [psr.next() for _ in range(NTB)]
                for tb in range(NTB):
                    op("pe", lambda e: e.matmul(pss[tb][0].ap(), lhsT=tri.ap()[:, 5, :], rhs=t2.ap()[:, tsl(tb)], start=True, stop=True), reads=["t2", "tri"], writes=[pss[tb][1]])
                for tb in range(NTB):
                    op("dve", lambda e: e.tensor_tensor(out=F["v"].ap()[:, tsl(tb)], in0=F["v"].ap()[:, tsl(tb)], in1=pss[tb][0].ap(), op=ALU.mult), reads=[("v", tb), pss[tb][1]], writes=[("v", tb)])
                m_b = L3[:, :, C // 2 - 1:C // 2].to_broadcast([128, NCH, C])
                v3 = lambda t_: t_.ap().rearrange("p (c t) -> p c t", t=C)
                op("dve", lambda e: e.tensor_tensor(out=v3(t2), in0=L3, in1=m_b, op=ALU.subtract), reads=["L"], writes=["t2"])
                op("dve", lambda e: e.tensor_tensor(out=F["lw"].ap(), in0=t2.ap(), in1=F["lw"].ap(), op=ALU.subtract), reads=["t2"] + allk("lw"), writes=allk("lw"))
                op("act", lambda e: e.activation(out=F["lw"].ap(), in_=F["lw"].ap(), func=AF.Exp), reads=allk("lw"), writes=allk("lw"))
                op("act", lambda e: e.activation(out=F["L"].ap(), in_=t2.ap(), func=AF.Exp, scale=-1.0), reads=["t2"], writes=["L"])
                op("act", lambda e: e.activation(out=t2.ap(), in_=t2.ap(), func=AF.Exp), reads=["t2"], writes=["t2"])
                op("dve", lambda e: e.tensor_tensor(out=F["r"].ap(), in0=F["r"].ap(), in1=t2.ap(), op=ALU.mult), reads=allk("r") + ["t2"], writes=allk("r"))
                op("act", lambda e: e.copy(out=AR.ap()[:, :, 1, :], in_=v3(F["r"])), reads=allk("r"), writes=["AR1"])
                dB = dcol.ap()[:, 1, :].unsqueeze(2).to_broadcast([128, NCH, C]); dA = dcol.ap()[:, 0, :].unsqueeze(2).to_broadcast([128, NCH, C])
                op("dve", lambda e: e.tensor_tensor(out=v3(t2), in0=v3(F["r"]), in1=dB, op=ALU.mult), reads=allk("r") + dck, writes=["t2"])
                op("act", lambda e: e.copy(out=Bh["rh"][0].ap(), in_=t2.ap()), reads=["t2"], writes=[("rh", 0), ("rh", 1)] + [("yo", tb) for tb in range(NTB)])
                op("dve", lambda e: e.scalar_tensor_tensor(out=F["lw"].ap(), in0=t1.ap(), scalar=-1.0, in1=F["lw"].ap(), op0=ALU.mult, op1=ALU.mult), reads=["t1"] + allk("lw"), writes=allk("lw"))
                op("act", lambda e: e.copy(out=AR.ap()[:, :, 0, :], in_=v3(F["lw"])), reads=allk("lw"), writes=["AR0"])
                op("dve", lambda e: e.tensor_tensor(out=v3(t2), in0=v3(F["lw"]), in1=dB, op=ALU.mult), reads=allk("lw") + dck + [("rh", 0), ("rh", 1)], writes=["t2"])
                op("act", lambda e: e.copy(out=Bh["ah"][0].ap(), in_=t2.ap()), reads=["t2"], writes=[("ah", 0), ("ah", 1)])
                for src, nt, ncn in ((F["a"], "bt", "bc"), (F["k"], "kt", "kc")):
                    sk_ = allk("a") if nt == "bt" else allk("k")
                    op("dve", lambda e: e.tensor_tensor(out=src.ap(), in0=src.ap(), in1=F["L"].ap(), op=ALU.mult), reads=sk_ + ["L"], writes=sk_)
                    op("dve", lambda e: e.tensor_tensor(out=v3(t1) if nt == "kt" else v3(t2), in0=v3(src), in1=dA, op=ALU.mult), reads=sk_ + dck + [("ah", 0), ("ah", 1)], writes=["t1" if nt == "kt" else "t2"])
                    tt_ = t1 if nt == "kt" else t2
                    for h in range(2):
                        op("act", lambda e: e.activation(out=Bh[nt][h].ap(), in_=src.ap(), func=AF.Copy, scale=hm(h)), reads=sk_ + ["tri"], writes=[(nt, h)])
                        op("act", lambda e: e.activation(out=Bh[ncn][h].ap(), in_=tt_.ap(), func=AF.Copy, scale=hm(h)), reads=["t1" if nt == "kt" else "t2", "tri"], writes=[(ncn, h)])
                k.phase_barrier()
                pool_t = []
                for nm_ in ("r", "k", "a", "L", "lw", "t2"):
                    vb = F[nm_].ap().bitcast(BF16)
                    pool_t += [vb[:, j * 128:(j + 1) * 128] for j in range(32)]
                RES = {}
                it_ = iter(pool_t)
                for c_ in range(NCH):
                    for h_ in range(2):
                        for nm_ in ("TT", "rb", "ak", "rk", "bcT", "kcT"):
                            RES[(nm_, h_, c_)] = next(it_)
                vt_b = vtok.ap().rearrange("p a b -> p (a b)")
                tmp_t = [t_.ap() for t_ in mt_r.t] + [t_.ap() for t_ in tk_r.t] + [t_.ap() for t_ in kp_r.t] + [vt_b[:, j * 128:(j + 1) * 128] for j in range(16)]
                GCH = 8
                assert len(tmp_t) >= 5 * GCH, len(tmp_t)

                def A_gen(group):
                    st = {}
                    for gi_, (h, c) in enumerate(group):
                        cs = slice(c * C, (c + 1) * C)
                        T5 = tmp_t[gi_ * 5:(gi_ + 1) * 5]
                        tk_ = lambda j: ("tmp", gi_, j)
                        ps, pk = psr.next()
                        op("pe", lambda e: e.matmul(ps.ap()[:, 0:256], lhsT=Bh["bt"][h].ap()[:, cs], rhs=AR.ap()[:, c].rearrange("p a b -> p (a b)"), start=True, stop=True), reads=[("bt", h), "AR0", "AR1"], writes=[pk])
                        op("dve", lambda e: e.tensor_tensor(out=T5[1], in0=ps.ap()[:, 0:128], in1=tri.ap()[:, 1, :], op=ALU.mult), reads=[pk, "tri"], writes=[tk_(1)])
                        op("dve", lambda e: e.tensor_tensor(out=RES[("rb", h, c)], in0=ps.ap()[:, 128:256], in1=tri.ap()[:, 0, :], op=ALU.mult), reads=[pk, "tri"], writes=[("res", "rb", h, c)])
                        ps, pk = psr.next()
                        op("pe", lambda e: e.matmul(ps.ap()[:, 0:256], lhsT=Bh["kt"][h].ap()[:, cs], rhs=AR.ap()[:, c].rearrange("p a b -> p (a b)"), start=True, stop=True), reads=[("kt", h), "AR0", "AR1"], writes=[pk])
                        op("dve", lambda e: e.tensor_tensor(out=RES[("ak", h, c)], in0=ps.ap()[:, 0:128], in1=tri.ap()[:, 1, :], op=ALU.mult), reads=[pk, "tri"], writes=[("res", "ak", h, c)])
                        op("dve", lambda e: e.tensor_tensor(out=RES[("rk", h, c)], in0=ps.ap()[:, 128:256], in1=tri.ap()[:, 0, :], op=ALU.mult), reads=[pk, "tri"], writes=[("res", "rk", h, c)])
                        ps, pk = psr.next()
                        op("pe", lambda e: e.matmul(ps.ap()[:, 0:128], lhsT=AR.ap()[:, c, 0, :], rhs=Bh["bt"][h].ap()[:, cs], start=True, stop=True), reads=["AR0", ("bt", h)], writes=[pk])
                        op("dve", lambda e: e.tensor_tensor(out=T5[0], in0=ps.ap()[:, 0:128], in1=tri.ap()[:, 4, :], op=ALU.mult), reads=[pk, "tri"], writes=[tk_(0)])
                        op("dve", lambda e: e.tensor_tensor(out=T5[4], in0=T5[1], in1=identb.ap(), op=ALU.add), reads=[tk_(1), "identb"], writes=[tk_(4)])
                        pst, pkt = psr.next()
                        op("pe", lambda e: e.transpose(out=pst.ap().bitcast(BF16)[:, 0:128], in_=Bh["bc"][h].ap()[:, cs], identity=identb.ap()), reads=[("bc", h), "identb"], writes=[pkt])
                        op("pe", lambda e: e.transpose(out=pst.ap().bitcast(BF16)[:, 128:256], in_=Bh["kc"][h].ap()[:, cs], identity=identb.ap()), reads=[("kc", h), "identb"], writes=[pkt])
                        op("act", lambda e: e.copy(out=RES[("bcT", h, c)], in_=pst.ap().bitcast(BF16)[:, 0:128]), reads=[pkt], writes=[("res", "bcT", h, c)])
                        op("act", lambda e: e.copy(out=RES[("kcT", h, c)], in_=pst.ap().bitcast(BF16)[:, 128:256]), reads=[pkt], writes=[("res", "kcT", h, c)])
                        st[gi_] = [0, 1, 2, 3]
                    yield
                    for lev in range(6):
                        for gi_, (h, c) in enumerate(group):
                            T5 = tmp_t[gi_ * 5:(gi_ + 1) * 5]
                            tk_ = lambda j: ("tmp", gi_, j)
                            iX, iXT, iXn, iXTn = st[gi_]
                            psx, pkx = psr.next()
                            op("pe", lambda e: e.matmul(psx.ap()[:, 0:128], lhsT=T5[iXT], rhs=T5[iX], start=True, stop=True), reads=[tk_(iXT), tk_(iX)], writes=[pkx])
                            if lev < 5:
                                op("pe", lambda e: e.matmul(psx.ap()[:, 128:256], lhsT=T5[iX], rhs=T5[iXT], start=True, stop=True), reads=[tk_(iXT), tk_(iX)], writes=[pkx])
                            op("act", lambda e: e.copy(out=T5[iXn], in_=psx.ap()[:, 0:128]), reads=[pkx], writes=[tk_(iXn)])
                            if lev < 5:
                                op("act", lambda e: e.copy(out=T5[iXTn], in_=psx.ap()[:, 128:256]), reads=[pkx], writes=[tk_(iXTn)])
                            st[gi_] = [iXn, iXTn, iX, iXT]
                        yield
                        for gi_, (h, c) in enumerate(group):
                            T5 = tmp_t[gi_ * 5:(gi_ + 1) * 5]
                            tk_ = lambda j: ("tmp", gi_, j)
                            iX = st[gi_][0]
                            psq, pkq = psr.next()
                            op("pe", lambda e: e.matmul(psq.ap()[:, 0:128], lhsT=T5[iX], rhs=T5[4], start=True, stop=True), reads=[tk_(iX), tk_(4)], writes=[pkq])
                            if lev < 5:
                                op("dve", lambda e: e.tensor_tensor(out=T5[4], in0=T5[4], in1=psq.ap()[:, 0:128], op=ALU.add), reads=[tk_(4), pkq], writes=[tk_(4)])
                            else:
                                op("dve", lambda e: e.tensor_tensor(out=RES[("TT", h, c)], in0=T5[4], in1=psq.ap()[:, 0:128], op=ALU.add), reads=[tk_(4), pkq], writes=[("res", "TT", h, c)])
                        yield

                Hb = [None, None]

                def B_gen(chunks):
                    for c in chunks:
                        cs = slice(c * C, (c + 1) * C)
                        rk_ = lambda nm_, h: ("res", nm_, h, c)
                        wus = []
                        for h in range(2):
                            psw, pkw = psr.next()
                            if c > 0:
                                op("pe", lambda e: e.matmul(psw.ap()[:, 0:128], lhsT=Bh["ah"][h].ap()[:, cs], rhs=Hb[h][0].ap(), start=True, stop=False), reads=[("ah", h), Hb[h][1]], writes=[pkw])
                            op("pe", lambda e: e.matmul(psw.ap()[:, 0:128], lhsT=RES[("ak", h, c)], rhs=vpad[h].ap()[:, c, :], start=(c == 0), stop=True), reads=[rk_("ak", h), ("vpad", h)], writes=[pkw])
                            wu, wuk = wu_r.next()
                            op("act", lambda e: e.copy(out=wu.ap(), in_=psw.ap()[:, 0:128]), reads=[pkw], writes=[wuk])
                            wus.append((wu, wuk))
                        yield
                        Ups = []
                        for h in range(2):
                            psu, pku = psr.next()
                            op("pe", lambda e: e.matmul(psu.ap()[:, 0:128], lhsT=RES[("TT", h, c)], rhs=wus[h][0].ap(), start=True, stop=True), reads=[rk_("TT", h), wus[h][1]], writes=[pku])
                            up_, upk = upad[h].next()
                            op("dve", lambda e: e.tensor_copy(out=up_.ap(), in_=psu.ap()[:, 0:128]), reads=[pku, ("upad0", h, up_.name)], writes=[upk])
                            Ups.append((up_, upk))
                        yield
                        if c < NCH - 1:
                            for h in range(2):
                                psh, pkh = psr.next()
                                op("pe", lambda e: e.matmul(psh.ap()[:, 0:128], lhsT=RES[("bcT", h, c)], rhs=Ups[h][0].ap(), start=True, stop=False), reads=[rk_("bcT", h), Ups[h][1]], writes=[pkh])
                                op("pe", lambda e: e.matmul(psh.ap()[:, 0:128], lhsT=RES[("kcT", h, c)], rhs=vpad[h].ap()[:, c, :], start=False, stop=True), reads=[rk_("kcT", h), ("vpad", h)], writes=[pkh])
                                if c == 0:
                                    op("dve", lambda e: e.tensor_copy(out=H[h].ap(), in_=psh.ap()[:, 0:128]), reads=[pkh], writes=[("H", h)])
                                else:
                                    op("dve", lambda e: e.scalar_tensor_tensor(out=H[h].ap(), in0=H[h].ap(), scalar=dcol.ap()[:, 2, c:c + 1], in1=psh.ap()[:, 0:128], op0=ALU.mult, op1=ALU.add),
                                       reads=[("H", h), pkh] + dck, writes=[("H", h)])
                        psy, pky = psr.next()
                        first = True
                        for h in range(2):
                            if c > 0:
                                op("pe", lambda e: e.matmul(psy.ap()[:, 0:128], lhsT=Hb[h][0].ap(), rhs=Bh["rh"][h].ap()[:, cs], start=first, stop=False), reads=[Hb[h][1], ("rh", h)], writes=[pky]); first = False
                            op("pe", lambda e: e.matmul(psy.ap()[:, 0:128], lhsT=Ups[h][0].ap(), rhs=RES[("rb", h, c)], start=first, stop=False), reads=[Ups[h][1], rk_("rb", h)], writes=[pky]); first = False
                            op("pe", lambda e: e.matmul(psy.ap()[:, 0:128], lhsT=vpad[h].ap()[:, c, :], rhs=RES[("rk", h, c)], start=False, stop=(h == 1)), reads=[("vpad", h), rk_("rk", h)], writes=[pky])
                        op("act", lambda e: e.copy(out=ya.ap()[:, cs], in_=psy.ap()[:, 0:128]), reads=[pky], writes=[("ya", c // 4)])
                        if c < NCH - 1:
                            for h in range(2):
                                Hb[h] = Hb_r[h].next()
                                op("act", lambda e: e.copy(out=Hb[h][0].ap(), in_=H[h].ap()), reads=[("H", h)], writes=[Hb[h][1]])
                        yield

                def zip2(ga, gb):
                    da = db = False
                    while not (da and db):
                        if not da and next(ga, "END") == "END":
                            da = True
                        if not db and next(gb, "END") == "END":
                            db = True

                groups = [[(h, c) for c in range(g * 4, g * 4 + 4) for h in range(2)] for g in range(NCH // 4)]
                for _ in A_gen(groups[0]):
                    pass
                for g in range(len(groups)):
                    bg = B_gen(range(g * 4, g * 4 + 4))
                    if g + 1 < len(groups):
                        zip2(A_gen(groups[g + 1]), bg)
                    else:
                        for _ in bg:
                            pass
                k.phase_barrier()
                gA, gB = F["r"], F["k"]
                yall = [("ya", tb) for tb in range(NTB)]
                pss = [psr.next() for _ in range(NTB)]
                for tb in range(NTB):
                    op("pe", lambda e: e.matmul(pss[tb][0].ap(), lhsT=tri.ap()[:, 5, :], rhs=ya.ap()[:, tsl(tb)], start=True, stop=True), reads=[("ya", tb), "tri"], writes=[pss[tb][1]])
                for tb in range(NTB):
                    op("dve", lambda e: e.scalar_tensor_tensor(out=gA.ap()[:, tsl(tb)], in0=pss[tb][0].ap(), scalar=-1.0 / 64, in1=ya.ap()[:, tsl(tb)], op0=ALU.mult, op1=ALU.add),
                       reads=[pss[tb][1], ("ya", tb)], writes=[("r", tb)])
                op("act", lambda e: e.activation(out=gB.ap(), in_=gA.ap(), func=AF.Square), reads=allk("r"), writes=allk("k"))
                pss = [psr.next() for _ in range(NTB)]
                for tb in range(NTB):
                    op("pe", lambda e: e.matmul(pss[tb][0].ap(), lhsT=tri.ap()[:, 5, :], rhs=gB.ap()[:, tsl(tb)], start=True, stop=True), reads=[("k", tb), "tri"], writes=[pss[tb][1]])
                for tb in range(NTB):
                    op("act", lambda e: e.activation(out=gB.ap()[:, tsl(tb)], in_=pss[tb][0].ap(), func=AF.Ln, bias=gnc.ap(), scale=1.0 / 64), reads=[pss[tb][1], "gnc"], writes=[("k", tb)])
                op("act", lambda e: e.activation(out=gB.ap(), in_=gB.ap(), func=AF.Exp, scale=-0.5), reads=allk("k"), writes=allk("k"))
                op("dve", lambda e: e.tensor_tensor(out=gA.ap(), in0=gA.ap(), in1=gB.ap(), op=ALU.mult), reads=allk("r") + allk("k"), writes=allk("r"))
                op("dve", lambda e: e.tensor_scalar(out=gA.ap(), in0=gA.ap(), scalar1=prm(5), scalar2=prm(6), op0=ALU.mult, op1=ALU.add), reads=allk("r") + pcall, writes=allk("r"))
                op("dve", lambda e: e.tensor_tensor(out=gA.ap(), in0=gA.ap(), in1=F["v"].ap(), op=ALU.add), reads=allk("r") + allk("v"), writes=allk("r"))
                op("dve", lambda e: e.tensor_tensor(out=yo.ap(), in0=gA.ap(), in1=F["g"].ap(), op=ALU.mult), reads=allk("r") + allk("g"), writes=[("yo", tb) for tb in range(NTB)])
                k.dma(yT_dram[0, :, hp, :], yo.ap(), reads=[("yo", tb) for tb in range(NTB)], writes=[("yT_dram", 0, hp)])

    for l in range(DEPTH):
        with Scope(k) as sc:
            hT = sc.sb("hT", [128, KC, S], F32)
            sqr = sc.ring("sq", [128, TB], BF16, 3)
            tmpr = sc.ring("nt", [128, TB], F32, 4)
            load_h(hT)
            rmsnorm(hT, 2 * l, u_dst, sqr, tmpr)
        if "yT_in0" not in dbg:
            mixer_rwkv(l)
        if "yT_in1" not in dbg:
            mixer_hgrn(l)
        if "yT_in2" not in dbg:
            mixer_dsa(l)
        with Scope(k) as sc:
            yT = sc.sb("yT", [128, 3, 4, S], BF16)
            mT = sc.sb("mT", [128, KC, S], BF16)
            stg_g = sc.ring("stg_g", [128, 3, KC, 128], F32, 2)
            wg_r = sc.ring("wg", [128, 3, KC, 128], BF16, 2)
            stg_b = sc.ring("stg_b", [128, 3, 4, 128], F32, 2)
            wb_r = sc.ring("wb", [128, 3, 4, 128], BF16, 2)
            sgr = sc.ring("sg", [128, TB], F32, 3)
            accr = sc.ring("acc", [128, TB], F32, 2)
            for n in range(3):
                k.dma(yT.ap()[:, n], yT_src[l][n], writes=[("yT", n)])
            for dc in range(KC):
                sg_t, sgk = stg_g.next(); wg, wgk = wg_r.next()
                sb_t, sbk = stg_b.next(); wb, wbk = wb_r.next()
                for n in range(3):
                    c0 = G_OFF + n * D + dc * 128
                    k.dma(sg_t.ap()[:, n], P["w_in"][l][:, c0:c0 + 128].rearrange("(c p) n -> p c n", p=128), writes=[(sgk, n)])
                    k.dma(sb_t.ap()[:, n], P["w_branch"][l, n][:, dc * 128:(dc + 1) * 128].rearrange("(c p) n -> p c n", p=128),
                          writes=[(sbk, n)])
                cast(wg.ap(), sg_t.ap(), [(sgk, n) for n in range(3)], [wgk])
                cast(wb.ap(), sb_t.ap(), [(sbk, n) for n in range(3)], [wbk])
                for tb in range(NTB):
                    acc, acck = accr.next()
                    for n in range(3):
                        pg, pgk = psr.next()
                        for kc in range(KC):
                            op("pe", lambda e: e.matmul(pg.ap(), lhsT=wg.ap()[:, n, kc, :], rhs=uT.ap()[:, kc, usl(tb)],
                                                        start=(kc == 0), stop=(kc == KC - 1)),
                               reads=[wgk] + ukeys(kc, tb), writes=[pgk])
                        pb, pbk = psr.next()
                        for c in range(4):
                            op("pe", lambda e: e.matmul(pb.ap(), lhsT=wb.ap()[:, n, c, :], rhs=yT.ap()[:, n, c, tsl(tb)],
                                                        start=(c == 0), stop=(c == 3)),
                               reads=[wbk, ("yT", n)], writes=[pbk])
                        sg, sgk2 = sgr.next()
                        op("act", lambda e: e.activation(out=sg.ap(), in_=pg.ap(), func=AF.Sigmoid), reads=[pgk], writes=[sgk2])
                        if n == 0:
                            op("dve", lambda e: e.tensor_tensor(out=acc.ap(), in0=sg.ap(), in1=pb.ap(), op=ALU.mult),
                               reads=[sgk2, pbk], writes=[acck])
                        else:
                            op("dve", lambda e: e.tensor_tensor(out=sg.ap(), in0=sg.ap(), in1=pb.ap(), op=ALU.mult),
                               reads=[sgk2, pbk], writes=[sgk2])
                            dstap = mT.ap()[:, dc, tsl(tb)] if n == 2 else acc.ap()
                            op("pool" if n == 1 else "dve", lambda e: e.tensor_tensor(out=dstap, in0=sg.ap(), in1=acc.ap(), op=ALU.add),
                               reads=[sgk2, acck], writes=[("mT", dc, tb)] if n == 2 else [acck])
            stg_o = sc.ring("stg_o", [128, KC, 128], F32, 2)
            wo_r = sc.ring("wo", [128, KC, 128], BF16, 2)
            hr = sc.ring("hr", [128, TB], F32, 3)
            for dc in range(KC):
                wo, wok = wload(P["w_o"][l][:, dc * 128:(dc + 1) * 128], KC, 128, stg_o, wo_r)
                for tb in range(NTB):
                    ht, htk = hr.next()
                    k.dma(ht.ap(), h_dram[:, dc, tsl(tb)], writes=[htk])
                    ps, pk = psr.next()
                    for c in range(KC):
                        op("pe", lambda e: e.matmul(ps.ap(), lhsT=wo.ap()[:, c, :], rhs=mT.ap()[:, c, tsl(tb)],
                                                    start=(c == 0), stop=(c == KC - 1)),
                           reads=[wok, ("mT", c, tb)], writes=[pk])
                    op("dve", lambda e: e.tensor_tensor(out=ht.ap(), in0=ht.ap(), in1=ps.ap(), op=ALU.add),
                       reads=[htk, pk], writes=[htk])
                    k.dma(h_dram[:, dc, tsl(tb)], ht.ap(), reads=[htk], writes=[("h_dram", dc, tb)])
        with Scope(k) as sc:
            hT = sc.sb("hT", [128, KC, S], F32)
            sc2 = Scope(k)
            sc_outer = sc; sc = sc2
            sqr = sc.ring("sq", [128, TB], BF16, 3)
            tmpr = sc.ring("nt", [128, TB], F32, 3)
            load_h(hT)
            rmsnorm(hT, 2 * l + 1, u_dst, sqr, tmpr)
            cw = sc.sb("cw", [128, 3, 2 * NFF], F32)
            cb = sc.sb("cb", [128, 2 * NFF], F32)
            for j in range(3):
                load_pvec(cw.ap()[:, j, :], P["conv_w"][l, j], 2 * NFF, ["cw"])
            load_pvec(cb.ap(), P["conv_b"][l], 2 * NFF, ["cb"])
            G = 2
            stg_u = sc.ring("stg_u", [128, 2, KC, 128], F32, 2)
            wu_r = sc.ring("wu", [128, 2, KC, 128], BF16, 2)
            stg_d = sc.ring("stg_d", [128, G, D], F32, 1)
            wd_r = sc.ring("wd", [128, G, D], BF16, 2)
            upr = sc.ring("up", [128, 2, S + 2], F32, 1)
            cr = sc.ring("cc", [128, 2, TB], F32, 2)
            actr = sc.ring("actT", [128, G, S], BF16, 2)
            up, upk = upr.next()
            op("pool", lambda e: e.memset(up.ap()[:, :, 0:2], 0.0), writes=["uppad"])
            for g0 in range(0, NFF, G):
                actT, actk = actr.next()
                wd, wdk = wload(P["w_down"][l][g0 * 128:(g0 + G) * 128, :], G, D, stg_d, wd_r)
                for gi in range(G):
                    j = g0 + gi
                    su, suk = stg_u.next(); wu, wuk = wu_r.next()
                    for hv in range(2):
                        c0 = hv * DFF + j * 128
                        k.dma(su.ap()[:, hv], P["w_up"][l][:, c0:c0 + 128].rearrange("(c p) n -> p c n", p=128), writes=[(suk, hv)])
                    cast(wu.ap(), su.ap(), [(suk, 0), (suk, 1)], [wuk])
                    for tb in range(NTB):
                        cc, cck = cr.next()
                        for hv in range(2):
                            ps, pk = psr.next()
                            for kc in range(KC):
                                op("pe", lambda e: e.matmul(ps.ap(), lhsT=wu.ap()[:, hv, kc, :], rhs=uT.ap()[:, kc, usl(tb)],
                                                            start=(kc == 0), stop=(kc == KC - 1)),
                                   reads=[wuk] + ukeys(kc, tb), writes=[pk])
                            ch = hv * NFF + j
                            op("act", lambda e: e.copy(out=up.ap()[:, hv, usl(tb)], in_=ps.ap()), reads=[pk, "uppad"], writes=[("up", hv, tb)])
                            op("act", lambda e: e.activation(out=cc.ap()[:, hv, :], in_=ps.ap(), func=AF.Identity,
                                                             scale=cw.ap()[:, 2, ch:ch + 1], bias=cb.ap()[:, ch:ch + 1]),
                               reads=[pk, "cw", "cb"], writes=[(cck, hv)])
                            for sh in (1, 2):
                                rk = [("up", hv, tb)] + ([("up", hv, tb - 1)] if tb > 0 else [])
                                op("dve", lambda e: e.scalar_tensor_tensor(out=cc.ap()[:, hv, :], in0=up.ap()[:, hv, usl(tb, sh)],
                                                                           scalar=cw.ap()[:, 2 - sh, ch:ch + 1], in1=cc.ap()[:, hv, :],
                                                                           op0=ALU.mult, op1=ALU.add),
                                   reads=rk + ["cw", (cck, hv)], writes=[(cck, hv)])
                        op("act", lambda e: e.activation(out=cc.ap()[:, 0, :], in_=cc.ap()[:, 0, :], func=AF.Silu),
                           reads=[(cck, 0)], writes=[(cck, 0)])
                        op("pool", lambda e: e.tensor_tensor(out=actT.ap()[:, gi, tsl(tb)], in0=cc.ap()[:, 0, :], in1=cc.ap()[:, 1, :], op=ALU.mult),
                           reads=[(cck, 0), (cck, 1)], writes=[(actk, gi, tb)])
                for dc in range(KC):
                    for tb in range(NTB):
                        ps, pk = psr.next()
                        for gi in range(G):
                            op("pe", lambda e: e.matmul(ps.ap(), lhsT=wd.ap()[:, gi, dc * 128:(dc + 1) * 128], rhs=actT.ap()[:, gi, tsl(tb)],
                                                        start=(gi == 0), stop=(gi == G - 1)),
                               reads=[wdk, (actk, gi, tb)], writes=[pk])
                        op("dve", lambda e: e.tensor_tensor(out=hT.ap()[:, dc, tsl(tb)], in0=hT.ap()[:, dc, tsl(tb)], in1=ps.ap(), op=ALU.add),
                           reads=hkeys(dc, tb) + [pk], writes=hkeys(dc, tb))
            sc2.__exit__(None, None, None)
            sc = Scope(k)
            sqr = sc.ring("sq", [128, TB], BF16, 3)
            tmpr = sc.ring("nt", [128, TB], F32, 3)
            if l < DEPTH - 1:
                store_h(hT)
            else:
                fo = sc.ring("fo", [128, KC, TB], F32, 1)
                ot = sc.ring("ot", [128, D], F32, 2)
                fT, fk = fo.next()
                cur = [0]

                def f_dst(kc, tb):
                    return fT.ap()[:, kc, :], [("fT", kc)]
                for tb in range(NTB):
                    ps, pk = psr.next()
                    for kc in range(KC):
                        sq, sk = sqr.next()
                        op("act", lambda e: e.activation(out=sq.ap(), in_=hT.ap()[:, kc, tsl(tb)], func=AF.Square),
                           reads=hkeys(kc, tb), writes=[sk])
                        op("pe", lambda e: e.matmul(ps.ap(), lhsT=ones16.ap(), rhs=sq.ap(), start=(kc == 0), stop=(kc == KC - 1)),
                           reads=[sk, "ones16"], writes=[pk])
                    sd, sdk = tmpr.next()
                    op("act", lambda e: e.activation(out=sd.ap(), in_=ps.ap(), func=AF.Ln, bias=epsc.ap(), scale=1.0 / D),
                       reads=[pk, "epsc"], writes=[sdk])
                    rs, rsk = tmpr.next()
                    op("act", lambda e: e.activation(out=rs.ap(), in_=sd.ap(), func=AF.Exp, scale=-0.5), reads=[sdk], writes=[rsk])
                    for kc in range(KC):
                        op("dve", lambda e: e.scalar_tensor_tensor(out=fT.ap()[:, kc, :], in0=hT.ap()[:, kc, tsl(tb)],
                                                                   scalar=gvec.ap()[:, 2 * DEPTH, kc:kc + 1], in1=rs.ap(),
                                                                   op0=ALU.mult, op1=ALU.mult),
                           reads=hkeys(kc, tb) + [rsk, "gvec"], writes=[("fT", kc)])
                    for t4 in range(4):
                        o, okk = ot.next()
                        for half in range(2):
                            ps2, pk2 = psr.next()
                            for j in range(4):
                                kc = half * 4 + j
                                op("pe", lambda e: e.transpose(out=ps2.ap()[:, j * 128:(j + 1) * 128],
                                                               in_=fT.ap()[:, kc, t4 * 128:(t4 + 1) * 128], identity=ident.ap()),
                                   reads=[("fT", kc), "ident"], writes=[pk2])
                            cast(o.ap()[:, half * 512:(half + 1) * 512], ps2.ap(), [pk2], [(okk, half)], eng=("dve", "act")[half])
                        r0 = tb * TB + t4 * 128
                        k.dma(out[r0:r0 + 128, :], o.ap(), reads=[(okk, 0), (okk, 1)], writes=[("out", r0)])
            sc.__exit__(None, None, None)
    k.drain("sp")
    return k


PARAM_NAMES = ["norm_mix_g", "w_in", "a_mu", "a_w0", "a_w_up", "a_a0", "a_a_up", "a_g_up", "a_k_k", "a_k_a", "a_r_k",
               "a_gn_g", "a_gn_b", "b_lb_logits", "b_gn_g", "w_branch", "w_o", "norm_ffn_g", "w_up", "conv_w", "conv_b",
               "w_down", "norm_final_g"]


def make_consts():
    i = np.arange(128)
    low = (i[:, None] >= i[None, :])
    bd = ((i[:, None] // 64) == (i[None, :] // 64))
    hm = np.zeros((128, 128)); hm[:64, 0] = 1; hm[64:, 1] = 1
    tri = np.stack([(i[:, None] <= i[None, :]), (i[:, None] < i[None, :]), low, np.where(low, 0.0, -1e30),
                    (i[:, None] > i[None, :]), bd, hm]).astype(np.float32)
    inv_freq = (np.float32(10000.0) ** (-np.arange(32, dtype=np.float32) * np.float32(2.0 / 64))).astype(np.float32)
    ang = (np.arange(S, dtype=np.float32)[:, None] * inv_freq[None, :]).astype(np.float32)
    cos = np.cos(ang).astype(np.float32).T; sin = np.sin(ang).astype(np.float32).T
    cosF = np.concatenate([cos, cos, cos, cos], axis=0)
    sinF = np.concatenate([-sin, sin, -sin, sin], axis=0)
    return {"c_ident": np.eye(128, dtype=np.float32), "c_tri": tri, "c_rope": np.stack([cosF, sinF]).astype(np.float32)}


def make_in_map(inputs, b, consts):
    m = {"x": np.ascontiguousarray(inputs["x"][b])}
    for n in PARAM_NAMES:
        a = np.ascontiguousarray(np.asarray(inputs[n], dtype=np.float32))
        if n == "a_r_k":
            a = a.reshape(DEPTH, 512)
        m[n] = a
    m.update(consts)
    return m


def kernel(**inputs):
    k = build()
    consts = make_consts()
    in_maps = [make_in_map(inputs, b, consts) for b in range(NB)]
    res = run_bass_kernel_spmd(k.nc, in_maps, core_ids=list(range(NB)))
    return np.stack([np.asarray(r["out"], dtype=np.float32) for r in res.results], axis=0)
```
